# Optimizing a Trainium2 kernel written in Bass

```python
import math
import jax, jax.numpy as jnp
from jax import lax
import numpy as np

D_MODEL = 2048
BATCH = 4
SEQ = 2048
DEPTH = 4
DEC_BATCH = 128
DEC_SEQ = 8
PAST_LEN = 16384
PAGE_SIZE = 128

N_AB = (DEPTH + 1) // 2
N_C = DEPTH // 2
HEAD = 64
W_A = D_MODEL // 2
H_A = W_A // HEAD
LORA_W = 64
LORA_A = 64
LORA_G = 160
COLS_A = 3 * W_A + LORA_W + LORA_A + LORA_G
W_B = D_MODEL // 2
N_BLK_B = W_B // HEAD
CONV_W = 4
LRU_C = 8.0
COLS_AB = COLS_A + 2 * W_B
W_C = D_MODEL
GRP_C = 16
N_GRP_C = W_C // GRP_C
P_C = 64
D_FF = 5632
D_PLE = 256
RMS_EPS = 1e-6
GN_EPS = 64e-5

kernel_name = 'hybrid_rwkv7_rglru_s5_step'


def rmsnorm(x, g):
    xf = x.astype(jnp.float32)
    y = xf * lax.rsqrt(jnp.mean(xf * xf, axis=-1, keepdims=True) + RMS_EPS)
    return (y * g.astype(jnp.float32)).astype(x.dtype)


def swiglu(h, wg, wu, wd):
    return (jax.nn.silu(h @ wg) * (h @ wu)) @ wd


def _linear_combine(e1, e2):
    a1, b1 = e1
    a2, b2 = e2
    return (a1 * a2, a2 * b1 + b2)


def _complex_linear_combine(e1, e2):
    a1r, a1i, b1r, b1i = e1
    a2r, a2i, b2r, b2i = e2
    return (a2r * a1r - a2i * a1i, a2r * a1i + a2i * a1r,
            a2r * b1r - a2i * b1i + b2r, a2r * b1i + a2i * b1r + b2i)


def rwkv7_mix(za, shift_prev, wkv0, mu, w0, w2, a0, a2, g2, k_k, k_a, r_k, lnx_g, lnx_b):
    f32 = jnp.float32
    bsz, t = za.shape[0], za.shape[1]
    za = za.astype(f32)
    zprev = jnp.concatenate([shift_prev[:, None, :].astype(f32), za[:, :-1]], axis=1)
    zs = za + (zprev - za) * mu.astype(f32)
    r, k, v, xw, xa, xg = jnp.split(
        zs, [W_A, 2 * W_A, 3 * W_A, 3 * W_A + LORA_W, 3 * W_A + LORA_W + LORA_A], axis=-1)
    w_log = -jax.nn.softplus(-(w0 + jnp.tanh(xw) @ w2)) - 0.5
    decay = jnp.exp(-jnp.exp(w_log))
    a = jax.nn.sigmoid(a0 + xa @ a2)
    g = jax.nn.sigmoid(xg) @ g2
    kk = k * k_k
    k = k * (1.0 + (a - 1.0) * k_a)
    heads = lambda u: u.reshape(bsz, t, H_A, HEAD)
    kk = heads(kk)
    kk = kk / jnp.maximum(jnp.sqrt(jnp.sum(kk * kk, axis=-1, keepdims=True)), 1e-12)
    r, k, v, decay, a = heads(r), heads(k), heads(v), heads(decay), heads(a)

    def step(S, inp):
        r_t, w_t, k_t, v_t, kk_t, a_t = inp
        sa = jnp.einsum('bhij,bhj->bhi', S, -kk_t)
        S = (S * w_t[:, :, None, :] + sa[..., None] * (kk_t * a_t)[:, :, None, :]
             + v_t[..., None] * k_t[:, :, None, :])
        return S, jnp.einsum('bhij,bhj->bhi', S, r_t)

    tmaj = lambda u: jnp.swapaxes(u, 0, 1)
    s_last, y = lax.scan(step, wkv0.astype(f32), tuple(tmaj(u) for u in (r, decay, k, v, kk, a)))
    y = tmaj(y)
    mu_y = jnp.mean(y, axis=-1, keepdims=True)
    var = jnp.mean(jnp.square(y - mu_y), axis=-1, keepdims=True)
    yn = ((y - mu_y) * lax.rsqrt(var + GN_EPS)).reshape(bsz, t, W_A) * lnx_g + lnx_b
    bonus = (jnp.sum(r * k * r_k, axis=-1, keepdims=True) * v).reshape(bsz, t, W_A)
    return (yn + bonus) * g, s_last, za[:, -1]


def rglru_mix(xb, gb, h0, conv_buf, conv_w, conv_b, wa, ba, wx, bx, lam):
    f32 = jnp.float32
    bsz, t = xb.shape[0], xb.shape[1]
    xpad = jnp.concatenate([conv_buf.astype(f32), xb.astype(f32)], axis=1)
    xc = conv_b.astype(f32)
    for j in range(CONV_W):
        xc = xc + xpad[:, j:j + t] * conv_w[j]
    blk = xc.reshape(bsz, t, N_BLK_B, HEAD)
    gate_r = jax.nn.sigmoid(jnp.einsum('btnh,nhk->btnk', blk, wa).reshape(bsz, t, W_B) + ba)
    gate_i = jax.nn.sigmoid(jnp.einsum('btnh,nhk->btnk', blk, wx).reshape(bsz, t, W_B) + bx)
    log_a = -LRU_C * gate_r * jax.nn.softplus(-lam)
    a = jnp.exp(log_a)
    b = jnp.sqrt(-jnp.expm1(2.0 * log_a)) * (gate_i * xc)
    b = b.at[:, 0].add(a[:, 0] * h0.astype(f32))
    _, h = lax.associative_scan(_linear_combine, (a, b), axis=1)
    y = h * jax.nn.gelu(gb.astype(f32))
    return y, h[:, -1], xpad[:, xpad.shape[1] - (CONV_W - 1):]


def s5_mix(u, h0_re, h0_im, a_re, a_im, log_dt, b_re, b_im, c_re, c_im, d_skip, w_glu, b_glu):
    f32 = jnp.float32
    bsz, t = u.shape[0], u.shape[1]
    uf = u.astype(f32)
    ug = uf.reshape(bsz, t, N_GRP_C, GRP_C)
    a_re = a_re.astype(f32)
    a_im = a_im.astype(f32)
    dt = jnp.exp(log_dt.astype(f32))[:, None]
    mag = jnp.exp(dt * a_re)
    ab_re, ab_im = mag * jnp.cos(dt * a_im), mag * jnp.sin(dt * a_im)
    den = a_re * a_re + a_im * a_im
    f_re = ((ab_re - 1.0) * a_re + ab_im * a_im) / den
    f_im = (ab_im * a_re - (ab_re - 1.0) * a_im) / den
    bb_re = f_re[..., None] * b_re - f_im[..., None] * b_im
    bb_im = f_re[..., None] * b_im + f_im[..., None] * b_re
    bu_re = jnp.einsum('btgc,gpc->btgp', ug, bb_re)
    bu_im = jnp.einsum('btgc,gpc->btgp', ug, bb_im)
    h0_re = h0_re.astype(f32)
    h0_im = h0_im.astype(f32)
    bu_re = bu_re.at[:, 0].add(ab_re * h0_re - ab_im * h0_im)
    bu_im = bu_im.at[:, 0].add(ab_re * h0_im + ab_im * h0_re)
    shp = (1, t, N_GRP_C, P_C)
    _, _, h_re, h_im = lax.associative_scan(
        _complex_linear_combine,
        (jnp.broadcast_to(ab_re, shp), jnp.broadcast_to(ab_im, shp), bu_re, bu_im), axis=1)
    y = (jnp.einsum('btgp,gcp->btgc', h_re, c_re)
         - jnp.einsum('btgp,gcp->btgc', h_im, c_im)).reshape(bsz, t, W_C) + d_skip * uf
    z = jax.nn.gelu(y)
    return z * jax.nn.sigmoid(z @ w_glu + b_glu), h_re[:, -1], h_im[:, -1]


def trunk(x, p, st_wkv, st_shift, st_h, st_conv, st_cre, st_cim, w):
    new_wkv, new_shift, new_h, new_conv, new_cre, new_cim = [], [], [], [], [], []
    for l in range(DEPTH):
        j = l // 2
        x = x + 0.5 * swiglu(rmsnorm(x, w['norm_ffn1'][l]), w['ffn1_wg'][l], w['ffn1_wu'][l], w['ffn1_wd'][l])
        hmix = rmsnorm(x, w['norm_mix'][l])
        if l % 2 == 0:
            z = hmix @ w['w_in_ab'][j]
            za, zbx, zbg = jnp.split(z, [COLS_A, COLS_A + W_B], axis=-1)
            ya, s_wkv, s_shift = rwkv7_mix(
                za, st_shift[j], st_wkv[j], w['mu_a'][j], w['w0_a'][j], w['w2_a'][j], w['a0_a'][j],
                w['a2_a'][j], w['g2_a'][j], w['kk_a'][j], w['ka_a'][j], w['rk_a'][j],
                w['lnx_g'][j], w['lnx_b'][j])
            yb, s_h, s_conv = rglru_mix(
                zbx, zbg, st_h[j], st_conv[j], w['conv_w_b'][j], w['conv_b_b'][j], w['wa_b'][j],
                w['ba_b'][j], w['wx_b'][j], w['bx_b'][j], w['lam_b'][j])
            x = x + jnp.concatenate([ya, yb], axis=-1).astype(x.dtype) @ w['w_out_ab'][j]
            new_wkv.append(s_wkv)
            new_shift.append(s_shift)
            new_h.append(s_h)
            new_conv.append(s_conv)
        else:
            yc, s_re, s_im = s5_mix(
                hmix, st_cre[j], st_cim[j], w['a_re_c'][j], w['a_im_c'][j], w['log_dt_c'][j],
                w['b_re_c'][j], w['b_im_c'][j], w['c_re_c'][j], w['c_im_c'][j], w['d_c'][j],
                w['w_glu_c'][j], w['b_glu_c'][j])
            x = x + yc.astype(x.dtype)
            new_cre.append(s_re)
            new_cim.append(s_im)
        x = x + 0.5 * swiglu(rmsnorm(x, w['norm_ffn2'][l]), w['ffn2_wg'][l], w['ffn2_wu'][l], w['ffn2_wd'][l])
        gate = jax.nn.sigmoid(rmsnorm(x, w['norm_ple'][l]) @ w['ple_gate'][l])
        x = x + gate * (p[l] @ w['ple_proj'][l])
    y = rmsnorm(x, w['final_norm'])
    stk = lambda lst, ref: jnp.stack(lst).astype(ref.dtype)
    return y, (stk(new_wkv, st_wkv), stk(new_shift, st_shift), stk(new_h, st_h),
               stk(new_conv, st_conv), stk(new_cre, st_cre), stk(new_cim, st_cim))


def setup_inputs(seed: int = 0) -> dict:
    key = jax.random.key(seed)
    keys = jax.random.split(key, 96)
    counter = [0]
    f32 = jnp.float32

    def nk():
        k = keys[counter[0]]
        counter[0] += 1
        return k

    nrm = lambda shape, scale: jax.random.normal(nk(), shape, f32) * scale
    gain = lambda shape: 1.0 + 0.01 * jax.random.normal(nk(), shape, f32)
    unif = lambda shape, lo, hi: jax.random.uniform(nk(), shape, f32, lo, hi)

    d = {}
    d['x_prompt'] = nrm((BATCH, SEQ, D_MODEL), 1.0)
    d['x_sample'] = nrm((DEC_BATCH, DEC_SEQ, D_MODEL), 1.0)
    d['state_a_wkv'] = nrm((N_AB, DEC_BATCH, H_A, HEAD, HEAD), 0.5)
    d['state_a_shift'] = nrm((N_AB, DEC_BATCH, COLS_A), 1.0)
    d['state_b_h'] = nrm((N_AB, DEC_BATCH, W_B), 0.5)
    d['state_b_conv'] = nrm((N_AB, DEC_BATCH, CONV_W - 1, W_B), 1.0)
    d['state_c_re'] = nrm((N_C, DEC_BATCH, N_GRP_C, P_C), 0.1)
    d['state_c_im'] = nrm((N_C, DEC_BATCH, N_GRP_C, P_C), 0.1)
    d['p_prompt'] = nrm((DEPTH, BATCH, SEQ, D_PLE), 1.0)
    d['p_sample'] = nrm((DEPTH, DEC_BATCH, DEC_SEQ, D_PLE), 1.0)
    d['norm_ffn1'] = gain((DEPTH, D_MODEL))
    d['ffn1_wg'] = nrm((DEPTH, D_MODEL, D_FF), D_MODEL ** -0.5)
    d['ffn1_wu'] = nrm((DEPTH, D_MODEL, D_FF), D_MODEL ** -0.5)
    d['ffn1_wd'] = nrm((DEPTH, D_FF, D_MODEL), D_FF ** -0.5)
    d['norm_mix'] = gain((DEPTH, D_MODEL))
    d['norm_ffn2'] = gain((DEPTH, D_MODEL))
    d['ffn2_wg'] = nrm((DEPTH, D_MODEL, D_FF), D_MODEL ** -0.5)
    d['ffn2_wu'] = nrm((DEPTH, D_MODEL, D_FF), D_MODEL ** -0.5)
    d['ffn2_wd'] = nrm((DEPTH, D_FF, D_MODEL), D_FF ** -0.5)
    d['norm_ple'] = gain((DEPTH, D_MODEL))
    d['ple_gate'] = nrm((DEPTH, D_MODEL, D_MODEL), D_MODEL ** -0.5)
    d['ple_proj'] = nrm((DEPTH, D_PLE, D_MODEL), D_PLE ** -0.5)
    d['w_in_ab'] = nrm((N_AB, D_MODEL, COLS_AB), D_MODEL ** -0.5)
    d['mu_a'] = unif((N_AB, COLS_A), 0.0, 1.0)
    ramp = jnp.arange(W_A, dtype=f32) / (W_A - 1)
    d['w0_a'] = (-6.5 + 5.0 * ramp)[None, :] + nrm((N_AB, W_A), 0.1)
    d['w2_a'] = nrm((N_AB, LORA_W, W_A), 0.1 * LORA_W ** -0.5)
    d['a0_a'] = nrm((N_AB, W_A), 0.1)
    d['a2_a'] = nrm((N_AB, LORA_A, W_A), LORA_A ** -0.5)
    d['g2_a'] = nrm((N_AB, LORA_G, W_A), LORA_G ** -0.5)
    d['kk_a'] = 0.85 + nrm((N_AB, W_A), 0.02)
    d['ka_a'] = 1.0 + nrm((N_AB, W_A), 0.02)
    d['rk_a'] = nrm((N_AB, H_A, HEAD), 0.1)
    d['lnx_g'] = gain((N_AB, W_A))
    d['lnx_b'] = nrm((N_AB, W_A), 0.01)
    d['conv_w_b'] = nrm((N_AB, CONV_W, W_B), CONV_W ** -0.5)
    d['conv_b_b'] = nrm((N_AB, W_B), 0.01)
    d['wa_b'] = nrm((N_AB, N_BLK_B, HEAD, HEAD), HEAD ** -0.5)
    d['ba_b'] = nrm((N_AB, W_B), 0.01)
    d['wx_b'] = nrm((N_AB, N_BLK_B, HEAD, HEAD), HEAD ** -0.5)
    d['bx_b'] = nrm((N_AB, W_B), 0.01)
    a_pow = unif((N_AB, W_B), 0.9, 0.999) ** (1.0 / LRU_C)
    d['lam_b'] = jnp.log(a_pow) - jnp.log1p(-a_pow)
    d['w_out_ab'] = nrm((N_AB, W_A + W_B, D_MODEL), (W_A + W_B) ** -0.5)
    d['a_re_c'] = -0.5 * jnp.exp(nrm((N_C, N_GRP_C, P_C), 0.05))
    d['a_im_c'] = math.pi * jnp.arange(P_C, dtype=f32)[None, None, :] + nrm((N_C, N_GRP_C, P_C), 0.01)
    d['log_dt_c'] = unif((N_C, N_GRP_C), math.log(1e-3), math.log(1e-1))
    d['b_re_c'] = nrm((N_C, N_GRP_C, P_C, GRP_C), (2 * GRP_C) ** -0.5)
    d['b_im_c'] = nrm((N_C, N_GRP_C, P_C, GRP_C), (2 * GRP_C) ** -0.5)
    d['c_re_c'] = nrm((N_C, N_GRP_C, GRP_C, P_C), P_C ** -0.5)
    d['c_im_c'] = nrm((N_C, N_GRP_C, GRP_C, P_C), P_C ** -0.5)
    d['d_c'] = nrm((N_C, W_C), 1.0)
    d['w_glu_c'] = nrm((N_C, W_C, W_C), W_C ** -0.5)
    d['b_glu_c'] = nrm((N_C, W_C), 0.01)
    d['final_norm'] = gain((D_MODEL,))
    return d


def reference(x_prompt, x_sample, state_a_wkv, state_a_shift, state_b_h, state_b_conv,
              state_c_re, state_c_im, p_prompt, p_sample,
              norm_ffn1, ffn1_wg, ffn1_wu, ffn1_wd, norm_mix, norm_ffn2, ffn2_wg, ffn2_wu, ffn2_wd,
              norm_ple, ple_gate, ple_proj,
              w_in_ab, mu_a, w0_a, w2_a, a0_a, a2_a, g2_a, kk_a, ka_a, rk_a, lnx_g, lnx_b,
              conv_w_b, conv_b_b, wa_b, ba_b, wx_b, bx_b, lam_b, w_out_ab,
              a_re_c, a_im_c, log_dt_c, b_re_c, b_im_c, c_re_c, c_im_c, d_c, w_glu_c, b_glu_c,
              final_norm):
    w = dict(norm_ffn1=norm_ffn1, ffn1_wg=ffn1_wg, ffn1_wu=ffn1_wu, ffn1_wd=ffn1_wd,
             norm_mix=norm_mix, norm_ffn2=norm_ffn2, ffn2_wg=ffn2_wg, ffn2_wu=ffn2_wu,
             ffn2_wd=ffn2_wd, norm_ple=norm_ple, ple_gate=ple_gate, ple_proj=ple_proj,
             w_in_ab=w_in_ab, mu_a=mu_a, w0_a=w0_a, w2_a=w2_a, a0_a=a0_a, a2_a=a2_a, g2_a=g2_a,
             kk_a=kk_a, ka_a=ka_a, rk_a=rk_a, lnx_g=lnx_g, lnx_b=lnx_b,
             conv_w_b=conv_w_b, conv_b_b=conv_b_b, wa_b=wa_b, ba_b=ba_b, wx_b=wx_b, bx_b=bx_b,
             lam_b=lam_b, w_out_ab=w_out_ab,
             a_re_c=a_re_c, a_im_c=a_im_c, log_dt_c=log_dt_c, b_re_c=b_re_c, b_im_c=b_im_c,
             c_re_c=c_re_c, c_im_c=c_im_c, d_c=d_c, w_glu_c=w_glu_c, b_glu_c=b_glu_c,
             final_norm=final_norm)
    bp = x_prompt.shape[0]
    z_wkv = jnp.zeros((N_AB, bp, H_A, HEAD, HEAD), state_a_wkv.dtype)
    z_shift = jnp.zeros((N_AB, bp, COLS_A), state_a_shift.dtype)
    z_h = jnp.zeros((N_AB, bp, W_B), state_b_h.dtype)
    z_conv = jnp.zeros((N_AB, bp, CONV_W - 1, W_B), state_b_conv.dtype)
    z_cre = jnp.zeros((N_C, bp, N_GRP_C, P_C), state_c_re.dtype)
    z_cim = jnp.zeros((N_C, bp, N_GRP_C, P_C), state_c_im.dtype)
    y_prompt, (pw, ps, ph, pc, pre, pim) = trunk(
        x_prompt, p_prompt, z_wkv, z_shift, z_h, z_conv, z_cre, z_cim, w)
    y_sample, (sw, ss, sh, sc, sre, sim) = trunk(
        x_sample, p_sample, state_a_wkv, state_a_shift, state_b_h, state_b_conv,
        state_c_re, state_c_im, w)
    return (y_prompt, y_sample, pw, ps, ph, pc, pre, pim, sw, ss, sh, sc, sre, sim)
```

```python
import contextlib
import numpy as np
import concourse.bass as bass
import concourse.mybir as mybir
from concourse.bass_utils import run_bass_kernel_spmd

F32 = mybir.dt.float32
BF16 = mybir.dt.bfloat16
I32 = mybir.dt.int32
ALU = mybir.AluOpType
AF = mybir.ActivationFunctionType
AX = mybir.AxisListType
import os as _os
EPOCH_N = int(_os.environ.get("KEPOCH", "30000"))


class Reg:
    __slots__ = ("name", "writes", "reads", "dsem", "dcnt")

    def __init__(self, name):
        self.name = name
        self.writes = []
        self.reads = []
        self.dsem = None
        self.dcnt = 0


class Tile:
    def __init__(self, t, name):
        self.t = t
        self.name = name
        self.reg = Reg(name)
        self.subs = {}

    def __getitem__(self, idx):
        return self.t[idx]

    def r(self, key=None):
        if key is None:
            return self.reg
        if key not in self.subs:
            self.subs[key] = Reg("%s/%s" % (self.name, key))
        return self.subs[key]

    def all(self):
        return [self.reg] + list(self.subs.values())


class K:
    def __init__(self):
        self.nc = bass.Bass("TRN2", target_bir_lowering=False)
        nc = self.nc
        self.es = contextlib.ExitStack()
        self.eng = {"pe": nc.tensor, "act": nc.scalar, "dve": nc.vector,
                    "pool": nc.gpsimd, "sp": nc.sync}
        self.sem = {}
        self.cnt = {}
        self.epoch = {}
        for e in ("pe", "act", "dve", "pool"):
            self.sem[(e, 0)] = self.es.enter_context(nc.semaphore("s_" + e))
            self.cnt[e] = 0
            self.epoch[e] = 0
        self.waited = {e: {} for e in self.eng}
        self.nsem = 4
        self.ninst = 0
        self.dregs = []

    def sb(self, name, shape, dt=F32):
        t = self.es.enter_context(self.nc.sbuf_tensor(name, list(shape), dt))
        return Tile(t, name)

    def ps(self, name, shape, dt=F32):
        t = self.es.enter_context(self.nc.psum_tensor(name, list(shape), dt))
        return Tile(t, name)

    def dram(self, name, shape, dt=F32, kind="Internal"):
        t = self.nc.dram_tensor(name, list(shape), dt, kind=kind)
        return Tile(t.ap(), name)

    def _need(self, e, ev):
        key, val, src = ev
        if src == e and e == "pe":
            return
        w = self.waited[e]
        if w.get(key, 0) >= val:
            return
        sem = self.sem[key] if isinstance(key, tuple) else key.dsem
        self.eng[e].wait_ge(sem, val)
        w[key] = val
        self.ninst += 1

    def _deps(self, e, R, W, acc=False):
        for r in R:
            for ev in r.writes:
                self._need(e, ev)
        for r in W:
            for ev in r.writes:
                if ev[2] != e:
                    self._need(e, ev)
            for ev in r.reads:
                if ev[2] != e:
                    self._need(e, ev)

    def _mark(self, ev, R, W, acc=False):
        for r in R:
            r.reads = [x for x in r.reads if x[0] != ev[0]] + [ev]
        for r in W:
            if acc:
                r.writes = [x for x in r.writes if x[0] != ev[0]] + [ev]
            else:
                r.writes = [ev]
            r.reads = []

    def op(self, e, fn, R=(), W=(), acc=False):
        self._deps(e, R, W)
        ins = fn(self.eng[e])
        if self.cnt[e] >= EPOCH_N:
            self.epoch[e] += 1
            self.cnt[e] = 0
            self.sem[(e, self.epoch[e])] = self.es.enter_context(
                self.nc.semaphore("s_%s_%d" % (e, self.epoch[e])))
        self.cnt[e] += 1
        key = (e, self.epoch[e])
        ins.then_inc(self.sem[key], 1)
        ev = (key, self.cnt[e], e)
        self._mark(ev, R, W, acc)
        self.ninst += 1
        return ins

    def dma(self, q, out, in_, R=(), W=(), acc=True, **kw):
        self._deps(q, R, W)
        d = W[0]
        if d.dsem is None:
            d.dsem = self.es.enter_context(self.nc.semaphore("d%d" % self.nsem))
            self.nsem += 1
            self.dregs.append(d)
        ins = self.eng[q].dma_start(out=out, in_=in_, **kw)
        d.dcnt += 16
        ins.then_inc(d.dsem, 16)
        ev = (d, d.dcnt, "dma")
        self._mark(ev, R, W, acc)
        self.ninst += 1
        return ins

    def barrier(self):
        evs = [((e, self.epoch[e]), self.cnt[e], e) for e in ("pe", "act", "dve", "pool")]
        evs += [(d, d.dcnt, "dma") for d in self.dregs]
        for e in ("pe", "act", "dve", "pool", "sp"):
            for ev in evs:
                if ev[1] > 0 and not (ev[2] == e and e == "pe"):
                    w = self.waited[e]
                    if w.get(ev[0], 0) < ev[1]:
                        sem = self.sem[ev[0]] if isinstance(ev[0], tuple) else ev[0].dsem
                        self.eng[e].wait_ge(sem, ev[1])
                        w[ev[0]] = ev[1]
                        self.ninst += 1

    def finish(self, regs):
        for r in regs:
            for ev in r.writes:
                self._need("sp", ev)

    def close(self):
        self.es.close()

D = 2048
DFF = 5632
KC = 16
NPT = 1024
NS = 16
TS = 8
NTOK = 2048 + NS * TS
EPS = 1e-6
import os
DEPTH = int(os.environ.get('KDEPTH', '4'))
NPASS = int(os.environ.get('KNPASS', '2'))


def vec_layout():
    lay = {}
    off = [0]

    def add(name, n):
        lay[name] = (off[0], n)
        off[0] += n
    for l in range(4):
        for nm in ("norm_ffn1", "norm_mix", "norm_ffn2", "norm_ple"):
            add("%s%d" % (nm, l), 16)
    add("final_norm", 16)
    for j in range(2):
        add("d_c%d" % j, 16)
        add("b_glu_c%d" % j, 16)
    for j in range(2):
        for nm in ("w0_a", "a0_a", "kk_a", "ka_a", "rk_a", "lnx_g", "lnx_b", "conv_b_b",
                   "ba_b", "bx_b", "lam_b", "mu_r", "mu_k", "mu_v"):
            add("%s%d" % (nm, j), 8)
        for q in range(4):
            add("conv_w%d_%d" % (q, j), 8)
        add("mu_x%d" % j, 4)
    return lay, off[0]


def pack_vec(inp):
    lay, n = vec_layout()
    v = np.zeros((128, n), np.float32)

    def put(name, arr):
        o, c = lay[name]
        a = np.asarray(arr, np.float32).reshape(-1)
        v[:, o:o + c] = a.reshape(c, 128).T
    for l in range(4):
        for nm in ("norm_ffn1", "norm_mix", "norm_ffn2", "norm_ple"):
            put("%s%d" % (nm, l), inp[nm][l])
    put("final_norm", inp["final_norm"])
    for j in range(2):
        put("d_c%d" % j, inp["d_c"][j])
        put("b_glu_c%d" % j, inp["b_glu_c"][j])
        for nm in ("w0_a", "a0_a", "kk_a", "ka_a", "rk_a", "lnx_g", "lnx_b", "conv_b_b",
                   "ba_b", "bx_b", "lam_b"):
            put("%s%d" % (nm, j), inp[nm][j])
        mu = np.asarray(inp["mu_a"][j], np.float32)
        put("mu_r%d" % j, mu[0:1024])
        put("mu_k%d" % j, mu[1024:2048])
        put("mu_v%d" % j, mu[2048:3072])
        for q in range(4):
            put("conv_w%d_%d" % (q, j), inp["conv_w_b"][j][q])
        o, _ = lay["mu_x%d" % j]
        v[:, o] = mu[3072:3200]
        v[:, o + 2] = mu[3200:3328]
        v[0:32, o + 3] = mu[3328:3360]
    return v


CST = {"ident": (0, 128), "iota": (128, 1024), "mask128": (1152, 128), "J": (1280, 128),
       "blk": (1408, 128), "sgn": (1536, 4), "rep": (1540, 128), "rowm": (1668, 8), "one": (1676, 1), "negpi": (1677, 1), "halfpi": (1678, 1)}
NCST = 1680
GN_EPS = 64e-5
W_A = 1024
COLS_A = 3360


def pack_cst():
    c = np.zeros((128, NCST), np.float32)

    def put(nm, arr):
        o, n = CST[nm]
        c[:arr.shape[0], o:o + n] = arr
    put("ident", np.eye(128, dtype=np.float32))
    put("iota", np.broadcast_to(np.arange(1, 1025, dtype=np.float32)[None, :], (128, 1024)))
    m = np.ones(128, np.float32)
    m[0::8] = 0.0
    put("mask128", np.broadcast_to(m[None, :], (128, 128)))
    J = np.zeros((128, 128), np.float32)
    for p in range(64):
        J[64 + p, p] = -1.0
        J[p, 64 + p] = 1.0
    put("J", J)
    blk = np.zeros((128, 128), np.float32)
    blk[0:64, 0:64] = 1.0
    blk[64:128, 64:128] = 1.0
    put("blk", blk)
    sg = np.zeros((128, 4), np.float32)
    sg[0:64, 0] = -1.0
    sg[64:, 0] = 1.0
    sg[0:64, 1] = 1.0
    sg[64:, 1] = -1.0
    sg[:, 2] = -1.0
    sg[:, 3] = 1.0
    put("sgn", sg)
    rep = np.zeros((16, 128), np.float32)
    for hh in range(16):
        rep[hh, hh * 8:hh * 8 + 8] = 1.0
    put("rep", rep)
    rm = np.zeros((128, 8), np.float32)
    for p in range(128):
        rm[p, p // 16] = 1.0
    put("rowm", rm)
    put("one", np.ones((128, 1), np.float32))
    put("negpi", np.full((128, 1), -np.pi, np.float32))
    put("halfpi", np.full((128, 1), 0.5 * np.pi, np.float32))
    return c


def pack_states(inp, core):
    f = lambda a: np.ascontiguousarray(np.asarray(a, np.float32))
    sl = slice(core * NS, (core + 1) * NS)
    o = {}
    o["st_h"] = f(np.asarray(inp["state_b_h"])[:, sl].reshape(2, NS, 8, 128).transpose(0, 3, 2, 1))
    o["st_conv"] = f(np.asarray(inp["state_b_conv"])[:, sl].reshape(2, NS, 3, 8, 128).transpose(0, 4, 3, 1, 2))
    re = np.asarray(inp["state_c_re"])[:, sl].transpose(0, 3, 2, 1)
    im = np.asarray(inp["state_c_im"])[:, sl].transpose(0, 3, 2, 1)
    o["st_s5"] = f(np.concatenate([re, im], 1))
    sh = np.asarray(inp["state_a_shift"], np.float32)[:, sl]
    shp = np.zeros((2, NS, 27 * 128), np.float32)
    shp[:, :, 0:3360] = sh
    o["st_shift"] = f(shp.reshape(2, NS, 27, 128).transpose(0, 3, 2, 1))
    wk = np.asarray(inp["state_a_wkv"], np.float32)[:, sl].reshape(2, NS, 16, 8, 8, 64)
    o["st_wkv"] = f(wk.transpose(0, 2, 3, 1, 4, 5).reshape(2, 128, NS, 512))
    for nm in ("w2_a", "a2_a", "g2_a"):
        o[nm] = f(inp[nm])
    for nm in ("wa_b", "wx_b"):
        w = np.asarray(inp[nm], np.float32)
        bd = np.zeros((2, 8, 128, 128), np.float32)
        for c in range(8):
            bd[:, c, 0:64, 0:64] = w[:, 2 * c]
            bd[:, c, 64:128, 64:128] = w[:, 2 * c + 1]
        o[nm + "d"] = f(bd.transpose(0, 2, 1, 3))
    are = np.asarray(inp["a_re_c"], np.float32).transpose(0, 2, 1)
    aim = np.asarray(inp["a_im_c"], np.float32).transpose(0, 2, 1)
    o["s5_are"] = f(np.concatenate([are, are], 1))
    o["s5_aim"] = f(np.concatenate([aim, aim], 1))
    o["s5_ldt"] = f(np.broadcast_to(np.asarray(inp["log_dt_c"], np.float32)[:, None, :], (2, 128, 128)))
    bre = np.asarray(inp["b_re_c"], np.float32).transpose(0, 2, 1, 3)
    bim = np.asarray(inp["b_im_c"], np.float32).transpose(0, 2, 1, 3)
    o["s5_b1"] = f(np.concatenate([bre, bim], 1))
    o["s5_b2"] = f(np.concatenate([bim, bre], 1))
    cre = np.asarray(inp["c_re_c"], np.float32).transpose(0, 3, 1, 2)
    cim = np.asarray(inp["c_im_c"], np.float32).transpose(0, 3, 1, 2)
    o["s5_c1"] = f(np.concatenate([cre, cim], 1))
    o["s5_c2"] = f(np.concatenate([cim, cre], 1))
    return o


class Prog:
    def __init__(self, enable=("ffn", "ple", "mix")):
        self.enable = enable
        k = self.k = K()
        nc = self.nc = k.nc
        self.lay, self.nv = vec_layout()

        def din(name, shape):
            return nc.dram_tensor(name, list(shape), F32, kind="ExternalInput").ap()
        self.xT = din("xT", [D, NTOK])
        self.pT = din("pT", [4, 256, NTOK])
        self.vecd = din("vec", [128, self.nv])
        self.W = {}
        for nm, shp in (("ffn1_wg", [4, D, DFF]), ("ffn1_wu", [4, D, DFF]), ("ffn1_wd", [4, DFF, D]),
                        ("ffn2_wg", [4, D, DFF]), ("ffn2_wu", [4, D, DFF]), ("ffn2_wd", [4, DFF, D]),
                        ("ple_gate", [4, D, D]), ("ple_proj", [4, 256, D])):
            self.W[nm] = din(nm, shp)
        for nm, shp in (("w_in_ab", [2, D, 5408]), ("w_out_ab", [2, D, D]), ("w_glu_c", [2, D, D])):
            self.W[nm] = din(nm, shp)
        self.cstd = din("cst", [128, NCST])
        self.I = {}
        for nm, shp in (("st_h", [2, 128, 8, 16]), ("st_conv", [2, 128, 8, 16, 3]), ("st_s5", [2, 128, 128, 16]),
                        ("wa_bd", [2, 128, 8, 128]), ("wx_bd", [2, 128, 8, 128]),
                        ("s5_are", [2, 128, 128]), ("s5_aim", [2, 128, 128]), ("s5_ldt", [2, 128, 128]),
                        ("s5_b1", [2, 128, 128, 16]), ("s5_b2", [2, 128, 128, 16]),
                        ("s5_c1", [2, 128, 128, 16]), ("s5_c2", [2, 128, 128, 16])):
            self.I[nm] = din(nm, shp)
        for nm, shp in (("w2_a", [2, 64, 1024]), ("a2_a", [2, 64, 1024]), ("g2_a", [2, 160, 1024]),
                        ("st_shift", [2, 128, 27, 16]), ("st_wkv", [2, 128, 16, 512])):
            self.I[nm] = din(nm, shp)
        self.o_shift = k.dram("o_shift", [2, 128, 27, 17], F32, kind="ExternalOutput")
        self.o_wkv = k.dram("o_wkv", [2, 128, 17, 512], F32, kind="ExternalOutput")
        self.Zs = k.dram("Zs", [1152, 16, 5, 64], F32)
        self.Vs = k.dram("Vs", [1152, 1024], F32)
        self.Ys = k.dram("Ys", [1152, 1024], F32)
        self.Bon = k.dram("Bon", [8, 128, 1152], F32)
        self.yT = k.dram("yT", [D, NTOK], F32, kind="ExternalOutput")
        self.o_h = k.dram("o_h", [2, 128, 8, 17], F32, kind="ExternalOutput")
        self.o_conv = k.dram("o_conv", [2, 128, 8, 17, 3], F32, kind="ExternalOutput")
        self.o_s5 = k.dram("o_s5", [2, 128, 128, 17], F32, kind="ExternalOutput")
        self.outs = [self.yT, self.o_h, self.o_conv, self.o_s5, self.o_shift, self.o_wkv]

        self.x = k.sb("x", [128, KC, 1152], F32)
        self.xn = k.sb("xn", [128, KC, 1152], BF16)
        self.rstd = k.sb("rstd", [128, 1152], F32)
        self.sq = [k.sb("sq%d" % i, [128, 1152], BF16) for i in range(2)]
        self.vec = k.sb("vecs", [128, self.nv], F32)
        self.ones_bf = k.sb("ones_bf", [128, 128], BF16)
        self.AW = 18688
        self._avc = {}
        self.A = k.sb("arena", [128, self.AW], F32)
        self.slab = [self.av("slab%d" % i, i * 2048, 2048, BF16) for i in range(6)]
        self.h = self.av("hbuf", 12288, 1152, BF16, a=2)
        self.t1 = [self.av("t1_%d" % i, 13440 + i * 512, 512) for i in range(2)]
        self.t2 = [self.av("t2_%d" % i, 14464 + i * 512, 512) for i in range(2)]
        self.pt = self.av("ptile", 15488, 1152, BF16, a=2)
        self.pproj = self.av("pproj", 16640, 2048, BF16)
        self.ytmp = [self.av("ytmp%d" % i, i * 1152, 1152) for i in range(2)]
        self.nslab = 6
        self.bank = [k.ps("bank%d" % i, [128, 512], F32) for i in range(8)]
        self.lru_small = (k.sb("l_hst", [128, 8, 17], F32), k.sb("l_cvo", [128, 8, 17, 3], F32),
                          k.sb("l_h0", [128, 8, 17], F32), k.sb("l_cv0", [128, 8, 17, 3], F32))
        self.lru_nsp = k.sb("l_nsp", [128, 16], F32)
        self.sg1b = k.sb("r_sg1b", [128, 1152], BF16)
        self.ybuf = self.av("ybuf", 4096, 4608, BF16, a=8)
        self.rr = {}

        k.dma("sp", self.vec[:], self.vecd, W=[self.vec.r()])
        self.cst = k.sb("cst_sb", [128, NCST], F32)
        k.dma("sp", self.cst[:], self.cstd, W=[self.cst.r()])
        k.op("dve", lambda e: e.memset(self.ones_bf[:], 1.0), W=[self.ones_bf.r()])
        self.epst = k.sb("epst", [128, 2], F32)
        k.op("dve", lambda e: e.memset(self.epst[:], EPS), W=[self.epst.r()])

        for ps in range(NPASS):
            self.run_pass(ps)
        k.finish([o.r() for o in self.outs])
        k.close()

    def av(self, name, off, n, dt=F32, a=None):
        assert off + n <= self.AW, (name, off, n)
        key = (name, off, n, str(dt), a)
        if key in self._avc:
            return self._avc[key]
        ap = self.A.t[:, off:off + n]
        if dt != F32:
            ap = ap.bitcast(dt)
        if a is not None:
            ap = ap.rearrange("p (a b) -> p a b", a=a)
        self._avc[key] = Tile(ap, name)
        return self._avc[key]

    def rot(self, name, n):
        i = self.rr.get(name, 0)
        self.rr[name] = i + 1
        return i % n

    def vcol(self, name, c):
        o, n = self.lay[name]
        return self.vec[:, o + c:o + c + 1]

    def run_pass(self, ps):
        k = self.k
        self.ps_ = ps
        self.TB = TB = 1024 if ps == 0 else 1152
        self.g0 = 0 if ps == 0 else 1024
        self.tbs = [(0, 512), (512, 512)] + ([(1024, 128)] if ps == 1 else [])
        x = self.x
        xTv = self.xT.rearrange("(c p) t -> p c t", p=128)
        k.dma("sp", x[:, :, 0:TB], xTv[:, :, self.g0:self.g0 + TB], W=[x.r(c) for c in range(KC)])
        for l in range(DEPTH):
            if "ffn" in self.enable:
                self.ffn(l, 1)
            if "mix" in self.enable:
                self.mixer(l)
            if "ffn" in self.enable:
                self.ffn(l, 2)
            if "ple" in self.enable:
                self.ple(l)
        k.barrier()
        self.norm("final_norm", final=True)
        k.barrier()

    def norm(self, gname, final=False):
        k, x, xn, TB = self.k, self.x, self.xn, self.TB
        nb = [5, 6, 7]
        for c in range(KC):
            s = self.sq[c % 2]
            k.op("act", lambda e: e.activation(out=s[:, 0:TB], in_=x[:, c, 0:TB], func=AF.Square),
                 R=[x.r(c)], W=[s.r()])
            for i, (t0, tw) in enumerate(self.tbs):
                b = self.bank[nb[i]]
                k.op("pe", lambda e: e.matmul(b[:, 0:tw], self.ones_bf[:], s[:, t0:t0 + tw],
                                               start=(c == 0), stop=(c == KC - 1)),
                     R=[self.ones_bf.r(), s.r()], W=[b.r()])
        for i, (t0, tw) in enumerate(self.tbs):
            b = self.bank[nb[i]]
            k.op("act", lambda e: e.activation(out=self.rstd[:, t0:t0 + tw], in_=b[:, 0:tw], func=AF.Sqrt,
                                               bias=self.epst[:, 0:1], scale=1.0 / D),
                 R=[b.r(), self.epst.r()], W=[self.rstd.r()])
        k.op("dve", lambda e: e.reciprocal(out=self.rstd[:, 0:TB], in_=self.rstd[:, 0:TB]),
             R=[self.rstd.r()], W=[self.rstd.r()])
        yTv = self.yT.t.rearrange("(c p) t -> p c t", p=128)
        for c in range(KC):
            if final:
                o = self.ytmp[c % 2]
                k.op("dve", lambda e: e.scalar_tensor_tensor(out=o[:, 0:TB], in0=x[:, c, 0:TB],
                                                              scalar=self.vcol(gname, c), in1=self.rstd[:, 0:TB],
                                                              op0=ALU.mult, op1=ALU.mult),
                     R=[x.r(c), self.rstd.r(), self.vec.r()], W=[o.r()])
                k.dma("sp", yTv[:, c, self.g0:self.g0 + TB], o[:, 0:TB], R=[o.r()], W=[self.yT.r()])
            else:
                k.op("dve", lambda e: e.scalar_tensor_tensor(out=xn[:, c, 0:TB], in0=x[:, c, 0:TB],
                                                              scalar=self.vcol(gname, c), in1=self.rstd[:, 0:TB],
                                                              op0=ALU.mult, op1=ALU.mult),
                     R=[x.r(c), self.rstd.r(), self.vec.r()], W=[xn.r()])

    def load_slab(self, W2d, kchunks, c0, cw, rows0=0):
        k = self.k
        s = self.slab[self.rot("slab", self.nslab)]
        src = W2d[rows0:rows0 + kchunks * 128, :].rearrange("(kc p) f -> p kc f", p=128)[:, :, c0:c0 + cw]
        dst = s[:, 0:kchunks * cw].rearrange("p (kc f) -> p kc f", kc=kchunks)
        step = 4 if kchunks > 4 else kchunks
        for q in range(0, kchunks, step):
            k.dma("pool", dst[:, q:q + step, :], src[:, q:q + step, :], W=[s.r()])
        return s, dst

    def ffn(self, l, which):
        k, x, xn, h = self.k, self.x, self.xn, self.h
        self.norm("norm_ffn%d%d" % (which, l))
        Wg = self.W["ffn%d_wg" % which][l]
        Wu = self.W["ffn%d_wu" % which][l]
        Wd = self.W["ffn%d_wd" % which][l]
        for s in range(DFF // 256):
            sg, vg = self.load_slab(Wg, KC, s * 256, 256)
            su, vu = self.load_slab(Wu, KC, s * 256, 256)
            sd, vd = self.load_slab(Wd, 2, 0, D, rows0=s * 256)
            for fc in range(2):
                for (t0, tw) in self.tbs:
                    i = self.rot("gu", 2)
                    bg, bu = self.bank[i], self.bank[2 + i]
                    for kc in range(KC):
                        k.op("pe", lambda e: e.matmul(bg[:, 0:tw], vg[:, kc, fc * 128:(fc + 1) * 128],
                                                       xn[:, kc, t0:t0 + tw], start=(kc == 0), stop=(kc == KC - 1)),
                             R=[sg.r(), xn.r()], W=[bg.r()])
                    for kc in range(KC):
                        k.op("pe", lambda e: e.matmul(bu[:, 0:tw], vu[:, kc, fc * 128:(fc + 1) * 128],
                                                       xn[:, kc, t0:t0 + tw], start=(kc == 0), stop=(kc == KC - 1)),
                             R=[su.r(), xn.r()], W=[bu.r()])
                    t1 = self.t1[self.rot("t1", 2)]
                    k.op("act", lambda e: e.activation(out=t1[:, 0:tw], in_=bg[:, 0:tw], func=AF.Silu),
                         R=[bg.r()], W=[t1.r()])
                    k.op("dve", lambda e: e.tensor_tensor(out=h[:, fc, t0:t0 + tw], in0=t1[:, 0:tw],
                                                           in1=bu[:, 0:tw], op=ALU.mult),
                         R=[t1.r(), bu.r()], W=[h.r(fc)])
            for dc in range(KC):
                for (t0, tw) in self.tbs:
                    bd = self.bank[4 + self.rot("dn", 2)]
                    for fc in range(2):
                        k.op("pe", lambda e: e.matmul(bd[:, 0:tw], vd[:, fc, dc * 128:(dc + 1) * 128],
                                                       h[:, fc, t0:t0 + tw], start=(fc == 0), stop=(fc == 1)),
                             R=[sd.r(), h.r(fc)], W=[bd.r()])
                    k.op("dve", lambda e: e.scalar_tensor_tensor(out=x[:, dc, t0:t0 + tw], in0=bd[:, 0:tw],
                                                                  scalar=0.5, in1=x[:, dc, t0:t0 + tw],
                                                                  op0=ALU.mult, op1=ALU.add),
                         R=[bd.r(), x.r(dc)], W=[x.r(dc)])

    def ple(self, l):
        k, x, xn, TB = self.k, self.x, self.xn, self.TB
        self.norm("norm_ple%d" % l)
        pt = self.pt
        k.dma("pool", pt[:, :, 0:TB],
              self.pT[l].rearrange("(c p) t -> p c t", p=128)[:, :, self.g0:self.g0 + TB], W=[pt.r()])
        sp_ = self.pproj
        vp = sp_[:, :].rearrange("p (kc f) -> p kc f", kc=2)
        k.dma("pool", vp, self.W["ple_proj"][l].rearrange("(kc p) f -> p kc f", p=128), W=[sp_.r()])
        for s in range(D // 256):
            sg, vg = self.load_slab(self.W["ple_gate"][l], KC, s * 256, 256)
            for oc2 in range(2):
                oc = s * 2 + oc2
                for (t0, tw) in self.tbs:
                    i = self.rot("gu", 2)
                    bg, bu = self.bank[i], self.bank[2 + i]
                    for kc in range(KC):
                        k.op("pe", lambda e: e.matmul(bg[:, 0:tw], vg[:, kc, oc2 * 128:(oc2 + 1) * 128],
                                                       xn[:, kc, t0:t0 + tw], start=(kc == 0), stop=(kc == KC - 1)),
                             R=[sg.r(), xn.r()], W=[bg.r()])
                    for kc in range(2):
                        k.op("pe", lambda e: e.matmul(bu[:, 0:tw], vp[:, kc, oc * 128:(oc + 1) * 128],
                                                       pt[:, kc, t0:t0 + tw], start=(kc == 0), stop=(kc == 1)),
                             R=[sp_.r(), pt.r()], W=[bu.r()])
                    t1 = self.t1[self.rot("t1", 2)]
                    k.op("act", lambda e: e.activation(out=t1[:, 0:tw], in_=bg[:, 0:tw], func=AF.Sigmoid),
                         R=[bg.r()], W=[t1.r()])
                    t2 = self.t2[self.rot("t2", 2)]
                    k.op("dve", lambda e: e.tensor_tensor(out=t2[:, 0:tw], in0=t1[:, 0:tw], in1=bu[:, 0:tw],
                                                           op=ALU.mult),
                         R=[t1.r(), bu.r()], W=[t2.r()])
                    k.op("pool", lambda e: e.tensor_tensor(out=x[:, oc, t0:t0 + tw], in0=x[:, oc, t0:t0 + tw],
                                                            in1=t2[:, 0:tw], op=ALU.add),
                         R=[t2.r(), x.r(oc)], W=[x.r(oc)])

    def cs(self, name, rows=128):
        o, n = CST[name]
        return self.cst[0:rows, o:o + n]

    def mixer(self, l):
        self.k.barrier()
        self.nslab = 2
        self.norm("norm_mix%d" % l)
        if l % 2 == 0:
            self.lru(l // 2)
            if "norwkv" not in self.enable:
                self.k.barrier()
                self.rwkv(l // 2)
        else:
            self.s5(l // 2)
        self.nslab = 6
        self.k.barrier()

    def gelu_(self, eng_out, src, tmp, TBc):
        k = self.k
        k.op("dve", lambda e: e.tensor_tensor(out=tmp[:, 0:TBc], in0=src[:, 0:TBc], in1=src[:, 0:TBc], op=ALU.mult),
             R=[src.r()], W=[tmp.r()])
        k.op("dve", lambda e: e.tensor_scalar(out=tmp[:, 0:TBc], in0=tmp[:, 0:TBc], scalar1=0.044715, scalar2=1.0,
                                               op0=ALU.mult, op1=ALU.add), R=[tmp.r()], W=[tmp.r()])
        k.op("dve", lambda e: e.tensor_tensor(out=tmp[:, 0:TBc], in0=tmp[:, 0:TBc], in1=src[:, 0:TBc], op=ALU.mult),
             R=[tmp.r(), src.r()], W=[tmp.r()])
        k.op("act", lambda e: e.activation(out=tmp[:, 0:TBc], in_=tmp[:, 0:TBc], func=AF.Sigmoid, scale=1.5957691216),
             R=[tmp.r()], W=[tmp.r()])
        k.op("dve", lambda e: e.tensor_tensor(out=eng_out[:, 0:TBc], in0=tmp[:, 0:TBc], in1=src[:, 0:TBc], op=ALU.mult),
             R=[tmp.r(), src.r()], W=[eng_out.r()])

    def proj_chunk(self, W2d, c0, cw, evac):
        k, xn = self.k, self.xn
        sg, vg = self.load_slab(W2d, KC, c0, cw)
        for (t0, tw) in self.tbs:
            bg = self.bank[self.rot("pj", 2)]
            for kc in range(KC):
                k.op("pe", lambda e: e.matmul(bg[0:cw, 0:tw], vg[:, kc, 0:cw], xn[:, kc, t0:t0 + tw],
                                               start=(kc == 0), stop=(kc == KC - 1)),
                     R=[sg.r(), xn.r()], W=[bg.r()])
            evac(bg, t0, tw)

    def lru(self, j):
        k, x, TB, ps = self.k, self.x, self.TB, self.ps_
        ns = NS if ps == 1 else 0
        Wi = self.W["w_in_ab"][j]
        wa = self.av("lwa", 8704, 512, BF16, a=8)
        wx = self.av("lwx", 9216, 512, BF16, a=8)
        k.dma("pool", wa[:, :, :], self.I["wa_bd"][j], W=[wa.r()])
        k.dma("pool", wx[:, :, :], self.I["wx_bd"][j], W=[wx.r()])
        XP = self.av("lXP", 9728, 1216)
        names = ["xc", "gr", "gi", "aa", "tt", "hh"]
        T = {nm: self.av("l" + nm, 10944 + i * 1152, 1152) for i, nm in enumerate(names)}
        xcb = self.sq[0]
        hst, cvo, h0s, cv0 = self.lru_small
        nsp = self.lru_nsp
        if ns:
            k.dma("sp", h0s[:, :, 0:16], self.I["st_h"][j], W=[h0s.r()])
            k.dma("sp", cv0[:, :, 0:16, :], self.I["st_conv"][j], W=[cv0.r()])
        if ps == 0:
            k.op("dve", lambda e: e.memset(h0s[:, :, 16:17], 0.0), W=[h0s.r()], acc=True)
            k.op("dve", lambda e: e.memset(cv0[:, :, 16:17, :], 0.0), W=[cv0.r()], acc=True)
        else:
            k.dma("sp", h0s[:, :, 16:17], self.o_h.t[j][:, :, 16:17], R=[self.o_h.r()], W=[h0s.r()], allow_slow_non_contiguous=True)
            k.dma("sp", cv0[:, :, 16:17, :], self.o_conv.t[j][:, :, 16:17, :], R=[self.o_conv.r()], W=[cv0.r()], allow_slow_non_contiguous=True)
        lo, _ = self.lay["lam_b%d" % j]
        k.op("act", lambda e: e.activation(out=nsp[:, 0:8], in_=self.vec[:, lo:lo + 8], func=AF.Exp, scale=-1.0),
             R=[self.vec.r()], W=[nsp.r()])
        k.op("act", lambda e: e.activation(out=nsp[:, 0:8], in_=nsp[:, 0:8], func=AF.Ln, bias=self.cs("one")),
             R=[nsp.r(), self.cst.r()], W=[nsp.r()])
        k.op("dve", lambda e: e.tensor_scalar(out=nsp[:, 8:16], in0=nsp[:, 0:8], scalar1=-16.0, scalar2=None,
                                               op0=ALU.mult), R=[nsp.r()], W=[nsp.r()])
        k.op("dve", lambda e: e.tensor_scalar(out=nsp[:, 0:8], in0=nsp[:, 0:8], scalar1=-8.0, scalar2=None,
                                               op0=ALU.mult), R=[nsp.r()], W=[nsp.r()])
        xc, gr, gi, aa, tt, hh = (T[n] for n in names)
        gb = gr
        XPp = XP[:, 0:1027]
        XPs = XP[:, 1027:1027 + 176].rearrange("p (b t) -> p b t", b=16)
        for c in range(8):
            def ev_x(bg, t0, tw):
                if t0 < 1024:
                    k.op("act", lambda e: e.copy(out=XP[:, 3 + t0:3 + t0 + tw], in_=bg[:, 0:tw]),
                         R=[bg.r()], W=[XP.r()])
                else:
                    k.op("act", lambda e: e.copy(out=XPs[:, :, 3:11],
                                                 in_=bg[:, 0:128].rearrange("p (b t) -> p b t", b=16)),
                         R=[bg.r()], W=[XP.r()])
            self.proj_chunk(Wi, COLS_A + c * 128, 128, ev_x)

            k.op("dve", lambda e: e.tensor_copy(out=XP[:, 0:3], in_=cv0[:, c, 16, :]), R=[cv0.r()], W=[XP.r()])
            if ns:
                k.op("dve", lambda e: e.tensor_copy(out=XPs[:, :, 0:3], in_=cv0[:, c, 0:16, :]),
                     R=[cv0.r()], W=[XP.r()])
            k.op("dve", lambda e: e.tensor_copy(out=cvo[:, c, 16, :], in_=XP[:, 1024:1027]), R=[XP.r()], W=[cvo.r()])
            if ns:
                k.op("dve", lambda e: e.tensor_copy(out=cvo[:, c, 0:16, :], in_=XPs[:, :, 8:11]),
                     R=[XP.r()], W=[cvo.r()])
            segs = [(xc[:, 0:1024], lambda q: XP[:, q:q + 1024])]
            if ns:
                segs.append((xc[:, 1024:1152].rearrange("p (b t) -> p b t", b=16), lambda q: XPs[:, :, q:q + 8]))
            for (dst, srcf) in segs:
                k.op("dve", lambda e: e.tensor_scalar(out=dst, in0=srcf(0), scalar1=self.vcol("conv_w0_%d" % j, c),
                                                       scalar2=self.vcol("conv_b_b%d" % j, c),
                                                       op0=ALU.mult, op1=ALU.add),
                     R=[XP.r(), self.vec.r()], W=[xc.r()])
                for q in range(1, 4):
                    k.op("dve", lambda e: e.scalar_tensor_tensor(out=dst, in0=srcf(q),
                                                                  scalar=self.vcol("conv_w%d_%d" % (q, j), c),
                                                                  in1=dst, op0=ALU.mult, op1=ALU.add),
                         R=[XP.r(), xc.r(), self.vec.r()], W=[xc.r()])
            k.op("act", lambda e: e.copy(out=xcb[:, 0:TB], in_=xc[:, 0:TB]), R=[xc.r()], W=[xcb.r()])
            for (wt, bname, dst) in ((wa, "ba_b%d" % j, gr), (wx, "bx_b%d" % j, gi)):
                for (t0, tw) in self.tbs:
                    bg = self.bank[self.rot("pj", 2)]
                    k.op("pe", lambda e: e.matmul(bg[:, 0:tw], wt[:, c, :], xcb[:, t0:t0 + tw], start=True, stop=True),
                         R=[wt.r(), xcb.r()], W=[bg.r()])
                    k.op("act", lambda e: e.activation(out=dst[:, t0:t0 + tw], in_=bg[:, 0:tw], func=AF.Sigmoid,
                                                       bias=self.vcol(bname, c)),
                         R=[bg.r(), self.vec.r()], W=[dst.r()])
            k.op("act", lambda e: e.activation(out=aa[:, 0:TB], in_=gr[:, 0:TB], func=AF.Exp, scale=nsp[:, c:c + 1]),
                 R=[gr.r(), nsp.r()], W=[aa.r()])
            k.op("act", lambda e: e.activation(out=tt[:, 0:TB], in_=gr[:, 0:TB], func=AF.Exp,
                                               scale=nsp[:, 8 + c:9 + c]), R=[gr.r(), nsp.r()], W=[tt.r()])
            k.op("dve", lambda e: e.tensor_scalar(out=tt[:, 0:TB], in0=tt[:, 0:TB], scalar1=-1.0, scalar2=1.0,
                                                   op0=ALU.mult, op1=ALU.add), R=[tt.r()], W=[tt.r()])
            k.op("act", lambda e: e.activation(out=tt[:, 0:TB], in_=tt[:, 0:TB], func=AF.Sqrt),
                 R=[tt.r()], W=[tt.r()])
            k.op("dve", lambda e: e.tensor_tensor(out=gi[:, 0:TB], in0=gi[:, 0:TB], in1=xc[:, 0:TB], op=ALU.mult),
                 R=[gi.r(), xc.r()], W=[gi.r()])
            k.op("dve", lambda e: e.tensor_tensor(out=tt[:, 0:TB], in0=tt[:, 0:TB], in1=gi[:, 0:TB], op=ALU.mult),
                 R=[gi.r(), tt.r()], W=[tt.r()])
            k.op("dve", lambda e: e.tensor_tensor_scan(out=hh[:, 0:1024], data0=aa[:, 0:1024], data1=tt[:, 0:1024],
                                                        initial=h0s[:, c, 16:17], op0=ALU.mult, op1=ALU.add),
                 R=[aa.r(), tt.r(), h0s.r()], W=[hh.r()])
            for bq in range(ns):
                o_ = 1024 + bq * 8
                k.op("dve", lambda e: e.tensor_tensor_scan(out=hh[:, o_:o_ + 8], data0=aa[:, o_:o_ + 8],
                                                            data1=tt[:, o_:o_ + 8], initial=h0s[:, c, bq:bq + 1],
                                                            op0=ALU.mult, op1=ALU.add),
                     R=[aa.r(), tt.r(), h0s.r()], W=[hh.r()], acc=True)
            k.op("dve", lambda e: e.tensor_copy(out=hst[:, c, 16:17], in_=hh[:, 1023:1024]), R=[hh.r()], W=[hst.r()])
            if ns:
                k.op("dve", lambda e: e.tensor_copy(
                    out=hst[:, c, 0:16], in_=hh[:, 1024:1152].rearrange("p (b t) -> p b t", b=16)[:, :, 7]),
                    R=[hh.r()], W=[hst.r()])
            def ev_g(bg, t0, tw):
                k.op("act", lambda e: e.copy(out=gb[:, t0:t0 + tw], in_=bg[:, 0:tw]), R=[bg.r()], W=[gb.r()])
            self.proj_chunk(Wi, COLS_A + 1024 + c * 128, 128, ev_g)
            self.gelu_(gi, gb, aa, TB)
            yb = self.ybuf
            k.op("dve", lambda e: e.tensor_tensor(out=yb[:, c, 0:TB], in0=gi[:, 0:TB], in1=hh[:, 0:TB], op=ALU.mult),
                 R=[gi.r(), hh.r()], W=[yb.r()])
        k.dma("sp", self.o_h.t[j], hst[:, :, :], R=[hst.r()], W=[self.o_h.r()])
        k.dma("sp", self.o_conv.t[j], cvo[:, :, :, :], R=[cvo.r()], W=[self.o_conv.r()])
        Wo = self.W["w_out_ab"][j]
        yb = self.ybuf
        for s in range(D // 256):
            sg, vg = self.load_slab(Wo, 8, s * 256, 256, rows0=1024)
            for oc2 in range(2):
                oc = s * 2 + oc2
                for (t0, tw) in self.tbs:
                    bd = self.bank[4 + self.rot("dn", 2)]
                    for kc in range(8):
                        k.op("pe", lambda e: e.matmul(bd[:, 0:tw], vg[:, kc, oc2 * 128:(oc2 + 1) * 128],
                                                       yb[:, kc, t0:t0 + tw], start=(kc == 0), stop=(kc == 7)),
                             R=[sg.r(), yb.r()], W=[bd.r()])
                    k.op("dve", lambda e: e.tensor_tensor(out=x[:, oc, t0:t0 + tw], in0=bd[:, 0:tw],
                                                           in1=x[:, oc, t0:t0 + tw], op=ALU.add),
                         R=[bd.r(), x.r(oc)], W=[x.r(oc)])


    def rwkv(self, j):
        k, x, xn, TB, ps = self.k, self.x, self.xn, self.TB, self.ps_
        ns = NS if ps == 1 else 0
        av, cst, vec = self.av, self.cst, self.vec
        Wi = self.W["w_in_ab"][j]
        ident, blk = self.cs("ident"), self.cs("blk")
        NT = TB // 128
        self.nslab = 1

        def dve(fn, R, W, **kw):
            return k.op("dve", fn, R=R, W=W, **kw)
        w2a2 = av("r_w2a2", 2048, 512, BF16)
        g2a = av("r_g2a", 2560, 512, BF16)
        g2b = av("r_g2b", 3072, 512, BF16)
        k.dma("pool", w2a2[0:64, :], self.I["w2_a"][j], W=[w2a2.r()])
        k.dma("pool", w2a2[64:128, :], self.I["a2_a"][j], W=[w2a2.r()])
        lx, xg0, xg1 = av("r_lx", 4096, 1152), av("r_xg0", 5248, 1152), av("r_xg1", 6400, 1152)
        Tt = [av("r_T%d" % i, 7552 + i * 1152, 1152) for i in range(7)]
        stg = av("r_stg", 15616, 1152)
        shin = av("r_shin", 16768, 459, a=27)
        shout = av("r_shout", 17227, 459, a=27)
        txa, sg0b, sg1b = self.sq[0], self.sq[1], self.sg1b
        if ns:
            k.dma("sp", shin[:, :, 0:16], self.I["st_shift"][j], W=[shin.r()])
        if ps == 0:
            dve(lambda e: e.memset(shin[:, :, 16:17], 0.0), [], [shin.r()], acc=True)
        else:
            k.dma("sp", shin[:, :, 16:17], self.o_shift.t[j][:, :, 16:17], R=[self.o_shift.r()], W=[shin.r()],
                  allow_slow_non_contiguous=True)
        r3 = lambda a: a.rearrange("p (b t) -> p b t", b=16)

        def project_shift(col0, cw, zc, Z, mu_ap):
            D_ = Tt[6]

            def ev(bg, t0, tw):
                k.op("act", lambda e: e.copy(out=Z[0:cw, t0:t0 + tw], in_=bg[0:cw, 0:tw]), R=[bg.r()], W=[Z.r()], acc=True)
            self.proj_chunk(Wi, col0, cw, ev)
            dve(lambda e: e.tensor_copy(out=shout[0:cw, zc, 16:17], in_=Z[0:cw, 1023:1024]), [Z.r()], [shout.r()], acc=True)
            dve(lambda e: e.tensor_tensor(out=D_[0:cw, 1:1024], in0=Z[0:cw, 0:1023], in1=Z[0:cw, 1:1024], op=ALU.subtract),
                [Z.r()], [D_.r()])
            dve(lambda e: e.tensor_tensor(out=D_[0:cw, 0:1], in0=shin[0:cw, zc, 16:17], in1=Z[0:cw, 0:1], op=ALU.subtract),
                [Z.r(), shin.r()], [D_.r()], acc=True)
            if ns:
                Zs_, Ds_ = r3(Z[0:cw, 1024:1152]), r3(D_[0:cw, 1024:1152])
                dve(lambda e: e.tensor_copy(out=shout[0:cw, zc, 0:16], in_=Zs_[:, :, 7]), [Z.r()], [shout.r()], acc=True)
                dve(lambda e: e.tensor_tensor(out=Ds_[:, :, 1:8], in0=Zs_[:, :, 0:7], in1=Zs_[:, :, 1:8], op=ALU.subtract),
                    [Z.r()], [D_.r()], acc=True)
                dve(lambda e: e.tensor_tensor(out=Ds_[:, :, 0], in0=shin[0:cw, zc, 0:16], in1=Zs_[:, :, 0], op=ALU.subtract),
                    [Z.r(), shin.r()], [D_.r()], acc=True)
            dve(lambda e: e.scalar_tensor_tensor(out=Z[0:cw, 0:TB], in0=D_[0:cw, 0:TB], scalar=mu_ap, in1=Z[0:cw, 0:TB],
                                                 op0=ALU.mult, op1=ALU.add), [D_.r(), Z.r(), vec.r()], [Z.r()])
        mo = self.lay["mu_x%d" % j][0]
        dve(lambda e: e.memset(shout[:, 26, :], 0.0), [], [shout.r()], acc=True)
        project_shift(3072, 128, 24, lx, vec[:, mo:mo + 1])
        project_shift(3200, 128, 25, xg0, vec[:, mo + 2:mo + 3])
        project_shift(3328, 32, 26, xg1, vec[0:32, mo + 3:mo + 4])
        k.op("act", lambda e: e.activation(out=txa[0:64, 0:TB], in_=lx[0:64, 0:TB], func=AF.Tanh), R=[lx.r()], W=[txa.r()])
        k.op("act", lambda e: e.copy(out=txa[64:128, 0:TB], in_=lx[64:128, 0:TB]), R=[lx.r()], W=[txa.r()], acc=True)
        k.op("act", lambda e: e.activation(out=sg0b[:, 0:TB], in_=xg0[:, 0:TB], func=AF.Sigmoid), R=[xg0.r()], W=[sg0b.r()])
        k.op("act", lambda e: e.activation(out=sg1b[0:32, 0:TB], in_=xg1[0:32, 0:TB], func=AF.Sigmoid),
             R=[xg1.r()], W=[sg1b.r()])
        Zsd, Ysd, Bon = self.Zs, self.Ys, self.Bon

        def emit(q, c, Q):
            for tt in range(NT):
                bk = self.bank[6 + (tt % 2)]
                k.op("pe", lambda e: e.transpose(bk[:, 0:128], Q[:, tt * 128:(tt + 1) * 128], ident),
                     R=[Q.r(), cst.r()], W=[bk.r()])
                if tt % 2 == 0:
                    k.op("act", lambda e: e.copy(out=stg[:, tt * 128:(tt + 1) * 128], in_=bk[:, 0:128]),
                         R=[bk.r()], W=[stg.r()], acc=True)
                else:
                    dve(lambda e: e.tensor_copy(out=stg[:, tt * 128:(tt + 1) * 128], in_=bk[:, 0:128]),
                        [bk.r()], [stg.r()], acc=True)
            sv = stg[:, 0:TB].rearrange("p (tt f) -> p tt f", f=128)
            if q == 5:
                dst = self.Vs.t[0:TB, c * 128:(c + 1) * 128].rearrange("(tt p) f -> p tt f", p=128)
                k.dma("sp", dst, sv, R=[stg.r()], W=[self.Vs.r()])
            else:
                for h2 in range(2):
                    dst = Zsd.t[0:TB, 2 * c + h2, q, :].rearrange("(tt p) j -> p tt j", p=128)
                    k.dma("sp", dst, sv[:, :, h2 * 64:(h2 + 1) * 64], R=[stg.r()], W=[Zsd.r()])

        def headsum(user, src):
            for (t0, tw) in self.tbs:
                bg = self.bank[4 + self.rot("hs", 2)]
                k.op("pe", lambda e: e.matmul(bg[:, 0:tw], blk, src[:, t0:t0 + tw], start=True, stop=True),
                     R=[cst.r(), src.r()], W=[bg.r()])
                user(bg, t0, tw)

        for c in range(8):
            T0, T1, T2, T3, T4, T5 = Tt[0:6]
            vc = lambda nm: self.vcol("%s%d" % (nm, j), c)
            project_shift(c * 128, 128, c, T0, vc("mu_r"))
            project_shift(1024 + c * 128, 128, 8 + c, T1, vc("mu_k"))
            project_shift(2048 + c * 128, 128, 16 + c, T2, vc("mu_v"))
            for (t0, tw) in self.tbs:
                bg = self.bank[4 + self.rot("hs", 2)]
                k.op("pe", lambda e: e.matmul(bg[:, 0:tw], w2a2[64:128, c * 128:(c + 1) * 128], txa[64:128, t0:t0 + tw],
                                               start=True, stop=True), R=[w2a2.r(), txa.r()], W=[bg.r()])
                k.op("act", lambda e: e.activation(out=T3[:, t0:t0 + tw], in_=bg[:, 0:tw], func=AF.Sigmoid, bias=vc("a0_a")),
                     R=[bg.r(), vec.r()], W=[T3.r()], acc=True)
            dve(lambda e: e.tensor_scalar(out=T4[:, 0:TB], in0=T1[:, 0:TB], scalar1=vc("kk_a"), scalar2=None, op0=ALU.mult),
                [T1.r(), vec.r()], [T4.r()])
            dve(lambda e: e.tensor_tensor(out=T5[:, 0:TB], in0=T4[:, 0:TB], in1=T4[:, 0:TB], op=ALU.mult), [T4.r()], [T5.r()])

            def nrm_ev(bg, t0, tw):
                k.op("act", lambda e: e.activation(out=Tt[6][:, t0:t0 + tw], in_=bg[:, 0:tw], func=AF.Sqrt),
                     R=[bg.r()], W=[Tt[6].r()], acc=True)
            headsum(nrm_ev, T5)
            dve(lambda e: e.tensor_scalar(out=Tt[6][:, 0:TB], in0=Tt[6][:, 0:TB], scalar1=1e-12, scalar2=None, op0=ALU.max),
                [Tt[6].r()], [Tt[6].r()])
            dve(lambda e: e.reciprocal(out=Tt[6][:, 0:TB], in_=Tt[6][:, 0:TB]), [Tt[6].r()], [Tt[6].r()])
            dve(lambda e: e.tensor_tensor(out=T4[:, 0:TB], in0=T4[:, 0:TB], in1=Tt[6][:, 0:TB], op=ALU.mult),
                [T4.r(), Tt[6].r()], [T4.r()])
            dve(lambda e: e.tensor_tensor(out=T5[:, 0:TB], in0=T4[:, 0:TB], in1=T3[:, 0:TB], op=ALU.mult),
                [T4.r(), T3.r()], [T5.r()])
            emit(0, c, T4)
            emit(2, c, T5)
            dve(lambda e: e.tensor_scalar(out=T4[:, 0:TB], in0=T3[:, 0:TB], scalar1=-1.0, scalar2=vc("ka_a"),
                                          op0=ALU.add, op1=ALU.mult), [T3.r(), vec.r()], [T4.r()])
            dve(lambda e: e.scalar_tensor_tensor(out=T4[:, 0:TB], in0=T4[:, 0:TB], scalar=1.0, in1=T1[:, 0:TB],
                                                 op0=ALU.add, op1=ALU.mult), [T4.r(), T1.r()], [T4.r()])
            emit(3, c, T4)
            dve(lambda e: e.scalar_tensor_tensor(out=T5[:, 0:TB], in0=T0[:, 0:TB], scalar=vc("rk_a"), in1=T4[:, 0:TB],
                                                 op0=ALU.mult, op1=ALU.mult), [T0.r(), T4.r(), vec.r()], [T5.r()])

            def bon_ev(bg, t0, tw):
                dve(lambda e: e.tensor_tensor(out=Tt[6][:, t0:t0 + tw], in0=bg[:, 0:tw], in1=T2[:, t0:t0 + tw], op=ALU.mult),
                    [bg.r(), T2.r()], [Tt[6].r()], acc=True)
            headsum(bon_ev, T5)
            k.dma("sp", Bon.t[c][:, 0:TB], Tt[6][:, 0:TB], R=[Tt[6].r()], W=[Bon.r()])
            for (t0, tw) in self.tbs:
                bg = self.bank[4 + self.rot("hs", 2)]
                k.op("pe", lambda e: e.matmul(bg[:, 0:tw], w2a2[0:64, c * 128:(c + 1) * 128], txa[0:64, t0:t0 + tw],
                                               start=True, stop=True), R=[w2a2.r(), txa.r()], W=[bg.r()])
                k.op("act", lambda e: e.activation(out=T1[:, t0:t0 + tw], in_=bg[:, 0:tw], func=AF.Sigmoid, bias=vc("w0_a")),
                     R=[bg.r(), vec.r()], W=[T1.r()], acc=True)
            k.op("act", lambda e: e.activation(out=T1[:, 0:TB], in_=T1[:, 0:TB], func=AF.Exp, scale=-float(np.exp(-0.5))),
                 R=[T1.r()], W=[T1.r()])
            emit(1, c, T1)
            emit(4, c, T0)
            emit(5, c, T2)
        k.dma("sp", self.o_shift.t[j], shout[:, :, :], R=[shout.r()], W=[self.o_shift.r()])
        k.barrier()
        Spf, T1p = av("q_Sp", 0, 512), av("q_T1p", 512, 512, a=8)
        Sp = Tile(Spf[:, :].rearrange("p (e j) -> p e j", e=8), "q_Sp3")
        Sp.reg = Spf.reg
        Ss = av("q_Ss", 1024, 8192)
        T1s = av("q_T1s", 9216, 2048)
        Xb = [av("q_X%d" % i, 11264 + i * 1280, 1280) for i in range(2)]
        vt = [av("q_vt%d" % i, 13824 + i * 128, 128, a=16) for i in range(2)]
        yt = [av("q_yt%d" % i, 14080 + i * 128, 128, a=16) for i in range(2)]
        vs_, ys_ = av("q_vs", 14336, 1024), av("q_ys", 15360, 1024)
        sa_t = av("q_sa", 16384, 32)
        rep = self.cs("rep", rows=16)
        Zv = Zsd.t.rearrange("t h q j -> h t (q j)")
        Vv = self.Vs.t.rearrange("t (p e) -> p t e", e=8)
        Vsr = self.Vs.r()
        Yv = Ysd.t.rearrange("t (p e) -> p t e", e=8)
        if ps == 0:
            dve(lambda e: e.memset(Sp[:, :, :], 0.0), [], [Sp.r()])
        else:
            k.dma("sp", Spf[:, :], self.o_wkv.t[j][:, 16, :], R=[self.o_wkv.r()], W=[Sp.r()])

        def step(S, T1, reg, shp, sa, vv, yy, RW):
            nd = len(shp)
            bj = lambda a: a.unsqueeze(nd - 1).broadcast_to([128] + shp)
            be = lambda a: a.unsqueeze(nd).broadcast_to([128] + shp)
            Rr, Wr = RW
            dve(lambda e: e.tensor_tensor(out=T1, in0=S, in1=bj(reg(0)), op=ALU.mult), Rr, Wr)
            dve(lambda e: e.tensor_reduce(out=sa, in_=T1, axis=AX.X, op=ALU.add), Wr, Wr)
            dve(lambda e: e.tensor_tensor(out=S, in0=S, in1=bj(reg(1)), op=ALU.mult), Rr, Wr)
            dve(lambda e: e.tensor_tensor(out=T1, in0=bj(reg(2)), in1=be(sa), op=ALU.mult), Rr, Wr)
            dve(lambda e: e.tensor_tensor(out=S, in0=S, in1=T1, op=ALU.subtract), Wr, Wr)
            dve(lambda e: e.tensor_tensor(out=T1, in0=bj(reg(3)), in1=be(vv), op=ALU.mult), Rr, Wr)
            dve(lambda e: e.tensor_tensor(out=S, in0=S, in1=T1, op=ALU.add), Wr, Wr)
            dve(lambda e: e.tensor_tensor(out=T1, in0=S, in1=bj(reg(4)), op=ALU.mult), Rr, Wr)
            dve(lambda e: e.tensor_reduce(out=yy, in_=T1, axis=AX.X, op=ALU.add), Wr, Wr)

        def rep_mm(Xt, bset):
            Xv = Xt[0:16, 0:1280].rearrange("h (t q j) -> h t q j", t=4, q=5)
            for q in range(5):
                bk = self.bank[bset * 3 + q // 2]
                k.op("pe", lambda e: e.matmul(bk[:, (q % 2) * 256:(q % 2) * 256 + 256], rep, Xv[:, :, q, :],
                                               start=True, stop=True), R=[cst.r(), Xt.r()], W=[bk.r()], acc=True)

        wreg = Reg("q_work")
        vtt = ytt = None
        for g4 in range(1024 // 4):
            t0 = g4 * 4
            Xt = Xb[g4 % 2]
            k.dma("sp", Xt[0:16, 0:1280].rearrange("h (t f) -> h t f", t=4), Zv[:, t0:t0 + 4, :],
                  R=[Zsd.r()], W=[Xt.r()])
            if t0 % 16 == 0:
                vtt, ytt = vt[(t0 // 16) % 2], yt[(t0 // 16) % 2]
                k.dma("sp", vtt[:, :, :], Vv[:, t0:t0 + 16, :], R=[Vsr], W=[vtt.r()], allow_slow_non_contiguous=True)
            bset = g4 % 2
            rep_mm(Xt, bset)
            banks = [self.bank[bset * 3 + i] for i in range(3)]
            for tl in range(4):
                reg = lambda q: banks[q // 2][:, (q % 2) * 256 + tl * 64:(q % 2) * 256 + tl * 64 + 64]
                ti = (t0 + tl) % 16
                step(Sp[:, :, :], T1p[:, :, :], reg, [8, 64], sa_t[:, 0:8], vtt[:, ti, :], ytt[:, ti, :],
                     ([b_.r() for b_ in banks] + [vtt.r(), wreg, Sp.r()], [wreg, ytt.r()]))
            if t0 % 16 == 12:
                k.dma("sp", Yv[:, t0 - 12:t0 + 4, :], ytt[:, :, :], R=[ytt.r()], W=[Ysd.r()], allow_slow_non_contiguous=True)
        k.dma("sp", self.o_wkv.t[j][:, 16, :], Spf[:, :], R=[Sp.r(), wreg], W=[self.o_wkv.r()])
        if ns:
            k.dma("sp", Ss[:, :], self.I["st_wkv"][j].rearrange("p b f -> p (b f)"), W=[Ss.r()])
            vs4 = vs_[:, :].rearrange("p (b t e) -> p b t e", b=16, t=8)
            ys4 = ys_[:, :].rearrange("p (b t e) -> p b t e", b=16, t=8)
            for bq in range(8):
                k.dma("sp", vs_[:, :].rearrange("p (n e) -> p n e", e=8)[:, 16 * bq:16 * bq + 16, :],
                      Vv[:, 1024 + 16 * bq:1024 + 16 * bq + 16, :],
                      R=[Vsr], W=[vs_.r()], allow_slow_non_contiguous=True)
            Ss4 = Ss[:, :].rearrange("p (b e j) -> p b e j", b=16, e=8)
            T1s4 = T1s[:, :].rearrange("p (b e j) -> p b e j", b=4, e=8)
            sreg = Reg("q_works")
            it = 0
            for t in range(8):
                for qd in range(4):
                    Xt = Xb[it % 2]
                    bset = it % 2
                    it += 1
                    r0 = 1024 + 32 * qd + t
                    k.dma("sp", Xt[0:16, 0:1280].rearrange("h (t f) -> h t f", t=4),
                          Zv[:, r0:r0 + 25:8, :], R=[Zsd.r()], W=[Xt.r()])
                    rep_mm(Xt, bset)
                    banks = [self.bank[bset * 3 + i] for i in range(3)]
                    reg = lambda q: banks[q // 2][:, (q % 2) * 256:(q % 2) * 256 + 256].rearrange("p (b j) -> p b j", b=4)
                    step(Ss4[:, 4 * qd:4 * qd + 4, :, :], T1s4[:, :, :, :], reg, [4, 8, 64],
                         sa_t[:, 0:32].rearrange("p (b e) -> p b e", b=4), vs4[:, 4 * qd:4 * qd + 4, t, :],
                         ys4[:, 4 * qd:4 * qd + 4, t, :],
                         ([b_.r() for b_ in banks] + [vs_.r(), Ss.r(), sreg], [sreg, ys_.r()]))
            k.dma("sp", Yv[:, 1024:1152, :], ys_[:, :].rearrange("p (n e) -> p n e", e=8), R=[ys_.r(), sreg], W=[Ysd.r()],
                  allow_slow_non_contiguous=True)
            k.dma("sp", self.o_wkv.t[j][:, 0:16, :].rearrange("p b f -> p (b f)"), Ss[:, :], R=[Ss.r(), sreg],
                  W=[self.o_wkv.r()])
        k.barrier()
        self.nslab = 2
        yab = av("p_yab", 4096, 4608, BF16, a=8)
        ytk = av("p_ytk", 8704, 1152)
        yc, ysq, mean, bon = (av("p_t%d" % i, 9856 + i * 1152, 1152) for i in range(4))
        g2a = av("p_g2a", 14464, 512, BF16)
        g2b = av("p_g2b", 14976, 512, BF16)
        gne = av("p_gne", 15488, 4)
        dve(lambda e: e.memset(gne[:, :], GN_EPS), [], [gne.r()])
        k.dma("pool", g2a[:, :], self.I["g2_a"][j][0:128, :], W=[g2a.r()])
        k.dma("pool", g2b[0:32, :], self.I["g2_a"][j][128:160, :], W=[g2b.r()])
        for c in range(8):
            vc = lambda nm: self.vcol("%s%d" % (nm, j), c)
            k.dma("sp", ytk[:, 0:TB].rearrange("p (tt f) -> p tt f", f=128),
                  Ysd.t[0:TB, c * 128:(c + 1) * 128].rearrange("(tt p) f -> p tt f", p=128), R=[Ysd.r()], W=[ytk.r()])
            k.dma("sp", bon[:, 0:TB], Bon.t[c][:, 0:TB], R=[Bon.r()], W=[bon.r()])
            for tt in range(NT):
                bk = self.bank[6 + (tt % 2)]
                k.op("pe", lambda e: e.transpose(bk[:, 0:128], ytk[:, tt * 128:(tt + 1) * 128], ident),
                     R=[ytk.r(), cst.r()], W=[bk.r()])
                k.op("act", lambda e: e.copy(out=yc[:, tt * 128:(tt + 1) * 128], in_=bk[:, 0:128]), R=[bk.r()], W=[yc.r()], acc=True)
            dve(lambda e: e.tensor_tensor(out=ysq[:, 0:TB], in0=yc[:, 0:TB], in1=yc[:, 0:TB], op=ALU.mult), [yc.r()], [ysq.r()])

            def mean_ev(bg, t0, tw):
                k.op("act", lambda e: e.activation(out=mean[:, t0:t0 + tw], in_=bg[:, 0:tw], func=AF.Copy, scale=1.0 / 64),
                     R=[bg.r()], W=[mean.r()], acc=True)
            headsum(mean_ev, yc)

            def var_ev(bg, t0, tw):
                k.op("act", lambda e: e.activation(out=ysq[:, t0:t0 + tw], in_=bg[:, 0:tw], func=AF.Copy, scale=1.0 / 64),
                     R=[bg.r()], W=[ysq.r()], acc=True)
            headsum(var_ev, ysq)
            dve(lambda e: e.tensor_tensor(out=yc[:, 0:TB], in0=yc[:, 0:TB], in1=mean[:, 0:TB], op=ALU.subtract),
                [yc.r(), mean.r()], [yc.r()])
            dve(lambda e: e.tensor_tensor(out=mean[:, 0:TB], in0=mean[:, 0:TB], in1=mean[:, 0:TB], op=ALU.mult),
                [mean.r()], [mean.r()])
            dve(lambda e: e.tensor_tensor(out=ysq[:, 0:TB], in0=ysq[:, 0:TB], in1=mean[:, 0:TB], op=ALU.subtract),
                [ysq.r(), mean.r()], [ysq.r()])
            k.op("act", lambda e: e.activation(out=ysq[:, 0:TB], in_=ysq[:, 0:TB], func=AF.Sqrt, bias=gne[:, 0:1]),
                 R=[ysq.r(), gne.r()], W=[ysq.r()])
            dve(lambda e: e.reciprocal(out=ysq[:, 0:TB], in_=ysq[:, 0:TB]), [ysq.r()], [ysq.r()])
            dve(lambda e: e.tensor_tensor(out=yc[:, 0:TB], in0=yc[:, 0:TB], in1=ysq[:, 0:TB], op=ALU.mult),
                [yc.r(), ysq.r()], [yc.r()])
            dve(lambda e: e.tensor_scalar(out=yc[:, 0:TB], in0=yc[:, 0:TB], scalar1=vc("lnx_g"), scalar2=vc("lnx_b"),
                                          op0=ALU.mult, op1=ALU.add), [yc.r(), vec.r()], [yc.r()])
            dve(lambda e: e.tensor_tensor(out=yc[:, 0:TB], in0=yc[:, 0:TB], in1=bon[:, 0:TB], op=ALU.add),
                [yc.r(), bon.r()], [yc.r()])
            for (t0, tw) in self.tbs:
                bg = self.bank[4 + self.rot("hs", 2)]
                k.op("pe", lambda e: e.matmul(bg[:, 0:tw], g2a[:, c * 128:(c + 1) * 128], sg0b[:, t0:t0 + tw],
                                               start=True, stop=False), R=[g2a.r(), sg0b.r()], W=[bg.r()])
                k.op("pe", lambda e: e.matmul(bg[:, 0:tw], g2b[0:32, c * 128:(c + 1) * 128], sg1b[0:32, t0:t0 + tw],
                                               start=False, stop=True), R=[g2b.r(), sg1b.r()], W=[bg.r()])
                dve(lambda e: e.tensor_tensor(out=yab[:, c, t0:t0 + tw], in0=yc[:, t0:t0 + tw], in1=bg[:, 0:tw], op=ALU.mult),
                    [yc.r(), bg.r()], [yab.r()], acc=True)
        Wo = self.W["w_out_ab"][j]
        for s in range(D // 256):
            sg, vg = self.load_slab(Wo, 8, s * 256, 256, rows0=0)
            for oc2 in range(2):
                oc = s * 2 + oc2
                for (t0, tw) in self.tbs:
                    bd = self.bank[self.rot("pj", 2)]
                    for kc in range(8):
                        k.op("pe", lambda e: e.matmul(bd[:, 0:tw], vg[:, kc, oc2 * 128:(oc2 + 1) * 128],
                                                       yab[:, kc, t0:t0 + tw], start=(kc == 0), stop=(kc == 7)),
                             R=[sg.r(), yab.r()], W=[bd.r()])
                    dve(lambda e: e.tensor_tensor(out=x[:, oc, t0:t0 + tw], in0=bd[:, 0:tw], in1=x[:, oc, t0:t0 + tw],
                                                  op=ALU.add), [bd.r(), x.r(oc)], [x.r(oc)])

    def s5(self, j):
        k, x, xn, TB, ps = self.k, self.x, self.xn, self.TB, self.ps_
        ns = NS if ps == 1 else 0
        PI = float(np.pi)
        av = self.av
        P = {nm: av("s5" + nm, 4096 + i * 128, 128) for i, nm in enumerate(
            ["are", "aim", "dtt", "rho", "tht", "ta", "tb", "fre", "fim", "tc"])}
        Bp1, Bp2, Cp1, Cp2, Bb1, Bb2, tA, tB = (av("s5s%d" % i, 5376 + i * 128, 128, a=8) for i in range(8))
        T1c = av("s5T1", 6400, 64, BF16)
        T2c = av("s5T2", 6464, 64, BF16)
        LB1 = [av("s5LB1%d" % i, 6528 + i * 64, 64, BF16) for i in range(2)]
        LB2 = [av("s5LB2%d" % i, 6656 + i * 64, 64, BF16) for i in range(2)]
        C1p = [av("s5C1%d" % i, 6784 + i * 64, 64, BF16) for i in range(2)]
        C2p = [av("s5C2%d" % i, 6912 + i * 64, 64, BF16) for i in range(2)]
        hin = av("s5hin", 7040, 136, a=8)
        hfin = av("s5hfin", 7176, 136, a=8)
        ftmp = av("s5ft", 7312, 68, a=4)
        Ct, St, targ = av("s5Ct", 7424, 1024), av("s5St", 8448, 1024), av("s5ta", 9472, 1024)
        rhof, m, G = av("s5rf", 10496, 1152), av("s5m", 11648, 1152), av("s5G", 12800, 1152)
        G1b, G2b = av("s5G1", 13952, 576, BF16), av("s5G2", 14528, 576, BF16)
        yf, gt = av("s5yf", 15104, 1152), av("s5gt", 16256, 1152)
        cst, vec = self.cst, self.vec
        ident = self.cs("ident")
        negpi = self.cs("negpi")
        Sg = lambda i: cst[:, CST["sgn"][0] + i:CST["sgn"][0] + i + 1]
        V = lambda e: e

        def dve(fn, R, W, **kw):
            return k.op("dve", fn, R=R, W=W, **kw)

        k.dma("sp", P["are"][:, :], self.I["s5_are"][j], W=[P["are"].r()])
        k.dma("sp", P["aim"][:, :], self.I["s5_aim"][j], W=[P["aim"].r()])
        k.dma("sp", P["dtt"][:, :], self.I["s5_ldt"][j], W=[P["dtt"].r()])
        k.op("act", lambda e: e.activation(out=P["dtt"][:, :], in_=P["dtt"][:, :], func=AF.Exp),
             R=[P["dtt"].r()], W=[P["dtt"].r()])
        dve(lambda e: e.tensor_tensor(out=P["tht"][:, :], in0=P["dtt"][:, :], in1=P["aim"][:, :], op=ALU.mult),
            [P["dtt"].r(), P["aim"].r()], [P["tht"].r()])
        dve(lambda e: e.tensor_tensor(out=P["rho"][:, :], in0=P["dtt"][:, :], in1=P["are"][:, :], op=ALU.mult),
            [P["dtt"].r(), P["are"].r()], [P["rho"].r()])
        k.op("act", lambda e: e.activation(out=P["rho"][:, :], in_=P["rho"][:, :], func=AF.Exp),
             R=[P["rho"].r()], W=[P["rho"].r()])

        qi = av("s5qi", 17408, 1024, I32)
        halfpi = self.cs("halfpi")

        def sincos(dst_s, dst_c, src, w, Rs):
            for (dst, addc, lo, hi, bias) in ((dst_s, 0.0, -PI, PI, None), (dst_c, 0.5 * PI, -1.5 * PI, 0.5 * PI, halfpi)):
                k.op("pool", lambda e: e.tensor_scalar(out=qi[:, 0:w], in0=src, scalar1=addc, scalar2=1.0 / (2 * PI),
                                                       op0=ALU.add, op1=ALU.mult), R=Rs, W=[qi.r()])
                k.op("pool", lambda e: e.tensor_copy(out=gt[:, 0:w], in_=qi[:, 0:w]), R=[qi.r()], W=[gt.r()])
                dve(lambda e: e.scalar_tensor_tensor(out=dst[:, 0:w], in0=gt[:, 0:w], scalar=-2 * PI, in1=src,
                                                     op0=ALU.mult, op1=ALU.add), Rs + [gt.r()], [dst.r()])
                dve(lambda e: e.tensor_scalar(out=dst[:, 0:w], in0=dst[:, 0:w], scalar1=lo, scalar2=hi,
                                              op0=ALU.max, op1=ALU.min), [dst.r()], [dst.r()])
                if bias is None:
                    k.op("act", lambda e: e.activation(out=dst[:, 0:w], in_=dst[:, 0:w], func=AF.Sin),
                         R=[dst.r()], W=[dst.r()])
                else:
                    k.op("act", lambda e: e.activation(out=dst[:, 0:w], in_=dst[:, 0:w], func=AF.Sin, bias=bias),
                         R=[dst.r(), cst.r()], W=[dst.r()])
        sincos(P["ta"], P["tb"], P["tht"][:, :], 128, [P["tht"].r()])
        dve(lambda e: e.tensor_tensor(out=P["ta"][:, :], in0=P["ta"][:, :], in1=P["rho"][:, :], op=ALU.mult),
            [P["ta"].r(), P["rho"].r()], [P["ta"].r()])
        dve(lambda e: e.tensor_tensor(out=P["tb"][:, :], in0=P["tb"][:, :], in1=P["rho"][:, :], op=ALU.mult),
            [P["tb"].r(), P["rho"].r()], [P["tb"].r()])
        dve(lambda e: e.tensor_scalar(out=P["tb"][:, :], in0=P["tb"][:, :], scalar1=-1.0, scalar2=None, op0=ALU.add),
            [P["tb"].r()], [P["tb"].r()])
        dve(lambda e: e.tensor_tensor(out=P["tc"][:, :], in0=P["are"][:, :], in1=P["are"][:, :], op=ALU.mult),
            [P["are"].r()], [P["tc"].r()])
        dve(lambda e: e.tensor_tensor(out=P["fre"][:, :], in0=P["aim"][:, :], in1=P["aim"][:, :], op=ALU.mult),
            [P["aim"].r()], [P["fre"].r()])
        dve(lambda e: e.tensor_tensor(out=P["tc"][:, :], in0=P["tc"][:, :], in1=P["fre"][:, :], op=ALU.add),
            [P["tc"].r(), P["fre"].r()], [P["tc"].r()])
        dve(lambda e: e.reciprocal(out=P["tc"][:, :], in_=P["tc"][:, :]), [P["tc"].r()], [P["tc"].r()])
        dve(lambda e: e.tensor_tensor(out=P["fre"][:, :], in0=P["tb"][:, :], in1=P["are"][:, :], op=ALU.mult),
            [P["tb"].r(), P["are"].r()], [P["fre"].r()])
        dve(lambda e: e.tensor_tensor(out=P["fim"][:, :], in0=P["ta"][:, :], in1=P["aim"][:, :], op=ALU.mult),
            [P["ta"].r(), P["aim"].r()], [P["fim"].r()])
        dve(lambda e: e.tensor_tensor(out=P["fre"][:, :], in0=P["fre"][:, :], in1=P["fim"][:, :], op=ALU.add),
            [P["fre"].r(), P["fim"].r()], [P["fre"].r()])
        dve(lambda e: e.tensor_tensor(out=P["fim"][:, :], in0=P["ta"][:, :], in1=P["are"][:, :], op=ALU.mult),
            [P["ta"].r(), P["are"].r()], [P["fim"].r()])
        dve(lambda e: e.tensor_tensor(out=P["ta"][:, :], in0=P["tb"][:, :], in1=P["aim"][:, :], op=ALU.mult),
            [P["tb"].r(), P["aim"].r()], [P["ta"].r()])
        dve(lambda e: e.tensor_tensor(out=P["fim"][:, :], in0=P["fim"][:, :], in1=P["ta"][:, :], op=ALU.subtract),
            [P["fim"].r(), P["ta"].r()], [P["fim"].r()])
        dve(lambda e: e.tensor_tensor(out=P["fre"][:, :], in0=P["fre"][:, :], in1=P["tc"][:, :], op=ALU.mult),
            [P["fre"].r(), P["tc"].r()], [P["fre"].r()])
        dve(lambda e: e.tensor_tensor(out=P["fim"][:, :], in0=P["fim"][:, :], in1=P["tc"][:, :], op=ALU.mult),
            [P["fim"].r(), P["tc"].r()], [P["fim"].r()])
        iota = self.cs("iota")
        nbk = [5, 6, 7]
        for c in range(KC):
            g0 = c * 8
            for (t_, src) in ((Bp1, "s5_b1"), (Bp2, "s5_b2"), (Cp1, "s5_c1"), (Cp2, "s5_c2")):
                k.dma("sp", t_[:, :, :], self.I[src][j][:, g0:g0 + 8, :], W=[t_.r()])
            if ns:
                k.dma("sp", hin[:, :, 0:16], self.I["st_s5"][j][:, g0:g0 + 8, :], W=[hin.r()])
            if ps == 0:
                dve(lambda e: e.memset(hin[:, :, 16:17], 0.0), [], [hin.r()], acc=True)
            else:
                k.dma("sp", hin[:, :, 16:17], self.o_s5.t[j][:, g0:g0 + 8, 16:17], R=[self.o_s5.r()], W=[hin.r()],
                      allow_slow_non_contiguous=True)
            F1b = P["fre"][:, g0:g0 + 8].unsqueeze(2).broadcast_to([128, 8, 16])
            F2b = P["fim"][:, g0:g0 + 8].unsqueeze(2).broadcast_to([128, 8, 16])
            RP = [P["fre"].r(), P["fim"].r()]
            dve(lambda e: e.tensor_tensor(out=tA[:, :, :], in0=Bp1[:, :, :], in1=F1b, op=ALU.mult), RP + [Bp1.r()], [tA.r()])
            dve(lambda e: e.tensor_tensor(out=tB[:, :, :], in0=Bp2[:, :, :], in1=F2b, op=ALU.mult), RP + [Bp2.r()], [tB.r()])
            dve(lambda e: e.scalar_tensor_tensor(out=Bb1[:, :, :], in0=tB[:, :, :], scalar=Sg(0), in1=tA[:, :, :],
                                                 op0=ALU.mult, op1=ALU.add), [tA.r(), tB.r(), cst.r()], [Bb1.r()])
            dve(lambda e: e.tensor_tensor(out=tA[:, :, :], in0=Bp2[:, :, :], in1=F1b, op=ALU.mult), RP + [Bp2.r()], [tA.r()])
            dve(lambda e: e.tensor_tensor(out=tB[:, :, :], in0=Bp1[:, :, :], in1=F2b, op=ALU.mult), RP + [Bp1.r()], [tB.r()])
            dve(lambda e: e.scalar_tensor_tensor(out=Bb2[:, :, :], in0=tA[:, :, :], scalar=Sg(1), in1=tB[:, :, :],
                                                 op0=ALU.mult, op1=ALU.add), [tA.r(), tB.r(), cst.r()], [Bb2.r()])
            for (Bb, Tc) in ((Bb1, T1c), (Bb2, T2c)):
                bk = self.bank[4]
                k.op("pe", lambda e: e.transpose(bk[:, 0:128], Bb[:, :, :].rearrange("p a b -> p (a b)"), ident),
                     R=[Bb.r(), cst.r()], W=[bk.r()])
                k.op("act", lambda e: e.copy(out=Tc[:, :], in_=bk[:, 0:128]), R=[bk.r()], W=[Tc.r()])
            for g8 in range(8):
                g = g0 + g8
                r2 = self.rot("s5g", 2)
                lb1, lb2, c1p, c2p = LB1[r2], LB2[r2], C1p[r2], C2p[r2]
                rm = cst[:, CST["rowm"][0] + g8:CST["rowm"][0] + g8 + 1]
                dve(lambda e: e.tensor_scalar(out=lb1[:, :], in0=T1c[:, :], scalar1=rm, scalar2=None, op0=ALU.mult),
                    [T1c.r(), cst.r()], [lb1.r()])
                dve(lambda e: e.tensor_scalar(out=lb2[:, :], in0=T2c[:, :], scalar1=rm, scalar2=None, op0=ALU.mult),
                    [T2c.r(), cst.r()], [lb2.r()])
                k.op("pool", lambda e: e.memset(c1p[:, :], 0.0), W=[c1p.r()])
                k.op("pool", lambda e: e.memset(c2p[:, :], 0.0), W=[c2p.r()])
                dve(lambda e: e.tensor_scalar(out=c1p[:, g8 * 16:(g8 + 1) * 16], in0=Cp1[:, g8, :], scalar1=Sg(1),
                                              scalar2=None, op0=ALU.mult), [Cp1.r(), cst.r()], [c1p.r()])
                dve(lambda e: e.tensor_scalar(out=c2p[:, g8 * 16:(g8 + 1) * 16], in0=Cp2[:, g8, :], scalar1=-1.0,
                                              scalar2=None, op0=ALU.mult), [Cp2.r()], [c2p.r()])
                th = P["tht"][:, g:g + 1]
                dve(lambda e: e.tensor_scalar(out=targ[:, :], in0=iota, scalar1=th, scalar2=None, op0=ALU.mult),
                    [cst.r(), P["tht"].r()], [targ.r()])
                sincos(St, Ct, targ[:, :], 1024, [targ.r()])
                rho = P["rho"][:, g:g + 1]
                dve(lambda e: e.tensor_scalar(out=rhof[:, 0:1024], in0=iota, scalar1=0.0, scalar2=rho,
                                              op0=ALU.mult, op1=ALU.add), [cst.r(), P["rho"].r()], [rhof.r()])
                if ns:
                    dve(lambda e: e.tensor_scalar(out=rhof[:, 1024:1152], in0=self.cs("mask128"), scalar1=rho,
                                                  scalar2=None, op0=ALU.mult), [cst.r(), P["rho"].r()], [rhof.r()], acc=True)
                for (t0, tw) in self.tbs:
                    i = self.rot("gu", 2)
                    b1, b2 = self.bank[i], self.bank[2 + i]
                    k.op("pe", lambda e: e.matmul(b1[:, 0:tw], lb1[:, :], xn[:, c, t0:t0 + tw], start=True, stop=True),
                         R=[lb1.r(), xn.r()], W=[b1.r()])
                    k.op("pe", lambda e: e.matmul(b2[:, 0:tw], lb2[:, :], xn[:, c, t0:t0 + tw], start=True, stop=True),
                         R=[lb2.r(), xn.r()], W=[b2.r()])
                    if t0 < 1024:
                        cv, sv = Ct[:, t0:t0 + tw], St[:, t0:t0 + tw]
                        z1, z2, mo, yo = b1[:, 0:tw], b2[:, 0:tw], m[:, t0:t0 + tw], yf[:, t0:t0 + tw]
                    else:
                        cv = Ct[:, 0:8].unsqueeze(1).broadcast_to([128, 16, 8])
                        sv = St[:, 0:8].unsqueeze(1).broadcast_to([128, 16, 8])
                        r3 = lambda a: a.rearrange("p (b t) -> p b t", b=16)
                        z1, z2, mo, yo = r3(b1[:, 0:128]), r3(b2[:, 0:128]), r3(m[:, 1024:1152]), r3(yf[:, 1024:1152])
                    dve(lambda e: e.tensor_tensor(out=mo, in0=z1, in1=cv, op=ALU.mult), [b1.r(), Ct.r()], [m.r()], acc=True)
                    dve(lambda e: e.tensor_tensor(out=yo, in0=z2, in1=sv, op=ALU.mult), [b2.r(), St.r()], [yf.r()], acc=True)
                    dve(lambda e: e.tensor_tensor(out=mo, in0=mo, in1=yo, op=ALU.add), [m.r(), yf.r()], [m.r()], acc=True)
                if ns:
                    ms0 = m[:, 1024:1152].rearrange("p (b t) -> p b t", b=16)[:, :, 0]
                    dve(lambda e: e.scalar_tensor_tensor(out=ms0, in0=hin[:, g8, 0:16], scalar=rho, in1=ms0,
                                                         op0=ALU.mult, op1=ALU.add),
                        [hin.r(), m.r(), P["rho"].r()], [m.r()], acc=True)
                dve(lambda e: e.tensor_tensor_scan(out=G[:, 0:TB], data0=rhof[:, 0:TB], data1=m[:, 0:TB],
                                                   initial=hin[:, g8, 16:17], op0=ALU.mult, op1=ALU.add),
                    [rhof.r(), m.r(), hin.r()], [G.r()])
                for (Gb, tab) in ((G1b, Ct), (G2b, St)):
                    k.op("pool", lambda e: e.tensor_tensor(out=Gb[:, 0:1024], in0=G[:, 0:1024], in1=tab[:, 0:1024],
                                                            op=ALU.mult), R=[G.r(), tab.r()], W=[Gb.r()])
                    if ns:
                        r3 = lambda a: a.rearrange("p (b t) -> p b t", b=16)
                        k.op("pool", lambda e: e.tensor_tensor(
                            out=r3(Gb[:, 1024:1152]), in0=r3(G[:, 1024:1152]),
                            in1=tab[:, 0:8].unsqueeze(1).broadcast_to([128, 16, 8]), op=ALU.mult),
                            R=[G.r(), tab.r()], W=[Gb.r()], acc=True)
                for i, (t0, tw) in enumerate(self.tbs):
                    by = self.bank[nbk[i]]
                    k.op("pe", lambda e: e.matmul(by[:, 0:tw], c1p[:, :], G1b[:, t0:t0 + tw], start=(g8 == 0), stop=False),
                         R=[c1p.r(), G1b.r()], W=[by.r()])
                    k.op("pe", lambda e: e.matmul(by[:, 0:tw], c2p[:, :], G2b[:, t0:t0 + tw], start=False, stop=(g8 == 7)),
                         R=[c2p.r(), G2b.r()], W=[by.r()])
                ncol = 17 if ns else 1
                cols = []
                if ns:
                    Gs7 = G[:, 1024:1152].rearrange("p (b t) -> p b t", b=16)[:, :, 7]
                    dve(lambda e: e.tensor_scalar(out=ftmp[:, 0, 0:16], in0=Gs7, scalar1=Ct[:, 7:8], scalar2=None,
                                                  op0=ALU.mult), [G.r(), Ct.r()], [ftmp.r()], acc=True)
                    dve(lambda e: e.tensor_scalar(out=ftmp[:, 1, 0:16], in0=Gs7, scalar1=St[:, 7:8], scalar2=None,
                                                  op0=ALU.mult), [G.r(), St.r()], [ftmp.r()], acc=True)
                dve(lambda e: e.tensor_tensor(out=ftmp[:, 0, 16:17], in0=G[:, 1023:1024], in1=Ct[:, 1023:1024],
                                              op=ALU.mult), [G.r(), Ct.r()], [ftmp.r()], acc=True)
                dve(lambda e: e.tensor_tensor(out=ftmp[:, 1, 16:17], in0=G[:, 1023:1024], in1=St[:, 1023:1024],
                                              op=ALU.mult), [G.r(), St.r()], [ftmp.r()], acc=True)
                c_lo = 0 if ns else 16
                bk = self.bank[4]
                k.op("pe", lambda e: e.matmul(bk[:, c_lo:17], ident, ftmp[:, 0, c_lo:17], start=True, stop=False),
                     R=[cst.r(), ftmp.r()], W=[bk.r()])
                k.op("pe", lambda e: e.matmul(bk[:, c_lo:17], self.cs("J"), ftmp[:, 1, c_lo:17], start=False, stop=True),
                     R=[cst.r(), ftmp.r()], W=[bk.r()])
                k.op("act", lambda e: e.copy(out=hfin[:, g8, c_lo:17], in_=bk[:, c_lo:17]), R=[bk.r()], W=[hfin.r()], acc=True)
            k.dma("sp", self.o_s5.t[j][:, g0:g0 + 8, c_lo:17], hfin[:, :, c_lo:17], R=[hfin.r()], W=[self.o_s5.r()],
                  allow_slow_non_contiguous=True)
            for i, (t0, tw) in enumerate(self.tbs):
                by = self.bank[nbk[i]]
                dve(lambda e: e.scalar_tensor_tensor(out=yf[:, t0:t0 + tw], in0=xn[:, c, t0:t0 + tw],
                                                     scalar=self.vcol("d_c%d" % j, c), in1=by[:, 0:tw],
                                                     op0=ALU.mult, op1=ALU.add),
                    [xn.r(), by.r(), vec.r()], [yf.r()])
            zt = Tile(xn[:, c, :], "xnc")
            zt.reg = xn.r()
            self.gelu_(zt, yf, gt, TB)
        Wg = self.W["w_glu_c"][j]
        for s in range(D // 256):
            sg, vg = self.load_slab(Wg, KC, s * 256, 256)
            for oc2 in range(2):
                oc = s * 2 + oc2
                for (t0, tw) in self.tbs:
                    bg = self.bank[self.rot("pj", 2)]
                    for kc in range(KC):
                        k.op("pe", lambda e: e.matmul(bg[:, 0:tw], vg[:, kc, oc2 * 128:(oc2 + 1) * 128],
                                                       xn[:, kc, t0:t0 + tw], start=(kc == 0), stop=(kc == KC - 1)),
                             R=[sg.r(), xn.r()], W=[bg.r()])
                    t1 = yf
                    k.op("act", lambda e: e.activation(out=t1[:, 0:tw], in_=bg[:, 0:tw], func=AF.Sigmoid,
                                                       bias=self.vcol("b_glu_c%d" % j, oc)),
                         R=[bg.r(), vec.r()], W=[t1.r()])
                    dve(lambda e: e.tensor_tensor(out=t1[:, 0:tw], in0=t1[:, 0:tw], in1=xn[:, oc, t0:t0 + tw], op=ALU.mult),
                        [t1.r(), xn.r()], [t1.r()])
                    dve(lambda e: e.tensor_tensor(out=x[:, oc, t0:t0 + tw], in0=x[:, oc, t0:t0 + tw], in1=t1[:, 0:tw],
                                                  op=ALU.add), [t1.r(), x.r(oc)], [x.r(oc)])


def build_inputs(inp, core):
    s = core % 4
    xs = np.asarray(inp["x_sample"], np.float32)[core * NS:(core + 1) * NS].reshape(NS * TS, D)
    xp = np.asarray(inp["x_prompt"], np.float32)[s]
    xT = np.ascontiguousarray(np.concatenate([xp, xs], 0).T)
    pp = np.asarray(inp["p_prompt"], np.float32)[:, s]
    psm = np.asarray(inp["p_sample"], np.float32)[:, core * NS:(core + 1) * NS].reshape(4, NS * TS, 256)
    pT = np.ascontiguousarray(np.concatenate([pp, psm], 1).transpose(0, 2, 1))
    return {"xT": xT, "pT": pT}


_PROG = {}


def kernel(**inputs):
    enable = inputs.pop("_enable", ("ffn", "ple", "mix"))
    if enable not in _PROG:
        _PROG[enable] = Prog(enable)
    prog = _PROG[enable]
    vec = pack_vec(inputs)
    shared = {"vec": vec, "cst": pack_cst()}
    for nm in prog.W:
        shared[nm] = np.ascontiguousarray(np.asarray(inputs[nm], np.float32))
    in_maps = []
    ncores = int(os.environ.get('KCORES', '8'))
    for c in range(ncores):
        m = dict(shared)
        m.update(build_inputs(inputs, c))
        m.update(pack_states(inputs, c))
        in_maps.append(m)
    res = run_bass_kernel_spmd(prog.nc, in_maps, core_ids=list(range(ncores)))
    R = list(res.results) + [res.results[0]] * (8 - ncores)
    global _LAST
    _LAST = R
    y_prompt = np.stack([R[s]["yT"][:, 0:2048].T for s in range(4)])
    y_sample = np.concatenate([R[c]["yT"][:, 2048:].T.reshape(NS, TS, D) for c in range(8)], 0)
    A = np.ascontiguousarray

    def per_layer(fn):
        return np.stack([fn(j) for j in range(2)])
    pw = per_layer(lambda j: np.stack([R[s]["o_wkv"][j][:, 16, :].reshape(16, 64, 64) for s in range(4)]))
    psh = per_layer(lambda j: np.stack([R[s]["o_shift"][j][:, :, 16].T.reshape(-1)[:3360] for s in range(4)]))
    ph = per_layer(lambda j: np.stack([R[s]["o_h"][j][:, :, 16].T.reshape(1024) for s in range(4)]))
    pc = per_layer(lambda j: np.stack([R[s]["o_conv"][j][:, :, 16, :].transpose(2, 1, 0).reshape(3, 1024)
                                       for s in range(4)]))
    pre = per_layer(lambda j: np.stack([R[s]["o_s5"][j][0:64, :, 16].T for s in range(4)]))
    pim = per_layer(lambda j: np.stack([R[s]["o_s5"][j][64:128, :, 16].T for s in range(4)]))
    sw = per_layer(lambda j: np.concatenate([
        R[c]["o_wkv"][j][:, 0:16, :].reshape(16, 8, 16, 8, 64).transpose(2, 0, 1, 3, 4).reshape(16, 16, 64, 64)
        for c in range(8)], 0))
    ssh = per_layer(lambda j: np.concatenate([
        R[c]["o_shift"][j][:, :, 0:16].transpose(2, 1, 0).reshape(16, -1)[:, :3360] for c in range(8)], 0))
    sh = per_layer(lambda j: np.concatenate([
        R[c]["o_h"][j][:, :, 0:16].transpose(2, 1, 0).reshape(16, 1024) for c in range(8)], 0))
    sc = per_layer(lambda j: np.concatenate([
        R[c]["o_conv"][j][:, :, 0:16, :].transpose(2, 3, 1, 0).reshape(16, 3, 1024) for c in range(8)], 0))
    sre = per_layer(lambda j: np.concatenate([R[c]["o_s5"][j][0:64, :, 0:16].transpose(2, 1, 0) for c in range(8)], 0))
    sim = per_layer(lambda j: np.concatenate([R[c]["o_s5"][j][64:128, :, 0:16].transpose(2, 1, 0) for c in range(8)], 0))
    outs = (y_prompt, y_sample, pw, psh, ph, pc, pre, pim, sw, ssh, sh, sc, sre, sim)
    return tuple(A(o.astype(np.float32)) for o in outs)
```

```python
import contextlib
import numpy as np
import concourse.bass as bass
import concourse.mybir as mybir
from concourse.bass_utils import run_bass_kernel_spmd

F32 = mybir.dt.float32
BF16 = mybir.dt.bfloat16
I32 = mybir.dt.int32
ALU = mybir.AluOpType
AF = mybir.ActivationFunctionType
AX = mybir.AxisListType
import os as _os
EPOCH_N = int(_os.environ.get("KEPOCH", "30000"))
SAME_ENGINE_WAITS = int(_os.environ.get("KSEW", "1"))


class Reg:
    __slots__ = ("name", "writes", "reads", "dsem", "dcnt")

    def __init__(self, name):
        self.name = name
        self.writes = []
        self.reads = []
        self.dsem = None
        self.dcnt = 0


class Tile:
    def __init__(self, t, name):
        self.t = t
        self.name = name
        self.reg = Reg(name)
        self.subs = {}

    def __getitem__(self, idx):
        return self.t[idx]

    def r(self, key=None):
        if key is None:
            return self.reg
        if key not in self.subs:
            self.subs[key] = Reg("%s/%s" % (self.name, key))
        return self.subs[key]

    def all(self):
        return [self.reg] + list(self.subs.values())


class K:
    def __init__(self):
        self.nc = bass.Bass("TRN2", target_bir_lowering=False)
        nc = self.nc
        self.es = contextlib.ExitStack()
        self.eng = {"pe": nc.tensor, "act": nc.scalar, "dve": nc.vector,
                    "pool": nc.gpsimd, "sp": nc.sync}
        self.sem = {}
        self.cnt = {}
        self.epoch = {}
        for e in ("pe", "act", "dve", "pool"):
            self.sem[(e, 0)] = self.es.enter_context(nc.semaphore("s_" + e))
            self.cnt[e] = 0
            self.epoch[e] = 0
        self.waited = {e: {} for e in self.eng}
        self.sew = True
        self.nsem = 4
        self.ninst = 0
        self.dregs = []

    def sb(self, name, shape, dt=F32):
        t = self.es.enter_context(self.nc.sbuf_tensor(name, list(shape), dt))
        return Tile(t, name)

    def ps(self, name, shape, dt=F32):
        t = self.es.enter_context(self.nc.psum_tensor(name, list(shape), dt))
        return Tile(t, name)

    def dram(self, name, shape, dt=F32, kind="Internal"):
        t = self.nc.dram_tensor(name, list(shape), dt, kind=kind)
        return Tile(t.ap(), name)

    def _need(self, e, ev):
        key, val, src = ev
        if src == e and (e == "pe" or not SAME_ENGINE_WAITS or not self.sew):
            return
        w = self.waited[e]
        if w.get(key, 0) >= val:
            return
        sem = self.sem[key] if isinstance(key, tuple) else key.dsem
        self.eng[e].wait_ge(sem, val)
        w[key] = val
        self.ninst += 1

    def _deps(self, e, R, W, acc=False):
        for r in R:
            for ev in r.writes:
                self._need(e, ev)
        for r in W:
            for ev in r.writes:
                if ev[2] != e:
                    self._need(e, ev)
            for ev in r.reads:
                if ev[2] != e:
                    self._need(e, ev)

    def _mark(self, ev, R, W, acc=False):
        for r in R:
            r.reads = [x for x in r.reads if x[0] != ev[0]] + [ev]
        for r in W:
            if acc:
                r.writes = [x for x in r.writes if x[0] != ev[0]] + [ev]
            else:
                r.writes = [ev]
            r.reads = []

    def op(self, e, fn, R=(), W=(), acc=False, sew=True):
        self.sew = sew
        self._deps(e, R, W)
        self.sew = True
        ins = fn(self.eng[e])
        if self.cnt[e] >= EPOCH_N:
            self.epoch[e] += 1
            self.cnt[e] = 0
            self.sem[(e, self.epoch[e])] = self.es.enter_context(
                self.nc.semaphore("s_%s_%d" % (e, self.epoch[e])))
        self.cnt[e] += 1
        key = (e, self.epoch[e])
        ins.then_inc(self.sem[key], 1)
        ev = (key, self.cnt[e], e)
        self._mark(ev, R, W, acc)
        self.ninst += 1
        return ins

    def dma(self, q, out, in_, R=(), W=(), acc=True, **kw):
        self._deps(q, R, W)
        d = W[0]
        if d.dsem is None:
            d.dsem = self.es.enter_context(self.nc.semaphore("d%d" % self.nsem))
            self.nsem += 1
            self.dregs.append(d)
        ins = self.eng[q].dma_start(out=out, in_=in_, **kw)
        d.dcnt += 16
        ins.then_inc(d.dsem, 16)
        ev = (d, d.dcnt, "dma")
        self._mark(ev, R, W, acc)
        self.ninst += 1
        return ins

    def barrier(self):
        evs = [((e, self.epoch[e]), self.cnt[e], e) for e in ("pe", "act", "dve", "pool")]
        evs += [(d, d.dcnt, "dma") for d in self.dregs]
        for e in ("pe", "act", "dve", "pool", "sp"):
            for ev in evs:
                if ev[1] > 0 and not (ev[2] == e and e == "pe"):
                    w = self.waited[e]
                    if w.get(ev[0], 0) < ev[1]:
                        sem = self.sem[ev[0]] if isinstance(ev[0], tuple) else ev[0].dsem
                        self.eng[e].wait_ge(sem, ev[1])
                        w[ev[0]] = ev[1]
                        self.ninst += 1

    def finish(self, regs):
        for r in regs:
            for ev in r.writes:
                self._need("sp", ev)

    def close(self):
        self.es.close()

D = 2048
DFF = 5632
KC = 16
NPT = 1024
NS = 16
TS = 8
NTOK = 2048 + NS * TS
EPS = 1e-6
import os
DEPTH = int(os.environ.get('KDEPTH', '4'))
NPASS = int(os.environ.get('KNPASS', '2'))


def vec_layout():
    lay = {}
    off = [0]

    def add(name, n):
        lay[name] = (off[0], n)
        off[0] += n
    for l in range(4):
        for nm in ("norm_ffn1", "norm_mix", "norm_ffn2", "norm_ple"):
            add("%s%d" % (nm, l), 16)
    add("final_norm", 16)
    for j in range(2):
        add("d_c%d" % j, 16)
        add("b_glu_c%d" % j, 16)
    for j in range(2):
        for nm in ("w0_a", "a0_a", "kk_a", "ka_a", "rk_a", "lnx_g", "lnx_b", "conv_b_b",
                   "ba_b", "bx_b", "lam_b", "mu_r", "mu_k", "mu_v"):
            add("%s%d" % (nm, j), 8)
        for q in range(4):
            add("conv_w%d_%d" % (q, j), 8)
        add("mu_x%d" % j, 4)
    return lay, off[0]


def pack_vec(inp):
    lay, n = vec_layout()
    v = np.zeros((128, n), np.float32)

    def put(name, arr):
        o, c = lay[name]
        a = np.asarray(arr, np.float32).reshape(-1)
        v[:, o:o + c] = a.reshape(c, 128).T
    for l in range(4):
        for nm in ("norm_ffn1", "norm_mix", "norm_ffn2", "norm_ple"):
            put("%s%d" % (nm, l), inp[nm][l])
    put("final_norm", inp["final_norm"])
    for j in range(2):
        put("d_c%d" % j, inp["d_c"][j])
        put("b_glu_c%d" % j, inp["b_glu_c"][j])
        for nm in ("w0_a", "a0_a", "kk_a", "ka_a", "rk_a", "lnx_g", "lnx_b", "conv_b_b",
                   "ba_b", "bx_b", "lam_b"):
            put("%s%d" % (nm, j), inp[nm][j])
        mu = np.asarray(inp["mu_a"][j], np.float32)
        put("mu_r%d" % j, mu[0:1024])
        put("mu_k%d" % j, mu[1024:2048])
        put("mu_v%d" % j, mu[2048:3072])
        for q in range(4):
            put("conv_w%d_%d" % (q, j), inp["conv_w_b"][j][q])
        o, _ = lay["mu_x%d" % j]
        v[:, o] = mu[3072:3200]
        v[:, o + 2] = mu[3200:3328]
        v[0:32, o + 3] = mu[3328:3360]
    return v


CST = {"ident": (0, 128), "iota": (128, 1024), "mask128": (1152, 128), "J": (1280, 128),
       "blk": (1408, 128), "sgn": (1536, 4), "rep": (1540, 128), "rowm": (1668, 8), "one": (1676, 1), "negpi": (1677, 1), "halfpi": (1678, 1)}
NCST = 1680
GN_EPS = 64e-5
W_A = 1024
COLS_A = 3360


def pack_cst():
    c = np.zeros((128, NCST), np.float32)

    def put(nm, arr):
        o, n = CST[nm]
        c[:arr.shape[0], o:o + n] = arr
    put("ident", np.eye(128, dtype=np.float32))
    put("iota", np.broadcast_to(np.arange(1, 1025, dtype=np.float32)[None, :], (128, 1024)))
    m = np.ones(128, np.float32)
    m[0::8] = 0.0
    put("mask128", np.broadcast_to(m[None, :], (128, 128)))
    J = np.zeros((128, 128), np.float32)
    for p in range(64):
        J[64 + p, p] = -1.0
        J[p, 64 + p] = 1.0
    put("J", J)
    blk = np.zeros((128, 128), np.float32)
    blk[0:64, 0:64] = 1.0
    blk[64:128, 64:128] = 1.0
    put("blk", blk)
    sg = np.zeros((128, 4), np.float32)
    sg[0:64, 0] = -1.0
    sg[64:, 0] = 1.0
    sg[0:64, 1] = 1.0
    sg[64:, 1] = -1.0
    sg[:, 2] = -1.0
    sg[:, 3] = 1.0
    put("sgn", sg)
    rep = np.zeros((16, 128), np.float32)
    for hh in range(16):
        rep[hh, hh * 8:hh * 8 + 8] = 1.0
    put("rep", rep)
    rm = np.zeros((128, 8), np.float32)
    for p in range(128):
        rm[p, p // 16] = 1.0
    put("rowm", rm)
    put("one", np.ones((128, 1), np.float32))
    put("negpi", np.full((128, 1), -np.pi, np.float32))
    put("halfpi", np.full((128, 1), 0.5 * np.pi, np.float32))
    return c


def pack_states(inp, core):
    f = lambda a: np.ascontiguousarray(np.asarray(a, np.float32))
    sl = slice(core * NS, (core + 1) * NS)
    o = {}
    o["st_h"] = f(np.asarray(inp["state_b_h"])[:, sl].reshape(2, NS, 8, 128).transpose(0, 3, 2, 1))
    o["st_conv"] = f(np.asarray(inp["state_b_conv"])[:, sl].reshape(2, NS, 3, 8, 128).transpose(0, 4, 3, 1, 2))
    re = np.asarray(inp["state_c_re"])[:, sl].transpose(0, 3, 2, 1)
    im = np.asarray(inp["state_c_im"])[:, sl].transpose(0, 3, 2, 1)
    o["st_s5"] = f(np.concatenate([re, im], 1))
    sh = np.asarray(inp["state_a_shift"], np.float32)[:, sl]
    shp = np.zeros((2, NS, 27 * 128), np.float32)
    shp[:, :, 0:3360] = sh
    o["st_shift"] = f(shp.reshape(2, NS, 27, 128).transpose(0, 3, 2, 1))
    wk = np.asarray(inp["state_a_wkv"], np.float32)[:, sl].reshape(2, NS, 16, 8, 8, 64)
    o["st_wkv"] = f(wk.transpose(0, 2, 3, 1, 4, 5).reshape(2, 128, NS, 512))
    for nm in ("w2_a", "a2_a", "g2_a"):
        o[nm] = f(inp[nm])
    for nm in ("wa_b", "wx_b"):
        w = np.asarray(inp[nm], np.float32)
        bd = np.zeros((2, 8, 128, 128), np.float32)
        for c in range(8):
            bd[:, c, 0:64, 0:64] = w[:, 2 * c]
            bd[:, c, 64:128, 64:128] = w[:, 2 * c + 1]
        o[nm + "d"] = f(bd.transpose(0, 2, 1, 3))
    are = np.asarray(inp["a_re_c"], np.float32).transpose(0, 2, 1)
    aim = np.asarray(inp["a_im_c"], np.float32).transpose(0, 2, 1)
    o["s5_are"] = f(np.concatenate([are, are], 1))
    o["s5_aim"] = f(np.concatenate([aim, aim], 1))
    o["s5_ldt"] = f(np.broadcast_to(np.asarray(inp["log_dt_c"], np.float32)[:, None, :], (2, 128, 128)))
    bre = np.asarray(inp["b_re_c"], np.float32).transpose(0, 2, 1, 3)
    bim = np.asarray(inp["b_im_c"], np.float32).transpose(0, 2, 1, 3)
    o["s5_b1"] = f(np.concatenate([bre, bim], 1))
    o["s5_b2"] = f(np.concatenate([bim, bre], 1))
    cre = np.asarray(inp["c_re_c"], np.float32).transpose(0, 3, 1, 2)
    cim = np.asarray(inp["c_im_c"], np.float32).transpose(0, 3, 1, 2)
    o["s5_c1"] = f(np.concatenate([cre, cim], 1))
    o["s5_c2"] = f(np.concatenate([cim, cre], 1))
    return o


class Prog:
    def __init__(self, enable=("ffn", "ple", "mix")):
        self.enable = enable
        k = self.k = K()
        nc = self.nc = k.nc
        self.lay, self.nv = vec_layout()

        def din(name, shape):
            return nc.dram_tensor(name, list(shape), F32, kind="ExternalInput").ap()
        self.xT = din("xT", [D, NTOK])
        self.pT = din("pT", [4, 256, NTOK])
        self.vecd = din("vec", [128, self.nv])
        self.W = {}
        for nm, shp in (("ffn1_wg", [4, D, DFF]), ("ffn1_wu", [4, D, DFF]), ("ffn1_wd", [4, DFF, D]),
                        ("ffn2_wg", [4, D, DFF]), ("ffn2_wu", [4, D, DFF]), ("ffn2_wd", [4, DFF, D]),
                        ("ple_gate", [4, D, D]), ("ple_proj", [4, 256, D])):
            self.W[nm] = din(nm, shp)
        for nm, shp in (("w_in_ab", [2, D, 5408]), ("w_out_ab", [2, D, D]), ("w_glu_c", [2, D, D])):
            self.W[nm] = din(nm, shp)
        self.cstd = din("cst", [128, NCST])
        self.I = {}
        for nm, shp in (("st_h", [2, 128, 8, 16]), ("st_conv", [2, 128, 8, 16, 3]), ("st_s5", [2, 128, 128, 16]),
                        ("wa_bd", [2, 128, 8, 128]), ("wx_bd", [2, 128, 8, 128]),
                        ("s5_are", [2, 128, 128]), ("s5_aim", [2, 128, 128]), ("s5_ldt", [2, 128, 128]),
                        ("s5_b1", [2, 128, 128, 16]), ("s5_b2", [2, 128, 128, 16]),
                        ("s5_c1", [2, 128, 128, 16]), ("s5_c2", [2, 128, 128, 16])):
            self.I[nm] = din(nm, shp)
        for nm, shp in (("w2_a", [2, 64, 1024]), ("a2_a", [2, 64, 1024]), ("g2_a", [2, 160, 1024]),
                        ("st_shift", [2, 128, 27, 16]), ("st_wkv", [2, 128, 16, 512])):
            self.I[nm] = din(nm, shp)
        self.o_shift = k.dram("o_shift", [2, 128, 27, 17], F32, kind="ExternalOutput")
        self.o_wkv = k.dram("o_wkv", [2, 128, 17, 512], F32, kind="ExternalOutput")
        self.Zs = k.dram("Zs", [1152, 16, 5, 64], F32)
        self.Vs = k.dram("Vs", [1152, 1024], F32)
        self.Ys = k.dram("Ys", [1152, 1024], F32)
        self.Bon = k.dram("Bon", [8, 128, 1152], F32)
        self.yT = k.dram("yT", [D, NTOK], F32, kind="ExternalOutput")
        self.o_h = k.dram("o_h", [2, 128, 8, 17], F32, kind="ExternalOutput")
        self.o_conv = k.dram("o_conv", [2, 128, 8, 17, 3], F32, kind="ExternalOutput")
        self.o_s5 = k.dram("o_s5", [2, 128, 128, 17], F32, kind="ExternalOutput")
        self.outs = [self.yT, self.o_h, self.o_conv, self.o_s5, self.o_shift, self.o_wkv]

        self.x = k.sb("x", [128, KC, 1152], F32)
        self.xn = k.sb("xn", [128, KC, 1152], BF16)
        self.rstd = k.sb("rstd", [128, 1152], F32)
        self.sq = [k.sb("sq%d" % i, [128, 1152], BF16) for i in range(2)]
        self.vec = k.sb("vecs", [128, self.nv], F32)
        self.ones_bf = k.sb("ones_bf", [128, 128], BF16)
        self.AW = 18688
        self._avc = {}
        self.A = k.sb("arena", [128, self.AW], F32)
        self.slab = [self.av("slab%d" % i, i * 2048, 2048, BF16) for i in range(6)]
        self.h = self.av("hbuf", 12288, 1152, BF16, a=2)
        self.t1 = [self.av("t1_%d" % i, 13440 + i * 512, 512) for i in range(2)]
        self.t2 = [self.av("t2_%d" % i, 14464 + i * 512, 512) for i in range(2)]
        self.pt = self.av("ptile", 15488, 1152, BF16, a=2)
        self.pproj = self.av("pproj", 16640, 2048, BF16)
        self.ytmp = [self.av("ytmp%d" % i, i * 1152, 1152) for i in range(2)]
        self.nslab = 6
        self.bank = [k.ps("bank%d" % i, [128, 512], F32) for i in range(8)]
        self.lru_small = (k.sb("l_hst", [128, 8, 17], F32), k.sb("l_cvo", [128, 8, 17, 3], F32),
                          k.sb("l_h0", [128, 8, 17], F32), k.sb("l_cv0", [128, 8, 17, 3], F32))
        self.lru_nsp = k.sb("l_nsp", [128, 16], F32)
        self.sg1b = k.sb("r_sg1b", [128, 1152], BF16)
        self.ybuf = self.av("ybuf", 4096, 4608, BF16, a=8)
        self.rr = {}

        k.dma("sp", self.vec[:], self.vecd, W=[self.vec.r()])
        self.cst = k.sb("cst_sb", [128, NCST], F32)
        k.dma("sp", self.cst[:], self.cstd, W=[self.cst.r()])
        k.op("dve", lambda e: e.memset(self.ones_bf[:], 1.0), W=[self.ones_bf.r()])
        self.epst = k.sb("epst", [128, 2], F32)
        k.op("dve", lambda e: e.memset(self.epst[:], EPS), W=[self.epst.r()])

        for ps in range(NPASS):
            self.run_pass(ps)
        k.finish([o.r() for o in self.outs])
        k.close()

    def av(self, name, off, n, dt=F32, a=None):
        assert off + n <= self.AW, (name, off, n)
        key = (name, off, n, str(dt), a)
        if key in self._avc:
            return self._avc[key]
        ap = self.A.t[:, off:off + n]
        if dt != F32:
            ap = ap.bitcast(dt)
        if a is not None:
            ap = ap.rearrange("p (a b) -> p a b", a=a)
        self._avc[key] = Tile(ap, name)
        return self._avc[key]

    def rot(self, name, n):
        i = self.rr.get(name, 0)
        self.rr[name] = i + 1
        return i % n

    def vcol(self, name, c):
        o, n = self.lay[name]
        return self.vec[:, o + c:o + c + 1]

    def run_pass(self, ps):
        k = self.k
        self.ps_ = ps
        self.TB = TB = 1024 if ps == 0 else 1152
        self.g0 = 0 if ps == 0 else 1024
        self.tbs = [(0, 512), (512, 512)] + ([(1024, 128)] if ps == 1 else [])
        x = self.x
        xTv = self.xT.rearrange("(c p) t -> p c t", p=128)
        k.dma("sp", x[:, :, 0:TB], xTv[:, :, self.g0:self.g0 + TB], W=[x.r(c) for c in range(KC)])
        for l in range(DEPTH):
            if "ffn" in self.enable:
                self.ffn(l, 1)
            if "mix" in self.enable:
                self.mixer(l)
            if "ffn" in self.enable:
                self.ffn(l, 2)
            if "ple" in self.enable:
                self.ple(l)
        k.barrier()
        self.norm("final_norm", final=True)
        k.barrier()

    def norm(self, gname, final=False):
        k, x, xn, TB = self.k, self.x, self.xn, self.TB
        nb = [5, 6, 7]
        for c in range(KC):
            s = self.sq[c % 2]
            k.op("act", lambda e: e.activation(out=s[:, 0:TB], in_=x[:, c, 0:TB], func=AF.Square),
                 R=[x.r(c)], W=[s.r()])
            for i, (t0, tw) in enumerate(self.tbs):
                b = self.bank[nb[i]]
                k.op("pe", lambda e: e.matmul(b[:, 0:tw], self.ones_bf[:], s[:, t0:t0 + tw],
                                               start=(c == 0), stop=(c == KC - 1)),
                     R=[self.ones_bf.r(), s.r()], W=[b.r()])
        for i, (t0, tw) in enumerate(self.tbs):
            b = self.bank[nb[i]]
            k.op("act", lambda e: e.activation(out=self.rstd[:, t0:t0 + tw], in_=b[:, 0:tw], func=AF.Sqrt,
                                               bias=self.epst[:, 0:1], scale=1.0 / D),
                 R=[b.r(), self.epst.r()], W=[self.rstd.r()])
        k.op("dve", lambda e: e.reciprocal(out=self.rstd[:, 0:TB], in_=self.rstd[:, 0:TB]),
             R=[self.rstd.r()], W=[self.rstd.r()])
        yTv = self.yT.t.rearrange("(c p) t -> p c t", p=128)
        for c in range(KC):
            if final:
                o = self.ytmp[c % 2]
                k.op("dve", lambda e: e.scalar_tensor_tensor(out=o[:, 0:TB], in0=x[:, c, 0:TB],
                                                              scalar=self.vcol(gname, c), in1=self.rstd[:, 0:TB],
                                                              op0=ALU.mult, op1=ALU.mult),
                     R=[x.r(c), self.rstd.r(), self.vec.r()], W=[o.r()])
                k.dma("sp", yTv[:, c, self.g0:self.g0 + TB], o[:, 0:TB], R=[o.r()], W=[self.yT.r()])
            else:
                k.op("dve", lambda e: e.scalar_tensor_tensor(out=xn[:, c, 0:TB], in0=x[:, c, 0:TB],
                                                              scalar=self.vcol(gname, c), in1=self.rstd[:, 0:TB],
                                                              op0=ALU.mult, op1=ALU.mult),
                     R=[x.r(c), self.rstd.r(), self.vec.r()], W=[xn.r()])

    def load_slab(self, W2d, kchunks, c0, cw, rows0=0):
        k = self.k
        s = self.slab[self.rot("slab", self.nslab)]
        src = W2d[rows0:rows0 + kchunks * 128, :].rearrange("(kc p) f -> p kc f", p=128)[:, :, c0:c0 + cw]
        dst = s[:, 0:kchunks * cw].rearrange("p (kc f) -> p kc f", kc=kchunks)
        step = 4 if kchunks > 4 else kchunks
        for q in range(0, kchunks, step):
            k.dma("pool", dst[:, q:q + step, :], src[:, q:q + step, :], W=[s.r()])
        return s, dst

    def ffn(self, l, which):
        k, x, xn, h = self.k, self.x, self.xn, self.h
        self.norm("norm_ffn%d%d" % (which, l))
        Wg = self.W["ffn%d_wg" % which][l]
        Wu = self.W["ffn%d_wu" % which][l]
        Wd = self.W["ffn%d_wd" % which][l]
        for s in range(DFF // 256):
            sg, vg = self.load_slab(Wg, KC, s * 256, 256)
            su, vu = self.load_slab(Wu, KC, s * 256, 256)
            sd, vd = self.load_slab(Wd, 2, 0, D, rows0=s * 256)
            for fc in range(2):
                for (t0, tw) in self.tbs:
                    i = self.rot("gu", 2)
                    bg, bu = self.bank[i], self.bank[2 + i]
                    for kc in range(KC):
                        k.op("pe", lambda e: e.matmul(bg[:, 0:tw], vg[:, kc, fc * 128:(fc + 1) * 128],
                                                       xn[:, kc, t0:t0 + tw], start=(kc == 0), stop=(kc == KC - 1)),
                             R=[sg.r(), xn.r()], W=[bg.r()])
                    for kc in range(KC):
                        k.op("pe", lambda e: e.matmul(bu[:, 0:tw], vu[:, kc, fc * 128:(fc + 1) * 128],
                                                       xn[:, kc, t0:t0 + tw], start=(kc == 0), stop=(kc == KC - 1)),
                             R=[su.r(), xn.r()], W=[bu.r()])
                    t1 = self.t1[self.rot("t1", 2)]
                    k.op("act", lambda e: e.activation(out=t1[:, 0:tw], in_=bg[:, 0:tw], func=AF.Silu),
                         R=[bg.r()], W=[t1.r()])
                    k.op("dve", lambda e: e.tensor_tensor(out=h[:, fc, t0:t0 + tw], in0=t1[:, 0:tw],
                                                           in1=bu[:, 0:tw], op=ALU.mult),
                         R=[t1.r(), bu.r()], W=[h.r(fc)])
            for dc in range(KC):
                for (t0, tw) in self.tbs:
                    bd = self.bank[4 + self.rot("dn", 2)]
                    for fc in range(2):
                        k.op("pe", lambda e: e.matmul(bd[:, 0:tw], vd[:, fc, dc * 128:(dc + 1) * 128],
                                                       h[:, fc, t0:t0 + tw], start=(fc == 0), stop=(fc == 1)),
                             R=[sd.r(), h.r(fc)], W=[bd.r()])
                    k.op("dve", lambda e: e.scalar_tensor_tensor(out=x[:, dc, t0:t0 + tw], in0=bd[:, 0:tw],
                                                                  scalar=0.5, in1=x[:, dc, t0:t0 + tw],
                                                                  op0=ALU.mult, op1=ALU.add),
                         R=[bd.r(), x.r(dc)], W=[x.r(dc)])

    def ple(self, l):
        k, x, xn, TB = self.k, self.x, self.xn, self.TB
        self.norm("norm_ple%d" % l)
        pt = self.pt
        k.dma("pool", pt[:, :, 0:TB],
              self.pT[l].rearrange("(c p) t -> p c t", p=128)[:, :, self.g0:self.g0 + TB], W=[pt.r()])
        sp_ = self.pproj
        vp = sp_[:, :].rearrange("p (kc f) -> p kc f", kc=2)
        k.dma("pool", vp, self.W["ple_proj"][l].rearrange("(kc p) f -> p kc f", p=128), W=[sp_.r()])
        for s in range(D // 256):
            sg, vg = self.load_slab(self.W["ple_gate"][l], KC, s * 256, 256)
            for oc2 in range(2):
                oc = s * 2 + oc2
                for (t0, tw) in self.tbs:
                    i = self.rot("gu", 2)
                    bg, bu = self.bank[i], self.bank[2 + i]
                    for kc in range(KC):
                        k.op("pe", lambda e: e.matmul(bg[:, 0:tw], vg[:, kc, oc2 * 128:(oc2 + 1) * 128],
                                                       xn[:, kc, t0:t0 + tw], start=(kc == 0), stop=(kc == KC - 1)),
                             R=[sg.r(), xn.r()], W=[bg.r()])
                    for kc in range(2):
                        k.op("pe", lambda e: e.matmul(bu[:, 0:tw], vp[:, kc, oc * 128:(oc + 1) * 128],
                                                       pt[:, kc, t0:t0 + tw], start=(kc == 0), stop=(kc == 1)),
                             R=[sp_.r(), pt.r()], W=[bu.r()])
                    t1 = self.t1[self.rot("t1", 2)]
                    k.op("act", lambda e: e.activation(out=t1[:, 0:tw], in_=bg[:, 0:tw], func=AF.Sigmoid),
                         R=[bg.r()], W=[t1.r()])
                    t2 = self.t2[self.rot("t2", 2)]
                    k.op("dve", lambda e: e.tensor_tensor(out=t2[:, 0:tw], in0=t1[:, 0:tw], in1=bu[:, 0:tw],
                                                           op=ALU.mult),
                         R=[t1.r(), bu.r()], W=[t2.r()])
                    k.op("pool", lambda e: e.tensor_tensor(out=x[:, oc, t0:t0 + tw], in0=x[:, oc, t0:t0 + tw],
                                                            in1=t2[:, 0:tw], op=ALU.add),
                         R=[t2.r(), x.r(oc)], W=[x.r(oc)])

    def cs(self, name, rows=128):
        o, n = CST[name]
        return self.cst[0:rows, o:o + n]

    def mixer(self, l):
        self.k.barrier()
        self.nslab = 2
        self.norm("norm_mix%d" % l)
        if l % 2 == 0:
            self.lru(l // 2)
            if "norwkv" not in self.enable:
                self.k.barrier()
                self.rwkv(l // 2)
        else:
            self.s5(l // 2)
        self.nslab = 6
        self.k.barrier()

    def gelu_(self, eng_out, src, tmp, TBc):
        k = self.k
        k.op("dve", lambda e: e.tensor_tensor(out=tmp[:, 0:TBc], in0=src[:, 0:TBc], in1=src[:, 0:TBc], op=ALU.mult),
             R=[src.r()], W=[tmp.r()])
        k.op("dve", lambda e: e.tensor_scalar(out=tmp[:, 0:TBc], in0=tmp[:, 0:TBc], scalar1=0.044715, scalar2=1.0,
                                               op0=ALU.mult, op1=ALU.add), R=[tmp.r()], W=[tmp.r()])
        k.op("dve", lambda e: e.tensor_tensor(out=tmp[:, 0:TBc], in0=tmp[:, 0:TBc], in1=src[:, 0:TBc], op=ALU.mult),
             R=[tmp.r(), src.r()], W=[tmp.r()])
        k.op("act", lambda e: e.activation(out=tmp[:, 0:TBc], in_=tmp[:, 0:TBc], func=AF.Sigmoid, scale=1.5957691216),
             R=[tmp.r()], W=[tmp.r()])
        k.op("dve", lambda e: e.tensor_tensor(out=eng_out[:, 0:TBc], in0=tmp[:, 0:TBc], in1=src[:, 0:TBc], op=ALU.mult),
             R=[tmp.r(), src.r()], W=[eng_out.r()])

    def proj_chunk(self, W2d, c0, cw, evac):
        k, xn = self.k, self.xn
        sg, vg = self.load_slab(W2d, KC, c0, cw)
        for (t0, tw) in self.tbs:
            bg = self.bank[self.rot("pj", 2)]
            for kc in range(KC):
                k.op("pe", lambda e: e.matmul(bg[0:cw, 0:tw], vg[:, kc, 0:cw], xn[:, kc, t0:t0 + tw],
                                               start=(kc == 0), stop=(kc == KC - 1)),
                     R=[sg.r(), xn.r()], W=[bg.r()])
            evac(bg, t0, tw)

    def lru(self, j):
        k, x, TB, ps = self.k, self.x, self.TB, self.ps_
        ns = NS if ps == 1 else 0
        Wi = self.W["w_in_ab"][j]
        wa = self.av("lwa", 8704, 512, BF16, a=8)
        wx = self.av("lwx", 9216, 512, BF16, a=8)
        k.dma("pool", wa[:, :, :], self.I["wa_bd"][j], W=[wa.r()])
        k.dma("pool", wx[:, :, :], self.I["wx_bd"][j], W=[wx.r()])
        XP = self.av("lXP", 9728, 1216)
        names = ["xc", "gr", "gi", "aa", "tt", "hh"]
        T = {nm: self.av("l" + nm, 10944 + i * 1152, 1152) for i, nm in enumerate(names)}
        xcb = self.sq[0]
        hst, cvo, h0s, cv0 = self.lru_small
        nsp = self.lru_nsp
        if ns:
            k.dma("sp", h0s[:, :, 0:16], self.I["st_h"][j], W=[h0s.r()])
            k.dma("sp", cv0[:, :, 0:16, :], self.I["st_conv"][j], W=[cv0.r()])
        if ps == 0:
            k.op("dve", lambda e: e.memset(h0s[:, :, 16:17], 0.0), W=[h0s.r()], acc=True)
            k.op("dve", lambda e: e.memset(cv0[:, :, 16:17, :], 0.0), W=[cv0.r()], acc=True)
        else:
            k.dma("sp", h0s[:, :, 16:17], self.o_h.t[j][:, :, 16:17], R=[self.o_h.r()], W=[h0s.r()], allow_slow_non_contiguous=True)
            k.dma("sp", cv0[:, :, 16:17, :], self.o_conv.t[j][:, :, 16:17, :], R=[self.o_conv.r()], W=[cv0.r()], allow_slow_non_contiguous=True)
        lo, _ = self.lay["lam_b%d" % j]
        k.op("act", lambda e: e.activation(out=nsp[:, 0:8], in_=self.vec[:, lo:lo + 8], func=AF.Exp, scale=-1.0),
             R=[self.vec.r()], W=[nsp.r()])
        k.op("act", lambda e: e.activation(out=nsp[:, 0:8], in_=nsp[:, 0:8], func=AF.Ln, bias=self.cs("one")),
             R=[nsp.r(), self.cst.r()], W=[nsp.r()])
        k.op("dve", lambda e: e.tensor_scalar(out=nsp[:, 8:16], in0=nsp[:, 0:8], scalar1=-16.0, scalar2=None,
                                               op0=ALU.mult), R=[nsp.r()], W=[nsp.r()])
        k.op("dve", lambda e: e.tensor_scalar(out=nsp[:, 0:8], in0=nsp[:, 0:8], scalar1=-8.0, scalar2=None,
                                               op0=ALU.mult), R=[nsp.r()], W=[nsp.r()])
        xc, gr, gi, aa, tt, hh = (T[n] for n in names)
        gb = gr
        XPp = XP[:, 0:1027]
        XPs = XP[:, 1027:1027 + 176].rearrange("p (b t) -> p b t", b=16)
        for c in range(8):
            def ev_x(bg, t0, tw):
                if t0 < 1024:
                    k.op("act", lambda e: e.copy(out=XP[:, 3 + t0:3 + t0 + tw], in_=bg[:, 0:tw]),
                         R=[bg.r()], W=[XP.r()])
                else:
                    k.op("act", lambda e: e.copy(out=XPs[:, :, 3:11],
                                                 in_=bg[:, 0:128].rearrange("p (b t) -> p b t", b=16)),
                         R=[bg.r()], W=[XP.r()])
            self.proj_chunk(Wi, COLS_A + c * 128, 128, ev_x)

            k.op("dve", lambda e: e.tensor_copy(out=XP[:, 0:3], in_=cv0[:, c, 16, :]), R=[cv0.r()], W=[XP.r()])
            if ns:
                k.op("dve", lambda e: e.tensor_copy(out=XPs[:, :, 0:3], in_=cv0[:, c, 0:16, :]),
                     R=[cv0.r()], W=[XP.r()])
            k.op("dve", lambda e: e.tensor_copy(out=cvo[:, c, 16, :], in_=XP[:, 1024:1027]), R=[XP.r()], W=[cvo.r()])
            if ns:
                k.op("dve", lambda e: e.tensor_copy(out=cvo[:, c, 0:16, :], in_=XPs[:, :, 8:11]),
                     R=[XP.r()], W=[cvo.r()])
            segs = [(xc[:, 0:1024], lambda q: XP[:, q:q + 1024])]
            if ns:
                segs.append((xc[:, 1024:1152].rearrange("p (b t) -> p b t", b=16), lambda q: XPs[:, :, q:q + 8]))
            for (dst, srcf) in segs:
                k.op("dve", lambda e: e.tensor_scalar(out=dst, in0=srcf(0), scalar1=self.vcol("conv_w0_%d" % j, c),
                                                       scalar2=self.vcol("conv_b_b%d" % j, c),
                                                       op0=ALU.mult, op1=ALU.add),
                     R=[XP.r(), self.vec.r()], W=[xc.r()])
                for q in range(1, 4):
                    k.op("dve", lambda e: e.scalar_tensor_tensor(out=dst, in0=srcf(q),
                                                                  scalar=self.vcol("conv_w%d_%d" % (q, j), c),
                                                                  in1=dst, op0=ALU.mult, op1=ALU.add),
                         R=[XP.r(), xc.r(), self.vec.r()], W=[xc.r()])
            k.op("act", lambda e: e.copy(out=xcb[:, 0:TB], in_=xc[:, 0:TB]), R=[xc.r()], W=[xcb.r()])
            for (wt, bname, dst) in ((wa, "ba_b%d" % j, gr), (wx, "bx_b%d" % j, gi)):
                for (t0, tw) in self.tbs:
                    bg = self.bank[self.rot("pj", 2)]
                    k.op("pe", lambda e: e.matmul(bg[:, 0:tw], wt[:, c, :], xcb[:, t0:t0 + tw], start=True, stop=True),
                         R=[wt.r(), xcb.r()], W=[bg.r()])
                    k.op("act", lambda e: e.activation(out=dst[:, t0:t0 + tw], in_=bg[:, 0:tw], func=AF.Sigmoid,
                                                       bias=self.vcol(bname, c)),
                         R=[bg.r(), self.vec.r()], W=[dst.r()])
            k.op("act", lambda e: e.activation(out=aa[:, 0:TB], in_=gr[:, 0:TB], func=AF.Exp, scale=nsp[:, c:c + 1]),
                 R=[gr.r(), nsp.r()], W=[aa.r()])
            k.op("act", lambda e: e.activation(out=tt[:, 0:TB], in_=gr[:, 0:TB], func=AF.Exp,
                                               scale=nsp[:, 8 + c:9 + c]), R=[gr.r(), nsp.r()], W=[tt.r()])
            k.op("dve", lambda e: e.tensor_scalar(out=tt[:, 0:TB], in0=tt[:, 0:TB], scalar1=-1.0, scalar2=1.0,
                                                   op0=ALU.mult, op1=ALU.add), R=[tt.r()], W=[tt.r()])
            k.op("act", lambda e: e.activation(out=tt[:, 0:TB], in_=tt[:, 0:TB], func=AF.Sqrt),
                 R=[tt.r()], W=[tt.r()])
            k.op("dve", lambda e: e.tensor_tensor(out=gi[:, 0:TB], in0=gi[:, 0:TB], in1=xc[:, 0:TB], op=ALU.mult),
                 R=[gi.r(), xc.r()], W=[gi.r()])
            k.op("dve", lambda e: e.tensor_tensor(out=tt[:, 0:TB], in0=tt[:, 0:TB], in1=gi[:, 0:TB], op=ALU.mult),
                 R=[gi.r(), tt.r()], W=[tt.r()])
            k.op("dve", lambda e: e.tensor_tensor_scan(out=hh[:, 0:1024], data0=aa[:, 0:1024], data1=tt[:, 0:1024],
                                                        initial=h0s[:, c, 16:17], op0=ALU.mult, op1=ALU.add),
                 R=[aa.r(), tt.r(), h0s.r()], W=[hh.r()])
            for bq in range(ns):
                o_ = 1024 + bq * 8
                k.op("dve", lambda e: e.tensor_tensor_scan(out=hh[:, o_:o_ + 8], data0=aa[:, o_:o_ + 8],
                                                            data1=tt[:, o_:o_ + 8], initial=h0s[:, c, bq:bq + 1],
                                                            op0=ALU.mult, op1=ALU.add),
                     R=[aa.r(), tt.r(), h0s.r()], W=[hh.r()], acc=True)
            k.op("dve", lambda e: e.tensor_copy(out=hst[:, c, 16:17], in_=hh[:, 1023:1024]), R=[hh.r()], W=[hst.r()])
            if ns:
                k.op("dve", lambda e: e.tensor_copy(
                    out=hst[:, c, 0:16], in_=hh[:, 1024:1152].rearrange("p (b t) -> p b t", b=16)[:, :, 7]),
                    R=[hh.r()], W=[hst.r()])
            def ev_g(bg, t0, tw):
                k.op("act", lambda e: e.copy(out=gb[:, t0:t0 + tw], in_=bg[:, 0:tw]), R=[bg.r()], W=[gb.r()])
            self.proj_chunk(Wi, COLS_A + 1024 + c * 128, 128, ev_g)
            self.gelu_(gi, gb, aa, TB)
            yb = self.ybuf
            k.op("dve", lambda e: e.tensor_tensor(out=yb[:, c, 0:TB], in0=gi[:, 0:TB], in1=hh[:, 0:TB], op=ALU.mult),
                 R=[gi.r(), hh.r()], W=[yb.r()])
        k.dma("sp", self.o_h.t[j], hst[:, :, :], R=[hst.r()], W=[self.o_h.r()])
        k.dma("sp", self.o_conv.t[j], cvo[:, :, :, :], R=[cvo.r()], W=[self.o_conv.r()])
        Wo = self.W["w_out_ab"][j]
        yb = self.ybuf
        for s in range(D // 256):
            sg, vg = self.load_slab(Wo, 8, s * 256, 256, rows0=1024)
            for oc2 in range(2):
                oc = s * 2 + oc2
                for (t0, tw) in self.tbs:
                    bd = self.bank[4 + self.rot("dn", 2)]
                    for kc in range(8):
                        k.op("pe", lambda e: e.matmul(bd[:, 0:tw], vg[:, kc, oc2 * 128:(oc2 + 1) * 128],
                                                       yb[:, kc, t0:t0 + tw], start=(kc == 0), stop=(kc == 7)),
                             R=[sg.r(), yb.r()], W=[bd.r()])
                    k.op("dve", lambda e: e.tensor_tensor(out=x[:, oc, t0:t0 + tw], in0=bd[:, 0:tw],
                                                           in1=x[:, oc, t0:t0 + tw], op=ALU.add),
                         R=[bd.r(), x.r(oc)], W=[x.r(oc)])


    def rwkv(self, j):
        k, x, xn, TB, ps = self.k, self.x, self.xn, self.TB, self.ps_
        ns = NS if ps == 1 else 0
        av, cst, vec = self.av, self.cst, self.vec
        Wi = self.W["w_in_ab"][j]
        ident, blk = self.cs("ident"), self.cs("blk")
        NT = TB // 128
        self.nslab = 1

        def dve(fn, R, W, **kw):
            return k.op("dve", fn, R=R, W=W, **kw)
        w2a2 = av("r_w2a2", 2048, 512, BF16)
        g2a = av("r_g2a", 2560, 512, BF16)
        g2b = av("r_g2b", 3072, 512, BF16)
        k.dma("pool", w2a2[0:64, :], self.I["w2_a"][j], W=[w2a2.r()])
        k.dma("pool", w2a2[64:128, :], self.I["a2_a"][j], W=[w2a2.r()])
        lx, xg0, xg1 = av("r_lx", 4096, 1152), av("r_xg0", 5248, 1152), av("r_xg1", 6400, 1152)
        Tt = [av("r_T%d" % i, 7552 + i * 1152, 1152) for i in range(7)]
        stg = av("r_stg", 15616, 1152)
        shin = av("r_shin", 16768, 459, a=27)
        shout = av("r_shout", 17227, 459, a=27)
        txa, sg0b, sg1b = self.sq[0], self.sq[1], self.sg1b
        if ns:
            k.dma("sp", shin[:, :, 0:16], self.I["st_shift"][j], W=[shin.r()])
        if ps == 0:
            dve(lambda e: e.memset(shin[:, :, 16:17], 0.0), [], [shin.r()], acc=True)
        else:
            k.dma("sp", shin[:, :, 16:17], self.o_shift.t[j][:, :, 16:17], R=[self.o_shift.r()], W=[shin.r()],
                  allow_slow_non_contiguous=True)
        r3 = lambda a: a.rearrange("p (b t) -> p b t", b=16)

        def project_shift(col0, cw, zc, Z, mu_ap):
            D_ = Tt[6]

            def ev(bg, t0, tw):
                k.op("act", lambda e: e.copy(out=Z[0:cw, t0:t0 + tw], in_=bg[0:cw, 0:tw]), R=[bg.r()], W=[Z.r()], acc=True)
            self.proj_chunk(Wi, col0, cw, ev)
            dve(lambda e: e.tensor_copy(out=shout[0:cw, zc, 16:17], in_=Z[0:cw, 1023:1024]), [Z.r()], [shout.r()], acc=True)
            dve(lambda e: e.tensor_tensor(out=D_[0:cw, 1:1024], in0=Z[0:cw, 0:1023], in1=Z[0:cw, 1:1024], op=ALU.subtract),
                [Z.r()], [D_.r()])
            dve(lambda e: e.tensor_tensor(out=D_[0:cw, 0:1], in0=shin[0:cw, zc, 16:17], in1=Z[0:cw, 0:1], op=ALU.subtract),
                [Z.r(), shin.r()], [D_.r()], acc=True)
            if ns:
                Zs_, Ds_ = r3(Z[0:cw, 1024:1152]), r3(D_[0:cw, 1024:1152])
                dve(lambda e: e.tensor_copy(out=shout[0:cw, zc, 0:16], in_=Zs_[:, :, 7]), [Z.r()], [shout.r()], acc=True)
                dve(lambda e: e.tensor_tensor(out=Ds_[:, :, 1:8], in0=Zs_[:, :, 0:7], in1=Zs_[:, :, 1:8], op=ALU.subtract),
                    [Z.r()], [D_.r()], acc=True)
                dve(lambda e: e.tensor_tensor(out=Ds_[:, :, 0], in0=shin[0:cw, zc, 0:16], in1=Zs_[:, :, 0], op=ALU.subtract),
                    [Z.r(), shin.r()], [D_.r()], acc=True)
            dve(lambda e: e.scalar_tensor_tensor(out=Z[0:cw, 0:TB], in0=D_[0:cw, 0:TB], scalar=mu_ap, in1=Z[0:cw, 0:TB],
                                                 op0=ALU.mult, op1=ALU.add), [D_.r(), Z.r(), vec.r()], [Z.r()])
        mo = self.lay["mu_x%d" % j][0]
        dve(lambda e: e.memset(shout[:, 26, :], 0.0), [], [shout.r()], acc=True)
        project_shift(3072, 128, 24, lx, vec[:, mo:mo + 1])
        project_shift(3200, 128, 25, xg0, vec[:, mo + 2:mo + 3])
        project_shift(3328, 32, 26, xg1, vec[0:32, mo + 3:mo + 4])
        k.op("act", lambda e: e.activation(out=txa[0:64, 0:TB], in_=lx[0:64, 0:TB], func=AF.Tanh), R=[lx.r()], W=[txa.r()])
        k.op("act", lambda e: e.copy(out=txa[64:128, 0:TB], in_=lx[64:128, 0:TB]), R=[lx.r()], W=[txa.r()], acc=True)
        k.op("act", lambda e: e.activation(out=sg0b[:, 0:TB], in_=xg0[:, 0:TB], func=AF.Sigmoid), R=[xg0.r()], W=[sg0b.r()])
        k.op("act", lambda e: e.activation(out=sg1b[0:32, 0:TB], in_=xg1[0:32, 0:TB], func=AF.Sigmoid),
             R=[xg1.r()], W=[sg1b.r()])
        Zsd, Ysd, Bon = self.Zs, self.Ys, self.Bon

        def emit(q, c, Q):
            for tt in range(NT):
                bk = self.bank[6 + (tt % 2)]
                k.op("pe", lambda e: e.transpose(bk[:, 0:128], Q[:, tt * 128:(tt + 1) * 128], ident),
                     R=[Q.r(), cst.r()], W=[bk.r()])
                if tt % 2 == 0:
                    k.op("act", lambda e: e.copy(out=stg[:, tt * 128:(tt + 1) * 128], in_=bk[:, 0:128]),
                         R=[bk.r()], W=[stg.r()], acc=True)
                else:
                    dve(lambda e: e.tensor_copy(out=stg[:, tt * 128:(tt + 1) * 128], in_=bk[:, 0:128]),
                        [bk.r()], [stg.r()], acc=True)
            sv = stg[:, 0:TB].rearrange("p (tt f) -> p tt f", f=128)
            if q == 5:
                dst = self.Vs.t[0:TB, c * 128:(c + 1) * 128].rearrange("(tt p) f -> p tt f", p=128)
                k.dma("sp", dst, sv, R=[stg.r()], W=[self.Vs.r()])
            else:
                for h2 in range(2):
                    dst = Zsd.t[0:TB, 2 * c + h2, q, :].rearrange("(tt p) j -> p tt j", p=128)
                    k.dma("sp", dst, sv[:, :, h2 * 64:(h2 + 1) * 64], R=[stg.r()], W=[Zsd.r()])

        def headsum(user, src):
            for (t0, tw) in self.tbs:
                bg = self.bank[4 + self.rot("hs", 2)]
                k.op("pe", lambda e: e.matmul(bg[:, 0:tw], blk, src[:, t0:t0 + tw], start=True, stop=True),
                     R=[cst.r(), src.r()], W=[bg.r()])
                user(bg, t0, tw)

        for c in range(8):
            T0, T1, T2, T3, T4, T5 = Tt[0:6]
            vc = lambda nm: self.vcol("%s%d" % (nm, j), c)
            project_shift(c * 128, 128, c, T0, vc("mu_r"))
            project_shift(1024 + c * 128, 128, 8 + c, T1, vc("mu_k"))
            project_shift(2048 + c * 128, 128, 16 + c, T2, vc("mu_v"))
            for (t0, tw) in self.tbs:
                bg = self.bank[4 + self.rot("hs", 2)]
                k.op("pe", lambda e: e.matmul(bg[:, 0:tw], w2a2[64:128, c * 128:(c + 1) * 128], txa[64:128, t0:t0 + tw],
                                               start=True, stop=True), R=[w2a2.r(), txa.r()], W=[bg.r()])
                k.op("act", lambda e: e.activation(out=T3[:, t0:t0 + tw], in_=bg[:, 0:tw], func=AF.Sigmoid, bias=vc("a0_a")),
                     R=[bg.r(), vec.r()], W=[T3.r()], acc=True)
            dve(lambda e: e.tensor_scalar(out=T4[:, 0:TB], in0=T1[:, 0:TB], scalar1=vc("kk_a"), scalar2=None, op0=ALU.mult),
                [T1.r(), vec.r()], [T4.r()])
            dve(lambda e: e.tensor_tensor(out=T5[:, 0:TB], in0=T4[:, 0:TB], in1=T4[:, 0:TB], op=ALU.mult), [T4.r()], [T5.r()])

            def nrm_ev(bg, t0, tw):
                k.op("act", lambda e: e.activation(out=Tt[6][:, t0:t0 + tw], in_=bg[:, 0:tw], func=AF.Sqrt),
                     R=[bg.r()], W=[Tt[6].r()], acc=True)
            headsum(nrm_ev, T5)
            dve(lambda e: e.tensor_scalar(out=Tt[6][:, 0:TB], in0=Tt[6][:, 0:TB], scalar1=1e-12, scalar2=None, op0=ALU.max),
                [Tt[6].r()], [Tt[6].r()])
            dve(lambda e: e.reciprocal(out=Tt[6][:, 0:TB], in_=Tt[6][:, 0:TB]), [Tt[6].r()], [Tt[6].r()])
            dve(lambda e: e.tensor_tensor(out=T4[:, 0:TB], in0=T4[:, 0:TB], in1=Tt[6][:, 0:TB], op=ALU.mult),
                [T4.r(), Tt[6].r()], [T4.r()])
            dve(lambda e: e.tensor_tensor(out=T5[:, 0:TB], in0=T4[:, 0:TB], in1=T3[:, 0:TB], op=ALU.mult),
                [T4.r(), T3.r()], [T5.r()])
            emit(0, c, T4)
            emit(2, c, T5)
            dve(lambda e: e.tensor_scalar(out=T4[:, 0:TB], in0=T3[:, 0:TB], scalar1=-1.0, scalar2=vc("ka_a"),
                                          op0=ALU.add, op1=ALU.mult), [T3.r(), vec.r()], [T4.r()])
            dve(lambda e: e.scalar_tensor_tensor(out=T4[:, 0:TB], in0=T4[:, 0:TB], scalar=1.0, in1=T1[:, 0:TB],
                                                 op0=ALU.add, op1=ALU.mult), [T4.r(), T1.r()], [T4.r()])
            emit(3, c, T4)
            dve(lambda e: e.scalar_tensor_tensor(out=T5[:, 0:TB], in0=T0[:, 0:TB], scalar=vc("rk_a"), in1=T4[:, 0:TB],
                                                 op0=ALU.mult, op1=ALU.mult), [T0.r(), T4.r(), vec.r()], [T5.r()])

            def bon_ev(bg, t0, tw):
                dve(lambda e: e.tensor_tensor(out=Tt[6][:, t0:t0 + tw], in0=bg[:, 0:tw], in1=T2[:, t0:t0 + tw], op=ALU.mult),
                    [bg.r(), T2.r()], [Tt[6].r()], acc=True)
            headsum(bon_ev, T5)
            k.dma("sp", Bon.t[c][:, 0:TB], Tt[6][:, 0:TB], R=[Tt[6].r()], W=[Bon.r()])
            for (t0, tw) in self.tbs:
                bg = self.bank[4 + self.rot("hs", 2)]
                k.op("pe", lambda e: e.matmul(bg[:, 0:tw], w2a2[0:64, c * 128:(c + 1) * 128], txa[0:64, t0:t0 + tw],
                                               start=True, stop=True), R=[w2a2.r(), txa.r()], W=[bg.r()])
                k.op("act", lambda e: e.activation(out=T1[:, t0:t0 + tw], in_=bg[:, 0:tw], func=AF.Sigmoid, bias=vc("w0_a")),
                     R=[bg.r(), vec.r()], W=[T1.r()], acc=True)
            k.op("act", lambda e: e.activation(out=T1[:, 0:TB], in_=T1[:, 0:TB], func=AF.Exp, scale=-float(np.exp(-0.5))),
                 R=[T1.r()], W=[T1.r()])
            emit(1, c, T1)
            emit(4, c, T0)
            emit(5, c, T2)
        k.dma("sp", self.o_shift.t[j], shout[:, :, :], R=[shout.r()], W=[self.o_shift.r()])
        k.barrier()
        Spf, T1p = av("q_Sp", 0, 512), av("q_T1p", 512, 512, a=8)
        Sp = Tile(Spf[:, :].rearrange("p (e j) -> p e j", e=8), "q_Sp3")
        Sp.reg = Spf.reg
        Ss = av("q_Ss", 1024, 8192)
        T1s = av("q_T1s", 9216, 2048)
        Xb = [av("q_X%d" % i, 11264 + i * 1280, 1280) for i in range(2)]
        vt = [av("q_vt%d" % i, 13824 + i * 128, 128, a=16) for i in range(2)]
        yt = [av("q_yt%d" % i, 14080 + i * 128, 128, a=16) for i in range(2)]
        vs_, ys_ = av("q_vs", 14336, 1024), av("q_ys", 15360, 1024)
        sa_t = av("q_sa", 16384, 32)
        rep = self.cs("rep", rows=16)
        Zv = Zsd.t.rearrange("t h q j -> h t (q j)")
        Vv = self.Vs.t.rearrange("t (p e) -> p t e", e=8)
        Vsr = self.Vs.r()
        Yv = Ysd.t.rearrange("t (p e) -> p t e", e=8)
        if ps == 0:
            dve(lambda e: e.memset(Sp[:, :, :], 0.0), [], [Sp.r()])
        else:
            k.dma("sp", Spf[:, :], self.o_wkv.t[j][:, 16, :], R=[self.o_wkv.r()], W=[Sp.r()])

        def step(S, T1, reg, shp, sa, vv, yy, RW):
            nd = len(shp)
            bj = lambda a: a.unsqueeze(nd - 1).broadcast_to([128] + shp)
            be = lambda a: a.unsqueeze(nd).broadcast_to([128] + shp)
            Rr, Wr = RW
            dve(lambda e: e.tensor_tensor(out=T1, in0=S, in1=bj(reg(0)), op=ALU.mult), Rr, Wr, sew=False)
            dve(lambda e: e.tensor_reduce(out=sa, in_=T1, axis=AX.X, op=ALU.add), Wr, Wr, sew=False)
            dve(lambda e: e.tensor_tensor(out=S, in0=S, in1=bj(reg(1)), op=ALU.mult), Rr, Wr, sew=False)
            dve(lambda e: e.tensor_tensor(out=T1, in0=bj(reg(2)), in1=be(sa), op=ALU.mult), Rr, Wr, sew=False)
            dve(lambda e: e.tensor_tensor(out=S, in0=S, in1=T1, op=ALU.subtract), Wr, Wr, sew=False)
            dve(lambda e: e.tensor_tensor(out=T1, in0=bj(reg(3)), in1=be(vv), op=ALU.mult), Rr, Wr, sew=False)
            dve(lambda e: e.tensor_tensor(out=S, in0=S, in1=T1, op=ALU.add), Wr, Wr, sew=False)
            dve(lambda e: e.tensor_tensor(out=T1, in0=S, in1=bj(reg(4)), op=ALU.mult), Rr, Wr, sew=False)
            dve(lambda e: e.tensor_reduce(out=yy, in_=T1, axis=AX.X, op=ALU.add), Wr, Wr, sew=False)

        def rep_mm(Xt, bset):
            Xv = Xt[0:16, 0:1280].rearrange("h (t q j) -> h t q j", t=4, q=5)
            for q in range(5):
                bk = self.bank[bset * 3 + q // 2]
                k.op("pe", lambda e: e.matmul(bk[:, (q % 2) * 256:(q % 2) * 256 + 256], rep, Xv[:, :, q, :],
                                               start=True, stop=True), R=[cst.r(), Xt.r()], W=[bk.r()], acc=True)

        wreg = Reg("q_work")
        vtt = ytt = None
        for g4 in range(1024 // 4):
            t0 = g4 * 4
            Xt = Xb[g4 % 2]
            k.dma("sp", Xt[0:16, 0:1280].rearrange("h (t f) -> h t f", t=4), Zv[:, t0:t0 + 4, :],
                  R=[Zsd.r()], W=[Xt.r()])
            if t0 % 16 == 0:
                vtt, ytt = vt[(t0 // 16) % 2], yt[(t0 // 16) % 2]
                k.dma("sp", vtt[:, :, :], Vv[:, t0:t0 + 16, :], R=[Vsr], W=[vtt.r()], allow_slow_non_contiguous=True)
            bset = g4 % 2
            rep_mm(Xt, bset)
            banks = [self.bank[bset * 3 + i] for i in range(3)]
            for tl in range(4):
                reg = lambda q: banks[q // 2][:, (q % 2) * 256 + tl * 64:(q % 2) * 256 + tl * 64 + 64]
                ti = (t0 + tl) % 16
                step(Sp[:, :, :], T1p[:, :, :], reg, [8, 64], sa_t[:, 0:8], vtt[:, ti, :], ytt[:, ti, :],
                     ([b_.r() for b_ in banks] + [vtt.r(), wreg, Sp.r()], [wreg, ytt.r()]))
            if t0 % 16 == 12:
                k.dma("sp", Yv[:, t0 - 12:t0 + 4, :], ytt[:, :, :], R=[ytt.r()], W=[Ysd.r()], allow_slow_non_contiguous=True)
        k.dma("sp", self.o_wkv.t[j][:, 16, :], Spf[:, :], R=[Sp.r(), wreg], W=[self.o_wkv.r()])
        if ns:
            k.dma("sp", Ss[:, :], self.I["st_wkv"][j].rearrange("p b f -> p (b f)"), W=[Ss.r()])
            vs4 = vs_[:, :].rearrange("p (b t e) -> p b t e", b=16, t=8)
            ys4 = ys_[:, :].rearrange("p (b t e) -> p b t e", b=16, t=8)
            for bq in range(8):
                k.dma("sp", vs_[:, :].rearrange("p (n e) -> p n e", e=8)[:, 16 * bq:16 * bq + 16, :],
                      Vv[:, 1024 + 16 * bq:1024 + 16 * bq + 16, :],
                      R=[Vsr], W=[vs_.r()], allow_slow_non_contiguous=True)
            Ss4 = Ss[:, :].rearrange("p (b e j) -> p b e j", b=16, e=8)
            T1s4 = T1s[:, :].rearrange("p (b e j) -> p b e j", b=4, e=8)
            sreg = Reg("q_works")
            it = 0
            for t in range(8):
                for qd in range(4):
                    Xt = Xb[it % 2]
                    bset = it % 2
                    it += 1
                    r0 = 1024 + 32 * qd + t
                    k.dma("sp", Xt[0:16, 0:1280].rearrange("h (t f) -> h t f", t=4),
                          Zv[:, r0:r0 + 25:8, :], R=[Zsd.r()], W=[Xt.r()])
                    rep_mm(Xt, bset)
                    banks = [self.bank[bset * 3 + i] for i in range(3)]
                    reg = lambda q: banks[q // 2][:, (q % 2) * 256:(q % 2) * 256 + 256].rearrange("p (b j) -> p b j", b=4)
                    step(Ss4[:, 4 * qd:4 * qd + 4, :, :], T1s4[:, :, :, :], reg, [4, 8, 64],
                         sa_t[:, 0:32].rearrange("p (b e) -> p b e", b=4), vs4[:, 4 * qd:4 * qd + 4, t, :],
                         ys4[:, 4 * qd:4 * qd + 4, t, :],
                         ([b_.r() for b_ in banks] + [vs_.r(), Ss.r(), sreg], [sreg, ys_.r()]))
            k.dma("sp", Yv[:, 1024:1152, :], ys_[:, :].rearrange("p (n e) -> p n e", e=8), R=[ys_.r(), sreg], W=[Ysd.r()],
                  allow_slow_non_contiguous=True)
            k.dma("sp", self.o_wkv.t[j][:, 0:16, :].rearrange("p b f -> p (b f)"), Ss[:, :], R=[Ss.r(), sreg],
                  W=[self.o_wkv.r()])
        k.barrier()
        self.nslab = 2
        yab = av("p_yab", 4096, 4608, BF16, a=8)
        ytk = av("p_ytk", 8704, 1152)
        yc, ysq, mean, bon = (av("p_t%d" % i, 9856 + i * 1152, 1152) for i in range(4))
        g2a = av("p_g2a", 14464, 512, BF16)
        g2b = av("p_g2b", 14976, 512, BF16)
        gne = av("p_gne", 15488, 4)
        dve(lambda e: e.memset(gne[:, :], GN_EPS), [], [gne.r()])
        k.dma("pool", g2a[:, :], self.I["g2_a"][j][0:128, :], W=[g2a.r()])
        k.dma("pool", g2b[0:32, :], self.I["g2_a"][j][128:160, :], W=[g2b.r()])
        for c in range(8):
            vc = lambda nm: self.vcol("%s%d" % (nm, j), c)
            k.dma("sp", ytk[:, 0:TB].rearrange("p (tt f) -> p tt f", f=128),
                  Ysd.t[0:TB, c * 128:(c + 1) * 128].rearrange("(tt p) f -> p tt f", p=128), R=[Ysd.r()], W=[ytk.r()])
            k.dma("sp", bon[:, 0:TB], Bon.t[c][:, 0:TB], R=[Bon.r()], W=[bon.r()])
            for tt in range(NT):
                bk = self.bank[6 + (tt % 2)]
                k.op("pe", lambda e: e.transpose(bk[:, 0:128], ytk[:, tt * 128:(tt + 1) * 128], ident),
                     R=[ytk.r(), cst.r()], W=[bk.r()])
                k.op("act", lambda e: e.copy(out=yc[:, tt * 128:(tt + 1) * 128], in_=bk[:, 0:128]), R=[bk.r()], W=[yc.r()], acc=True)
            dve(lambda e: e.tensor_tensor(out=ysq[:, 0:TB], in0=yc[:, 0:TB], in1=yc[:, 0:TB], op=ALU.mult), [yc.r()], [ysq.r()])

            def mean_ev(bg, t0, tw):
                k.op("act", lambda e: e.activation(out=mean[:, t0:t0 + tw], in_=bg[:, 0:tw], func=AF.Copy, scale=1.0 / 64),
                     R=[bg.r()], W=[mean.r()], acc=True)
            headsum(mean_ev, yc)

            def var_ev(bg, t0, tw):
                k.op("act", lambda e: e.activation(out=ysq[:, t0:t0 + tw], in_=bg[:, 0:tw], func=AF.Copy, scale=1.0 / 64),
                     R=[bg.r()], W=[ysq.r()], acc=True)
            headsum(var_ev, ysq)
            dve(lambda e: e.tensor_tensor(out=yc[:, 0:TB], in0=yc[:, 0:TB], in1=mean[:, 0:TB], op=ALU.subtract),
                [yc.r(), mean.r()], [yc.r()])
            dve(lambda e: e.tensor_tensor(out=mean[:, 0:TB], in0=mean[:, 0:TB], in1=mean[:, 0:TB], op=ALU.mult),
                [mean.r()], [mean.r()])
            dve(lambda e: e.tensor_tensor(out=ysq[:, 0:TB], in0=ysq[:, 0:TB], in1=mean[:, 0:TB], op=ALU.subtract),
                [ysq.r(), mean.r()], [ysq.r()])
            k.op("act", lambda e: e.activation(out=ysq[:, 0:TB], in_=ysq[:, 0:TB], func=AF.Sqrt, bias=gne[:, 0:1]),
                 R=[ysq.r(), gne.r()], W=[ysq.r()])
            dve(lambda e: e.reciprocal(out=ysq[:, 0:TB], in_=ysq[:, 0:TB]), [ysq.r()], [ysq.r()])
            dve(lambda e: e.tensor_tensor(out=yc[:, 0:TB], in0=yc[:, 0:TB], in1=ysq[:, 0:TB], op=ALU.mult),
                [yc.r(), ysq.r()], [yc.r()])
            dve(lambda e: e.tensor_scalar(out=yc[:, 0:TB], in0=yc[:, 0:TB], scalar1=vc("lnx_g"), scalar2=vc("lnx_b"),
                                          op0=ALU.mult, op1=ALU.add), [yc.r(), vec.r()], [yc.r()])
            dve(lambda e: e.tensor_tensor(out=yc[:, 0:TB], in0=yc[:, 0:TB], in1=bon[:, 0:TB], op=ALU.add),
                [yc.r(), bon.r()], [yc.r()])
            for (t0, tw) in self.tbs:
                bg = self.bank[4 + self.rot("hs", 2)]
                k.op("pe", lambda e: e.matmul(bg[:, 0:tw], g2a[:, c * 128:(c + 1) * 128], sg0b[:, t0:t0 + tw],
                                               start=True, stop=False), R=[g2a.r(), sg0b.r()], W=[bg.r()])
                k.op("pe", lambda e: e.matmul(bg[:, 0:tw], g2b[0:32, c * 128:(c + 1) * 128], sg1b[0:32, t0:t0 + tw],
                                               start=False, stop=True), R=[g2b.r(), sg1b.r()], W=[bg.r()])
                dve(lambda e: e.tensor_tensor(out=yab[:, c, t0:t0 + tw], in0=yc[:, t0:t0 + tw], in1=bg[:, 0:tw], op=ALU.mult),
                    [yc.r(), bg.r()], [yab.r()], acc=True)
        Wo = self.W["w_out_ab"][j]
        for s in range(D // 256):
            sg, vg = self.load_slab(Wo, 8, s * 256, 256, rows0=0)
            for oc2 in range(2):
                oc = s * 2 + oc2
                for (t0, tw) in self.tbs:
                    bd = self.bank[self.rot("pj", 2)]
                    for kc in range(8):
                        k.op("pe", lambda e: e.matmul(bd[:, 0:tw], vg[:, kc, oc2 * 128:(oc2 + 1) * 128],
                                                       yab[:, kc, t0:t0 + tw], start=(kc == 0), stop=(kc == 7)),
                             R=[sg.r(), yab.r()], W=[bd.r()])
                    dve(lambda e: e.tensor_tensor(out=x[:, oc, t0:t0 + tw], in0=bd[:, 0:tw], in1=x[:, oc, t0:t0 + tw],
                                                  op=ALU.add), [bd.r(), x.r(oc)], [x.r(oc)])

    def s5(self, j):
        k, x, xn, TB, ps = self.k, self.x, self.xn, self.TB, self.ps_
        ns = NS if ps == 1 else 0
        PI = float(np.pi)
        av = self.av
        P = {nm: av("s5" + nm, 4096 + i * 128, 128) for i, nm in enumerate(
            ["are", "aim", "dtt", "rho", "tht", "ta", "tb", "fre", "fim", "tc"])}
        Bp1, Bp2, Cp1, Cp2, Bb1, Bb2, tA, tB = (av("s5s%d" % i, 5376 + i * 128, 128, a=8) for i in range(8))
        T1c = av("s5T1", 6400, 64, BF16)
        T2c = av("s5T2", 6464, 64, BF16)
        LB1 = [av("s5LB1%d" % i, 6528 + i * 64, 64, BF16) for i in range(2)]
        LB2 = [av("s5LB2%d" % i, 6656 + i * 64, 64, BF16) for i in range(2)]
        C1p = [av("s5C1%d" % i, 6784 + i * 64, 64, BF16) for i in range(2)]
        C2p = [av("s5C2%d" % i, 6912 + i * 64, 64, BF16) for i in range(2)]
        hin = av("s5hin", 7040, 136, a=8)
        hfin = av("s5hfin", 7176, 136, a=8)
        ftmp = av("s5ft", 7312, 68, a=4)
        Ct, St, targ = av("s5Ct", 7424, 1024), av("s5St", 8448, 1024), av("s5ta", 9472, 1024)
        rhof, m, G = av("s5rf", 10496, 1152), av("s5m", 11648, 1152), av("s5G", 12800, 1152)
        G1b, G2b = av("s5G1", 13952, 576, BF16), av("s5G2", 14528, 576, BF16)
        yf, gt = av("s5yf", 15104, 1152), av("s5gt", 16256, 1152)
        cst, vec = self.cst, self.vec
        ident = self.cs("ident")
        negpi = self.cs("negpi")
        Sg = lambda i: cst[:, CST["sgn"][0] + i:CST["sgn"][0] + i + 1]
        V = lambda e: e

        def dve(fn, R, W, **kw):
            return k.op("dve", fn, R=R, W=W, **kw)

        k.dma("sp", P["are"][:, :], self.I["s5_are"][j], W=[P["are"].r()])
        k.dma("sp", P["aim"][:, :], self.I["s5_aim"][j], W=[P["aim"].r()])
        k.dma("sp", P["dtt"][:, :], self.I["s5_ldt"][j], W=[P["dtt"].r()])
        k.op("act", lambda e: e.activation(out=P["dtt"][:, :], in_=P["dtt"][:, :], func=AF.Exp),
             R=[P["dtt"].r()], W=[P["dtt"].r()])
        dve(lambda e: e.tensor_tensor(out=P["tht"][:, :], in0=P["dtt"][:, :], in1=P["aim"][:, :], op=ALU.mult),
            [P["dtt"].r(), P["aim"].r()], [P["tht"].r()])
        dve(lambda e: e.tensor_tensor(out=P["rho"][:, :], in0=P["dtt"][:, :], in1=P["are"][:, :], op=ALU.mult),
            [P["dtt"].r(), P["are"].r()], [P["rho"].r()])
        k.op("act", lambda e: e.activation(out=P["rho"][:, :], in_=P["rho"][:, :], func=AF.Exp),
             R=[P["rho"].r()], W=[P["rho"].r()])

        qi = av("s5qi", 17408, 1024, I32)
        halfpi = self.cs("halfpi")

        def sincos(dst_s, dst_c, src, w, Rs):
            for (dst, addc, lo, hi, bias) in ((dst_s, 0.0, -PI, PI, None), (dst_c, 0.5 * PI, -1.5 * PI, 0.5 * PI, halfpi)):
                k.op("pool", lambda e: e.tensor_scalar(out=qi[:, 0:w], in0=src, scalar1=addc, scalar2=1.0 / (2 * PI),
                                                       op0=ALU.add, op1=ALU.mult), R=Rs, W=[qi.r()])
                k.op("pool", lambda e: e.tensor_copy(out=gt[:, 0:w], in_=qi[:, 0:w]), R=[qi.r()], W=[gt.r()])
                dve(lambda e: e.scalar_tensor_tensor(out=dst[:, 0:w], in0=gt[:, 0:w], scalar=-2 * PI, in1=src,
                                                     op0=ALU.mult, op1=ALU.add), Rs + [gt.r()], [dst.r()])
                dve(lambda e: e.tensor_scalar(out=dst[:, 0:w], in0=dst[:, 0:w], scalar1=lo, scalar2=hi,
                                              op0=ALU.max, op1=ALU.min), [dst.r()], [dst.r()])
                if bias is None:
                    k.op("act", lambda e: e.activation(out=dst[:, 0:w], in_=dst[:, 0:w], func=AF.Sin),
                         R=[dst.r()], W=[dst.r()])
                else:
                    k.op("act", lambda e: e.activation(out=dst[:, 0:w], in_=dst[:, 0:w], func=AF.Sin, bias=bias),
                         R=[dst.r(), cst.r()], W=[dst.r()])
        sincos(P["ta"], P["tb"], P["tht"][:, :], 128, [P["tht"].r()])
        dve(lambda e: e.tensor_tensor(out=P["ta"][:, :], in0=P["ta"][:, :], in1=P["rho"][:, :], op=ALU.mult),
            [P["ta"].r(), P["rho"].r()], [P["ta"].r()])
        dve(lambda e: e.tensor_tensor(out=P["tb"][:, :], in0=P["tb"][:, :], in1=P["rho"][:, :], op=ALU.mult),
            [P["tb"].r(), P["rho"].r()], [P["tb"].r()])
        dve(lambda e: e.tensor_scalar(out=P["tb"][:, :], in0=P["tb"][:, :], scalar1=-1.0, scalar2=None, op0=ALU.add),
            [P["tb"].r()], [P["tb"].r()])
        dve(lambda e: e.tensor_tensor(out=P["tc"][:, :], in0=P["are"][:, :], in1=P["are"][:, :], op=ALU.mult),
            [P["are"].r()], [P["tc"].r()])
        dve(lambda e: e.tensor_tensor(out=P["fre"][:, :], in0=P["aim"][:, :], in1=P["aim"][:, :], op=ALU.mult),
            [P["aim"].r()], [P["fre"].r()])
        dve(lambda e: e.tensor_tensor(out=P["tc"][:, :], in0=P["tc"][:, :], in1=P["fre"][:, :], op=ALU.add),
            [P["tc"].r(), P["fre"].r()], [P["tc"].r()])
        dve(lambda e: e.reciprocal(out=P["tc"][:, :], in_=P["tc"][:, :]), [P["tc"].r()], [P["tc"].r()])
        dve(lambda e: e.tensor_tensor(out=P["fre"][:, :], in0=P["tb"][:, :], in1=P["are"][:, :], op=ALU.mult),
            [P["tb"].r(), P["are"].r()], [P["fre"].r()])
        dve(lambda e: e.tensor_tensor(out=P["fim"][:, :], in0=P["ta"][:, :], in1=P["aim"][:, :], op=ALU.mult),
            [P["ta"].r(), P["aim"].r()], [P["fim"].r()])
        dve(lambda e: e.tensor_tensor(out=P["fre"][:, :], in0=P["fre"][:, :], in1=P["fim"][:, :], op=ALU.add),
            [P["fre"].r(), P["fim"].r()], [P["fre"].r()])
        dve(lambda e: e.tensor_tensor(out=P["fim"][:, :], in0=P["ta"][:, :], in1=P["are"][:, :], op=ALU.mult),
            [P["ta"].r(), P["are"].r()], [P["fim"].r()])
        dve(lambda e: e.tensor_tensor(out=P["ta"][:, :], in0=P["tb"][:, :], in1=P["aim"][:, :], op=ALU.mult),
            [P["tb"].r(), P["aim"].r()], [P["ta"].r()])
        dve(lambda e: e.tensor_tensor(out=P["fim"][:, :], in0=P["fim"][:, :], in1=P["ta"][:, :], op=ALU.subtract),
            [P["fim"].r(), P["ta"].r()], [P["fim"].r()])
        dve(lambda e: e.tensor_tensor(out=P["fre"][:, :], in0=P["fre"][:, :], in1=P["tc"][:, :], op=ALU.mult),
            [P["fre"].r(), P["tc"].r()], [P["fre"].r()])
        dve(lambda e: e.tensor_tensor(out=P["fim"][:, :], in0=P["fim"][:, :], in1=P["tc"][:, :], op=ALU.mult),
            [P["fim"].r(), P["tc"].r()], [P["fim"].r()])
        iota = self.cs("iota")
        nbk = [5, 6, 7]
        for c in range(KC):
            g0 = c * 8
            for (t_, src) in ((Bp1, "s5_b1"), (Bp2, "s5_b2"), (Cp1, "s5_c1"), (Cp2, "s5_c2")):
                k.dma("sp", t_[:, :, :], self.I[src][j][:, g0:g0 + 8, :], W=[t_.r()])
            if ns:
                k.dma("sp", hin[:, :, 0:16], self.I["st_s5"][j][:, g0:g0 + 8, :], W=[hin.r()])
            if ps == 0:
                dve(lambda e: e.memset(hin[:, :, 16:17], 0.0), [], [hin.r()], acc=True)
            else:
                k.dma("sp", hin[:, :, 16:17], self.o_s5.t[j][:, g0:g0 + 8, 16:17], R=[self.o_s5.r()], W=[hin.r()],
                      allow_slow_non_contiguous=True)
            F1b = P["fre"][:, g0:g0 + 8].unsqueeze(2).broadcast_to([128, 8, 16])
            F2b = P["fim"][:, g0:g0 + 8].unsqueeze(2).broadcast_to([128, 8, 16])
            RP = [P["fre"].r(), P["fim"].r()]
            dve(lambda e: e.tensor_tensor(out=tA[:, :, :], in0=Bp1[:, :, :], in1=F1b, op=ALU.mult), RP + [Bp1.r()], [tA.r()])
            dve(lambda e: e.tensor_tensor(out=tB[:, :, :], in0=Bp2[:, :, :], in1=F2b, op=ALU.mult), RP + [Bp2.r()], [tB.r()])
            dve(lambda e: e.scalar_tensor_tensor(out=Bb1[:, :, :], in0=tB[:, :, :], scalar=Sg(0), in1=tA[:, :, :],
                                                 op0=ALU.mult, op1=ALU.add), [tA.r(), tB.r(), cst.r()], [Bb1.r()])
            dve(lambda e: e.tensor_tensor(out=tA[:, :, :], in0=Bp2[:, :, :], in1=F1b, op=ALU.mult), RP + [Bp2.r()], [tA.r()])
            dve(lambda e: e.tensor_tensor(out=tB[:, :, :], in0=Bp1[:, :, :], in1=F2b, op=ALU.mult), RP + [Bp1.r()], [tB.r()])
            dve(lambda e: e.scalar_tensor_tensor(out=Bb2[:, :, :], in0=tA[:, :, :], scalar=Sg(1), in1=tB[:, :, :],
                                                 op0=ALU.mult, op1=ALU.add), [tA.r(), tB.r(), cst.r()], [Bb2.r()])
            for (Bb, Tc) in ((Bb1, T1c), (Bb2, T2c)):
                bk = self.bank[4]
                k.op("pe", lambda e: e.transpose(bk[:, 0:128], Bb[:, :, :].rearrange("p a b -> p (a b)"), ident),
                     R=[Bb.r(), cst.r()], W=[bk.r()])
                k.op("act", lambda e: e.copy(out=Tc[:, :], in_=bk[:, 0:128]), R=[bk.r()], W=[Tc.r()])
            for g8 in range(8):
                g = g0 + g8
                r2 = self.rot("s5g", 2)
                lb1, lb2, c1p, c2p = LB1[r2], LB2[r2], C1p[r2], C2p[r2]
                rm = cst[:, CST["rowm"][0] + g8:CST["rowm"][0] + g8 + 1]
                dve(lambda e: e.tensor_scalar(out=lb1[:, :], in0=T1c[:, :], scalar1=rm, scalar2=None, op0=ALU.mult),
                    [T1c.r(), cst.r()], [lb1.r()])
                dve(lambda e: e.tensor_scalar(out=lb2[:, :], in0=T2c[:, :], scalar1=rm, scalar2=None, op0=ALU.mult),
                    [T2c.r(), cst.r()], [lb2.r()])
                k.op("pool", lambda e: e.memset(c1p[:, :], 0.0), W=[c1p.r()])
                k.op("pool", lambda e: e.memset(c2p[:, :], 0.0), W=[c2p.r()])
                dve(lambda e: e.tensor_scalar(out=c1p[:, g8 * 16:(g8 + 1) * 16], in0=Cp1[:, g8, :], scalar1=Sg(1),
                                              scalar2=None, op0=ALU.mult), [Cp1.r(), cst.r()], [c1p.r()])
                dve(lambda e: e.tensor_scalar(out=c2p[:, g8 * 16:(g8 + 1) * 16], in0=Cp2[:, g8, :], scalar1=-1.0,
                                              scalar2=None, op0=ALU.mult), [Cp2.r()], [c2p.r()])
                th = P["tht"][:, g:g + 1]
                dve(lambda e: e.tensor_scalar(out=targ[:, :], in0=iota, scalar1=th, scalar2=None, op0=ALU.mult),
                    [cst.r(), P["tht"].r()], [targ.r()])
                sincos(St, Ct, targ[:, :], 1024, [targ.r()])
                rho = P["rho"][:, g:g + 1]
                dve(lambda e: e.tensor_scalar(out=rhof[:, 0:1024], in0=iota, scalar1=0.0, scalar2=rho,
                                              op0=ALU.mult, op1=ALU.add), [cst.r(), P["rho"].r()], [rhof.r()])
                if ns:
                    dve(lambda e: e.tensor_scalar(out=rhof[:, 1024:1152], in0=self.cs("mask128"), scalar1=rho,
                                                  scalar2=None, op0=ALU.mult), [cst.r(), P["rho"].r()], [rhof.r()], acc=True)
                for (t0, tw) in self.tbs:
                    i = self.rot("gu", 2)
                    b1, b2 = self.bank[i], self.bank[2 + i]
                    k.op("pe", lambda e: e.matmul(b1[:, 0:tw], lb1[:, :], xn[:, c, t0:t0 + tw], start=True, stop=True),
                         R=[lb1.r(), xn.r()], W=[b1.r()])
                    k.op("pe", lambda e: e.matmul(b2[:, 0:tw], lb2[:, :], xn[:, c, t0:t0 + tw], start=True, stop=True),
                         R=[lb2.r(), xn.r()], W=[b2.r()])
                    if t0 < 1024:
                        cv, sv = Ct[:, t0:t0 + tw], St[:, t0:t0 + tw]
                        z1, z2, mo, yo = b1[:, 0:tw], b2[:, 0:tw], m[:, t0:t0 + tw], yf[:, t0:t0 + tw]
                    else:
                        cv = Ct[:, 0:8].unsqueeze(1).broadcast_to([128, 16, 8])
                        sv = St[:, 0:8].unsqueeze(1).broadcast_to([128, 16, 8])
                        r3 = lambda a: a.rearrange("p (b t) -> p b t", b=16)
                        z1, z2, mo, yo = r3(b1[:, 0:128]), r3(b2[:, 0:128]), r3(m[:, 1024:1152]), r3(yf[:, 1024:1152])
                    dve(lambda e: e.tensor_tensor(out=mo, in0=z1, in1=cv, op=ALU.mult), [b1.r(), Ct.r()], [m.r()], acc=True)
                    dve(lambda e: e.tensor_tensor(out=yo, in0=z2, in1=sv, op=ALU.mult), [b2.r(), St.r()], [yf.r()], acc=True)
                    dve(lambda e: e.tensor_tensor(out=mo, in0=mo, in1=yo, op=ALU.add), [m.r(), yf.r()], [m.r()], acc=True)
                if ns:
                    ms0 = m[:, 1024:1152].rearrange("p (b t) -> p b t", b=16)[:, :, 0]
                    dve(lambda e: e.scalar_tensor_tensor(out=ms0, in0=hin[:, g8, 0:16], scalar=rho, in1=ms0,
                                                         op0=ALU.mult, op1=ALU.add),
                        [hin.r(), m.r(), P["rho"].r()], [m.r()], acc=True)
                dve(lambda e: e.tensor_tensor_scan(out=G[:, 0:TB], data0=rhof[:, 0:TB], data1=m[:, 0:TB],
                                                   initial=hin[:, g8, 16:17], op0=ALU.mult, op1=ALU.add),
                    [rhof.r(), m.r(), hin.r()], [G.r()])
                for (Gb, tab) in ((G1b, Ct), (G2b, St)):
                    k.op("pool", lambda e: e.tensor_tensor(out=Gb[:, 0:1024], in0=G[:, 0:1024], in1=tab[:, 0:1024],
                                                            op=ALU.mult), R=[G.r(), tab.r()], W=[Gb.r()])
                    if ns:
                        r3 = lambda a: a.rearrange("p (b t) -> p b t", b=16)
                        k.op("pool", lambda e: e.tensor_tensor(
                            out=r3(Gb[:, 1024:1152]), in0=r3(G[:, 1024:1152]),
                            in1=tab[:, 0:8].unsqueeze(1).broadcast_to([128, 16, 8]), op=ALU.mult),
                            R=[G.r(), tab.r()], W=[Gb.r()], acc=True)
                for i, (t0, tw) in enumerate(self.tbs):
                    by = self.bank[nbk[i]]
                    k.op("pe", lambda e: e.matmul(by[:, 0:tw], c1p[:, :], G1b[:, t0:t0 + tw], start=(g8 == 0), stop=False),
                         R=[c1p.r(), G1b.r()], W=[by.r()])
                    k.op("pe", lambda e: e.matmul(by[:, 0:tw], c2p[:, :], G2b[:, t0:t0 + tw], start=False, stop=(g8 == 7)),
                         R=[c2p.r(), G2b.r()], W=[by.r()])
                ncol = 17 if ns else 1
                cols = []
                if ns:
                    Gs7 = G[:, 1024:1152].rearrange("p (b t) -> p b t", b=16)[:, :, 7]
                    dve(lambda e: e.tensor_scalar(out=ftmp[:, 0, 0:16], in0=Gs7, scalar1=Ct[:, 7:8], scalar2=None,
                                                  op0=ALU.mult), [G.r(), Ct.r()], [ftmp.r()], acc=True)
                    dve(lambda e: e.tensor_scalar(out=ftmp[:, 1, 0:16], in0=Gs7, scalar1=St[:, 7:8], scalar2=None,
                                                  op0=ALU.mult), [G.r(), St.r()], [ftmp.r()], acc=True)
                dve(lambda e: e.tensor_tensor(out=ftmp[:, 0, 16:17], in0=G[:, 1023:1024], in1=Ct[:, 1023:1024],
                                              op=ALU.mult), [G.r(), Ct.r()], [ftmp.r()], acc=True)
                dve(lambda e: e.tensor_tensor(out=ftmp[:, 1, 16:17], in0=G[:, 1023:1024], in1=St[:, 1023:1024],
                                              op=ALU.mult), [G.r(), St.r()], [ftmp.r()], acc=True)
                c_lo = 0 if ns else 16
                bk = self.bank[4]
                k.op("pe", lambda e: e.matmul(bk[:, c_lo:17], ident, ftmp[:, 0, c_lo:17], start=True, stop=False),
                     R=[cst.r(), ftmp.r()], W=[bk.r()])
                k.op("pe", lambda e: e.matmul(bk[:, c_lo:17], self.cs("J"), ftmp[:, 1, c_lo:17], start=False, stop=True),
                     R=[cst.r(), ftmp.r()], W=[bk.r()])
                k.op("act", lambda e: e.copy(out=hfin[:, g8, c_lo:17], in_=bk[:, c_lo:17]), R=[bk.r()], W=[hfin.r()], acc=True)
            k.dma("sp", self.o_s5.t[j][:, g0:g0 + 8, c_lo:17], hfin[:, :, c_lo:17], R=[hfin.r()], W=[self.o_s5.r()],
                  allow_slow_non_contiguous=True)
            for i, (t0, tw) in enumerate(self.tbs):
                by = self.bank[nbk[i]]
                dve(lambda e: e.scalar_tensor_tensor(out=yf[:, t0:t0 + tw], in0=xn[:, c, t0:t0 + tw],
                                                     scalar=self.vcol("d_c%d" % j, c), in1=by[:, 0:tw],
                                                     op0=ALU.mult, op1=ALU.add),
                    [xn.r(), by.r(), vec.r()], [yf.r()])
            zt = Tile(xn[:, c, :], "xnc")
            zt.reg = xn.r()
            self.gelu_(zt, yf, gt, TB)
        Wg = self.W["w_glu_c"][j]
        for s in range(D // 256):
            sg, vg = self.load_slab(Wg, KC, s * 256, 256)
            for oc2 in range(2):
                oc = s * 2 + oc2
                for (t0, tw) in self.tbs:
                    bg = self.bank[self.rot("pj", 2)]
                    for kc in range(KC):
                        k.op("pe", lambda e: e.matmul(bg[:, 0:tw], vg[:, kc, oc2 * 128:(oc2 + 1) * 128],
                                                       xn[:, kc, t0:t0 + tw], start=(kc == 0), stop=(kc == KC - 1)),
                             R=[sg.r(), xn.r()], W=[bg.r()])
                    t1 = yf
                    k.op("act", lambda e: e.activation(out=t1[:, 0:tw], in_=bg[:, 0:tw], func=AF.Sigmoid,
                                                       bias=self.vcol("b_glu_c%d" % j, oc)),
                         R=[bg.r(), vec.r()], W=[t1.r()])
                    dve(lambda e: e.tensor_tensor(out=t1[:, 0:tw], in0=t1[:, 0:tw], in1=xn[:, oc, t0:t0 + tw], op=ALU.mult),
                        [t1.r(), xn.r()], [t1.r()])
                    dve(lambda e: e.tensor_tensor(out=x[:, oc, t0:t0 + tw], in0=x[:, oc, t0:t0 + tw], in1=t1[:, 0:tw],
                                                  op=ALU.add), [t1.r(), x.r(oc)], [x.r(oc)])


def build_inputs(inp, core):
    s = core % 4
    xs = np.asarray(inp["x_sample"], np.float32)[core * NS:(core + 1) * NS].reshape(NS * TS, D)
    xp = np.asarray(inp["x_prompt"], np.float32)[s]
    xT = np.ascontiguousarray(np.concatenate([xp, xs], 0).T)
    pp = np.asarray(inp["p_prompt"], np.float32)[:, s]
    psm = np.asarray(inp["p_sample"], np.float32)[:, core * NS:(core + 1) * NS].reshape(4, NS * TS, 256)
    pT = np.ascontiguousarray(np.concatenate([pp, psm], 1).transpose(0, 2, 1))
    return {"xT": xT, "pT": pT}


_PROG = {}


def kernel(**inputs):
    enable = inputs.pop("_enable", ("ffn", "ple", "mix"))
    if enable not in _PROG:
        _PROG[enable] = Prog(enable)
    prog = _PROG[enable]
    vec = pack_vec(inputs)
    shared = {"vec": vec, "cst": pack_cst()}
    for nm in prog.W:
        shared[nm] = np.ascontiguousarray(np.asarray(inputs[nm], np.float32))
    in_maps = []
    ncores = int(os.environ.get('KCORES', '8'))
    for c in range(ncores):
        m = dict(shared)
        m.update(build_inputs(inputs, c))
        m.update(pack_states(inputs, c))
        in_maps.append(m)
    res = run_bass_kernel_spmd(prog.nc, in_maps, core_ids=list(range(ncores)))
    R = list(res.results) + [res.results[0]] * (8 - ncores)
    global _LAST
    _LAST = R
    y_prompt = np.stack([R[s]["yT"][:, 0:2048].T for s in range(4)])
    y_sample = np.concatenate([R[c]["yT"][:, 2048:].T.reshape(NS, TS, D) for c in range(8)], 0)
    A = np.ascontiguousarray

    def per_layer(fn):
        return np.stack([fn(j) for j in range(2)])
    pw = per_layer(lambda j: np.stack([R[s]["o_wkv"][j][:, 16, :].reshape(16, 64, 64) for s in range(4)]))
    psh = per_layer(lambda j: np.stack([R[s]["o_shift"][j][:, :, 16].T.reshape(-1)[:3360] for s in range(4)]))
    ph = per_layer(lambda j: np.stack([R[s]["o_h"][j][:, :, 16].T.reshape(1024) for s in range(4)]))
    pc = per_layer(lambda j: np.stack([R[s]["o_conv"][j][:, :, 16, :].transpose(2, 1, 0).reshape(3, 1024)
                                       for s in range(4)]))
    pre = per_layer(lambda j: np.stack([R[s]["o_s5"][j][0:64, :, 16].T for s in range(4)]))
    pim = per_layer(lambda j: np.stack([R[s]["o_s5"][j][64:128, :, 16].T for s in range(4)]))
    sw = per_layer(lambda j: np.concatenate([
        R[c]["o_wkv"][j][:, 0:16, :].reshape(16, 8, 16, 8, 64).transpose(2, 0, 1, 3, 4).reshape(16, 16, 64, 64)
        for c in range(8)], 0))
    ssh = per_layer(lambda j: np.concatenate([
        R[c]["o_shift"][j][:, :, 0:16].transpose(2, 1, 0).reshape(16, -1)[:, :3360] for c in range(8)], 0))
    sh = per_layer(lambda j: np.concatenate([
        R[c]["o_h"][j][:, :, 0:16].transpose(2, 1, 0).reshape(16, 1024) for c in range(8)], 0))
    sc = per_layer(lambda j: np.concatenate([
        R[c]["o_conv"][j][:, :, 0:16, :].transpose(2, 3, 1, 0).reshape(16, 3, 1024) for c in range(8)], 0))
    sre = per_layer(lambda j: np.concatenate([R[c]["o_s5"][j][0:64, :, 0:16].transpose(2, 1, 0) for c in range(8)], 0))
    sim = per_layer(lambda j: np.concatenate([R[c]["o_s5"][j][64:128, :, 0:16].transpose(2, 1, 0) for c in range(8)], 0))
    outs = (y_prompt, y_sample, pw, psh, ph, pc, pre, pim, sw, ssh, sh, sc, sre, sim)
    return tuple(A(o.astype(np.float32)) for o in outs)
```

```python
import contextlib
import numpy as np
import concourse.bass as bass
import concourse.mybir as mybir
from concourse.bass_utils import run_bass_kernel_spmd

F32 = mybir.dt.float32
BF16 = mybir.dt.bfloat16
I32 = mybir.dt.int32
ALU = mybir.AluOpType
AF = mybir.ActivationFunctionType
AX = mybir.AxisListType
import os as _os
EPOCH_N = int(_os.environ.get("KEPOCH", "30000"))
SAME_ENGINE_WAITS = int(_os.environ.get("KSEW", "1"))


class Reg:
    __slots__ = ("name", "writes", "reads", "dsem", "dcnt")

    def __init__(self, name):
        self.name = name
        self.writes = []
        self.reads = []
        self.dsem = None
        self.dcnt = 0


class Tile:
    def __init__(self, t, name):
        self.t = t
        self.name = name
        self.reg = Reg(name)
        self.subs = {}

    def __getitem__(self, idx):
        return self.t[idx]

    def r(self, key=None):
        if key is None:
            return self.reg
        if key not in self.subs:
            self.subs[key] = Reg("%s/%s" % (self.name, key))
        return self.subs[key]

    def all(self):
        return [self.reg] + list(self.subs.values())


class K:
    def __init__(self):
        self.nc = bass.Bass("TRN2", target_bir_lowering=False)
        nc = self.nc
        self.es = contextlib.ExitStack()
        self.eng = {"pe": nc.tensor, "act": nc.scalar, "dve": nc.vector,
                    "pool": nc.gpsimd, "sp": nc.sync}
        self.sem = {}
        self.cnt = {}
        self.epoch = {}
        for e in ("pe", "act", "dve", "pool"):
            self.sem[(e, 0)] = self.es.enter_context(nc.semaphore("s_" + e))
            self.cnt[e] = 0
            self.epoch[e] = 0
        self.waited = {e: {} for e in self.eng}
        self.sew = True
        self.nsem = 4
        self.ninst = 0
        self.dregs = []

    def sb(self, name, shape, dt=F32):
        t = self.es.enter_context(self.nc.sbuf_tensor(name, list(shape), dt))
        return Tile(t, name)

    def ps(self, name, shape, dt=F32):
        t = self.es.enter_context(self.nc.psum_tensor(name, list(shape), dt))
        return Tile(t, name)

    def dram(self, name, shape, dt=F32, kind="Internal"):
        t = self.nc.dram_tensor(name, list(shape), dt, kind=kind)
        return Tile(t.ap(), name)

    def _need(self, e, ev):
        key, val, src = ev
        if src == e and (e == "pe" or not SAME_ENGINE_WAITS or not self.sew):
            return
        w = self.waited[e]
        if w.get(key, 0) >= val:
            return
        sem = self.sem[key] if isinstance(key, tuple) else key.dsem
        self.eng[e].wait_ge(sem, val)
        w[key] = val
        self.ninst += 1

    def _deps(self, e, R, W, acc=False, dma_fill=False):
        for r in R:
            for ev in r.writes:
                self._need(e, ev)
        for r in W:
            for ev in r.writes:
                if dma_fill and ev[2] == "dma":
                    continue
                if ev[2] != e:
                    self._need(e, ev)
            for ev in r.reads:
                if ev[2] != e:
                    self._need(e, ev)

    def _mark(self, ev, R, W, acc=False):
        for r in R:
            r.reads = [x for x in r.reads if x[0] != ev[0]] + [ev]
        for r in W:
            if acc:
                r.writes = [x for x in r.writes if x[0] != ev[0]] + [ev]
            else:
                r.writes = [ev]
            r.reads = []

    def op(self, e, fn, R=(), W=(), acc=False, sew=True):
        self.sew = sew
        self._deps(e, R, W)
        self.sew = True
        ins = fn(self.eng[e])
        if self.cnt[e] >= EPOCH_N:
            self.epoch[e] += 1
            self.cnt[e] = 0
            self.sem[(e, self.epoch[e])] = self.es.enter_context(
                self.nc.semaphore("s_%s_%d" % (e, self.epoch[e])))
        self.cnt[e] += 1
        key = (e, self.epoch[e])
        ins.then_inc(self.sem[key], 1)
        ev = (key, self.cnt[e], e)
        self._mark(ev, R, W, acc)
        self.ninst += 1
        return ins

    def dma(self, q, out, in_, R=(), W=(), acc=True, **kw):
        self._deps(q, R, W, dma_fill=acc)
        d = W[0]
        if d.dsem is None:
            d.dsem = self.es.enter_context(self.nc.semaphore("d%d" % self.nsem))
            self.nsem += 1
            self.dregs.append(d)
        ins = self.eng[q].dma_start(out=out, in_=in_, **kw)
        d.dcnt += 16
        ins.then_inc(d.dsem, 16)
        ev = (d, d.dcnt, "dma")
        self._mark(ev, R, W, acc)
        self.ninst += 1
        return ins

    def barrier(self):
        evs = [((e, self.epoch[e]), self.cnt[e], e) for e in ("pe", "act", "dve", "pool")]
        evs += [(d, d.dcnt, "dma") for d in self.dregs]
        for e in ("pe", "act", "dve", "pool", "sp"):
            for ev in evs:
                if ev[1] > 0 and not (ev[2] == e and e == "pe"):
                    w = self.waited[e]
                    if w.get(ev[0], 0) < ev[1]:
                        sem = self.sem[ev[0]] if isinstance(ev[0], tuple) else ev[0].dsem
                        self.eng[e].wait_ge(sem, ev[1])
                        w[ev[0]] = ev[1]
                        self.ninst += 1

    def finish(self, regs):
        for r in regs:
            for ev in r.writes:
                self._need("sp", ev)

    def close(self):
        self.es.close()

D = 2048
DFF = 5632
KC = 16
NPT = 1024
NS = 16
TS = 8
NTOK = 2048 + NS * TS
EPS = 1e-6
import os
DEPTH = int(os.environ.get('KDEPTH', '4'))
NPASS = int(os.environ.get('KNPASS', '2'))


def vec_layout():
    lay = {}
    off = [0]

    def add(name, n):
        lay[name] = (off[0], n)
        off[0] += n
    for l in range(4):
        for nm in ("norm_ffn1", "norm_mix", "norm_ffn2", "norm_ple"):
            add("%s%d" % (nm, l), 16)
    add("final_norm", 16)
    for j in range(2):
        add("d_c%d" % j, 16)
        add("b_glu_c%d" % j, 16)
    for j in range(2):
        for nm in ("w0_a", "a0_a", "kk_a", "ka_a", "rk_a", "lnx_g", "lnx_b", "conv_b_b",
                   "ba_b", "bx_b", "lam_b", "mu_r", "mu_k", "mu_v"):
            add("%s%d" % (nm, j), 8)
        for q in range(4):
            add("conv_w%d_%d" % (q, j), 8)
        add("mu_x%d" % j, 4)
    return lay, off[0]


def pack_vec(inp):
    lay, n = vec_layout()
    v = np.zeros((128, n), np.float32)

    def put(name, arr):
        o, c = lay[name]
        a = np.asarray(arr, np.float32).reshape(-1)
        v[:, o:o + c] = a.reshape(c, 128).T
    for l in range(4):
        for nm in ("norm_ffn1", "norm_mix", "norm_ffn2", "norm_ple"):
            put("%s%d" % (nm, l), inp[nm][l])
    put("final_norm", inp["final_norm"])
    for j in range(2):
        put("d_c%d" % j, inp["d_c"][j])
        put("b_glu_c%d" % j, inp["b_glu_c"][j])
        for nm in ("w0_a", "a0_a", "kk_a", "ka_a", "rk_a", "lnx_g", "lnx_b", "conv_b_b",
                   "ba_b", "bx_b", "lam_b"):
            put("%s%d" % (nm, j), inp[nm][j])
        mu = np.asarray(inp["mu_a"][j], np.float32)
        put("mu_r%d" % j, mu[0:1024])
        put("mu_k%d" % j, mu[1024:2048])
        put("mu_v%d" % j, mu[2048:3072])
        for q in range(4):
            put("conv_w%d_%d" % (q, j), inp["conv_w_b"][j][q])
        o, _ = lay["mu_x%d" % j]
        v[:, o] = mu[3072:3200]
        v[:, o + 2] = mu[3200:3328]
        v[0:32, o + 3] = mu[3328:3360]
    return v


CST = {"ident": (0, 128), "iota": (128, 1024), "mask128": (1152, 128), "J": (1280, 128),
       "blk": (1408, 128), "sgn": (1536, 4), "rep": (1540, 128), "rowm": (1668, 8), "one": (1676, 1), "negpi": (1677, 1), "halfpi": (1678, 1)}
NCST = 1680
GN_EPS = 64e-5
W_A = 1024
COLS_A = 3360


def pack_cst():
    c = np.zeros((128, NCST), np.float32)

    def put(nm, arr):
        o, n = CST[nm]
        c[:arr.shape[0], o:o + n] = arr
    put("ident", np.eye(128, dtype=np.float32))
    put("iota", np.broadcast_to(np.arange(1, 1025, dtype=np.float32)[None, :], (128, 1024)))
    m = np.ones(128, np.float32)
    m[0::8] = 0.0
    put("mask128", np.broadcast_to(m[None, :], (128, 128)))
    J = np.zeros((128, 128), np.float32)
    for p in range(64):
        J[64 + p, p] = -1.0
        J[p, 64 + p] = 1.0
    put("J", J)
    blk = np.zeros((128, 128), np.float32)
    blk[0:64, 0:64] = 1.0
    blk[64:128, 64:128] = 1.0
    put("blk", blk)
    sg = np.zeros((128, 4), np.float32)
    sg[0:64, 0] = -1.0
    sg[64:, 0] = 1.0
    sg[0:64, 1] = 1.0
    sg[64:, 1] = -1.0
    sg[:, 2] = -1.0
    sg[:, 3] = 1.0
    put("sgn", sg)
    rep = np.zeros((16, 128), np.float32)
    for hh in range(16):
        rep[hh, hh * 8:hh * 8 + 8] = 1.0
    put("rep", rep)
    rm = np.zeros((128, 8), np.float32)
    for p in range(128):
        rm[p, p // 16] = 1.0
    put("rowm", rm)
    put("one", np.ones((128, 1), np.float32))
    put("negpi", np.full((128, 1), -np.pi, np.float32))
    put("halfpi", np.full((128, 1), 0.5 * np.pi, np.float32))
    return c


def pack_states(inp, core):
    f = lambda a: np.ascontiguousarray(np.asarray(a, np.float32))
    sl = slice(core * NS, (core + 1) * NS)
    o = {}
    o["st_h"] = f(np.asarray(inp["state_b_h"])[:, sl].reshape(2, NS, 8, 128).transpose(0, 3, 2, 1))
    o["st_conv"] = f(np.asarray(inp["state_b_conv"])[:, sl].reshape(2, NS, 3, 8, 128).transpose(0, 4, 3, 1, 2))
    re = np.asarray(inp["state_c_re"])[:, sl].transpose(0, 3, 2, 1)
    im = np.asarray(inp["state_c_im"])[:, sl].transpose(0, 3, 2, 1)
    o["st_s5"] = f(np.concatenate([re, im], 1))
    sh = np.asarray(inp["state_a_shift"], np.float32)[:, sl]
    shp = np.zeros((2, NS, 27 * 128), np.float32)
    shp[:, :, 0:3360] = sh
    o["st_shift"] = f(shp.reshape(2, NS, 27, 128).transpose(0, 3, 2, 1))
    wk = np.asarray(inp["state_a_wkv"], np.float32)[:, sl].reshape(2, NS, 16, 8, 8, 64)
    o["st_wkv"] = f(wk.transpose(0, 2, 3, 1, 4, 5).reshape(2, 128, NS, 512))
    for nm in ("w2_a", "a2_a", "g2_a"):
        o[nm] = f(inp[nm])
    for nm in ("wa_b", "wx_b"):
        w = np.asarray(inp[nm], np.float32)
        bd = np.zeros((2, 8, 128, 128), np.float32)
        for c in range(8):
            bd[:, c, 0:64, 0:64] = w[:, 2 * c]
            bd[:, c, 64:128, 64:128] = w[:, 2 * c + 1]
        o[nm + "d"] = f(bd.transpose(0, 2, 1, 3))
    are = np.asarray(inp["a_re_c"], np.float32).transpose(0, 2, 1)
    aim = np.asarray(inp["a_im_c"], np.float32).transpose(0, 2, 1)
    o["s5_are"] = f(np.concatenate([are, are], 1))
    o["s5_aim"] = f(np.concatenate([aim, aim], 1))
    o["s5_ldt"] = f(np.broadcast_to(np.asarray(inp["log_dt_c"], np.float32)[:, None, :], (2, 128, 128)))
    bre = np.asarray(inp["b_re_c"], np.float32).transpose(0, 2, 1, 3)
    bim = np.asarray(inp["b_im_c"], np.float32).transpose(0, 2, 1, 3)
    o["s5_b1"] = f(np.concatenate([bre, bim], 1))
    o["s5_b2"] = f(np.concatenate([bim, bre], 1))
    cre = np.asarray(inp["c_re_c"], np.float32).transpose(0, 3, 1, 2)
    cim = np.asarray(inp["c_im_c"], np.float32).transpose(0, 3, 1, 2)
    o["s5_c1"] = f(np.concatenate([cre, cim], 1))
    o["s5_c2"] = f(np.concatenate([cim, cre], 1))
    return o


class Prog:
    def __init__(self, enable=("ffn", "ple", "mix")):
        self.enable = enable
        k = self.k = K()
        nc = self.nc = k.nc
        self.lay, self.nv = vec_layout()

        def din(name, shape):
            return nc.dram_tensor(name, list(shape), F32, kind="ExternalInput").ap()
        self.xT = din("xT", [D, NTOK])
        self.pT = din("pT", [4, 256, NTOK])
        self.vecd = din("vec", [128, self.nv])
        self.W = {}
        for nm, shp in (("ffn1_wg", [4, D, DFF]), ("ffn1_wu", [4, D, DFF]), ("ffn1_wd", [4, DFF, D]),
                        ("ffn2_wg", [4, D, DFF]), ("ffn2_wu", [4, D, DFF]), ("ffn2_wd", [4, DFF, D]),
                        ("ple_gate", [4, D, D]), ("ple_proj", [4, 256, D])):
            self.W[nm] = din(nm, shp)
        for nm, shp in (("w_in_ab", [2, D, 5408]), ("w_out_ab", [2, D, D]), ("w_glu_c", [2, D, D])):
            self.W[nm] = din(nm, shp)
        self.cstd = din("cst", [128, NCST])
        self.I = {}
        for nm, shp in (("st_h", [2, 128, 8, 16]), ("st_conv", [2, 128, 8, 16, 3]), ("st_s5", [2, 128, 128, 16]),
                        ("wa_bd", [2, 128, 8, 128]), ("wx_bd", [2, 128, 8, 128]),
                        ("s5_are", [2, 128, 128]), ("s5_aim", [2, 128, 128]), ("s5_ldt", [2, 128, 128]),
                        ("s5_b1", [2, 128, 128, 16]), ("s5_b2", [2, 128, 128, 16]),
                        ("s5_c1", [2, 128, 128, 16]), ("s5_c2", [2, 128, 128, 16])):
            self.I[nm] = din(nm, shp)
        for nm, shp in (("w2_a", [2, 64, 1024]), ("a2_a", [2, 64, 1024]), ("g2_a", [2, 160, 1024]),
                        ("st_shift", [2, 128, 27, 16]), ("st_wkv", [2, 128, 16, 512])):
            self.I[nm] = din(nm, shp)
        self.o_shift = k.dram("o_shift", [2, 128, 27, 17], F32, kind="ExternalOutput")
        self.o_wkv = k.dram("o_wkv", [2, 128, 17, 512], F32, kind="ExternalOutput")
        self.Zs = k.dram("Zs", [1152, 16, 5, 64], F32)
        self.Vs = k.dram("Vs", [1152, 1024], F32)
        self.Ys = k.dram("Ys", [1152, 1024], F32)
        self.Bon = k.dram("Bon", [8, 128, 1152], F32)
        self.yT = k.dram("yT", [D, NTOK], F32, kind="ExternalOutput")
        self.o_h = k.dram("o_h", [2, 128, 8, 17], F32, kind="ExternalOutput")
        self.o_conv = k.dram("o_conv", [2, 128, 8, 17, 3], F32, kind="ExternalOutput")
        self.o_s5 = k.dram("o_s5", [2, 128, 128, 17], F32, kind="ExternalOutput")
        self.outs = [self.yT, self.o_h, self.o_conv, self.o_s5, self.o_shift, self.o_wkv]

        self.x = k.sb("x", [128, KC, 1152], F32)
        self.xn = k.sb("xn", [128, KC, 1152], BF16)
        self.rstd = k.sb("rstd", [128, 1152], F32)
        self.sq = [k.sb("sq%d" % i, [128, 1152], BF16) for i in range(2)]
        self.vec = k.sb("vecs", [128, self.nv], F32)
        self.ones_bf = k.sb("ones_bf", [128, 128], BF16)
        self.AW = 18688
        self._avc = {}
        self.A = k.sb("arena", [128, self.AW], F32)
        self.slab = [self.av("slab%d" % i, i * 2048, 2048, BF16) for i in range(6)]
        self.h = self.av("hbuf", 12288, 1152, BF16, a=2)
        self.t1 = [self.av("t1_%d" % i, 13440 + i * 512, 512) for i in range(2)]
        self.t2 = [self.av("t2_%d" % i, 14464 + i * 512, 512) for i in range(2)]
        self.pt = self.av("ptile", 15488, 1152, BF16, a=2)
        self.pproj = self.av("pproj", 16640, 2048, BF16)
        self.ytmp = [self.av("ytmp%d" % i, i * 1152, 1152) for i in range(2)]
        self.nslab = 6
        self.bank = [k.ps("bank%d" % i, [128, 512], F32) for i in range(8)]
        self.lru_small = (k.sb("l_hst", [128, 8, 17], F32), k.sb("l_cvo", [128, 8, 17, 3], F32),
                          k.sb("l_h0", [128, 8, 17], F32), k.sb("l_cv0", [128, 8, 17, 3], F32))
        self.lru_nsp = k.sb("l_nsp", [128, 16], F32)
        self.sg1b = k.sb("r_sg1b", [128, 1152], BF16)
        self.ybuf = self.av("ybuf", 4096, 4608, BF16, a=8)
        self.rr = {}

        k.dma("sp", self.vec[:], self.vecd, W=[self.vec.r()])
        self.cst = k.sb("cst_sb", [128, NCST], F32)
        k.dma("sp", self.cst[:], self.cstd, W=[self.cst.r()])
        k.op("dve", lambda e: e.memset(self.ones_bf[:], 1.0), W=[self.ones_bf.r()])
        self.epst = k.sb("epst", [128, 2], F32)
        k.op("dve", lambda e: e.memset(self.epst[:], EPS), W=[self.epst.r()])

        for ps in range(NPASS):
            self.run_pass(ps)
        k.finish([o.r() for o in self.outs])
        k.close()

    def av(self, name, off, n, dt=F32, a=None):
        assert off + n <= self.AW, (name, off, n)
        key = (name, off, n, str(dt), a)
        if key in self._avc:
            return self._avc[key]
        ap = self.A.t[:, off:off + n]
        if dt != F32:
            ap = ap.bitcast(dt)
        if a is not None:
            ap = ap.rearrange("p (a b) -> p a b", a=a)
        self._avc[key] = Tile(ap, name)
        return self._avc[key]

    def rot(self, name, n):
        i = self.rr.get(name, 0)
        self.rr[name] = i + 1
        return i % n

    def vcol(self, name, c):
        o, n = self.lay[name]
        return self.vec[:, o + c:o + c + 1]

    def run_pass(self, ps):
        k = self.k
        self.ps_ = ps
        self.TB = TB = 1024 if ps == 0 else 1152
        self.g0 = 0 if ps == 0 else 1024
        self.tbs = [(0, 512), (512, 512)] + ([(1024, 128)] if ps == 1 else [])
        x = self.x
        xTv = self.xT.rearrange("(c p) t -> p c t", p=128)
        k.dma("sp", x[:, :, 0:TB], xTv[:, :, self.g0:self.g0 + TB], W=[x.r(c) for c in range(KC)])
        for l in range(DEPTH):
            if "ffn" in self.enable:
                self.ffn(l, 1)
            if "mix" in self.enable:
                self.mixer(l)
            if "ffn" in self.enable:
                self.ffn(l, 2)
            if "ple" in self.enable:
                self.ple(l)
        k.barrier()
        self.norm("final_norm", final=True)
        k.barrier()

    def norm(self, gname, final=False):
        k, x, xn, TB = self.k, self.x, self.xn, self.TB
        nb = [5, 6, 7]
        for c in range(KC):
            s = self.sq[c % 2]
            k.op("act", lambda e: e.activation(out=s[:, 0:TB], in_=x[:, c, 0:TB], func=AF.Square),
                 R=[x.r(c)], W=[s.r()])
            for i, (t0, tw) in enumerate(self.tbs):
                b = self.bank[nb[i]]
                k.op("pe", lambda e: e.matmul(b[:, 0:tw], self.ones_bf[:], s[:, t0:t0 + tw],
                                               start=(c == 0), stop=(c == KC - 1)),
                     R=[self.ones_bf.r(), s.r()], W=[b.r()])
        for i, (t0, tw) in enumerate(self.tbs):
            b = self.bank[nb[i]]
            k.op("act", lambda e: e.activation(out=self.rstd[:, t0:t0 + tw], in_=b[:, 0:tw], func=AF.Sqrt,
                                               bias=self.epst[:, 0:1], scale=1.0 / D),
                 R=[b.r(), self.epst.r()], W=[self.rstd.r()])
        k.op("dve", lambda e: e.reciprocal(out=self.rstd[:, 0:TB], in_=self.rstd[:, 0:TB]),
             R=[self.rstd.r()], W=[self.rstd.r()])
        yTv = self.yT.t.rearrange("(c p) t -> p c t", p=128)
        for c in range(KC):
            if final:
                o = self.ytmp[c % 2]
                k.op("dve", lambda e: e.scalar_tensor_tensor(out=o[:, 0:TB], in0=x[:, c, 0:TB],
                                                              scalar=self.vcol(gname, c), in1=self.rstd[:, 0:TB],
                                                              op0=ALU.mult, op1=ALU.mult),
                     R=[x.r(c), self.rstd.r(), self.vec.r()], W=[o.r()])
                k.dma("sp", yTv[:, c, self.g0:self.g0 + TB], o[:, 0:TB], R=[o.r()], W=[self.yT.r()])
            else:
                k.op("dve", lambda e: e.scalar_tensor_tensor(out=xn[:, c, 0:TB], in0=x[:, c, 0:TB],
                                                              scalar=self.vcol(gname, c), in1=self.rstd[:, 0:TB],
                                                              op0=ALU.mult, op1=ALU.mult),
                     R=[x.r(c), self.rstd.r(), self.vec.r()], W=[xn.r()])

    def load_slab(self, W2d, kchunks, c0, cw, rows0=0):
        k = self.k
        s = self.slab[self.rot("slab", self.nslab)]
        src = W2d[rows0:rows0 + kchunks * 128, :].rearrange("(kc p) f -> p kc f", p=128)[:, :, c0:c0 + cw]
        dst = s[:, 0:kchunks * cw].rearrange("p (kc f) -> p kc f", kc=kchunks)
        step = 4 if kchunks > 4 else kchunks
        for q in range(0, kchunks, step):
            k.dma("pool", dst[:, q:q + step, :], src[:, q:q + step, :], W=[s.r()])
        return s, dst

    def ffn(self, l, which):
        k, x, xn, h = self.k, self.x, self.xn, self.h
        self.norm("norm_ffn%d%d" % (which, l))
        Wg = self.W["ffn%d_wg" % which][l]
        Wu = self.W["ffn%d_wu" % which][l]
        Wd = self.W["ffn%d_wd" % which][l]
        for s in range(DFF // 256):
            sg, vg = self.load_slab(Wg, KC, s * 256, 256)
            su, vu = self.load_slab(Wu, KC, s * 256, 256)
            sd, vd = self.load_slab(Wd, 2, 0, D, rows0=s * 256)
            for fc in range(2):
                for (t0, tw) in self.tbs:
                    i = self.rot("gu", 2)
                    bg, bu = self.bank[i], self.bank[2 + i]
                    for kc in range(KC):
                        k.op("pe", lambda e: e.matmul(bg[:, 0:tw], vg[:, kc, fc * 128:(fc + 1) * 128],
                                                       xn[:, kc, t0:t0 + tw], start=(kc == 0), stop=(kc == KC - 1)),
                             R=[sg.r(), xn.r()], W=[bg.r()])
                    for kc in range(KC):
                        k.op("pe", lambda e: e.matmul(bu[:, 0:tw], vu[:, kc, fc * 128:(fc + 1) * 128],
                                                       xn[:, kc, t0:t0 + tw], start=(kc == 0), stop=(kc == KC - 1)),
                             R=[su.r(), xn.r()], W=[bu.r()])
                    t1 = self.t1[self.rot("t1", 2)]
                    k.op("act", lambda e: e.activation(out=t1[:, 0:tw], in_=bg[:, 0:tw], func=AF.Silu),
                         R=[bg.r()], W=[t1.r()])
                    k.op("dve", lambda e: e.tensor_tensor(out=h[:, fc, t0:t0 + tw], in0=t1[:, 0:tw],
                                                           in1=bu[:, 0:tw], op=ALU.mult),
                         R=[t1.r(), bu.r()], W=[h.r(fc)])
            for dc in range(KC):
                for (t0, tw) in self.tbs:
                    bd = self.bank[4 + self.rot("dn", 2)]
                    for fc in range(2):
                        k.op("pe", lambda e: e.matmul(bd[:, 0:tw], vd[:, fc, dc * 128:(dc + 1) * 128],
                                                       h[:, fc, t0:t0 + tw], start=(fc == 0), stop=(fc == 1)),
                             R=[sd.r(), h.r(fc)], W=[bd.r()])
                    k.op("dve", lambda e: e.scalar_tensor_tensor(out=x[:, dc, t0:t0 + tw], in0=bd[:, 0:tw],
                                                                  scalar=0.5, in1=x[:, dc, t0:t0 + tw],
                                                                  op0=ALU.mult, op1=ALU.add),
                         R=[bd.r(), x.r(dc)], W=[x.r(dc)])

    def ple(self, l):
        k, x, xn, TB = self.k, self.x, self.xn, self.TB
        self.norm("norm_ple%d" % l)
        pt = self.pt
        k.dma("pool", pt[:, :, 0:TB],
              self.pT[l].rearrange("(c p) t -> p c t", p=128)[:, :, self.g0:self.g0 + TB], W=[pt.r()])
        sp_ = self.pproj
        vp = sp_[:, :].rearrange("p (kc f) -> p kc f", kc=2)
        k.dma("pool", vp, self.W["ple_proj"][l].rearrange("(kc p) f -> p kc f", p=128), W=[sp_.r()])
        for s in range(D // 256):
            sg, vg = self.load_slab(self.W["ple_gate"][l], KC, s * 256, 256)
            for oc2 in range(2):
                oc = s * 2 + oc2
                for (t0, tw) in self.tbs:
                    i = self.rot("gu", 2)
                    bg, bu = self.bank[i], self.bank[2 + i]
                    for kc in range(KC):
                        k.op("pe", lambda e: e.matmul(bg[:, 0:tw], vg[:, kc, oc2 * 128:(oc2 + 1) * 128],
                                                       xn[:, kc, t0:t0 + tw], start=(kc == 0), stop=(kc == KC - 1)),
                             R=[sg.r(), xn.r()], W=[bg.r()])
                    for kc in range(2):
                        k.op("pe", lambda e: e.matmul(bu[:, 0:tw], vp[:, kc, oc * 128:(oc + 1) * 128],
                                                       pt[:, kc, t0:t0 + tw], start=(kc == 0), stop=(kc == 1)),
                             R=[sp_.r(), pt.r()], W=[bu.r()])
                    t1 = self.t1[self.rot("t1", 2)]
                    k.op("act", lambda e: e.activation(out=t1[:, 0:tw], in_=bg[:, 0:tw], func=AF.Sigmoid),
                         R=[bg.r()], W=[t1.r()])
                    t2 = self.t2[self.rot("t2", 2)]
                    k.op("dve", lambda e: e.tensor_tensor(out=t2[:, 0:tw], in0=t1[:, 0:tw], in1=bu[:, 0:tw],
                                                           op=ALU.mult),
                         R=[t1.r(), bu.r()], W=[t2.r()])
                    k.op("pool", lambda e: e.tensor_tensor(out=x[:, oc, t0:t0 + tw], in0=x[:, oc, t0:t0 + tw],
                                                            in1=t2[:, 0:tw], op=ALU.add),
                         R=[t2.r(), x.r(oc)], W=[x.r(oc)])

    def cs(self, name, rows=128):
        o, n = CST[name]
        return self.cst[0:rows, o:o + n]

    def mixer(self, l):
        self.k.barrier()
        self.nslab = 2
        self.norm("norm_mix%d" % l)
        if l % 2 == 0:
            self.lru(l // 2)
            if "norwkv" not in self.enable:
                self.k.barrier()
                self.rwkv(l // 2)
        else:
            self.s5(l // 2)
        self.nslab = 6
        self.k.barrier()

    def gelu_(self, eng_out, src, tmp, TBc):
        k = self.k
        k.op("dve", lambda e: e.tensor_tensor(out=tmp[:, 0:TBc], in0=src[:, 0:TBc], in1=src[:, 0:TBc], op=ALU.mult),
             R=[src.r()], W=[tmp.r()])
        k.op("dve", lambda e: e.tensor_scalar(out=tmp[:, 0:TBc], in0=tmp[:, 0:TBc], scalar1=0.044715, scalar2=1.0,
                                               op0=ALU.mult, op1=ALU.add), R=[tmp.r()], W=[tmp.r()])
        k.op("dve", lambda e: e.tensor_tensor(out=tmp[:, 0:TBc], in0=tmp[:, 0:TBc], in1=src[:, 0:TBc], op=ALU.mult),
             R=[tmp.r(), src.r()], W=[tmp.r()])
        k.op("act", lambda e: e.activation(out=tmp[:, 0:TBc], in_=tmp[:, 0:TBc], func=AF.Sigmoid, scale=1.5957691216),
             R=[tmp.r()], W=[tmp.r()])
        k.op("dve", lambda e: e.tensor_tensor(out=eng_out[:, 0:TBc], in0=tmp[:, 0:TBc], in1=src[:, 0:TBc], op=ALU.mult),
             R=[tmp.r(), src.r()], W=[eng_out.r()])

    def proj_chunk(self, W2d, c0, cw, evac):
        k, xn = self.k, self.xn
        sg, vg = self.load_slab(W2d, KC, c0, cw)
        for (t0, tw) in self.tbs:
            bg = self.bank[self.rot("pj", 2)]
            for kc in range(KC):
                k.op("pe", lambda e: e.matmul(bg[0:cw, 0:tw], vg[:, kc, 0:cw], xn[:, kc, t0:t0 + tw],
                                               start=(kc == 0), stop=(kc == KC - 1)),
                     R=[sg.r(), xn.r()], W=[bg.r()])
            evac(bg, t0, tw)

    def lru(self, j):
        k, x, TB, ps = self.k, self.x, self.TB, self.ps_
        ns = NS if ps == 1 else 0
        Wi = self.W["w_in_ab"][j]
        wa = self.av("lwa", 8704, 512, BF16, a=8)
        wx = self.av("lwx", 9216, 512, BF16, a=8)
        k.dma("pool", wa[:, :, :], self.I["wa_bd"][j], W=[wa.r()])
        k.dma("pool", wx[:, :, :], self.I["wx_bd"][j], W=[wx.r()])
        XP = self.av("lXP", 9728, 1216)
        names = ["xc", "gr", "gi", "aa", "tt", "hh"]
        T = {nm: self.av("l" + nm, 10944 + i * 1152, 1152) for i, nm in enumerate(names)}
        xcb = self.sq[0]
        hst, cvo, h0s, cv0 = self.lru_small
        nsp = self.lru_nsp
        if ns:
            k.dma("sp", h0s[:, :, 0:16], self.I["st_h"][j], W=[h0s.r()])
            k.dma("sp", cv0[:, :, 0:16, :], self.I["st_conv"][j], W=[cv0.r()])
        if ps == 0:
            k.op("dve", lambda e: e.memset(h0s[:, :, 16:17], 0.0), W=[h0s.r()], acc=True)
            k.op("dve", lambda e: e.memset(cv0[:, :, 16:17, :], 0.0), W=[cv0.r()], acc=True)
        else:
            k.dma("sp", h0s[:, :, 16:17], self.o_h.t[j][:, :, 16:17], R=[self.o_h.r()], W=[h0s.r()], allow_slow_non_contiguous=True)
            k.dma("sp", cv0[:, :, 16:17, :], self.o_conv.t[j][:, :, 16:17, :], R=[self.o_conv.r()], W=[cv0.r()], allow_slow_non_contiguous=True)
        lo, _ = self.lay["lam_b%d" % j]
        k.op("act", lambda e: e.activation(out=nsp[:, 0:8], in_=self.vec[:, lo:lo + 8], func=AF.Exp, scale=-1.0),
             R=[self.vec.r()], W=[nsp.r()])
        k.op("act", lambda e: e.activation(out=nsp[:, 0:8], in_=nsp[:, 0:8], func=AF.Ln, bias=self.cs("one")),
             R=[nsp.r(), self.cst.r()], W=[nsp.r()])
        k.op("dve", lambda e: e.tensor_scalar(out=nsp[:, 8:16], in0=nsp[:, 0:8], scalar1=-16.0, scalar2=None,
                                               op0=ALU.mult), R=[nsp.r()], W=[nsp.r()])
        k.op("dve", lambda e: e.tensor_scalar(out=nsp[:, 0:8], in0=nsp[:, 0:8], scalar1=-8.0, scalar2=None,
                                               op0=ALU.mult), R=[nsp.r()], W=[nsp.r()])
        xc, gr, gi, aa, tt, hh = (T[n] for n in names)
        gb = gr
        XPp = XP[:, 0:1027]
        XPs = XP[:, 1027:1027 + 176].rearrange("p (b t) -> p b t", b=16)
        for c in range(8):
            def ev_x(bg, t0, tw):
                if t0 < 1024:
                    k.op("act", lambda e: e.copy(out=XP[:, 3 + t0:3 + t0 + tw], in_=bg[:, 0:tw]),
                         R=[bg.r()], W=[XP.r()])
                else:
                    k.op("act", lambda e: e.copy(out=XPs[:, :, 3:11],
                                                 in_=bg[:, 0:128].rearrange("p (b t) -> p b t", b=16)),
                         R=[bg.r()], W=[XP.r()])
            self.proj_chunk(Wi, COLS_A + c * 128, 128, ev_x)

            k.op("dve", lambda e: e.tensor_copy(out=XP[:, 0:3], in_=cv0[:, c, 16, :]), R=[cv0.r()], W=[XP.r()])
            if ns:
                k.op("dve", lambda e: e.tensor_copy(out=XPs[:, :, 0:3], in_=cv0[:, c, 0:16, :]),
                     R=[cv0.r()], W=[XP.r()])
            k.op("dve", lambda e: e.tensor_copy(out=cvo[:, c, 16, :], in_=XP[:, 1024:1027]), R=[XP.r()], W=[cvo.r()])
            if ns:
                k.op("dve", lambda e: e.tensor_copy(out=cvo[:, c, 0:16, :], in_=XPs[:, :, 8:11]),
                     R=[XP.r()], W=[cvo.r()])
            segs = [(xc[:, 0:1024], lambda q: XP[:, q:q + 1024])]
            if ns:
                segs.append((xc[:, 1024:1152].rearrange("p (b t) -> p b t", b=16), lambda q: XPs[:, :, q:q + 8]))
            for (dst, srcf) in segs:
                k.op("dve", lambda e: e.tensor_scalar(out=dst, in0=srcf(0), scalar1=self.vcol("conv_w0_%d" % j, c),
                                                       scalar2=self.vcol("conv_b_b%d" % j, c),
                                                       op0=ALU.mult, op1=ALU.add),
                     R=[XP.r(), self.vec.r()], W=[xc.r()])
                for q in range(1, 4):
                    k.op("dve", lambda e: e.scalar_tensor_tensor(out=dst, in0=srcf(q),
                                                                  scalar=self.vcol("conv_w%d_%d" % (q, j), c),
                                                                  in1=dst, op0=ALU.mult, op1=ALU.add),
                         R=[XP.r(), xc.r(), self.vec.r()], W=[xc.r()])
            k.op("act", lambda e: e.copy(out=xcb[:, 0:TB], in_=xc[:, 0:TB]), R=[xc.r()], W=[xcb.r()])
            for (wt, bname, dst) in ((wa, "ba_b%d" % j, gr), (wx, "bx_b%d" % j, gi)):
                for (t0, tw) in self.tbs:
                    bg = self.bank[self.rot("pj", 2)]
                    k.op("pe", lambda e: e.matmul(bg[:, 0:tw], wt[:, c, :], xcb[:, t0:t0 + tw], start=True, stop=True),
                         R=[wt.r(), xcb.r()], W=[bg.r()])
                    k.op("act", lambda e: e.activation(out=dst[:, t0:t0 + tw], in_=bg[:, 0:tw], func=AF.Sigmoid,
                                                       bias=self.vcol(bname, c)),
                         R=[bg.r(), self.vec.r()], W=[dst.r()])
            k.op("act", lambda e: e.activation(out=aa[:, 0:TB], in_=gr[:, 0:TB], func=AF.Exp, scale=nsp[:, c:c + 1]),
                 R=[gr.r(), nsp.r()], W=[aa.r()])
            k.op("act", lambda e: e.activation(out=tt[:, 0:TB], in_=gr[:, 0:TB], func=AF.Exp,
                                               scale=nsp[:, 8 + c:9 + c]), R=[gr.r(), nsp.r()], W=[tt.r()])
            k.op("dve", lambda e: e.tensor_scalar(out=tt[:, 0:TB], in0=tt[:, 0:TB], scalar1=-1.0, scalar2=1.0,
                                                   op0=ALU.mult, op1=ALU.add), R=[tt.r()], W=[tt.r()])
            k.op("act", lambda e: e.activation(out=tt[:, 0:TB], in_=tt[:, 0:TB], func=AF.Sqrt),
                 R=[tt.r()], W=[tt.r()])
            k.op("dve", lambda e: e.tensor_tensor(out=gi[:, 0:TB], in0=gi[:, 0:TB], in1=xc[:, 0:TB], op=ALU.mult),
                 R=[gi.r(), xc.r()], W=[gi.r()])
            k.op("dve", lambda e: e.tensor_tensor(out=tt[:, 0:TB], in0=tt[:, 0:TB], in1=gi[:, 0:TB], op=ALU.mult),
                 R=[gi.r(), tt.r()], W=[tt.r()])
            k.op("dve", lambda e: e.tensor_tensor_scan(out=hh[:, 0:1024], data0=aa[:, 0:1024], data1=tt[:, 0:1024],
                                                        initial=h0s[:, c, 16:17], op0=ALU.mult, op1=ALU.add),
                 R=[aa.r(), tt.r(), h0s.r()], W=[hh.r()])
            for bq in range(ns):
                o_ = 1024 + bq * 8
                k.op("dve", lambda e: e.tensor_tensor_scan(out=hh[:, o_:o_ + 8], data0=aa[:, o_:o_ + 8],
                                                            data1=tt[:, o_:o_ + 8], initial=h0s[:, c, bq:bq + 1],
                                                            op0=ALU.mult, op1=ALU.add),
                     R=[aa.r(), tt.r(), h0s.r()], W=[hh.r()], acc=True)
            k.op("dve", lambda e: e.tensor_copy(out=hst[:, c, 16:17], in_=hh[:, 1023:1024]), R=[hh.r()], W=[hst.r()])
            if ns:
                k.op("dve", lambda e: e.tensor_copy(
                    out=hst[:, c, 0:16], in_=hh[:, 1024:1152].rearrange("p (b t) -> p b t", b=16)[:, :, 7]),
                    R=[hh.r()], W=[hst.r()])
            def ev_g(bg, t0, tw):
                k.op("act", lambda e: e.copy(out=gb[:, t0:t0 + tw], in_=bg[:, 0:tw]), R=[bg.r()], W=[gb.r()])
            self.proj_chunk(Wi, COLS_A + 1024 + c * 128, 128, ev_g)
            self.gelu_(gi, gb, aa, TB)
            yb = self.ybuf
            k.op("dve", lambda e: e.tensor_tensor(out=yb[:, c, 0:TB], in0=gi[:, 0:TB], in1=hh[:, 0:TB], op=ALU.mult),
                 R=[gi.r(), hh.r()], W=[yb.r()])
        k.dma("sp", self.o_h.t[j], hst[:, :, :], R=[hst.r()], W=[self.o_h.r()])
        k.dma("sp", self.o_conv.t[j], cvo[:, :, :, :], R=[cvo.r()], W=[self.o_conv.r()])
        Wo = self.W["w_out_ab"][j]
        yb = self.ybuf
        for s in range(D // 256):
            sg, vg = self.load_slab(Wo, 8, s * 256, 256, rows0=1024)
            for oc2 in range(2):
                oc = s * 2 + oc2
                for (t0, tw) in self.tbs:
                    bd = self.bank[4 + self.rot("dn", 2)]
                    for kc in range(8):
                        k.op("pe", lambda e: e.matmul(bd[:, 0:tw], vg[:, kc, oc2 * 128:(oc2 + 1) * 128],
                                                       yb[:, kc, t0:t0 + tw], start=(kc == 0), stop=(kc == 7)),
                             R=[sg.r(), yb.r()], W=[bd.r()])
                    k.op("dve", lambda e: e.tensor_tensor(out=x[:, oc, t0:t0 + tw], in0=bd[:, 0:tw],
                                                           in1=x[:, oc, t0:t0 + tw], op=ALU.add),
                         R=[bd.r(), x.r(oc)], W=[x.r(oc)])


    def rwkv(self, j):
        k, x, xn, TB, ps = self.k, self.x, self.xn, self.TB, self.ps_
        ns = NS if ps == 1 else 0
        av, cst, vec = self.av, self.cst, self.vec
        Wi = self.W["w_in_ab"][j]
        ident, blk = self.cs("ident"), self.cs("blk")
        NT = TB // 128
        self.nslab = 1

        def dve(fn, R, W, **kw):
            return k.op("dve", fn, R=R, W=W, **kw)
        w2a2 = av("r_w2a2", 2048, 512, BF16)
        g2a = av("r_g2a", 2560, 512, BF16)
        g2b = av("r_g2b", 3072, 512, BF16)
        k.dma("pool", w2a2[0:64, :], self.I["w2_a"][j], W=[w2a2.r()])
        k.dma("pool", w2a2[64:128, :], self.I["a2_a"][j], W=[w2a2.r()])
        lx, xg0, xg1 = av("r_lx", 4096, 1152), av("r_xg0", 5248, 1152), av("r_xg1", 6400, 1152)
        Tt = [av("r_T%d" % i, 7552 + i * 1152, 1152) for i in range(7)]
        stg = av("r_stg", 15616, 1152)
        shin = av("r_shin", 16768, 459, a=27)
        shout = av("r_shout", 17227, 459, a=27)
        txa, sg0b, sg1b = self.sq[0], self.sq[1], self.sg1b
        if ns:
            k.dma("sp", shin[:, :, 0:16], self.I["st_shift"][j], W=[shin.r()])
        if ps == 0:
            dve(lambda e: e.memset(shin[:, :, 16:17], 0.0), [], [shin.r()], acc=True)
        else:
            k.dma("sp", shin[:, :, 16:17], self.o_shift.t[j][:, :, 16:17], R=[self.o_shift.r()], W=[shin.r()],
                  allow_slow_non_contiguous=True)
        r3 = lambda a: a.rearrange("p (b t) -> p b t", b=16)

        def project_shift(col0, cw, zc, Z, mu_ap):
            D_ = Tt[6]

            def ev(bg, t0, tw):
                k.op("act", lambda e: e.copy(out=Z[0:cw, t0:t0 + tw], in_=bg[0:cw, 0:tw]), R=[bg.r()], W=[Z.r()], acc=True)
            self.proj_chunk(Wi, col0, cw, ev)
            dve(lambda e: e.tensor_copy(out=shout[0:cw, zc, 16:17], in_=Z[0:cw, 1023:1024]), [Z.r()], [shout.r()], acc=True)
            dve(lambda e: e.tensor_tensor(out=D_[0:cw, 1:1024], in0=Z[0:cw, 0:1023], in1=Z[0:cw, 1:1024], op=ALU.subtract),
                [Z.r()], [D_.r()])
            dve(lambda e: e.tensor_tensor(out=D_[0:cw, 0:1], in0=shin[0:cw, zc, 16:17], in1=Z[0:cw, 0:1], op=ALU.subtract),
                [Z.r(), shin.r()], [D_.r()], acc=True)
            if ns:
                Zs_, Ds_ = r3(Z[0:cw, 1024:1152]), r3(D_[0:cw, 1024:1152])
                dve(lambda e: e.tensor_copy(out=shout[0:cw, zc, 0:16], in_=Zs_[:, :, 7]), [Z.r()], [shout.r()], acc=True)
                dve(lambda e: e.tensor_tensor(out=Ds_[:, :, 1:8], in0=Zs_[:, :, 0:7], in1=Zs_[:, :, 1:8], op=ALU.subtract),
                    [Z.r()], [D_.r()], acc=True)
                dve(lambda e: e.tensor_tensor(out=Ds_[:, :, 0], in0=shin[0:cw, zc, 0:16], in1=Zs_[:, :, 0], op=ALU.subtract),
                    [Z.r(), shin.r()], [D_.r()], acc=True)
            dve(lambda e: e.scalar_tensor_tensor(out=Z[0:cw, 0:TB], in0=D_[0:cw, 0:TB], scalar=mu_ap, in1=Z[0:cw, 0:TB],
                                                 op0=ALU.mult, op1=ALU.add), [D_.r(), Z.r(), vec.r()], [Z.r()])
        mo = self.lay["mu_x%d" % j][0]
        dve(lambda e: e.memset(shout[:, 26, :], 0.0), [], [shout.r()], acc=True)
        project_shift(3072, 128, 24, lx, vec[:, mo:mo + 1])
        project_shift(3200, 128, 25, xg0, vec[:, mo + 2:mo + 3])
        project_shift(3328, 32, 26, xg1, vec[0:32, mo + 3:mo + 4])
        k.op("act", lambda e: e.activation(out=txa[0:64, 0:TB], in_=lx[0:64, 0:TB], func=AF.Tanh), R=[lx.r()], W=[txa.r()])
        k.op("act", lambda e: e.copy(out=txa[64:128, 0:TB], in_=lx[64:128, 0:TB]), R=[lx.r()], W=[txa.r()], acc=True)
        k.op("act", lambda e: e.activation(out=sg0b[:, 0:TB], in_=xg0[:, 0:TB], func=AF.Sigmoid), R=[xg0.r()], W=[sg0b.r()])
        k.op("act", lambda e: e.activation(out=sg1b[0:32, 0:TB], in_=xg1[0:32, 0:TB], func=AF.Sigmoid),
             R=[xg1.r()], W=[sg1b.r()])
        Zsd, Ysd, Bon = self.Zs, self.Ys, self.Bon

        def emit(q, c, Q):
            for tt in range(NT):
                bk = self.bank[6 + (tt % 2)]
                k.op("pe", lambda e: e.transpose(bk[:, 0:128], Q[:, tt * 128:(tt + 1) * 128], ident),
                     R=[Q.r(), cst.r()], W=[bk.r()])
                if tt % 2 == 0:
                    k.op("act", lambda e: e.copy(out=stg[:, tt * 128:(tt + 1) * 128], in_=bk[:, 0:128]),
                         R=[bk.r()], W=[stg.r()], acc=True)
                else:
                    dve(lambda e: e.tensor_copy(out=stg[:, tt * 128:(tt + 1) * 128], in_=bk[:, 0:128]),
                        [bk.r()], [stg.r()], acc=True)
            sv = stg[:, 0:TB].rearrange("p (tt f) -> p tt f", f=128)
            if q == 5:
                dst = self.Vs.t[0:TB, c * 128:(c + 1) * 128].rearrange("(tt p) f -> p tt f", p=128)
                k.dma("sp", dst, sv, R=[stg.r()], W=[self.Vs.r()])
            else:
                for h2 in range(2):
                    dst = Zsd.t[0:TB, 2 * c + h2, q, :].rearrange("(tt p) j -> p tt j", p=128)
                    k.dma("sp", dst, sv[:, :, h2 * 64:(h2 + 1) * 64], R=[stg.r()], W=[Zsd.r()])

        def headsum(user, src):
            for (t0, tw) in self.tbs:
                bg = self.bank[4 + self.rot("hs", 2)]
                k.op("pe", lambda e: e.matmul(bg[:, 0:tw], blk, src[:, t0:t0 + tw], start=True, stop=True),
                     R=[cst.r(), src.r()], W=[bg.r()])
                user(bg, t0, tw)

        for c in range(8):
            T0, T1, T2, T3, T4, T5 = Tt[0:6]
            vc = lambda nm: self.vcol("%s%d" % (nm, j), c)
            project_shift(c * 128, 128, c, T0, vc("mu_r"))
            project_shift(1024 + c * 128, 128, 8 + c, T1, vc("mu_k"))
            project_shift(2048 + c * 128, 128, 16 + c, T2, vc("mu_v"))
            for (t0, tw) in self.tbs:
                bg = self.bank[4 + self.rot("hs", 2)]
                k.op("pe", lambda e: e.matmul(bg[:, 0:tw], w2a2[64:128, c * 128:(c + 1) * 128], txa[64:128, t0:t0 + tw],
                                               start=True, stop=True), R=[w2a2.r(), txa.r()], W=[bg.r()])
                k.op("act", lambda e: e.activation(out=T3[:, t0:t0 + tw], in_=bg[:, 0:tw], func=AF.Sigmoid, bias=vc("a0_a")),
                     R=[bg.r(), vec.r()], W=[T3.r()], acc=True)
            dve(lambda e: e.tensor_scalar(out=T4[:, 0:TB], in0=T1[:, 0:TB], scalar1=vc("kk_a"), scalar2=None, op0=ALU.mult),
                [T1.r(), vec.r()], [T4.r()])
            dve(lambda e: e.tensor_tensor(out=T5[:, 0:TB], in0=T4[:, 0:TB], in1=T4[:, 0:TB], op=ALU.mult), [T4.r()], [T5.r()])

            def nrm_ev(bg, t0, tw):
                k.op("act", lambda e: e.activation(out=Tt[6][:, t0:t0 + tw], in_=bg[:, 0:tw], func=AF.Sqrt),
                     R=[bg.r()], W=[Tt[6].r()], acc=True)
            headsum(nrm_ev, T5)
            dve(lambda e: e.tensor_scalar(out=Tt[6][:, 0:TB], in0=Tt[6][:, 0:TB], scalar1=1e-12, scalar2=None, op0=ALU.max),
                [Tt[6].r()], [Tt[6].r()])
            dve(lambda e: e.reciprocal(out=Tt[6][:, 0:TB], in_=Tt[6][:, 0:TB]), [Tt[6].r()], [Tt[6].r()])
            dve(lambda e: e.tensor_tensor(out=T4[:, 0:TB], in0=T4[:, 0:TB], in1=Tt[6][:, 0:TB], op=ALU.mult),
                [T4.r(), Tt[6].r()], [T4.r()])
            dve(lambda e: e.tensor_tensor(out=T5[:, 0:TB], in0=T4[:, 0:TB], in1=T3[:, 0:TB], op=ALU.mult),
                [T4.r(), T3.r()], [T5.r()])
            emit(0, c, T4)
            emit(2, c, T5)
            dve(lambda e: e.tensor_scalar(out=T4[:, 0:TB], in0=T3[:, 0:TB], scalar1=-1.0, scalar2=vc("ka_a"),
                                          op0=ALU.add, op1=ALU.mult), [T3.r(), vec.r()], [T4.r()])
            dve(lambda e: e.scalar_tensor_tensor(out=T4[:, 0:TB], in0=T4[:, 0:TB], scalar=1.0, in1=T1[:, 0:TB],
                                                 op0=ALU.add, op1=ALU.mult), [T4.r(), T1.r()], [T4.r()])
            emit(3, c, T4)
            dve(lambda e: e.scalar_tensor_tensor(out=T5[:, 0:TB], in0=T0[:, 0:TB], scalar=vc("rk_a"), in1=T4[:, 0:TB],
                                                 op0=ALU.mult, op1=ALU.mult), [T0.r(), T4.r(), vec.r()], [T5.r()])

            def bon_ev(bg, t0, tw):
                dve(lambda e: e.tensor_tensor(out=Tt[6][:, t0:t0 + tw], in0=bg[:, 0:tw], in1=T2[:, t0:t0 + tw], op=ALU.mult),
                    [bg.r(), T2.r()], [Tt[6].r()], acc=True)
            headsum(bon_ev, T5)
            k.dma("sp", Bon.t[c][:, 0:TB], Tt[6][:, 0:TB], R=[Tt[6].r()], W=[Bon.r()])
            for (t0, tw) in self.tbs:
                bg = self.bank[4 + self.rot("hs", 2)]
                k.op("pe", lambda e: e.matmul(bg[:, 0:tw], w2a2[0:64, c * 128:(c + 1) * 128], txa[0:64, t0:t0 + tw],
                                               start=True, stop=True), R=[w2a2.r(), txa.r()], W=[bg.r()])
                k.op("act", lambda e: e.activation(out=T1[:, t0:t0 + tw], in_=bg[:, 0:tw], func=AF.Sigmoid, bias=vc("w0_a")),
                     R=[bg.r(), vec.r()], W=[T1.r()], acc=True)
            k.op("act", lambda e: e.activation(out=T1[:, 0:TB], in_=T1[:, 0:TB], func=AF.Exp, scale=-float(np.exp(-0.5))),
                 R=[T1.r()], W=[T1.r()])
            emit(1, c, T1)
            emit(4, c, T0)
            emit(5, c, T2)
        k.dma("sp", self.o_shift.t[j], shout[:, :, :], R=[shout.r()], W=[self.o_shift.r()])
        k.barrier()
        Spf, T1p = av("q_Sp", 0, 512), av("q_T1p", 512, 512, a=8)
        Sp = Tile(Spf[:, :].rearrange("p (e j) -> p e j", e=8), "q_Sp3")
        Sp.reg = Spf.reg
        Ss = av("q_Ss", 1024, 8192)
        T1s = av("q_T1s", 9216, 2048)
        Xb = [av("q_X%d" % i, 11264 + i * 1280, 1280) for i in range(2)]
        vt = [av("q_vt%d" % i, 13824 + i * 128, 128, a=16) for i in range(2)]
        yt = [av("q_yt%d" % i, 14080 + i * 128, 128, a=16) for i in range(2)]
        vs_, ys_ = av("q_vs", 14336, 1024), av("q_ys", 15360, 1024)
        sa_t = av("q_sa", 16384, 32)
        rep = self.cs("rep", rows=16)
        Zv = Zsd.t.rearrange("t h q j -> h t (q j)")
        Vv = self.Vs.t.rearrange("t (p e) -> p t e", e=8)
        Vsr = self.Vs.r()
        Yv = Ysd.t.rearrange("t (p e) -> p t e", e=8)
        if ps == 0:
            dve(lambda e: e.memset(Sp[:, :, :], 0.0), [], [Sp.r()])
        else:
            k.dma("sp", Spf[:, :], self.o_wkv.t[j][:, 16, :], R=[self.o_wkv.r()], W=[Sp.r()])

        def step(S, T1, reg, shp, sa, vv, yy, RW):
            nd = len(shp)
            bj = lambda a: a.unsqueeze(nd - 1).broadcast_to([128] + shp)
            be = lambda a: a.unsqueeze(nd).broadcast_to([128] + shp)
            Rr, Wr = RW
            dve(lambda e: e.tensor_tensor(out=T1, in0=S, in1=bj(reg(0)), op=ALU.mult), Rr, Wr, sew=False)
            dve(lambda e: e.tensor_reduce(out=sa, in_=T1, axis=AX.X, op=ALU.add), Wr, Wr, sew=False)
            dve(lambda e: e.tensor_tensor(out=S, in0=S, in1=bj(reg(1)), op=ALU.mult), Rr, Wr, sew=False)
            dve(lambda e: e.tensor_tensor(out=T1, in0=bj(reg(2)), in1=be(sa), op=ALU.mult), Rr, Wr, sew=False)
            dve(lambda e: e.tensor_tensor(out=S, in0=S, in1=T1, op=ALU.subtract), Wr, Wr, sew=False)
            dve(lambda e: e.tensor_tensor(out=T1, in0=bj(reg(3)), in1=be(vv), op=ALU.mult), Rr, Wr, sew=False)
            dve(lambda e: e.tensor_tensor(out=S, in0=S, in1=T1, op=ALU.add), Wr, Wr, sew=False)
            dve(lambda e: e.tensor_tensor(out=T1, in0=S, in1=bj(reg(4)), op=ALU.mult), Rr, Wr, sew=False)
            dve(lambda e: e.tensor_reduce(out=yy, in_=T1, axis=AX.X, op=ALU.add), Wr, Wr, sew=False)

        def rep_mm(Xt, bset):
            Xv = Xt[0:16, 0:1280].rearrange("h (t q j) -> h t q j", t=4, q=5)
            for q in range(5):
                bk = self.bank[bset * 3 + q // 2]
                k.op("pe", lambda e: e.matmul(bk[:, (q % 2) * 256:(q % 2) * 256 + 256], rep, Xv[:, :, q, :],
                                               start=True, stop=True), R=[cst.r(), Xt.r()], W=[bk.r()], acc=True)

        wreg = Reg("q_work")
        vtt = ytt = None
        for g4 in range(1024 // 4):
            t0 = g4 * 4
            Xt = Xb[g4 % 2]
            k.dma("sp", Xt[0:16, 0:1280].rearrange("h (t f) -> h t f", t=4), Zv[:, t0:t0 + 4, :],
                  R=[Zsd.r()], W=[Xt.r()])
            if t0 % 16 == 0:
                vtt, ytt = vt[(t0 // 16) % 2], yt[(t0 // 16) % 2]
                k.dma("sp", vtt[:, :, :], Vv[:, t0:t0 + 16, :], R=[Vsr], W=[vtt.r()], allow_slow_non_contiguous=True)
            bset = g4 % 2
            rep_mm(Xt, bset)
            banks = [self.bank[bset * 3 + i] for i in range(3)]
            for tl in range(4):
                reg = lambda q: banks[q // 2][:, (q % 2) * 256 + tl * 64:(q % 2) * 256 + tl * 64 + 64]
                ti = (t0 + tl) % 16
                step(Sp[:, :, :], T1p[:, :, :], reg, [8, 64], sa_t[:, 0:8], vtt[:, ti, :], ytt[:, ti, :],
                     ([b_.r() for b_ in banks] + [vtt.r(), wreg, Sp.r()], [wreg, ytt.r()]))
            if t0 % 16 == 12:
                k.dma("sp", Yv[:, t0 - 12:t0 + 4, :], ytt[:, :, :], R=[ytt.r()], W=[Ysd.r()], allow_slow_non_contiguous=True)
        k.dma("sp", self.o_wkv.t[j][:, 16, :], Spf[:, :], R=[Sp.r(), wreg], W=[self.o_wkv.r()])
        if ns:
            k.dma("sp", Ss[:, :], self.I["st_wkv"][j].rearrange("p b f -> p (b f)"), W=[Ss.r()])
            vs4 = vs_[:, :].rearrange("p (b t e) -> p b t e", b=16, t=8)
            ys4 = ys_[:, :].rearrange("p (b t e) -> p b t e", b=16, t=8)
            for bq in range(8):
                k.dma("sp", vs_[:, :].rearrange("p (n e) -> p n e", e=8)[:, 16 * bq:16 * bq + 16, :],
                      Vv[:, 1024 + 16 * bq:1024 + 16 * bq + 16, :],
                      R=[Vsr], W=[vs_.r()], allow_slow_non_contiguous=True)
            Ss4 = Ss[:, :].rearrange("p (b e j) -> p b e j", b=16, e=8)
            T1s4 = T1s[:, :].rearrange("p (b e j) -> p b e j", b=4, e=8)
            sreg = Reg("q_works")
            it = 0
            for t in range(8):
                for qd in range(4):
                    Xt = Xb[it % 2]
                    bset = it % 2
                    it += 1
                    r0 = 1024 + 32 * qd + t
                    k.dma("sp", Xt[0:16, 0:1280].rearrange("h (t f) -> h t f", t=4),
                          Zv[:, r0:r0 + 25:8, :], R=[Zsd.r()], W=[Xt.r()])
                    rep_mm(Xt, bset)
                    banks = [self.bank[bset * 3 + i] for i in range(3)]
                    reg = lambda q: banks[q // 2][:, (q % 2) * 256:(q % 2) * 256 + 256].rearrange("p (b j) -> p b j", b=4)
                    step(Ss4[:, 4 * qd:4 * qd + 4, :, :], T1s4[:, :, :, :], reg, [4, 8, 64],
                         sa_t[:, 0:32].rearrange("p (b e) -> p b e", b=4), vs4[:, 4 * qd:4 * qd + 4, t, :],
                         ys4[:, 4 * qd:4 * qd + 4, t, :],
                         ([b_.r() for b_ in banks] + [vs_.r(), Ss.r(), sreg], [sreg, ys_.r()]))
            k.dma("sp", Yv[:, 1024:1152, :], ys_[:, :].rearrange("p (n e) -> p n e", e=8), R=[ys_.r(), sreg], W=[Ysd.r()],
                  allow_slow_non_contiguous=True)
            k.dma("sp", self.o_wkv.t[j][:, 0:16, :].rearrange("p b f -> p (b f)"), Ss[:, :], R=[Ss.r(), sreg],
                  W=[self.o_wkv.r()])
        k.barrier()
        self.nslab = 2
        yab = av("p_yab", 4096, 4608, BF16, a=8)
        ytk = av("p_ytk", 8704, 1152)
        yc, ysq, mean, bon = (av("p_t%d" % i, 9856 + i * 1152, 1152) for i in range(4))
        g2a = av("p_g2a", 14464, 512, BF16)
        g2b = av("p_g2b", 14976, 512, BF16)
        gne = av("p_gne", 15488, 4)
        dve(lambda e: e.memset(gne[:, :], GN_EPS), [], [gne.r()])
        k.dma("pool", g2a[:, :], self.I["g2_a"][j][0:128, :], W=[g2a.r()])
        k.dma("pool", g2b[0:32, :], self.I["g2_a"][j][128:160, :], W=[g2b.r()])
        for c in range(8):
            vc = lambda nm: self.vcol("%s%d" % (nm, j), c)
            k.dma("sp", ytk[:, 0:TB].rearrange("p (tt f) -> p tt f", f=128),
                  Ysd.t[0:TB, c * 128:(c + 1) * 128].rearrange("(tt p) f -> p tt f", p=128), R=[Ysd.r()], W=[ytk.r()])
            k.dma("sp", bon[:, 0:TB], Bon.t[c][:, 0:TB], R=[Bon.r()], W=[bon.r()])
            for tt in range(NT):
                bk = self.bank[6 + (tt % 2)]
                k.op("pe", lambda e: e.transpose(bk[:, 0:128], ytk[:, tt * 128:(tt + 1) * 128], ident),
                     R=[ytk.r(), cst.r()], W=[bk.r()])
                k.op("act", lambda e: e.copy(out=yc[:, tt * 128:(tt + 1) * 128], in_=bk[:, 0:128]), R=[bk.r()], W=[yc.r()], acc=True)
            dve(lambda e: e.tensor_tensor(out=ysq[:, 0:TB], in0=yc[:, 0:TB], in1=yc[:, 0:TB], op=ALU.mult), [yc.r()], [ysq.r()])

            def mean_ev(bg, t0, tw):
                k.op("act", lambda e: e.activation(out=mean[:, t0:t0 + tw], in_=bg[:, 0:tw], func=AF.Copy, scale=1.0 / 64),
                     R=[bg.r()], W=[mean.r()], acc=True)
            headsum(mean_ev, yc)

            def var_ev(bg, t0, tw):
                k.op("act", lambda e: e.activation(out=ysq[:, t0:t0 + tw], in_=bg[:, 0:tw], func=AF.Copy, scale=1.0 / 64),
                     R=[bg.r()], W=[ysq.r()], acc=True)
            headsum(var_ev, ysq)
            dve(lambda e: e.tensor_tensor(out=yc[:, 0:TB], in0=yc[:, 0:TB], in1=mean[:, 0:TB], op=ALU.subtract),
                [yc.r(), mean.r()], [yc.r()])
            dve(lambda e: e.tensor_tensor(out=mean[:, 0:TB], in0=mean[:, 0:TB], in1=mean[:, 0:TB], op=ALU.mult),
                [mean.r()], [mean.r()])
            dve(lambda e: e.tensor_tensor(out=ysq[:, 0:TB], in0=ysq[:, 0:TB], in1=mean[:, 0:TB], op=ALU.subtract),
                [ysq.r(), mean.r()], [ysq.r()])
            k.op("act", lambda e: e.activation(out=ysq[:, 0:TB], in_=ysq[:, 0:TB], func=AF.Sqrt, bias=gne[:, 0:1]),
                 R=[ysq.r(), gne.r()], W=[ysq.r()])
            dve(lambda e: e.reciprocal(out=ysq[:, 0:TB], in_=ysq[:, 0:TB]), [ysq.r()], [ysq.r()])
            dve(lambda e: e.tensor_tensor(out=yc[:, 0:TB], in0=yc[:, 0:TB], in1=ysq[:, 0:TB], op=ALU.mult),
                [yc.r(), ysq.r()], [yc.r()])
            dve(lambda e: e.tensor_scalar(out=yc[:, 0:TB], in0=yc[:, 0:TB], scalar1=vc("lnx_g"), scalar2=vc("lnx_b"),
                                          op0=ALU.mult, op1=ALU.add), [yc.r(), vec.r()], [yc.r()])
            dve(lambda e: e.tensor_tensor(out=yc[:, 0:TB], in0=yc[:, 0:TB], in1=bon[:, 0:TB], op=ALU.add),
                [yc.r(), bon.r()], [yc.r()])
            for (t0, tw) in self.tbs:
                bg = self.bank[4 + self.rot("hs", 2)]
                k.op("pe", lambda e: e.matmul(bg[:, 0:tw], g2a[:, c * 128:(c + 1) * 128], sg0b[:, t0:t0 + tw],
                                               start=True, stop=False), R=[g2a.r(), sg0b.r()], W=[bg.r()])
                k.op("pe", lambda e: e.matmul(bg[:, 0:tw], g2b[0:32, c * 128:(c + 1) * 128], sg1b[0:32, t0:t0 + tw],
                                               start=False, stop=True), R=[g2b.r(), sg1b.r()], W=[bg.r()])
                dve(lambda e: e.tensor_tensor(out=yab[:, c, t0:t0 + tw], in0=yc[:, t0:t0 + tw], in1=bg[:, 0:tw], op=ALU.mult),
                    [yc.r(), bg.r()], [yab.r()], acc=True)
        Wo = self.W["w_out_ab"][j]
        for s in range(D // 256):
            sg, vg = self.load_slab(Wo, 8, s * 256, 256, rows0=0)
            for oc2 in range(2):
                oc = s * 2 + oc2
                for (t0, tw) in self.tbs:
                    bd = self.bank[self.rot("pj", 2)]
                    for kc in range(8):
                        k.op("pe", lambda e: e.matmul(bd[:, 0:tw], vg[:, kc, oc2 * 128:(oc2 + 1) * 128],
                                                       yab[:, kc, t0:t0 + tw], start=(kc == 0), stop=(kc == 7)),
                             R=[sg.r(), yab.r()], W=[bd.r()])
                    dve(lambda e: e.tensor_tensor(out=x[:, oc, t0:t0 + tw], in0=bd[:, 0:tw], in1=x[:, oc, t0:t0 + tw],
                                                  op=ALU.add), [bd.r(), x.r(oc)], [x.r(oc)])

    def s5(self, j):
        k, x, xn, TB, ps = self.k, self.x, self.xn, self.TB, self.ps_
        ns = NS if ps == 1 else 0
        PI = float(np.pi)
        av = self.av
        P = {nm: av("s5" + nm, 4096 + i * 128, 128) for i, nm in enumerate(
            ["are", "aim", "dtt", "rho", "tht", "ta", "tb", "fre", "fim", "tc"])}
        Bp1, Bp2, Cp1, Cp2, Bb1, Bb2, tA, tB = (av("s5s%d" % i, 5376 + i * 128, 128, a=8) for i in range(8))
        T1c = av("s5T1", 6400, 64, BF16)
        T2c = av("s5T2", 6464, 64, BF16)
        LB1 = [av("s5LB1%d" % i, 6528 + i * 64, 64, BF16) for i in range(2)]
        LB2 = [av("s5LB2%d" % i, 6656 + i * 64, 64, BF16) for i in range(2)]
        C1p = [av("s5C1%d" % i, 6784 + i * 64, 64, BF16) for i in range(2)]
        C2p = [av("s5C2%d" % i, 6912 + i * 64, 64, BF16) for i in range(2)]
        hin = av("s5hin", 7040, 136, a=8)
        hfin = av("s5hfin", 7176, 136, a=8)
        ftmp = av("s5ft", 7312, 68, a=4)
        Ct, St, targ = av("s5Ct", 7424, 1024), av("s5St", 8448, 1024), av("s5ta", 9472, 1024)
        rhof, m, G = av("s5rf", 10496, 1152), av("s5m", 11648, 1152), av("s5G", 12800, 1152)
        G1b, G2b = av("s5G1", 13952, 576, BF16), av("s5G2", 14528, 576, BF16)
        yf, gt = av("s5yf", 15104, 1152), av("s5gt", 16256, 1152)
        cst, vec = self.cst, self.vec
        ident = self.cs("ident")
        negpi = self.cs("negpi")
        Sg = lambda i: cst[:, CST["sgn"][0] + i:CST["sgn"][0] + i + 1]
        V = lambda e: e

        def dve(fn, R, W, **kw):
            return k.op("dve", fn, R=R, W=W, **kw)

        k.dma("sp", P["are"][:, :], self.I["s5_are"][j], W=[P["are"].r()])
        k.dma("sp", P["aim"][:, :], self.I["s5_aim"][j], W=[P["aim"].r()])
        k.dma("sp", P["dtt"][:, :], self.I["s5_ldt"][j], W=[P["dtt"].r()])
        k.op("act", lambda e: e.activation(out=P["dtt"][:, :], in_=P["dtt"][:, :], func=AF.Exp),
             R=[P["dtt"].r()], W=[P["dtt"].r()])
        dve(lambda e: e.tensor_tensor(out=P["tht"][:, :], in0=P["dtt"][:, :], in1=P["aim"][:, :], op=ALU.mult),
            [P["dtt"].r(), P["aim"].r()], [P["tht"].r()])
        dve(lambda e: e.tensor_tensor(out=P["rho"][:, :], in0=P["dtt"][:, :], in1=P["are"][:, :], op=ALU.mult),
            [P["dtt"].r(), P["are"].r()], [P["rho"].r()])
        k.op("act", lambda e: e.activation(out=P["rho"][:, :], in_=P["rho"][:, :], func=AF.Exp),
             R=[P["rho"].r()], W=[P["rho"].r()])

        qi = av("s5qi", 17408, 1024, I32)
        halfpi = self.cs("halfpi")

        def sincos(dst_s, dst_c, src, w, Rs):
            for (dst, addc, lo, hi, bias) in ((dst_s, 0.0, -PI, PI, None), (dst_c, 0.5 * PI, -1.5 * PI, 0.5 * PI, halfpi)):
                k.op("pool", lambda e: e.tensor_scalar(out=qi[:, 0:w], in0=src, scalar1=addc, scalar2=1.0 / (2 * PI),
                                                       op0=ALU.add, op1=ALU.mult), R=Rs, W=[qi.r()])
                k.op("pool", lambda e: e.tensor_copy(out=gt[:, 0:w], in_=qi[:, 0:w]), R=[qi.r()], W=[gt.r()])
                dve(lambda e: e.scalar_tensor_tensor(out=dst[:, 0:w], in0=gt[:, 0:w], scalar=-2 * PI, in1=src,
                                                     op0=ALU.mult, op1=ALU.add), Rs + [gt.r()], [dst.r()])
                dve(lambda e: e.tensor_scalar(out=dst[:, 0:w], in0=dst[:, 0:w], scalar1=lo, scalar2=hi,
                                              op0=ALU.max, op1=ALU.min), [dst.r()], [dst.r()])
                if bias is None:
                    k.op("act", lambda e: e.activation(out=dst[:, 0:w], in_=dst[:, 0:w], func=AF.Sin),
                         R=[dst.r()], W=[dst.r()])
                else:
                    k.op("act", lambda e: e.activation(out=dst[:, 0:w], in_=dst[:, 0:w], func=AF.Sin, bias=bias),
                         R=[dst.r(), cst.r()], W=[dst.r()])
        sincos(P["ta"], P["tb"], P["tht"][:, :], 128, [P["tht"].r()])
        dve(lambda e: e.tensor_tensor(out=P["ta"][:, :], in0=P["ta"][:, :], in1=P["rho"][:, :], op=ALU.mult),
            [P["ta"].r(), P["rho"].r()], [P["ta"].r()])
        dve(lambda e: e.tensor_tensor(out=P["tb"][:, :], in0=P["tb"][:, :], in1=P["rho"][:, :], op=ALU.mult),
            [P["tb"].r(), P["rho"].r()], [P["tb"].r()])
        dve(lambda e: e.tensor_scalar(out=P["tb"][:, :], in0=P["tb"][:, :], scalar1=-1.0, scalar2=None, op0=ALU.add),
            [P["tb"].r()], [P["tb"].r()])
        dve(lambda e: e.tensor_tensor(out=P["tc"][:, :], in0=P["are"][:, :], in1=P["are"][:, :], op=ALU.mult),
            [P["are"].r()], [P["tc"].r()])
        dve(lambda e: e.tensor_tensor(out=P["fre"][:, :], in0=P["aim"][:, :], in1=P["aim"][:, :], op=ALU.mult),
            [P["aim"].r()], [P["fre"].r()])
        dve(lambda e: e.tensor_tensor(out=P["tc"][:, :], in0=P["tc"][:, :], in1=P["fre"][:, :], op=ALU.add),
            [P["tc"].r(), P["fre"].r()], [P["tc"].r()])
        dve(lambda e: e.reciprocal(out=P["tc"][:, :], in_=P["tc"][:, :]), [P["tc"].r()], [P["tc"].r()])
        dve(lambda e: e.tensor_tensor(out=P["fre"][:, :], in0=P["tb"][:, :], in1=P["are"][:, :], op=ALU.mult),
            [P["tb"].r(), P["are"].r()], [P["fre"].r()])
        dve(lambda e: e.tensor_tensor(out=P["fim"][:, :], in0=P["ta"][:, :], in1=P["aim"][:, :], op=ALU.mult),
            [P["ta"].r(), P["aim"].r()], [P["fim"].r()])
        dve(lambda e: e.tensor_tensor(out=P["fre"][:, :], in0=P["fre"][:, :], in1=P["fim"][:, :], op=ALU.add),
            [P["fre"].r(), P["fim"].r()], [P["fre"].r()])
        dve(lambda e: e.tensor_tensor(out=P["fim"][:, :], in0=P["ta"][:, :], in1=P["are"][:, :], op=ALU.mult),
            [P["ta"].r(), P["are"].r()], [P["fim"].r()])
        dve(lambda e: e.tensor_tensor(out=P["ta"][:, :], in0=P["tb"][:, :], in1=P["aim"][:, :], op=ALU.mult),
            [P["tb"].r(), P["aim"].r()], [P["ta"].r()])
        dve(lambda e: e.tensor_tensor(out=P["fim"][:, :], in0=P["fim"][:, :], in1=P["ta"][:, :], op=ALU.subtract),
            [P["fim"].r(), P["ta"].r()], [P["fim"].r()])
        dve(lambda e: e.tensor_tensor(out=P["fre"][:, :], in0=P["fre"][:, :], in1=P["tc"][:, :], op=ALU.mult),
            [P["fre"].r(), P["tc"].r()], [P["fre"].r()])
        dve(lambda e: e.tensor_tensor(out=P["fim"][:, :], in0=P["fim"][:, :], in1=P["tc"][:, :], op=ALU.mult),
            [P["fim"].r(), P["tc"].r()], [P["fim"].r()])
        iota = self.cs("iota")
        nbk = [5, 6, 7]
        for c in range(KC):
            g0 = c * 8
            for (t_, src) in ((Bp1, "s5_b1"), (Bp2, "s5_b2"), (Cp1, "s5_c1"), (Cp2, "s5_c2")):
                k.dma("sp", t_[:, :, :], self.I[src][j][:, g0:g0 + 8, :], W=[t_.r()])
            if ns:
                k.dma("sp", hin[:, :, 0:16], self.I["st_s5"][j][:, g0:g0 + 8, :], W=[hin.r()])
            if ps == 0:
                dve(lambda e: e.memset(hin[:, :, 16:17], 0.0), [], [hin.r()], acc=True)
            else:
                k.dma("sp", hin[:, :, 16:17], self.o_s5.t[j][:, g0:g0 + 8, 16:17], R=[self.o_s5.r()], W=[hin.r()],
                      allow_slow_non_contiguous=True)
            F1b = P["fre"][:, g0:g0 + 8].unsqueeze(2).broadcast_to([128, 8, 16])
            F2b = P["fim"][:, g0:g0 + 8].unsqueeze(2).broadcast_to([128, 8, 16])
            RP = [P["fre"].r(), P["fim"].r()]
            dve(lambda e: e.tensor_tensor(out=tA[:, :, :], in0=Bp1[:, :, :], in1=F1b, op=ALU.mult), RP + [Bp1.r()], [tA.r()])
            dve(lambda e: e.tensor_tensor(out=tB[:, :, :], in0=Bp2[:, :, :], in1=F2b, op=ALU.mult), RP + [Bp2.r()], [tB.r()])
            dve(lambda e: e.scalar_tensor_tensor(out=Bb1[:, :, :], in0=tB[:, :, :], scalar=Sg(0), in1=tA[:, :, :],
                                                 op0=ALU.mult, op1=ALU.add), [tA.r(), tB.r(), cst.r()], [Bb1.r()])
            dve(lambda e: e.tensor_tensor(out=tA[:, :, :], in0=Bp2[:, :, :], in1=F1b, op=ALU.mult), RP + [Bp2.r()], [tA.r()])
            dve(lambda e: e.tensor_tensor(out=tB[:, :, :], in0=Bp1[:, :, :], in1=F2b, op=ALU.mult), RP + [Bp1.r()], [tB.r()])
            dve(lambda e: e.scalar_tensor_tensor(out=Bb2[:, :, :], in0=tA[:, :, :], scalar=Sg(1), in1=tB[:, :, :],
                                                 op0=ALU.mult, op1=ALU.add), [tA.r(), tB.r(), cst.r()], [Bb2.r()])
            for (Bb, Tc) in ((Bb1, T1c), (Bb2, T2c)):
                bk = self.bank[4]
                k.op("pe", lambda e: e.transpose(bk[:, 0:128], Bb[:, :, :].rearrange("p a b -> p (a b)"), ident),
                     R=[Bb.r(), cst.r()], W=[bk.r()])
                k.op("act", lambda e: e.copy(out=Tc[:, :], in_=bk[:, 0:128]), R=[bk.r()], W=[Tc.r()])
            for g8 in range(8):
                g = g0 + g8
                r2 = self.rot("s5g", 2)
                lb1, lb2, c1p, c2p = LB1[r2], LB2[r2], C1p[r2], C2p[r2]
                rm = cst[:, CST["rowm"][0] + g8:CST["rowm"][0] + g8 + 1]
                dve(lambda e: e.tensor_scalar(out=lb1[:, :], in0=T1c[:, :], scalar1=rm, scalar2=None, op0=ALU.mult),
                    [T1c.r(), cst.r()], [lb1.r()])
                dve(lambda e: e.tensor_scalar(out=lb2[:, :], in0=T2c[:, :], scalar1=rm, scalar2=None, op0=ALU.mult),
                    [T2c.r(), cst.r()], [lb2.r()])
                k.op("pool", lambda e: e.memset(c1p[:, :], 0.0), W=[c1p.r()])
                k.op("pool", lambda e: e.memset(c2p[:, :], 0.0), W=[c2p.r()])
                dve(lambda e: e.tensor_scalar(out=c1p[:, g8 * 16:(g8 + 1) * 16], in0=Cp1[:, g8, :], scalar1=Sg(1),
                                              scalar2=None, op0=ALU.mult), [Cp1.r(), cst.r()], [c1p.r()])
                dve(lambda e: e.tensor_scalar(out=c2p[:, g8 * 16:(g8 + 1) * 16], in0=Cp2[:, g8, :], scalar1=-1.0,
                                              scalar2=None, op0=ALU.mult), [Cp2.r()], [c2p.r()])
                th = P["tht"][:, g:g + 1]
                dve(lambda e: e.tensor_scalar(out=targ[:, :], in0=iota, scalar1=th, scalar2=None, op0=ALU.mult),
                    [cst.r(), P["tht"].r()], [targ.r()])
                sincos(St, Ct, targ[:, :], 1024, [targ.r()])
                rho = P["rho"][:, g:g + 1]
                dve(lambda e: e.tensor_scalar(out=rhof[:, 0:1024], in0=iota, scalar1=0.0, scalar2=rho,
                                              op0=ALU.mult, op1=ALU.add), [cst.r(), P["rho"].r()], [rhof.r()])
                if ns:
                    dve(lambda e: e.tensor_scalar(out=rhof[:, 1024:1152], in0=self.cs("mask128"), scalar1=rho,
                                                  scalar2=None, op0=ALU.mult), [cst.r(), P["rho"].r()], [rhof.r()], acc=True)
                for (t0, tw) in self.tbs:
                    i = self.rot("gu", 2)
                    b1, b2 = self.bank[i], self.bank[2 + i]
                    k.op("pe", lambda e: e.matmul(b1[:, 0:tw], lb1[:, :], xn[:, c, t0:t0 + tw], start=True, stop=True),
                         R=[lb1.r(), xn.r()], W=[b1.r()])
                    k.op("pe", lambda e: e.matmul(b2[:, 0:tw], lb2[:, :], xn[:, c, t0:t0 + tw], start=True, stop=True),
                         R=[lb2.r(), xn.r()], W=[b2.r()])
                    if t0 < 1024:
                        cv, sv = Ct[:, t0:t0 + tw], St[:, t0:t0 + tw]
                        z1, z2, mo, yo = b1[:, 0:tw], b2[:, 0:tw], m[:, t0:t0 + tw], yf[:, t0:t0 + tw]
                    else:
                        cv = Ct[:, 0:8].unsqueeze(1).broadcast_to([128, 16, 8])
                        sv = St[:, 0:8].unsqueeze(1).broadcast_to([128, 16, 8])
                        r3 = lambda a: a.rearrange("p (b t) -> p b t", b=16)
                        z1, z2, mo, yo = r3(b1[:, 0:128]), r3(b2[:, 0:128]), r3(m[:, 1024:1152]), r3(yf[:, 1024:1152])
                    dve(lambda e: e.tensor_tensor(out=mo, in0=z1, in1=cv, op=ALU.mult), [b1.r(), Ct.r()], [m.r()], acc=True)
                    dve(lambda e: e.tensor_tensor(out=yo, in0=z2, in1=sv, op=ALU.mult), [b2.r(), St.r()], [yf.r()], acc=True)
                    dve(lambda e: e.tensor_tensor(out=mo, in0=mo, in1=yo, op=ALU.add), [m.r(), yf.r()], [m.r()], acc=True)
                if ns:
                    ms0 = m[:, 1024:1152].rearrange("p (b t) -> p b t", b=16)[:, :, 0]
                    dve(lambda e: e.scalar_tensor_tensor(out=ms0, in0=hin[:, g8, 0:16], scalar=rho, in1=ms0,
                                                         op0=ALU.mult, op1=ALU.add),
                        [hin.r(), m.r(), P["rho"].r()], [m.r()], acc=True)
                dve(lambda e: e.tensor_tensor_scan(out=G[:, 0:TB], data0=rhof[:, 0:TB], data1=m[:, 0:TB],
                                                   initial=hin[:, g8, 16:17], op0=ALU.mult, op1=ALU.add),
                    [rhof.r(), m.r(), hin.r()], [G.r()])
                for (Gb, tab) in ((G1b, Ct), (G2b, St)):
                    k.op("pool", lambda e: e.tensor_tensor(out=Gb[:, 0:1024], in0=G[:, 0:1024], in1=tab[:, 0:1024],
                                                            op=ALU.mult), R=[G.r(), tab.r()], W=[Gb.r()])
                    if ns:
                        r3 = lambda a: a.rearrange("p (b t) -> p b t", b=16)
                        k.op("pool", lambda e: e.tensor_tensor(
                            out=r3(Gb[:, 1024:1152]), in0=r3(G[:, 1024:1152]),
                            in1=tab[:, 0:8].unsqueeze(1).broadcast_to([128, 16, 8]), op=ALU.mult),
                            R=[G.r(), tab.r()], W=[Gb.r()], acc=True)
                for i, (t0, tw) in enumerate(self.tbs):
                    by = self.bank[nbk[i]]
                    k.op("pe", lambda e: e.matmul(by[:, 0:tw], c1p[:, :], G1b[:, t0:t0 + tw], start=(g8 == 0), stop=False),
                         R=[c1p.r(), G1b.r()], W=[by.r()])
                    k.op("pe", lambda e: e.matmul(by[:, 0:tw], c2p[:, :], G2b[:, t0:t0 + tw], start=False, stop=(g8 == 7)),
                         R=[c2p.r(), G2b.r()], W=[by.r()])
                ncol = 17 if ns else 1
                cols = []
                if ns:
                    Gs7 = G[:, 1024:1152].rearrange("p (b t) -> p b t", b=16)[:, :, 7]
                    dve(lambda e: e.tensor_scalar(out=ftmp[:, 0, 0:16], in0=Gs7, scalar1=Ct[:, 7:8], scalar2=None,
                                                  op0=ALU.mult), [G.r(), Ct.r()], [ftmp.r()], acc=True)
                    dve(lambda e: e.tensor_scalar(out=ftmp[:, 1, 0:16], in0=Gs7, scalar1=St[:, 7:8], scalar2=None,
                                                  op0=ALU.mult), [G.r(), St.r()], [ftmp.r()], acc=True)
                dve(lambda e: e.tensor_tensor(out=ftmp[:, 0, 16:17], in0=G[:, 1023:1024], in1=Ct[:, 1023:1024],
                                              op=ALU.mult), [G.r(), Ct.r()], [ftmp.r()], acc=True)
                dve(lambda e: e.tensor_tensor(out=ftmp[:, 1, 16:17], in0=G[:, 1023:1024], in1=St[:, 1023:1024],
                                              op=ALU.mult), [G.r(), St.r()], [ftmp.r()], acc=True)
                c_lo = 0 if ns else 16
                bk = self.bank[4]
                k.op("pe", lambda e: e.matmul(bk[:, c_lo:17], ident, ftmp[:, 0, c_lo:17], start=True, stop=False),
                     R=[cst.r(), ftmp.r()], W=[bk.r()])
                k.op("pe", lambda e: e.matmul(bk[:, c_lo:17], self.cs("J"), ftmp[:, 1, c_lo:17], start=False, stop=True),
                     R=[cst.r(), ftmp.r()], W=[bk.r()])
                k.op("act", lambda e: e.copy(out=hfin[:, g8, c_lo:17], in_=bk[:, c_lo:17]), R=[bk.r()], W=[hfin.r()], acc=True)
            k.dma("sp", self.o_s5.t[j][:, g0:g0 + 8, c_lo:17], hfin[:, :, c_lo:17], R=[hfin.r()], W=[self.o_s5.r()],
                  allow_slow_non_contiguous=True)
            for i, (t0, tw) in enumerate(self.tbs):
                by = self.bank[nbk[i]]
                dve(lambda e: e.scalar_tensor_tensor(out=yf[:, t0:t0 + tw], in0=xn[:, c, t0:t0 + tw],
                                                     scalar=self.vcol("d_c%d" % j, c), in1=by[:, 0:tw],
                                                     op0=ALU.mult, op1=ALU.add),
                    [xn.r(), by.r(), vec.r()], [yf.r()])
            zt = Tile(xn[:, c, :], "xnc")
            zt.reg = xn.r()
            self.gelu_(zt, yf, gt, TB)
        Wg = self.W["w_glu_c"][j]
        for s in range(D // 256):
            sg, vg = self.load_slab(Wg, KC, s * 256, 256)
            for oc2 in range(2):
                oc = s * 2 + oc2
                for (t0, tw) in self.tbs:
                    bg = self.bank[self.rot("pj", 2)]
                    for kc in range(KC):
                        k.op("pe", lambda e: e.matmul(bg[:, 0:tw], vg[:, kc, oc2 * 128:(oc2 + 1) * 128],
                                                       xn[:, kc, t0:t0 + tw], start=(kc == 0), stop=(kc == KC - 1)),
                             R=[sg.r(), xn.r()], W=[bg.r()])
                    t1 = yf
                    k.op("act", lambda e: e.activation(out=t1[:, 0:tw], in_=bg[:, 0:tw], func=AF.Sigmoid,
                                                       bias=self.vcol("b_glu_c%d" % j, oc)),
                         R=[bg.r(), vec.r()], W=[t1.r()])
                    dve(lambda e: e.tensor_tensor(out=t1[:, 0:tw], in0=t1[:, 0:tw], in1=xn[:, oc, t0:t0 + tw], op=ALU.mult),
                        [t1.r(), xn.r()], [t1.r()])
                    dve(lambda e: e.tensor_tensor(out=x[:, oc, t0:t0 + tw], in0=x[:, oc, t0:t0 + tw], in1=t1[:, 0:tw],
                                                  op=ALU.add), [t1.r(), x.r(oc)], [x.r(oc)])


def build_inputs(inp, core):
    s = core % 4
    xs = np.asarray(inp["x_sample"], np.float32)[core * NS:(core + 1) * NS].reshape(NS * TS, D)
    xp = np.asarray(inp["x_prompt"], np.float32)[s]
    xT = np.ascontiguousarray(np.concatenate([xp, xs], 0).T)
    pp = np.asarray(inp["p_prompt"], np.float32)[:, s]
    psm = np.asarray(inp["p_sample"], np.float32)[:, core * NS:(core + 1) * NS].reshape(4, NS * TS, 256)
    pT = np.ascontiguousarray(np.concatenate([pp, psm], 1).transpose(0, 2, 1))
    return {"xT": xT, "pT": pT}


_PROG = {}


def kernel(**inputs):
    enable = inputs.pop("_enable", ("ffn", "ple", "mix"))
    if enable not in _PROG:
        _PROG[enable] = Prog(enable)
    prog = _PROG[enable]
    vec = pack_vec(inputs)
    shared = {"vec": vec, "cst": pack_cst()}
    for nm in prog.W:
        shared[nm] = np.ascontiguousarray(np.asarray(inputs[nm], np.float32))
    in_maps = []
    ncores = int(os.environ.get('KCORES', '8'))
    for c in range(ncores):
        m = dict(shared)
        m.update(build_inputs(inputs, c))
        m.update(pack_states(inputs, c))
        in_maps.append(m)
    res = run_bass_kernel_spmd(prog.nc, in_maps, core_ids=list(range(ncores)))
    R = list(res.results) + [res.results[0]] * (8 - ncores)
    global _LAST
    _LAST = R
    y_prompt = np.stack([R[s]["yT"][:, 0:2048].T for s in range(4)])
    y_sample = np.concatenate([R[c]["yT"][:, 2048:].T.reshape(NS, TS, D) for c in range(8)], 0)
    A = np.ascontiguousarray

    def per_layer(fn):
        return np.stack([fn(j) for j in range(2)])
    pw = per_layer(lambda j: np.stack([R[s]["o_wkv"][j][:, 16, :].reshape(16, 64, 64) for s in range(4)]))
    psh = per_layer(lambda j: np.stack([R[s]["o_shift"][j][:, :, 16].T.reshape(-1)[:3360] for s in range(4)]))
    ph = per_layer(lambda j: np.stack([R[s]["o_h"][j][:, :, 16].T.reshape(1024) for s in range(4)]))
    pc = per_layer(lambda j: np.stack([R[s]["o_conv"][j][:, :, 16, :].transpose(2, 1, 0).reshape(3, 1024)
                                       for s in range(4)]))
    pre = per_layer(lambda j: np.stack([R[s]["o_s5"][j][0:64, :, 16].T for s in range(4)]))
    pim = per_layer(lambda j: np.stack([R[s]["o_s5"][j][64:128, :, 16].T for s in range(4)]))
    sw = per_layer(lambda j: np.concatenate([
        R[c]["o_wkv"][j][:, 0:16, :].reshape(16, 8, 16, 8, 64).transpose(2, 0, 1, 3, 4).reshape(16, 16, 64, 64)
        for c in range(8)], 0))
    ssh = per_layer(lambda j: np.concatenate([
        R[c]["o_shift"][j][:, :, 0:16].transpose(2, 1, 0).reshape(16, -1)[:, :3360] for c in range(8)], 0))
    sh = per_layer(lambda j: np.concatenate([
        R[c]["o_h"][j][:, :, 0:16].transpose(2, 1, 0).reshape(16, 1024) for c in range(8)], 0))
    sc = per_layer(lambda j: np.concatenate([
        R[c]["o_conv"][j][:, :, 0:16, :].transpose(2, 3, 1, 0).reshape(16, 3, 1024) for c in range(8)], 0))
    sre = per_layer(lambda j: np.concatenate([R[c]["o_s5"][j][0:64, :, 0:16].transpose(2, 1, 0) for c in range(8)], 0))
    sim = per_layer(lambda j: np.concatenate([R[c]["o_s5"][j][64:128, :, 0:16].transpose(2, 1, 0) for c in range(8)], 0))
    outs = (y_prompt, y_sample, pw, psh, ph, pc, pre, pim, sw, ssh, sh, sc, sre, sim)
    return tuple(A(o.astype(np.float32)) for o in outs)
```

```python
import contextlib
import numpy as np
import concourse.bass as bass
import concourse.mybir as mybir
from concourse.bass_utils import run_bass_kernel_spmd

F32 = mybir.dt.float32
BF16 = mybir.dt.bfloat16
I32 = mybir.dt.int32
ALU = mybir.AluOpType
AF = mybir.ActivationFunctionType
AX = mybir.AxisListType
import os as _os
EPOCH_N = int(_os.environ.get("KEPOCH", "30000"))
SAME_ENGINE_WAITS = int(_os.environ.get("KSEW", "1"))


class Reg:
    __slots__ = ("name", "writes", "reads", "dsem", "dcnt")

    def __init__(self, name):
        self.name = name
        self.writes = []
        self.reads = []
        self.dsem = None
        self.dcnt = 0


class Tile:
    def __init__(self, t, name):
        self.t = t
        self.name = name
        self.reg = Reg(name)
        self.subs = {}

    def __getitem__(self, idx):
        return self.t[idx]

    def r(self, key=None):
        if key is None:
            return self.reg
        if key not in self.subs:
            self.subs[key] = Reg("%s/%s" % (self.name, key))
        return self.subs[key]

    def all(self):
        return [self.reg] + list(self.subs.values())


class K:
    def __init__(self):
        self.nc = bass.Bass("TRN2", target_bir_lowering=False)
        nc = self.nc
        self.es = contextlib.ExitStack()
        self.eng = {"pe": nc.tensor, "act": nc.scalar, "dve": nc.vector,
                    "pool": nc.gpsimd, "sp": nc.sync}
        self.sem = {}
        self.cnt = {}
        self.epoch = {}
        for e in ("pe", "act", "dve", "pool"):
            self.sem[(e, 0)] = self.es.enter_context(nc.semaphore("s_" + e))
            self.cnt[e] = 0
            self.epoch[e] = 0
        self.waited = {e: {} for e in self.eng}
        self.sew = True
        self.nsem = 4
        self.ninst = 0
        self.dregs = []

    def sb(self, name, shape, dt=F32):
        t = self.es.enter_context(self.nc.sbuf_tensor(name, list(shape), dt))
        return Tile(t, name)

    def ps(self, name, shape, dt=F32):
        t = self.es.enter_context(self.nc.psum_tensor(name, list(shape), dt))
        return Tile(t, name)

    def dram(self, name, shape, dt=F32, kind="Internal"):
        t = self.nc.dram_tensor(name, list(shape), dt, kind=kind)
        return Tile(t.ap(), name)

    def _need(self, e, ev):
        key, val, src = ev
        if src == e and (e == "pe" or not SAME_ENGINE_WAITS or not self.sew):
            return
        w = self.waited[e]
        if w.get(key, 0) >= val:
            return
        sem = self.sem[key] if isinstance(key, tuple) else key.dsem
        self.eng[e].wait_ge(sem, val)
        w[key] = val
        self.ninst += 1

    def _deps(self, e, R, W, acc=False, dma_fill=False):
        for r in R:
            for ev in r.writes:
                self._need(e, ev)
        for r in W:
            for ev in r.writes:
                if dma_fill and ev[2] == "dma":
                    continue
                if ev[2] != e:
                    self._need(e, ev)
            for ev in r.reads:
                if ev[2] != e:
                    self._need(e, ev)

    def _mark(self, ev, R, W, acc=False):
        for r in R:
            r.reads = [x for x in r.reads if x[0] != ev[0]] + [ev]
        for r in W:
            if acc:
                r.writes = [x for x in r.writes if x[0] != ev[0]] + [ev]
            else:
                r.writes = [ev]
            r.reads = []

    def op(self, e, fn, R=(), W=(), acc=False, sew=True):
        self.sew = sew
        self._deps(e, R, W)
        self.sew = True
        ins = fn(self.eng[e])
        if self.cnt[e] >= EPOCH_N:
            self.epoch[e] += 1
            self.cnt[e] = 0
            self.sem[(e, self.epoch[e])] = self.es.enter_context(
                self.nc.semaphore("s_%s_%d" % (e, self.epoch[e])))
        self.cnt[e] += 1
        key = (e, self.epoch[e])
        ins.then_inc(self.sem[key], 1)
        ev = (key, self.cnt[e], e)
        self._mark(ev, R, W, acc)
        self.ninst += 1
        return ins

    def dma(self, q, out, in_, R=(), W=(), acc=True, **kw):
        self._deps(q, R, W, dma_fill=acc)
        d = W[0]
        if d.dsem is None:
            d.dsem = self.es.enter_context(self.nc.semaphore("d%d" % self.nsem))
            self.nsem += 1
            self.dregs.append(d)
        ins = self.eng[q].dma_start(out=out, in_=in_, **kw)
        d.dcnt += 16
        ins.then_inc(d.dsem, 16)
        ev = (d, d.dcnt, "dma")
        self._mark(ev, R, W, acc)
        self.ninst += 1
        return ins

    def barrier(self):
        evs = [((e, self.epoch[e]), self.cnt[e], e) for e in ("pe", "act", "dve", "pool")]
        evs += [(d, d.dcnt, "dma") for d in self.dregs]
        for e in ("pe", "act", "dve", "pool", "sp"):
            for ev in evs:
                if ev[1] > 0 and not (ev[2] == e and e == "pe"):
                    w = self.waited[e]
                    if w.get(ev[0], 0) < ev[1]:
                        sem = self.sem[ev[0]] if isinstance(ev[0], tuple) else ev[0].dsem
                        self.eng[e].wait_ge(sem, ev[1])
                        w[ev[0]] = ev[1]
                        self.ninst += 1

    def finish(self, regs):
        for r in regs:
            for ev in r.writes:
                self._need("sp", ev)

    def close(self):
        self.es.close()

D = 2048
DFF = 5632
KC = 16
NPT = 1024
NS = 16
TS = 8
NTOK = 2048 + NS * TS
EPS = 1e-6
import os
DEPTH = int(os.environ.get('KDEPTH', '4'))
NPASS = int(os.environ.get('KNPASS', '2'))


def vec_layout():
    lay = {}
    off = [0]

    def add(name, n):
        lay[name] = (off[0], n)
        off[0] += n
    for l in range(4):
        for nm in ("norm_ffn1", "norm_mix", "norm_ffn2", "norm_ple"):
            add("%s%d" % (nm, l), 16)
    add("final_norm", 16)
    for j in range(2):
        add("d_c%d" % j, 16)
        add("b_glu_c%d" % j, 16)
    for j in range(2):
        for nm in ("w0_a", "a0_a", "kk_a", "ka_a", "rk_a", "lnx_g", "lnx_b", "conv_b_b",
                   "ba_b", "bx_b", "lam_b", "mu_r", "mu_k", "mu_v"):
            add("%s%d" % (nm, j), 8)
        for q in range(4):
            add("conv_w%d_%d" % (q, j), 8)
        add("mu_x%d" % j, 4)
    return lay, off[0]


def pack_vec(inp):
    lay, n = vec_layout()
    v = np.zeros((128, n), np.float32)

    def put(name, arr):
        o, c = lay[name]
        a = np.asarray(arr, np.float32).reshape(-1)
        v[:, o:o + c] = a.reshape(c, 128).T
    for l in range(4):
        for nm in ("norm_ffn1", "norm_mix", "norm_ffn2", "norm_ple"):
            put("%s%d" % (nm, l), inp[nm][l])
    put("final_norm", inp["final_norm"])
    for j in range(2):
        put("d_c%d" % j, inp["d_c"][j])
        put("b_glu_c%d" % j, inp["b_glu_c"][j])
        for nm in ("w0_a", "a0_a", "kk_a", "ka_a", "rk_a", "lnx_g", "lnx_b", "conv_b_b",
                   "ba_b", "bx_b", "lam_b"):
            put("%s%d" % (nm, j), inp[nm][j])
        mu = np.asarray(inp["mu_a"][j], np.float32)
        put("mu_r%d" % j, mu[0:1024])
        put("mu_k%d" % j, mu[1024:2048])
        put("mu_v%d" % j, mu[2048:3072])
        for q in range(4):
            put("conv_w%d_%d" % (q, j), inp["conv_w_b"][j][q])
        o, _ = lay["mu_x%d" % j]
        v[:, o] = mu[3072:3200]
        v[:, o + 2] = mu[3200:3328]
        v[0:32, o + 3] = mu[3328:3360]
    return v


CST = {"ident": (0, 128), "iota": (128, 1024), "mask128": (1152, 128), "J": (1280, 128),
       "blk": (1408, 128), "sgn": (1536, 4), "rep": (1540, 128), "rowm": (1668, 8), "one": (1676, 1), "negpi": (1677, 1), "halfpi": (1678, 1)}
NCST = 1680
GN_EPS = 64e-5
W_A = 1024
COLS_A = 3360


def pack_cst():
    c = np.zeros((128, NCST), np.float32)

    def put(nm, arr):
        o, n = CST[nm]
        c[:arr.shape[0], o:o + n] = arr
    put("ident", np.eye(128, dtype=np.float32))
    put("iota", np.broadcast_to(np.arange(1, 1025, dtype=np.float32)[None, :], (128, 1024)))
    m = np.ones(128, np.float32)
    m[0::8] = 0.0
    put("mask128", np.broadcast_to(m[None, :], (128, 128)))
    J = np.zeros((128, 128), np.float32)
    for p in range(64):
        J[64 + p, p] = -1.0
        J[p, 64 + p] = 1.0
    put("J", J)
    blk = np.zeros((128, 128), np.float32)
    blk[0:64, 0:64] = 1.0
    blk[64:128, 64:128] = 1.0
    put("blk", blk)
    sg = np.zeros((128, 4), np.float32)
    sg[0:64, 0] = -1.0
    sg[64:, 0] = 1.0
    sg[0:64, 1] = 1.0
    sg[64:, 1] = -1.0
    sg[:, 2] = -1.0
    sg[:, 3] = 1.0
    put("sgn", sg)
    rep = np.zeros((16, 128), np.float32)
    for hh in range(16):
        rep[hh, hh * 8:hh * 8 + 8] = 1.0
    put("rep", rep)
    rm = np.zeros((128, 8), np.float32)
    for p in range(128):
        rm[p, p // 16] = 1.0
    put("rowm", rm)
    put("one", np.ones((128, 1), np.float32))
    put("negpi", np.full((128, 1), -np.pi, np.float32))
    put("halfpi", np.full((128, 1), 0.5 * np.pi, np.float32))
    return c


def pack_states(inp, core):
    f = lambda a: np.ascontiguousarray(np.asarray(a, np.float32))
    sl = slice(core * NS, (core + 1) * NS)
    o = {}
    o["st_h"] = f(np.asarray(inp["state_b_h"])[:, sl].reshape(2, NS, 8, 128).transpose(0, 3, 2, 1))
    o["st_conv"] = f(np.asarray(inp["state_b_conv"])[:, sl].reshape(2, NS, 3, 8, 128).transpose(0, 4, 3, 1, 2))
    re = np.asarray(inp["state_c_re"])[:, sl].transpose(0, 3, 2, 1)
    im = np.asarray(inp["state_c_im"])[:, sl].transpose(0, 3, 2, 1)
    o["st_s5"] = f(np.concatenate([re, im], 1))
    sh = np.asarray(inp["state_a_shift"], np.float32)[:, sl]
    shp = np.zeros((2, NS, 27 * 128), np.float32)
    shp[:, :, 0:3360] = sh
    o["st_shift"] = f(shp.reshape(2, NS, 27, 128).transpose(0, 3, 2, 1))
    wk = np.asarray(inp["state_a_wkv"], np.float32)[:, sl].reshape(2, NS, 16, 8, 8, 64)
    o["st_wkv"] = f(wk.transpose(0, 2, 3, 1, 4, 5).reshape(2, 128, NS, 512))
    for nm in ("w2_a", "a2_a", "g2_a"):
        o[nm] = f(inp[nm])
    for nm in ("wa_b", "wx_b"):
        w = np.asarray(inp[nm], np.float32)
        bd = np.zeros((2, 8, 128, 128), np.float32)
        for c in range(8):
            bd[:, c, 0:64, 0:64] = w[:, 2 * c]
            bd[:, c, 64:128, 64:128] = w[:, 2 * c + 1]
        o[nm + "d"] = f(bd.transpose(0, 2, 1, 3))
    are = np.asarray(inp["a_re_c"], np.float32).transpose(0, 2, 1)
    aim = np.asarray(inp["a_im_c"], np.float32).transpose(0, 2, 1)
    o["s5_are"] = f(np.concatenate([are, are], 1))
    o["s5_aim"] = f(np.concatenate([aim, aim], 1))
    o["s5_ldt"] = f(np.broadcast_to(np.asarray(inp["log_dt_c"], np.float32)[:, None, :], (2, 128, 128)))
    bre = np.asarray(inp["b_re_c"], np.float32).transpose(0, 2, 1, 3)
    bim = np.asarray(inp["b_im_c"], np.float32).transpose(0, 2, 1, 3)
    o["s5_b1"] = f(np.concatenate([bre, bim], 1))
    o["s5_b2"] = f(np.concatenate([bim, bre], 1))
    cre = np.asarray(inp["c_re_c"], np.float32).transpose(0, 3, 1, 2)
    cim = np.asarray(inp["c_im_c"], np.float32).transpose(0, 3, 1, 2)
    o["s5_c1"] = f(np.concatenate([cre, cim], 1))
    o["s5_c2"] = f(np.concatenate([cim, cre], 1))
    return o


class Prog:
    def __init__(self, enable=("ffn", "ple", "mix")):
        self.enable = enable
        k = self.k = K()
        nc = self.nc = k.nc
        self.lay, self.nv = vec_layout()

        def din(name, shape):
            return nc.dram_tensor(name, list(shape), F32, kind="ExternalInput").ap()
        self.xT = din("xT", [D, NTOK])
        self.pT = din("pT", [4, 256, NTOK])
        self.vecd = din("vec", [128, self.nv])
        self.W = {}
        for nm, shp in (("ffn1_wg", [4, D, DFF]), ("ffn1_wu", [4, D, DFF]), ("ffn1_wd", [4, DFF, D]),
                        ("ffn2_wg", [4, D, DFF]), ("ffn2_wu", [4, D, DFF]), ("ffn2_wd", [4, DFF, D]),
                        ("ple_gate", [4, D, D]), ("ple_proj", [4, 256, D])):
            self.W[nm] = din(nm, shp)
        for nm, shp in (("w_in_ab", [2, D, 5408]), ("w_out_ab", [2, D, D]), ("w_glu_c", [2, D, D])):
            self.W[nm] = din(nm, shp)
        self.cstd = din("cst", [128, NCST])
        self.I = {}
        for nm, shp in (("st_h", [2, 128, 8, 16]), ("st_conv", [2, 128, 8, 16, 3]), ("st_s5", [2, 128, 128, 16]),
                        ("wa_bd", [2, 128, 8, 128]), ("wx_bd", [2, 128, 8, 128]),
                        ("s5_are", [2, 128, 128]), ("s5_aim", [2, 128, 128]), ("s5_ldt", [2, 128, 128]),
                        ("s5_b1", [2, 128, 128, 16]), ("s5_b2", [2, 128, 128, 16]),
                        ("s5_c1", [2, 128, 128, 16]), ("s5_c2", [2, 128, 128, 16])):
            self.I[nm] = din(nm, shp)
        for nm, shp in (("w2_a", [2, 64, 1024]), ("a2_a", [2, 64, 1024]), ("g2_a", [2, 160, 1024]),
                        ("st_shift", [2, 128, 27, 16]), ("st_wkv", [2, 128, 16, 512])):
            self.I[nm] = din(nm, shp)
        self.o_shift = k.dram("o_shift", [2, 128, 27, 17], F32, kind="ExternalOutput")
        self.o_wkv = k.dram("o_wkv", [2, 128, 17, 512], F32, kind="ExternalOutput")
        self.Zs = k.dram("Zs", [1152, 16, 5, 64], F32)
        self.Vs = k.dram("Vs", [1152, 1024], F32)
        self.Ys = k.dram("Ys", [1152, 1024], F32)
        self.Bon = k.dram("Bon", [8, 128, 1152], F32)
        self.yT = k.dram("yT", [D, NTOK], F32, kind="ExternalOutput")
        self.o_h = k.dram("o_h", [2, 128, 8, 17], F32, kind="ExternalOutput")
        self.o_conv = k.dram("o_conv", [2, 128, 8, 17, 3], F32, kind="ExternalOutput")
        self.o_s5 = k.dram("o_s5", [2, 128, 128, 17], F32, kind="ExternalOutput")
        self.outs = [self.yT, self.o_h, self.o_conv, self.o_s5, self.o_shift, self.o_wkv]

        self.x = k.sb("x", [128, KC, 1152], F32)
        self.xn = k.sb("xn", [128, KC, 1152], BF16)
        self.rstd = k.sb("rstd", [128, 1152], F32)
        self.sq = [k.sb("sq%d" % i, [128, 1152], BF16) for i in range(2)]
        self.vec = k.sb("vecs", [128, self.nv], F32)
        self.ones_bf = k.sb("ones_bf", [128, 128], BF16)
        self.AW = 18688
        self._avc = {}
        self.A = k.sb("arena", [128, self.AW], F32)
        self.slab = [self.av("slab%d" % i, i * 2048, 2048, BF16) for i in range(6)]
        self.h = self.av("hbuf", 12288, 1152, BF16, a=2)
        self.t1 = [self.av("t1_%d" % i, 13440 + i * 512, 512) for i in range(2)]
        self.t2 = [self.av("t2_%d" % i, 14464 + i * 512, 512) for i in range(2)]
        self.pt = self.av("ptile", 15488, 1152, BF16, a=2)
        self.pproj = self.av("pproj", 16640, 2048, BF16)
        self.ytmp = [self.av("ytmp%d" % i, i * 1152, 1152) for i in range(2)]
        self.nslab = 6
        self.bank = [k.ps("bank%d" % i, [128, 512], F32) for i in range(8)]
        self.lru_small = (k.sb("l_hst", [128, 8, 17], F32), k.sb("l_cvo", [128, 8, 17, 3], F32),
                          k.sb("l_h0", [128, 8, 17], F32), k.sb("l_cv0", [128, 8, 17, 3], F32))
        self.lru_nsp = k.sb("l_nsp", [128, 16], F32)
        self.sg1b = k.sb("r_sg1b", [128, 1152], BF16)
        self.ybuf = self.av("ybuf", 4096, 4608, BF16, a=8)
        self.rr = {}

        k.dma("sp", self.vec[:], self.vecd, W=[self.vec.r()])
        self.cst = k.sb("cst_sb", [128, NCST], F32)
        k.dma("sp", self.cst[:], self.cstd, W=[self.cst.r()])
        k.op("dve", lambda e: e.memset(self.ones_bf[:], 1.0), W=[self.ones_bf.r()])
        self.epst = k.sb("epst", [128, 2], F32)
        k.op("dve", lambda e: e.memset(self.epst[:], EPS), W=[self.epst.r()])

        for ps in range(NPASS):
            self.run_pass(ps)
        k.finish([o.r() for o in self.outs])
        k.close()

    def av(self, name, off, n, dt=F32, a=None):
        assert off + n <= self.AW, (name, off, n)
        key = (name, off, n, str(dt), a)
        if key in self._avc:
            return self._avc[key]
        ap = self.A.t[:, off:off + n]
        if dt != F32:
            ap = ap.bitcast(dt)
        if a is not None:
            ap = ap.rearrange("p (a b) -> p a b", a=a)
        self._avc[key] = Tile(ap, name)
        return self._avc[key]

    def rot(self, name, n):
        i = self.rr.get(name, 0)
        self.rr[name] = i + 1
        return i % n

    def vcol(self, name, c):
        o, n = self.lay[name]
        return self.vec[:, o + c:o + c + 1]

    def run_pass(self, ps):
        k = self.k
        self.ps_ = ps
        self.TB = TB = 1024 if ps == 0 else 1152
        self.g0 = 0 if ps == 0 else 1024
        self.tbs = [(0, 512), (512, 512)] + ([(1024, 128)] if ps == 1 else [])
        x = self.x
        xTv = self.xT.rearrange("(c p) t -> p c t", p=128)
        k.dma("sp", x[:, :, 0:TB], xTv[:, :, self.g0:self.g0 + TB], W=[x.r(c) for c in range(KC)])
        for l in range(DEPTH):
            if "ffn" in self.enable:
                self.ffn(l, 1)
            if "mix" in self.enable:
                self.mixer(l)
            if "ffn" in self.enable:
                self.ffn(l, 2)
            if "ple" in self.enable:
                self.ple(l)
        k.barrier()
        self.norm("final_norm", final=True)
        k.barrier()

    def norm(self, gname, final=False):
        k, x, xn, TB = self.k, self.x, self.xn, self.TB
        nb = [5, 6, 7]
        for c in range(KC):
            s = self.sq[c % 2]
            k.op("act", lambda e: e.activation(out=s[:, 0:TB], in_=x[:, c, 0:TB], func=AF.Square),
                 R=[x.r(c)], W=[s.r()])
            for i, (t0, tw) in enumerate(self.tbs):
                b = self.bank[nb[i]]
                k.op("pe", lambda e: e.matmul(b[:, 0:tw], self.ones_bf[:], s[:, t0:t0 + tw],
                                               start=(c == 0), stop=(c == KC - 1)),
                     R=[self.ones_bf.r(), s.r()], W=[b.r()])
        for i, (t0, tw) in enumerate(self.tbs):
            b = self.bank[nb[i]]
            k.op("act", lambda e: e.activation(out=self.rstd[:, t0:t0 + tw], in_=b[:, 0:tw], func=AF.Sqrt,
                                               bias=self.epst[:, 0:1], scale=1.0 / D),
                 R=[b.r(), self.epst.r()], W=[self.rstd.r()])
        k.op("dve", lambda e: e.reciprocal(out=self.rstd[:, 0:TB], in_=self.rstd[:, 0:TB]),
             R=[self.rstd.r()], W=[self.rstd.r()])
        yTv = self.yT.t.rearrange("(c p) t -> p c t", p=128)
        for c in range(KC):
            if final:
                o = self.ytmp[c % 2]
                k.op("dve", lambda e: e.scalar_tensor_tensor(out=o[:, 0:TB], in0=x[:, c, 0:TB],
                                                              scalar=self.vcol(gname, c), in1=self.rstd[:, 0:TB],
                                                              op0=ALU.mult, op1=ALU.mult),
                     R=[x.r(c), self.rstd.r(), self.vec.r()], W=[o.r()])
                k.dma("sp", yTv[:, c, self.g0:self.g0 + TB], o[:, 0:TB], R=[o.r()], W=[self.yT.r()])
            else:
                k.op("dve", lambda e: e.scalar_tensor_tensor(out=xn[:, c, 0:TB], in0=x[:, c, 0:TB],
                                                              scalar=self.vcol(gname, c), in1=self.rstd[:, 0:TB],
                                                              op0=ALU.mult, op1=ALU.mult),
                     R=[x.r(c), self.rstd.r(), self.vec.r()], W=[xn.r()])

    def load_slab(self, W2d, kchunks, c0, cw, rows0=0):
        k = self.k
        s = self.slab[self.rot("slab", self.nslab)]
        src = W2d[rows0:rows0 + kchunks * 128, :].rearrange("(kc p) f -> p kc f", p=128)[:, :, c0:c0 + cw]
        dst = s[:, 0:kchunks * cw].rearrange("p (kc f) -> p kc f", kc=kchunks)
        step = 4 if kchunks > 4 else kchunks
        for q in range(0, kchunks, step):
            k.dma("pool", dst[:, q:q + step, :], src[:, q:q + step, :], W=[s.r()])
        return s, dst

    def ffn(self, l, which):
        k, x, xn, h = self.k, self.x, self.xn, self.h
        self.norm("norm_ffn%d%d" % (which, l))
        Wg = self.W["ffn%d_wg" % which][l]
        Wu = self.W["ffn%d_wu" % which][l]
        Wd = self.W["ffn%d_wd" % which][l]
        for s in range(DFF // 256):
            sg, vg = self.load_slab(Wg, KC, s * 256, 256)
            su, vu = self.load_slab(Wu, KC, s * 256, 256)
            sd, vd = self.load_slab(Wd, 2, 0, D, rows0=s * 256)
            for fc in range(2):
                for (t0, tw) in self.tbs:
                    i = self.rot("gu", 2)
                    bg, bu = self.bank[i], self.bank[2 + i]
                    for kc in range(KC):
                        k.op("pe", lambda e: e.matmul(bg[:, 0:tw], vg[:, kc, fc * 128:(fc + 1) * 128],
                                                       xn[:, kc, t0:t0 + tw], start=(kc == 0), stop=(kc == KC - 1)),
                             R=[sg.r(), xn.r()], W=[bg.r()])
                    for kc in range(KC):
                        k.op("pe", lambda e: e.matmul(bu[:, 0:tw], vu[:, kc, fc * 128:(fc + 1) * 128],
                                                       xn[:, kc, t0:t0 + tw], start=(kc == 0), stop=(kc == KC - 1)),
                             R=[su.r(), xn.r()], W=[bu.r()])
                    t1 = self.t1[self.rot("t1", 2)]
                    k.op("act", lambda e: e.activation(out=t1[:, 0:tw], in_=bg[:, 0:tw], func=AF.Silu),
                         R=[bg.r()], W=[t1.r()])
                    k.op("dve", lambda e: e.tensor_tensor(out=h[:, fc, t0:t0 + tw], in0=t1[:, 0:tw],
                                                           in1=bu[:, 0:tw], op=ALU.mult),
                         R=[t1.r(), bu.r()], W=[h.r(fc)])
            for dc in range(KC):
                for (t0, tw) in self.tbs:
                    bd = self.bank[4 + self.rot("dn", 2)]
                    for fc in range(2):
                        k.op("pe", lambda e: e.matmul(bd[:, 0:tw], vd[:, fc, dc * 128:(dc + 1) * 128],
                                                       h[:, fc, t0:t0 + tw], start=(fc == 0), stop=(fc == 1)),
                             R=[sd.r(), h.r(fc)], W=[bd.r()])
                    k.op("dve", lambda e: e.scalar_tensor_tensor(out=x[:, dc, t0:t0 + tw], in0=bd[:, 0:tw],
                                                                  scalar=0.5, in1=x[:, dc, t0:t0 + tw],
                                                                  op0=ALU.mult, op1=ALU.add),
                         R=[bd.r(), x.r(dc)], W=[x.r(dc)])

    def ple(self, l):
        k, x, xn, TB = self.k, self.x, self.xn, self.TB
        self.norm("norm_ple%d" % l)
        pt = self.pt
        k.dma("pool", pt[:, :, 0:TB],
              self.pT[l].rearrange("(c p) t -> p c t", p=128)[:, :, self.g0:self.g0 + TB], W=[pt.r()])
        sp_ = self.pproj
        vp = sp_[:, :].rearrange("p (kc f) -> p kc f", kc=2)
        k.dma("pool", vp, self.W["ple_proj"][l].rearrange("(kc p) f -> p kc f", p=128), W=[sp_.r()])
        for s in range(D // 256):
            sg, vg = self.load_slab(self.W["ple_gate"][l], KC, s * 256, 256)
            for oc2 in range(2):
                oc = s * 2 + oc2
                for (t0, tw) in self.tbs:
                    i = self.rot("gu", 2)
                    bg, bu = self.bank[i], self.bank[2 + i]
                    for kc in range(KC):
                        k.op("pe", lambda e: e.matmul(bg[:, 0:tw], vg[:, kc, oc2 * 128:(oc2 + 1) * 128],
                                                       xn[:, kc, t0:t0 + tw], start=(kc == 0), stop=(kc == KC - 1)),
                             R=[sg.r(), xn.r()], W=[bg.r()])
                    for kc in range(2):
                        k.op("pe", lambda e: e.matmul(bu[:, 0:tw], vp[:, kc, oc * 128:(oc + 1) * 128],
                                                       pt[:, kc, t0:t0 + tw], start=(kc == 0), stop=(kc == 1)),
                             R=[sp_.r(), pt.r()], W=[bu.r()])
                    t1 = self.t1[self.rot("t1", 2)]
                    k.op("act", lambda e: e.activation(out=t1[:, 0:tw], in_=bg[:, 0:tw], func=AF.Sigmoid),
                         R=[bg.r()], W=[t1.r()])
                    t2 = self.t2[self.rot("t2", 2)]
                    k.op("dve", lambda e: e.tensor_tensor(out=t2[:, 0:tw], in0=t1[:, 0:tw], in1=bu[:, 0:tw],
                                                           op=ALU.mult),
                         R=[t1.r(), bu.r()], W=[t2.r()])
                    k.op("pool", lambda e: e.tensor_tensor(out=x[:, oc, t0:t0 + tw], in0=x[:, oc, t0:t0 + tw],
                                                            in1=t2[:, 0:tw], op=ALU.add),
                         R=[t2.r(), x.r(oc)], W=[x.r(oc)])

    def cs(self, name, rows=128):
        o, n = CST[name]
        return self.cst[0:rows, o:o + n]

    def mixer(self, l):
        self.k.barrier()
        self.nslab = 2
        self.norm("norm_mix%d" % l)
        if l % 2 == 0:
            self.lru(l // 2)
            if "norwkv" not in self.enable:
                self.k.barrier()
                self.rwkv(l // 2)
        else:
            self.s5(l // 2)
        self.nslab = 6
        self.k.barrier()

    def gelu_(self, eng_out, src, tmp, TBc):
        k = self.k
        k.op("dve", lambda e: e.tensor_tensor(out=tmp[:, 0:TBc], in0=src[:, 0:TBc], in1=src[:, 0:TBc], op=ALU.mult),
             R=[src.r()], W=[tmp.r()])
        k.op("dve", lambda e: e.tensor_scalar(out=tmp[:, 0:TBc], in0=tmp[:, 0:TBc], scalar1=0.044715, scalar2=1.0,
                                               op0=ALU.mult, op1=ALU.add), R=[tmp.r()], W=[tmp.r()])
        k.op("dve", lambda e: e.tensor_tensor(out=tmp[:, 0:TBc], in0=tmp[:, 0:TBc], in1=src[:, 0:TBc], op=ALU.mult),
             R=[tmp.r(), src.r()], W=[tmp.r()])
        k.op("act", lambda e: e.activation(out=tmp[:, 0:TBc], in_=tmp[:, 0:TBc], func=AF.Sigmoid, scale=1.5957691216),
             R=[tmp.r()], W=[tmp.r()])
        k.op("dve", lambda e: e.tensor_tensor(out=eng_out[:, 0:TBc], in0=tmp[:, 0:TBc], in1=src[:, 0:TBc], op=ALU.mult),
             R=[tmp.r(), src.r()], W=[eng_out.r()])

    def proj_chunk(self, W2d, c0, cw, evac):
        k, xn = self.k, self.xn
        sg, vg = self.load_slab(W2d, KC, c0, cw)
        for (t0, tw) in self.tbs:
            bg = self.bank[self.rot("pj", 2)]
            for kc in range(KC):
                k.op("pe", lambda e: e.matmul(bg[0:cw, 0:tw], vg[:, kc, 0:cw], xn[:, kc, t0:t0 + tw],
                                               start=(kc == 0), stop=(kc == KC - 1)),
                     R=[sg.r(), xn.r()], W=[bg.r()])
            evac(bg, t0, tw)

    def lru(self, j):
        k, x, TB, ps = self.k, self.x, self.TB, self.ps_
        ns = NS if ps == 1 else 0
        Wi = self.W["w_in_ab"][j]
        wa = self.av("lwa", 8704, 512, BF16, a=8)
        wx = self.av("lwx", 9216, 512, BF16, a=8)
        k.dma("pool", wa[:, :, :], self.I["wa_bd"][j], W=[wa.r()])
        k.dma("pool", wx[:, :, :], self.I["wx_bd"][j], W=[wx.r()])
        XP = self.av("lXP", 9728, 1216)
        names = ["xc", "gr", "gi", "aa", "tt", "hh"]
        T = {nm: self.av("l" + nm, 10944 + i * 1152, 1152) for i, nm in enumerate(names)}
        xcb = self.sq[0]
        hst, cvo, h0s, cv0 = self.lru_small
        nsp = self.lru_nsp
        if ns:
            k.dma("sp", h0s[:, :, 0:16], self.I["st_h"][j], W=[h0s.r()])
            k.dma("sp", cv0[:, :, 0:16, :], self.I["st_conv"][j], W=[cv0.r()])
        if ps == 0:
            k.op("dve", lambda e: e.memset(h0s[:, :, 16:17], 0.0), W=[h0s.r()], acc=True)
            k.op("dve", lambda e: e.memset(cv0[:, :, 16:17, :], 0.0), W=[cv0.r()], acc=True)
        else:
            k.dma("sp", h0s[:, :, 16:17], self.o_h.t[j][:, :, 16:17], R=[self.o_h.r()], W=[h0s.r()], allow_slow_non_contiguous=True)
            k.dma("sp", cv0[:, :, 16:17, :], self.o_conv.t[j][:, :, 16:17, :], R=[self.o_conv.r()], W=[cv0.r()], allow_slow_non_contiguous=True)
        lo, _ = self.lay["lam_b%d" % j]
        k.op("act", lambda e: e.activation(out=nsp[:, 0:8], in_=self.vec[:, lo:lo + 8], func=AF.Exp, scale=-1.0),
             R=[self.vec.r()], W=[nsp.r()])
        k.op("act", lambda e: e.activation(out=nsp[:, 0:8], in_=nsp[:, 0:8], func=AF.Ln, bias=self.cs("one")),
             R=[nsp.r(), self.cst.r()], W=[nsp.r()])
        k.op("dve", lambda e: e.tensor_scalar(out=nsp[:, 8:16], in0=nsp[:, 0:8], scalar1=-16.0, scalar2=None,
                                               op0=ALU.mult), R=[nsp.r()], W=[nsp.r()])
        k.op("dve", lambda e: e.tensor_scalar(out=nsp[:, 0:8], in0=nsp[:, 0:8], scalar1=-8.0, scalar2=None,
                                               op0=ALU.mult), R=[nsp.r()], W=[nsp.r()])
        xc, gr, gi, aa, tt, hh = (T[n] for n in names)
        gb = gr
        XPp = XP[:, 0:1027]
        XPs = XP[:, 1027:1027 + 176].rearrange("p (b t) -> p b t", b=16)
        for c in range(8):
            def ev_x(bg, t0, tw):
                if t0 < 1024:
                    k.op("act", lambda e: e.copy(out=XP[:, 3 + t0:3 + t0 + tw], in_=bg[:, 0:tw]),
                         R=[bg.r()], W=[XP.r()])
                else:
                    k.op("act", lambda e: e.copy(out=XPs[:, :, 3:11],
                                                 in_=bg[:, 0:128].rearrange("p (b t) -> p b t", b=16)),
                         R=[bg.r()], W=[XP.r()])
            self.proj_chunk(Wi, COLS_A + c * 128, 128, ev_x)

            k.op("dve", lambda e: e.tensor_copy(out=XP[:, 0:3], in_=cv0[:, c, 16, :]), R=[cv0.r()], W=[XP.r()])
            if ns:
                k.op("dve", lambda e: e.tensor_copy(out=XPs[:, :, 0:3], in_=cv0[:, c, 0:16, :]),
                     R=[cv0.r()], W=[XP.r()])
            k.op("dve", lambda e: e.tensor_copy(out=cvo[:, c, 16, :], in_=XP[:, 1024:1027]), R=[XP.r()], W=[cvo.r()])
            if ns:
                k.op("dve", lambda e: e.tensor_copy(out=cvo[:, c, 0:16, :], in_=XPs[:, :, 8:11]),
                     R=[XP.r()], W=[cvo.r()])
            segs = [(xc[:, 0:1024], lambda q: XP[:, q:q + 1024])]
            if ns:
                segs.append((xc[:, 1024:1152].rearrange("p (b t) -> p b t", b=16), lambda q: XPs[:, :, q:q + 8]))
            for (dst, srcf) in segs:
                k.op("dve", lambda e: e.tensor_scalar(out=dst, in0=srcf(0), scalar1=self.vcol("conv_w0_%d" % j, c),
                                                       scalar2=self.vcol("conv_b_b%d" % j, c),
                                                       op0=ALU.mult, op1=ALU.add),
                     R=[XP.r(), self.vec.r()], W=[xc.r()])
                for q in range(1, 4):
                    k.op("dve", lambda e: e.scalar_tensor_tensor(out=dst, in0=srcf(q),
                                                                  scalar=self.vcol("conv_w%d_%d" % (q, j), c),
                                                                  in1=dst, op0=ALU.mult, op1=ALU.add),
                         R=[XP.r(), xc.r(), self.vec.r()], W=[xc.r()])
            k.op("act", lambda e: e.copy(out=xcb[:, 0:TB], in_=xc[:, 0:TB]), R=[xc.r()], W=[xcb.r()])
            for (wt, bname, dst) in ((wa, "ba_b%d" % j, gr), (wx, "bx_b%d" % j, gi)):
                for (t0, tw) in self.tbs:
                    bg = self.bank[self.rot("pj", 2)]
                    k.op("pe", lambda e: e.matmul(bg[:, 0:tw], wt[:, c, :], xcb[:, t0:t0 + tw], start=True, stop=True),
                         R=[wt.r(), xcb.r()], W=[bg.r()])
                    k.op("act", lambda e: e.activation(out=dst[:, t0:t0 + tw], in_=bg[:, 0:tw], func=AF.Sigmoid,
                                                       bias=self.vcol(bname, c)),
                         R=[bg.r(), self.vec.r()], W=[dst.r()])
            k.op("act", lambda e: e.activation(out=aa[:, 0:TB], in_=gr[:, 0:TB], func=AF.Exp, scale=nsp[:, c:c + 1]),
                 R=[gr.r(), nsp.r()], W=[aa.r()])
            k.op("act", lambda e: e.activation(out=tt[:, 0:TB], in_=gr[:, 0:TB], func=AF.Exp,
                                               scale=nsp[:, 8 + c:9 + c]), R=[gr.r(), nsp.r()], W=[tt.r()])
            k.op("dve", lambda e: e.tensor_scalar(out=tt[:, 0:TB], in0=tt[:, 0:TB], scalar1=-1.0, scalar2=1.0,
                                                   op0=ALU.mult, op1=ALU.add), R=[tt.r()], W=[tt.r()])
            k.op("act", lambda e: e.activation(out=tt[:, 0:TB], in_=tt[:, 0:TB], func=AF.Sqrt),
                 R=[tt.r()], W=[tt.r()])
            k.op("dve", lambda e: e.tensor_tensor(out=gi[:, 0:TB], in0=gi[:, 0:TB], in1=xc[:, 0:TB], op=ALU.mult),
                 R=[gi.r(), xc.r()], W=[gi.r()])
            k.op("dve", lambda e: e.tensor_tensor(out=tt[:, 0:TB], in0=tt[:, 0:TB], in1=gi[:, 0:TB], op=ALU.mult),
                 R=[gi.r(), tt.r()], W=[tt.r()])
            k.op("dve", lambda e: e.tensor_tensor_scan(out=hh[:, 0:1024], data0=aa[:, 0:1024], data1=tt[:, 0:1024],
                                                        initial=h0s[:, c, 16:17], op0=ALU.mult, op1=ALU.add),
                 R=[aa.r(), tt.r(), h0s.r()], W=[hh.r()])
            for bq in range(ns):
                o_ = 1024 + bq * 8
                k.op("dve", lambda e: e.tensor_tensor_scan(out=hh[:, o_:o_ + 8], data0=aa[:, o_:o_ + 8],
                                                            data1=tt[:, o_:o_ + 8], initial=h0s[:, c, bq:bq + 1],
                                                            op0=ALU.mult, op1=ALU.add),
                     R=[aa.r(), tt.r(), h0s.r()], W=[hh.r()], acc=True)
            k.op("dve", lambda e: e.tensor_copy(out=hst[:, c, 16:17], in_=hh[:, 1023:1024]), R=[hh.r()], W=[hst.r()])
            if ns:
                k.op("dve", lambda e: e.tensor_copy(
                    out=hst[:, c, 0:16], in_=hh[:, 1024:1152].rearrange("p (b t) -> p b t", b=16)[:, :, 7]),
                    R=[hh.r()], W=[hst.r()])
            def ev_g(bg, t0, tw):
                k.op("act", lambda e: e.copy(out=gb[:, t0:t0 + tw], in_=bg[:, 0:tw]), R=[bg.r()], W=[gb.r()])
            self.proj_chunk(Wi, COLS_A + 1024 + c * 128, 128, ev_g)
            self.gelu_(gi, gb, aa, TB)
            yb = self.ybuf
            k.op("dve", lambda e: e.tensor_tensor(out=yb[:, c, 0:TB], in0=gi[:, 0:TB], in1=hh[:, 0:TB], op=ALU.mult),
                 R=[gi.r(), hh.r()], W=[yb.r()])
        k.dma("sp", self.o_h.t[j], hst[:, :, :], R=[hst.r()], W=[self.o_h.r()])
        k.dma("sp", self.o_conv.t[j], cvo[:, :, :, :], R=[cvo.r()], W=[self.o_conv.r()])
        Wo = self.W["w_out_ab"][j]
        yb = self.ybuf
        for s in range(D // 256):
            sg, vg = self.load_slab(Wo, 8, s * 256, 256, rows0=1024)
            for oc2 in range(2):
                oc = s * 2 + oc2
                for (t0, tw) in self.tbs:
                    bd = self.bank[4 + self.rot("dn", 2)]
                    for kc in range(8):
                        k.op("pe", lambda e: e.matmul(bd[:, 0:tw], vg[:, kc, oc2 * 128:(oc2 + 1) * 128],
                                                       yb[:, kc, t0:t0 + tw], start=(kc == 0), stop=(kc == 7)),
                             R=[sg.r(), yb.r()], W=[bd.r()])
                    k.op("dve", lambda e: e.tensor_tensor(out=x[:, oc, t0:t0 + tw], in0=bd[:, 0:tw],
                                                           in1=x[:, oc, t0:t0 + tw], op=ALU.add),
                         R=[bd.r(), x.r(oc)], W=[x.r(oc)])


    def rwkv(self, j):
        k, x, xn, TB, ps = self.k, self.x, self.xn, self.TB, self.ps_
        ns = NS if ps == 1 else 0
        av, cst, vec = self.av, self.cst, self.vec
        Wi = self.W["w_in_ab"][j]
        ident, blk = self.cs("ident"), self.cs("blk")
        NT = TB // 128
        self.nslab = 1

        def dve(fn, R, W, **kw):
            return k.op("dve", fn, R=R, W=W, **kw)
        w2a2 = av("r_w2a2", 2048, 512, BF16)
        g2a = av("r_g2a", 2560, 512, BF16)
        g2b = av("r_g2b", 3072, 512, BF16)
        k.dma("pool", w2a2[0:64, :], self.I["w2_a"][j], W=[w2a2.r()])
        k.dma("pool", w2a2[64:128, :], self.I["a2_a"][j], W=[w2a2.r()])
        lx, xg0, xg1 = av("r_lx", 4096, 1152), av("r_xg0", 5248, 1152), av("r_xg1", 6400, 1152)
        Tt = [av("r_T%d" % i, 7552 + i * 1152, 1152) for i in range(7)]
        stg = av("r_stg", 15616, 1152)
        shin = av("r_shin", 16768, 459, a=27)
        shout = av("r_shout", 17227, 459, a=27)
        txa, sg0b, sg1b = self.sq[0], self.sq[1], self.sg1b
        if ns:
            k.dma("sp", shin[:, :, 0:16], self.I["st_shift"][j], W=[shin.r()])
        if ps == 0:
            dve(lambda e: e.memset(shin[:, :, 16:17], 0.0), [], [shin.r()], acc=True)
        else:
            k.dma("sp", shin[:, :, 16:17], self.o_shift.t[j][:, :, 16:17], R=[self.o_shift.r()], W=[shin.r()],
                  allow_slow_non_contiguous=True)
        r3 = lambda a: a.rearrange("p (b t) -> p b t", b=16)

        def project_shift(col0, cw, zc, Z, mu_ap):
            D_ = Tt[6]

            def ev(bg, t0, tw):
                k.op("act", lambda e: e.copy(out=Z[0:cw, t0:t0 + tw], in_=bg[0:cw, 0:tw]), R=[bg.r()], W=[Z.r()], acc=True)
            self.proj_chunk(Wi, col0, cw, ev)
            dve(lambda e: e.tensor_copy(out=shout[0:cw, zc, 16:17], in_=Z[0:cw, 1023:1024]), [Z.r()], [shout.r()], acc=True)
            dve(lambda e: e.tensor_tensor(out=D_[0:cw, 1:1024], in0=Z[0:cw, 0:1023], in1=Z[0:cw, 1:1024], op=ALU.subtract),
                [Z.r()], [D_.r()])
            dve(lambda e: e.tensor_tensor(out=D_[0:cw, 0:1], in0=shin[0:cw, zc, 16:17], in1=Z[0:cw, 0:1], op=ALU.subtract),
                [Z.r(), shin.r()], [D_.r()], acc=True)
            if ns:
                Zs_, Ds_ = r3(Z[0:cw, 1024:1152]), r3(D_[0:cw, 1024:1152])
                dve(lambda e: e.tensor_copy(out=shout[0:cw, zc, 0:16], in_=Zs_[:, :, 7]), [Z.r()], [shout.r()], acc=True)
                dve(lambda e: e.tensor_tensor(out=Ds_[:, :, 1:8], in0=Zs_[:, :, 0:7], in1=Zs_[:, :, 1:8], op=ALU.subtract),
                    [Z.r()], [D_.r()], acc=True)
                dve(lambda e: e.tensor_tensor(out=Ds_[:, :, 0], in0=shin[0:cw, zc, 0:16], in1=Zs_[:, :, 0], op=ALU.subtract),
                    [Z.r(), shin.r()], [D_.r()], acc=True)
            dve(lambda e: e.scalar_tensor_tensor(out=Z[0:cw, 0:TB], in0=D_[0:cw, 0:TB], scalar=mu_ap, in1=Z[0:cw, 0:TB],
                                                 op0=ALU.mult, op1=ALU.add), [D_.r(), Z.r(), vec.r()], [Z.r()])
        mo = self.lay["mu_x%d" % j][0]
        dve(lambda e: e.memset(shout[:, 26, :], 0.0), [], [shout.r()], acc=True)
        project_shift(3072, 128, 24, lx, vec[:, mo:mo + 1])
        project_shift(3200, 128, 25, xg0, vec[:, mo + 2:mo + 3])
        project_shift(3328, 32, 26, xg1, vec[0:32, mo + 3:mo + 4])
        k.op("act", lambda e: e.activation(out=txa[0:64, 0:TB], in_=lx[0:64, 0:TB], func=AF.Tanh), R=[lx.r()], W=[txa.r()])
        k.op("act", lambda e: e.copy(out=txa[64:128, 0:TB], in_=lx[64:128, 0:TB]), R=[lx.r()], W=[txa.r()], acc=True)
        k.op("act", lambda e: e.activation(out=sg0b[:, 0:TB], in_=xg0[:, 0:TB], func=AF.Sigmoid), R=[xg0.r()], W=[sg0b.r()])
        k.op("act", lambda e: e.activation(out=sg1b[0:32, 0:TB], in_=xg1[0:32, 0:TB], func=AF.Sigmoid),
             R=[xg1.r()], W=[sg1b.r()])
        Zsd, Ysd, Bon = self.Zs, self.Ys, self.Bon

        def emit(q, c, Q):
            for tt in range(NT):
                bk = self.bank[6 + (tt % 2)]
                k.op("pe", lambda e: e.transpose(bk[:, 0:128], Q[:, tt * 128:(tt + 1) * 128], ident),
                     R=[Q.r(), cst.r()], W=[bk.r()])
                if tt % 2 == 0:
                    k.op("act", lambda e: e.copy(out=stg[:, tt * 128:(tt + 1) * 128], in_=bk[:, 0:128]),
                         R=[bk.r()], W=[stg.r()], acc=True)
                else:
                    dve(lambda e: e.tensor_copy(out=stg[:, tt * 128:(tt + 1) * 128], in_=bk[:, 0:128]),
                        [bk.r()], [stg.r()], acc=True)
            sv = stg[:, 0:TB].rearrange("p (tt f) -> p tt f", f=128)
            if q == 5:
                dst = self.Vs.t[0:TB, c * 128:(c + 1) * 128].rearrange("(tt p) f -> p tt f", p=128)
                k.dma("sp", dst, sv, R=[stg.r()], W=[self.Vs.r()])
            else:
                for h2 in range(2):
                    dst = Zsd.t[0:TB, 2 * c + h2, q, :].rearrange("(tt p) j -> p tt j", p=128)
                    k.dma("sp", dst, sv[:, :, h2 * 64:(h2 + 1) * 64], R=[stg.r()], W=[Zsd.r()])

        def headsum(user, src):
            for (t0, tw) in self.tbs:
                bg = self.bank[4 + self.rot("hs", 2)]
                k.op("pe", lambda e: e.matmul(bg[:, 0:tw], blk, src[:, t0:t0 + tw], start=True, stop=True),
                     R=[cst.r(), src.r()], W=[bg.r()])
                user(bg, t0, tw)

        for c in range(8):
            T0, T1, T2, T3, T4, T5 = Tt[0:6]
            vc = lambda nm: self.vcol("%s%d" % (nm, j), c)
            project_shift(c * 128, 128, c, T0, vc("mu_r"))
            project_shift(1024 + c * 128, 128, 8 + c, T1, vc("mu_k"))
            project_shift(2048 + c * 128, 128, 16 + c, T2, vc("mu_v"))
            for (t0, tw) in self.tbs:
                bg = self.bank[4 + self.rot("hs", 2)]
                k.op("pe", lambda e: e.matmul(bg[:, 0:tw], w2a2[64:128, c * 128:(c + 1) * 128], txa[64:128, t0:t0 + tw],
                                               start=True, stop=True), R=[w2a2.r(), txa.r()], W=[bg.r()])
                k.op("act", lambda e: e.activation(out=T3[:, t0:t0 + tw], in_=bg[:, 0:tw], func=AF.Sigmoid, bias=vc("a0_a")),
                     R=[bg.r(), vec.r()], W=[T3.r()], acc=True)
            dve(lambda e: e.tensor_scalar(out=T4[:, 0:TB], in0=T1[:, 0:TB], scalar1=vc("kk_a"), scalar2=None, op0=ALU.mult),
                [T1.r(), vec.r()], [T4.r()])
            dve(lambda e: e.tensor_tensor(out=T5[:, 0:TB], in0=T4[:, 0:TB], in1=T4[:, 0:TB], op=ALU.mult), [T4.r()], [T5.r()])

            def nrm_ev(bg, t0, tw):
                k.op("act", lambda e: e.activation(out=Tt[6][:, t0:t0 + tw], in_=bg[:, 0:tw], func=AF.Sqrt),
                     R=[bg.r()], W=[Tt[6].r()], acc=True)
            headsum(nrm_ev, T5)
            dve(lambda e: e.tensor_scalar(out=Tt[6][:, 0:TB], in0=Tt[6][:, 0:TB], scalar1=1e-12, scalar2=None, op0=ALU.max),
                [Tt[6].r()], [Tt[6].r()])
            dve(lambda e: e.reciprocal(out=Tt[6][:, 0:TB], in_=Tt[6][:, 0:TB]), [Tt[6].r()], [Tt[6].r()])
            dve(lambda e: e.tensor_tensor(out=T4[:, 0:TB], in0=T4[:, 0:TB], in1=Tt[6][:, 0:TB], op=ALU.mult),
                [T4.r(), Tt[6].r()], [T4.r()])
            dve(lambda e: e.tensor_tensor(out=T5[:, 0:TB], in0=T4[:, 0:TB], in1=T3[:, 0:TB], op=ALU.mult),
                [T4.r(), T3.r()], [T5.r()])
            emit(0, c, T4)
            emit(2, c, T5)
            dve(lambda e: e.tensor_scalar(out=T4[:, 0:TB], in0=T3[:, 0:TB], scalar1=-1.0, scalar2=vc("ka_a"),
                                          op0=ALU.add, op1=ALU.mult), [T3.r(), vec.r()], [T4.r()])
            dve(lambda e: e.scalar_tensor_tensor(out=T4[:, 0:TB], in0=T4[:, 0:TB], scalar=1.0, in1=T1[:, 0:TB],
                                                 op0=ALU.add, op1=ALU.mult), [T4.r(), T1.r()], [T4.r()])
            emit(3, c, T4)
            dve(lambda e: e.scalar_tensor_tensor(out=T5[:, 0:TB], in0=T0[:, 0:TB], scalar=vc("rk_a"), in1=T4[:, 0:TB],
                                                 op0=ALU.mult, op1=ALU.mult), [T0.r(), T4.r(), vec.r()], [T5.r()])

            def bon_ev(bg, t0, tw):
                dve(lambda e: e.tensor_tensor(out=Tt[6][:, t0:t0 + tw], in0=bg[:, 0:tw], in1=T2[:, t0:t0 + tw], op=ALU.mult),
                    [bg.r(), T2.r()], [Tt[6].r()], acc=True)
            headsum(bon_ev, T5)
            k.dma("sp", Bon.t[c][:, 0:TB], Tt[6][:, 0:TB], R=[Tt[6].r()], W=[Bon.r()])
            for (t0, tw) in self.tbs:
                bg = self.bank[4 + self.rot("hs", 2)]
                k.op("pe", lambda e: e.matmul(bg[:, 0:tw], w2a2[0:64, c * 128:(c + 1) * 128], txa[0:64, t0:t0 + tw],
                                               start=True, stop=True), R=[w2a2.r(), txa.r()], W=[bg.r()])
                k.op("act", lambda e: e.activation(out=T1[:, t0:t0 + tw], in_=bg[:, 0:tw], func=AF.Sigmoid, bias=vc("w0_a")),
                     R=[bg.r(), vec.r()], W=[T1.r()], acc=True)
            k.op("act", lambda e: e.activation(out=T1[:, 0:TB], in_=T1[:, 0:TB], func=AF.Exp, scale=-float(np.exp(-0.5))),
                 R=[T1.r()], W=[T1.r()])
            emit(1, c, T1)
            emit(4, c, T0)
            emit(5, c, T2)
        k.dma("sp", self.o_shift.t[j], shout[:, :, :], R=[shout.r()], W=[self.o_shift.r()])
        k.barrier()
        Spf, T1p = av("q_Sp", 0, 512), av("q_T1p", 512, 512, a=8)
        Sp = Tile(Spf[:, :].rearrange("p (e j) -> p e j", e=8), "q_Sp3")
        Sp.reg = Spf.reg
        Ss = av("q_Ss", 1024, 8192)
        T1s = av("q_T1s", 9216, 2048)
        Xb = [av("q_X%d" % i, 11264 + i * 1280, 1280) for i in range(2)]
        vt = [av("q_vt%d" % i, 13824 + i * 128, 128, a=16) for i in range(2)]
        yt = [av("q_yt%d" % i, 14080 + i * 128, 128, a=16) for i in range(2)]
        vs_, ys_ = av("q_vs", 14336, 1024), av("q_ys", 15360, 1024)
        sa_t = av("q_sa", 16384, 32)
        rep = self.cs("rep", rows=16)
        Zv = Zsd.t.rearrange("t h q j -> h t (q j)")
        Vv = self.Vs.t.rearrange("t (p e) -> p t e", e=8)
        Vsr = self.Vs.r()
        Yv = Ysd.t.rearrange("t (p e) -> p t e", e=8)
        if ps == 0:
            dve(lambda e: e.memset(Sp[:, :, :], 0.0), [], [Sp.r()])
        else:
            k.dma("sp", Spf[:, :], self.o_wkv.t[j][:, 16, :], R=[self.o_wkv.r()], W=[Sp.r()])

        def step(S, T1, reg, shp, sa, vv, yy, RW):
            nd = len(shp)
            bj = lambda a: a.unsqueeze(nd - 1).broadcast_to([128] + shp)
            be = lambda a: a.unsqueeze(nd).broadcast_to([128] + shp)
            Rr, Wr = RW
            dve(lambda e: e.tensor_tensor(out=T1, in0=S, in1=bj(reg(0)), op=ALU.mult), Rr, Wr, sew=False)
            dve(lambda e: e.tensor_reduce(out=sa, in_=T1, axis=AX.X, op=ALU.add), Wr, Wr, sew=False)
            dve(lambda e: e.tensor_tensor(out=S, in0=S, in1=bj(reg(1)), op=ALU.mult), Rr, Wr, sew=False)
            dve(lambda e: e.tensor_tensor(out=T1, in0=bj(reg(2)), in1=be(sa), op=ALU.mult), Rr, Wr, sew=False)
            dve(lambda e: e.tensor_tensor(out=S, in0=S, in1=T1, op=ALU.subtract), Wr, Wr, sew=False)
            dve(lambda e: e.tensor_tensor(out=T1, in0=bj(reg(3)), in1=be(vv), op=ALU.mult), Rr, Wr, sew=False)
            dve(lambda e: e.tensor_tensor(out=S, in0=S, in1=T1, op=ALU.add), Wr, Wr, sew=False)
            dve(lambda e: e.tensor_tensor(out=T1, in0=S, in1=bj(reg(4)), op=ALU.mult), Rr, Wr, sew=False)
            dve(lambda e: e.tensor_reduce(out=yy, in_=T1, axis=AX.X, op=ALU.add), Wr, Wr, sew=False)

        def rep_mm(Xt, bset):
            Xv = Xt[0:16, 0:1280].rearrange("h (t q j) -> h t q j", t=4, q=5)
            for q in range(5):
                bk = self.bank[bset * 3 + q // 2]
                k.op("pe", lambda e: e.matmul(bk[:, (q % 2) * 256:(q % 2) * 256 + 256], rep, Xv[:, :, q, :],
                                               start=True, stop=True), R=[cst.r(), Xt.r()], W=[bk.r()], acc=True)

        wreg = Reg("q_work")
        vtt = ytt = None
        for g4 in range(1024 // 4):
            t0 = g4 * 4
            Xt = Xb[g4 % 2]
            k.dma("sp", Xt[0:16, 0:1280].rearrange("h (t f) -> h t f", t=4), Zv[:, t0:t0 + 4, :],
                  R=[Zsd.r()], W=[Xt.r()])
            if t0 % 16 == 0:
                vtt, ytt = vt[(t0 // 16) % 2], yt[(t0 // 16) % 2]
                k.dma("sp", vtt[:, :, :], Vv[:, t0:t0 + 16, :], R=[Vsr], W=[vtt.r()], allow_slow_non_contiguous=True)
            bset = g4 % 2
            rep_mm(Xt, bset)
            banks = [self.bank[bset * 3 + i] for i in range(3)]
            for tl in range(4):
                reg = lambda q: banks[q // 2][:, (q % 2) * 256 + tl * 64:(q % 2) * 256 + tl * 64 + 64]
                ti = (t0 + tl) % 16
                step(Sp[:, :, :], T1p[:, :, :], reg, [8, 64], sa_t[:, 0:8], vtt[:, ti, :], ytt[:, ti, :],
                     ([b_.r() for b_ in banks] + [vtt.r(), wreg, Sp.r()], [wreg, ytt.r()]))
            if t0 % 16 == 12:
                k.dma("sp", Yv[:, t0 - 12:t0 + 4, :], ytt[:, :, :], R=[ytt.r()], W=[Ysd.r()], allow_slow_non_contiguous=True)
        k.dma("sp", self.o_wkv.t[j][:, 16, :], Spf[:, :], R=[Sp.r(), wreg], W=[self.o_wkv.r()])
        if ns:
            k.dma("sp", Ss[:, :], self.I["st_wkv"][j].rearrange("p b f -> p (b f)"), W=[Ss.r()])
            vs4 = vs_[:, :].rearrange("p (b t e) -> p b t e", b=16, t=8)
            ys4 = ys_[:, :].rearrange("p (b t e) -> p b t e", b=16, t=8)
            for bq in range(8):
                k.dma("sp", vs_[:, :].rearrange("p (n e) -> p n e", e=8)[:, 16 * bq:16 * bq + 16, :],
                      Vv[:, 1024 + 16 * bq:1024 + 16 * bq + 16, :],
                      R=[Vsr], W=[vs_.r()], allow_slow_non_contiguous=True)
            Ss4 = Ss[:, :].rearrange("p (b e j) -> p b e j", b=16, e=8)
            T1s4 = T1s[:, :].rearrange("p (b e j) -> p b e j", b=4, e=8)
            sreg = Reg("q_works")
            it = 0
            for t in range(8):
                for qd in range(4):
                    Xt = Xb[it % 2]
                    bset = it % 2
                    it += 1
                    r0 = 1024 + 32 * qd + t
                    k.dma("sp", Xt[0:16, 0:1280].rearrange("h (t f) -> h t f", t=4),
                          Zv[:, r0:r0 + 25:8, :], R=[Zsd.r()], W=[Xt.r()])
                    rep_mm(Xt, bset)
                    banks = [self.bank[bset * 3 + i] for i in range(3)]
                    reg = lambda q: banks[q // 2][:, (q % 2) * 256:(q % 2) * 256 + 256].rearrange("p (b j) -> p b j", b=4)
                    step(Ss4[:, 4 * qd:4 * qd + 4, :, :], T1s4[:, :, :, :], reg, [4, 8, 64],
                         sa_t[:, 0:32].rearrange("p (b e) -> p b e", b=4), vs4[:, 4 * qd:4 * qd + 4, t, :],
                         ys4[:, 4 * qd:4 * qd + 4, t, :],
                         ([b_.r() for b_ in banks] + [vs_.r(), Ss.r(), sreg], [sreg, ys_.r()]))
            k.dma("sp", Yv[:, 1024:1152, :], ys_[:, :].rearrange("p (n e) -> p n e", e=8), R=[ys_.r(), sreg], W=[Ysd.r()],
                  allow_slow_non_contiguous=True)
            k.dma("sp", self.o_wkv.t[j][:, 0:16, :].rearrange("p b f -> p (b f)"), Ss[:, :], R=[Ss.r(), sreg],
                  W=[self.o_wkv.r()])
        k.barrier()
        self.nslab = 2
        yab = av("p_yab", 4096, 4608, BF16, a=8)
        ytk = av("p_ytk", 8704, 1152)
        yc, ysq, mean, bon = (av("p_t%d" % i, 9856 + i * 1152, 1152) for i in range(4))
        g2a = av("p_g2a", 14464, 512, BF16)
        g2b = av("p_g2b", 14976, 512, BF16)
        gne = av("p_gne", 15488, 4)
        dve(lambda e: e.memset(gne[:, :], GN_EPS), [], [gne.r()])
        k.dma("pool", g2a[:, :], self.I["g2_a"][j][0:128, :], W=[g2a.r()])
        k.dma("pool", g2b[0:32, :], self.I["g2_a"][j][128:160, :], W=[g2b.r()])
        for c in range(8):
            vc = lambda nm: self.vcol("%s%d" % (nm, j), c)
            k.dma("sp", ytk[:, 0:TB].rearrange("p (tt f) -> p tt f", f=128),
                  Ysd.t[0:TB, c * 128:(c + 1) * 128].rearrange("(tt p) f -> p tt f", p=128), R=[Ysd.r()], W=[ytk.r()])
            k.dma("sp", bon[:, 0:TB], Bon.t[c][:, 0:TB], R=[Bon.r()], W=[bon.r()])
            for tt in range(NT):
                bk = self.bank[6 + (tt % 2)]
                k.op("pe", lambda e: e.transpose(bk[:, 0:128], ytk[:, tt * 128:(tt + 1) * 128], ident),
                     R=[ytk.r(), cst.r()], W=[bk.r()])
                k.op("act", lambda e: e.copy(out=yc[:, tt * 128:(tt + 1) * 128], in_=bk[:, 0:128]), R=[bk.r()], W=[yc.r()], acc=True)
            dve(lambda e: e.tensor_tensor(out=ysq[:, 0:TB], in0=yc[:, 0:TB], in1=yc[:, 0:TB], op=ALU.mult), [yc.r()], [ysq.r()])

            def mean_ev(bg, t0, tw):
                k.op("act", lambda e: e.activation(out=mean[:, t0:t0 + tw], in_=bg[:, 0:tw], func=AF.Copy, scale=1.0 / 64),
                     R=[bg.r()], W=[mean.r()], acc=True)
            headsum(mean_ev, yc)

            def var_ev(bg, t0, tw):
                k.op("act", lambda e: e.activation(out=ysq[:, t0:t0 + tw], in_=bg[:, 0:tw], func=AF.Copy, scale=1.0 / 64),
                     R=[bg.r()], W=[ysq.r()], acc=True)
            headsum(var_ev, ysq)
            dve(lambda e: e.tensor_tensor(out=yc[:, 0:TB], in0=yc[:, 0:TB], in1=mean[:, 0:TB], op=ALU.subtract),
                [yc.r(), mean.r()], [yc.r()])
            dve(lambda e: e.tensor_tensor(out=mean[:, 0:TB], in0=mean[:, 0:TB], in1=mean[:, 0:TB], op=ALU.mult),
                [mean.r()], [mean.r()])
            dve(lambda e: e.tensor_tensor(out=ysq[:, 0:TB], in0=ysq[:, 0:TB], in1=mean[:, 0:TB], op=ALU.subtract),
                [ysq.r(), mean.r()], [ysq.r()])
            k.op("act", lambda e: e.activation(out=ysq[:, 0:TB], in_=ysq[:, 0:TB], func=AF.Sqrt, bias=gne[:, 0:1]),
                 R=[ysq.r(), gne.r()], W=[ysq.r()])
            dve(lambda e: e.reciprocal(out=ysq[:, 0:TB], in_=ysq[:, 0:TB]), [ysq.r()], [ysq.r()])
            dve(lambda e: e.tensor_tensor(out=yc[:, 0:TB], in0=yc[:, 0:TB], in1=ysq[:, 0:TB], op=ALU.mult),
                [yc.r(), ysq.r()], [yc.r()])
            dve(lambda e: e.tensor_scalar(out=yc[:, 0:TB], in0=yc[:, 0:TB], scalar1=vc("lnx_g"), scalar2=vc("lnx_b"),
                                          op0=ALU.mult, op1=ALU.add), [yc.r(), vec.r()], [yc.r()])
            dve(lambda e: e.tensor_tensor(out=yc[:, 0:TB], in0=yc[:, 0:TB], in1=bon[:, 0:TB], op=ALU.add),
                [yc.r(), bon.r()], [yc.r()])
            for (t0, tw) in self.tbs:
                bg = self.bank[4 + self.rot("hs", 2)]
                k.op("pe", lambda e: e.matmul(bg[:, 0:tw], g2a[:, c * 128:(c + 1) * 128], sg0b[:, t0:t0 + tw],
                                               start=True, stop=False), R=[g2a.r(), sg0b.r()], W=[bg.r()])
                k.op("pe", lambda e: e.matmul(bg[:, 0:tw], g2b[0:32, c * 128:(c + 1) * 128], sg1b[0:32, t0:t0 + tw],
                                               start=False, stop=True), R=[g2b.r(), sg1b.r()], W=[bg.r()])
                dve(lambda e: e.tensor_tensor(out=yab[:, c, t0:t0 + tw], in0=yc[:, t0:t0 + tw], in1=bg[:, 0:tw], op=ALU.mult),
                    [yc.r(), bg.r()], [yab.r()], acc=True)
        Wo = self.W["w_out_ab"][j]
        for s in range(D // 256):
            sg, vg = self.load_slab(Wo, 8, s * 256, 256, rows0=0)
            for oc2 in range(2):
                oc = s * 2 + oc2
                for (t0, tw) in self.tbs:
                    bd = self.bank[self.rot("pj", 2)]
                    for kc in range(8):
                        k.op("pe", lambda e: e.matmul(bd[:, 0:tw], vg[:, kc, oc2 * 128:(oc2 + 1) * 128],
                                                       yab[:, kc, t0:t0 + tw], start=(kc == 0), stop=(kc == 7)),
                             R=[sg.r(), yab.r()], W=[bd.r()])
                    dve(lambda e: e.tensor_tensor(out=x[:, oc, t0:t0 + tw], in0=bd[:, 0:tw], in1=x[:, oc, t0:t0 + tw],
                                                  op=ALU.add), [bd.r(), x.r(oc)], [x.r(oc)])

    def s5(self, j):
        k, x, xn, TB, ps = self.k, self.x, self.xn, self.TB, self.ps_
        ns = NS if ps == 1 else 0
        PI = float(np.pi)
        av = self.av
        P = {nm: av("s5" + nm, 4096 + i * 128, 128) for i, nm in enumerate(
            ["are", "aim", "dtt", "rho", "tht", "ta", "tb", "fre", "fim", "tc"])}
        Bp1, Bp2, Cp1, Cp2, Bb1, Bb2, tA, tB = (av("s5s%d" % i, 5376 + i * 128, 128, a=8) for i in range(8))
        T1c = av("s5T1", 6400, 64, BF16)
        T2c = av("s5T2", 6464, 64, BF16)
        LB1 = [av("s5LB1%d" % i, 6528 + i * 64, 64, BF16) for i in range(2)]
        LB2 = [av("s5LB2%d" % i, 6656 + i * 64, 64, BF16) for i in range(2)]
        C1p = [av("s5C1%d" % i, 6784 + i * 64, 64, BF16) for i in range(2)]
        C2p = [av("s5C2%d" % i, 6912 + i * 64, 64, BF16) for i in range(2)]
        hin = av("s5hin", 7040, 136, a=8)
        hfin = av("s5hfin", 7176, 136, a=8)
        ftmp = av("s5ft", 7312, 68, a=4)
        Ct, St, targ = av("s5Ct", 7424, 1024), av("s5St", 8448, 1024), av("s5ta", 9472, 1024)
        Ct1, St1, targ1 = av("s5Ct1", 0, 1024), av("s5St1", 1024, 1024), av("s5ta1", 2048, 1024)
        rhof, m, G = av("s5rf", 10496, 1152), av("s5m", 11648, 1152), av("s5G", 12800, 1152)
        G1b, G2b = av("s5G1", 13952, 576, BF16), av("s5G2", 14528, 576, BF16)
        yf, gt = av("s5yf", 15104, 1152), av("s5gt", 16256, 1152)
        cst, vec = self.cst, self.vec
        ident = self.cs("ident")
        negpi = self.cs("negpi")
        Sg = lambda i: cst[:, CST["sgn"][0] + i:CST["sgn"][0] + i + 1]
        V = lambda e: e

        def dve(fn, R, W, **kw):
            return k.op("dve", fn, R=R, W=W, **kw)

        k.dma("sp", P["are"][:, :], self.I["s5_are"][j], W=[P["are"].r()])
        k.dma("sp", P["aim"][:, :], self.I["s5_aim"][j], W=[P["aim"].r()])
        k.dma("sp", P["dtt"][:, :], self.I["s5_ldt"][j], W=[P["dtt"].r()])
        k.op("act", lambda e: e.activation(out=P["dtt"][:, :], in_=P["dtt"][:, :], func=AF.Exp),
             R=[P["dtt"].r()], W=[P["dtt"].r()])
        dve(lambda e: e.tensor_tensor(out=P["tht"][:, :], in0=P["dtt"][:, :], in1=P["aim"][:, :], op=ALU.mult),
            [P["dtt"].r(), P["aim"].r()], [P["tht"].r()])
        dve(lambda e: e.tensor_tensor(out=P["rho"][:, :], in0=P["dtt"][:, :], in1=P["are"][:, :], op=ALU.mult),
            [P["dtt"].r(), P["are"].r()], [P["rho"].r()])
        k.op("act", lambda e: e.activation(out=P["rho"][:, :], in_=P["rho"][:, :], func=AF.Exp),
             R=[P["rho"].r()], W=[P["rho"].r()])

        qi = av("s5qi", 17408, 1024, I32)
        qi1 = av("s5qi1", 3072, 1024, I32)
        halfpi = self.cs("halfpi")
        TSET = [(Ct, St, targ, qi, gt), (Ct1, St1, targ1, qi1, self.rstd)]

        def sincos(dst_s, dst_c, src, w, Rs, qi=qi, gt=gt):
            for (dst, addc, lo, hi, bias) in ((dst_s, 0.0, -PI, PI, None), (dst_c, 0.5 * PI, -1.5 * PI, 0.5 * PI, halfpi)):
                k.op("pool", lambda e: e.tensor_scalar(out=qi[:, 0:w], in0=src, scalar1=addc, scalar2=1.0 / (2 * PI),
                                                       op0=ALU.add, op1=ALU.mult), R=Rs, W=[qi.r()])
                k.op("pool", lambda e: e.tensor_copy(out=gt[:, 0:w], in_=qi[:, 0:w]), R=[qi.r()], W=[gt.r()])
                dve(lambda e: e.scalar_tensor_tensor(out=dst[:, 0:w], in0=gt[:, 0:w], scalar=-2 * PI, in1=src,
                                                     op0=ALU.mult, op1=ALU.add), Rs + [gt.r()], [dst.r()])
                dve(lambda e: e.tensor_scalar(out=dst[:, 0:w], in0=dst[:, 0:w], scalar1=lo, scalar2=hi,
                                              op0=ALU.max, op1=ALU.min), [dst.r()], [dst.r()])
                if bias is None:
                    k.op("act", lambda e: e.activation(out=dst[:, 0:w], in_=dst[:, 0:w], func=AF.Sin),
                         R=[dst.r()], W=[dst.r()])
                else:
                    k.op("act", lambda e: e.activation(out=dst[:, 0:w], in_=dst[:, 0:w], func=AF.Sin, bias=bias),
                         R=[dst.r(), cst.r()], W=[dst.r()])
        sincos(P["ta"], P["tb"], P["tht"][:, :], 128, [P["tht"].r()])
        dve(lambda e: e.tensor_tensor(out=P["ta"][:, :], in0=P["ta"][:, :], in1=P["rho"][:, :], op=ALU.mult),
            [P["ta"].r(), P["rho"].r()], [P["ta"].r()])
        dve(lambda e: e.tensor_tensor(out=P["tb"][:, :], in0=P["tb"][:, :], in1=P["rho"][:, :], op=ALU.mult),
            [P["tb"].r(), P["rho"].r()], [P["tb"].r()])
        dve(lambda e: e.tensor_scalar(out=P["tb"][:, :], in0=P["tb"][:, :], scalar1=-1.0, scalar2=None, op0=ALU.add),
            [P["tb"].r()], [P["tb"].r()])
        dve(lambda e: e.tensor_tensor(out=P["tc"][:, :], in0=P["are"][:, :], in1=P["are"][:, :], op=ALU.mult),
            [P["are"].r()], [P["tc"].r()])
        dve(lambda e: e.tensor_tensor(out=P["fre"][:, :], in0=P["aim"][:, :], in1=P["aim"][:, :], op=ALU.mult),
            [P["aim"].r()], [P["fre"].r()])
        dve(lambda e: e.tensor_tensor(out=P["tc"][:, :], in0=P["tc"][:, :], in1=P["fre"][:, :], op=ALU.add),
            [P["tc"].r(), P["fre"].r()], [P["tc"].r()])
        dve(lambda e: e.reciprocal(out=P["tc"][:, :], in_=P["tc"][:, :]), [P["tc"].r()], [P["tc"].r()])
        dve(lambda e: e.tensor_tensor(out=P["fre"][:, :], in0=P["tb"][:, :], in1=P["are"][:, :], op=ALU.mult),
            [P["tb"].r(), P["are"].r()], [P["fre"].r()])
        dve(lambda e: e.tensor_tensor(out=P["fim"][:, :], in0=P["ta"][:, :], in1=P["aim"][:, :], op=ALU.mult),
            [P["ta"].r(), P["aim"].r()], [P["fim"].r()])
        dve(lambda e: e.tensor_tensor(out=P["fre"][:, :], in0=P["fre"][:, :], in1=P["fim"][:, :], op=ALU.add),
            [P["fre"].r(), P["fim"].r()], [P["fre"].r()])
        dve(lambda e: e.tensor_tensor(out=P["fim"][:, :], in0=P["ta"][:, :], in1=P["are"][:, :], op=ALU.mult),
            [P["ta"].r(), P["are"].r()], [P["fim"].r()])
        dve(lambda e: e.tensor_tensor(out=P["ta"][:, :], in0=P["tb"][:, :], in1=P["aim"][:, :], op=ALU.mult),
            [P["tb"].r(), P["aim"].r()], [P["ta"].r()])
        dve(lambda e: e.tensor_tensor(out=P["fim"][:, :], in0=P["fim"][:, :], in1=P["ta"][:, :], op=ALU.subtract),
            [P["fim"].r(), P["ta"].r()], [P["fim"].r()])
        dve(lambda e: e.tensor_tensor(out=P["fre"][:, :], in0=P["fre"][:, :], in1=P["tc"][:, :], op=ALU.mult),
            [P["fre"].r(), P["tc"].r()], [P["fre"].r()])
        dve(lambda e: e.tensor_tensor(out=P["fim"][:, :], in0=P["fim"][:, :], in1=P["tc"][:, :], op=ALU.mult),
            [P["fim"].r(), P["tc"].r()], [P["fim"].r()])
        iota = self.cs("iota")
        nbk = [5, 6, 7]

        def tables(g):
            Ct_, St_, targ_, qi_, qf_ = TSET[g % 2]
            th = P["tht"][:, g:g + 1]
            dve(lambda e: e.tensor_scalar(out=targ_[:, :], in0=iota, scalar1=th, scalar2=None, op0=ALU.mult),
                [cst.r(), P["tht"].r()], [targ_.r()])
            sincos(St_, Ct_, targ_[:, :], 1024, [targ_.r()], qi=qi_, gt=qf_)
        tables(0)
        for c in range(KC):
            g0 = c * 8
            for (t_, src) in ((Bp1, "s5_b1"), (Bp2, "s5_b2"), (Cp1, "s5_c1"), (Cp2, "s5_c2")):
                k.dma("sp", t_[:, :, :], self.I[src][j][:, g0:g0 + 8, :], W=[t_.r()])
            if ns:
                k.dma("sp", hin[:, :, 0:16], self.I["st_s5"][j][:, g0:g0 + 8, :], W=[hin.r()])
            if ps == 0:
                dve(lambda e: e.memset(hin[:, :, 16:17], 0.0), [], [hin.r()], acc=True)
            else:
                k.dma("sp", hin[:, :, 16:17], self.o_s5.t[j][:, g0:g0 + 8, 16:17], R=[self.o_s5.r()], W=[hin.r()],
                      allow_slow_non_contiguous=True)
            F1b = P["fre"][:, g0:g0 + 8].unsqueeze(2).broadcast_to([128, 8, 16])
            F2b = P["fim"][:, g0:g0 + 8].unsqueeze(2).broadcast_to([128, 8, 16])
            RP = [P["fre"].r(), P["fim"].r()]
            dve(lambda e: e.tensor_tensor(out=tA[:, :, :], in0=Bp1[:, :, :], in1=F1b, op=ALU.mult), RP + [Bp1.r()], [tA.r()])
            dve(lambda e: e.tensor_tensor(out=tB[:, :, :], in0=Bp2[:, :, :], in1=F2b, op=ALU.mult), RP + [Bp2.r()], [tB.r()])
            dve(lambda e: e.scalar_tensor_tensor(out=Bb1[:, :, :], in0=tB[:, :, :], scalar=Sg(0), in1=tA[:, :, :],
                                                 op0=ALU.mult, op1=ALU.add), [tA.r(), tB.r(), cst.r()], [Bb1.r()])
            dve(lambda e: e.tensor_tensor(out=tA[:, :, :], in0=Bp2[:, :, :], in1=F1b, op=ALU.mult), RP + [Bp2.r()], [tA.r()])
            dve(lambda e: e.tensor_tensor(out=tB[:, :, :], in0=Bp1[:, :, :], in1=F2b, op=ALU.mult), RP + [Bp1.r()], [tB.r()])
            dve(lambda e: e.scalar_tensor_tensor(out=Bb2[:, :, :], in0=tA[:, :, :], scalar=Sg(1), in1=tB[:, :, :],
                                                 op0=ALU.mult, op1=ALU.add), [tA.r(), tB.r(), cst.r()], [Bb2.r()])
            for (Bb, Tc) in ((Bb1, T1c), (Bb2, T2c)):
                bk = self.bank[4]
                k.op("pe", lambda e: e.transpose(bk[:, 0:128], Bb[:, :, :].rearrange("p a b -> p (a b)"), ident),
                     R=[Bb.r(), cst.r()], W=[bk.r()])
                k.op("act", lambda e: e.copy(out=Tc[:, :], in_=bk[:, 0:128]), R=[bk.r()], W=[Tc.r()])
            for g8 in range(8):
                g = g0 + g8
                if g + 1 < 128:
                    tables(g + 1)
                Ct, St = TSET[g % 2][0], TSET[g % 2][1]
                r2 = self.rot("s5g", 2)
                lb1, lb2, c1p, c2p = LB1[r2], LB2[r2], C1p[r2], C2p[r2]
                rm = cst[:, CST["rowm"][0] + g8:CST["rowm"][0] + g8 + 1]
                dve(lambda e: e.tensor_scalar(out=lb1[:, :], in0=T1c[:, :], scalar1=rm, scalar2=None, op0=ALU.mult),
                    [T1c.r(), cst.r()], [lb1.r()])
                dve(lambda e: e.tensor_scalar(out=lb2[:, :], in0=T2c[:, :], scalar1=rm, scalar2=None, op0=ALU.mult),
                    [T2c.r(), cst.r()], [lb2.r()])
                k.op("pool", lambda e: e.memset(c1p[:, :], 0.0), W=[c1p.r()])
                k.op("pool", lambda e: e.memset(c2p[:, :], 0.0), W=[c2p.r()])
                dve(lambda e: e.tensor_scalar(out=c1p[:, g8 * 16:(g8 + 1) * 16], in0=Cp1[:, g8, :], scalar1=Sg(1),
                                              scalar2=None, op0=ALU.mult), [Cp1.r(), cst.r()], [c1p.r()])
                dve(lambda e: e.tensor_scalar(out=c2p[:, g8 * 16:(g8 + 1) * 16], in0=Cp2[:, g8, :], scalar1=-1.0,
                                              scalar2=None, op0=ALU.mult), [Cp2.r()], [c2p.r()])
                rho = P["rho"][:, g:g + 1]
                k.op("act", lambda e: e.activation(out=rhof[:, 0:1024], in_=iota, func=AF.Identity, bias=rho, scale=0.0),
                     R=[cst.r(), P["rho"].r()], W=[rhof.r()])
                if ns:
                    dve(lambda e: e.tensor_scalar(out=rhof[:, 1024:1152], in0=self.cs("mask128"), scalar1=rho,
                                                  scalar2=None, op0=ALU.mult), [cst.r(), P["rho"].r()], [rhof.r()], acc=True)
                for (t0, tw) in self.tbs:
                    i = self.rot("gu", 2)
                    b1, b2 = self.bank[i], self.bank[2 + i]
                    k.op("pe", lambda e: e.matmul(b1[:, 0:tw], lb1[:, :], xn[:, c, t0:t0 + tw], start=True, stop=True),
                         R=[lb1.r(), xn.r()], W=[b1.r()])
                    k.op("pe", lambda e: e.matmul(b2[:, 0:tw], lb2[:, :], xn[:, c, t0:t0 + tw], start=True, stop=True),
                         R=[lb2.r(), xn.r()], W=[b2.r()])
                    if t0 < 1024:
                        cv, sv = Ct[:, t0:t0 + tw], St[:, t0:t0 + tw]
                        z1, z2, mo, yo = b1[:, 0:tw], b2[:, 0:tw], m[:, t0:t0 + tw], yf[:, t0:t0 + tw]
                    else:
                        cv = Ct[:, 0:8].unsqueeze(1).broadcast_to([128, 16, 8])
                        sv = St[:, 0:8].unsqueeze(1).broadcast_to([128, 16, 8])
                        r3 = lambda a: a.rearrange("p (b t) -> p b t", b=16)
                        z1, z2, mo, yo = r3(b1[:, 0:128]), r3(b2[:, 0:128]), r3(m[:, 1024:1152]), r3(yf[:, 1024:1152])
                    dve(lambda e: e.tensor_tensor(out=mo, in0=z1, in1=cv, op=ALU.mult), [b1.r(), Ct.r()], [m.r()], acc=True)
                    dve(lambda e: e.tensor_tensor(out=yo, in0=z2, in1=sv, op=ALU.mult), [b2.r(), St.r()], [yf.r()], acc=True)
                    dve(lambda e: e.tensor_tensor(out=mo, in0=mo, in1=yo, op=ALU.add), [m.r(), yf.r()], [m.r()], acc=True)
                if ns:
                    ms0 = m[:, 1024:1152].rearrange("p (b t) -> p b t", b=16)[:, :, 0]
                    dve(lambda e: e.scalar_tensor_tensor(out=ms0, in0=hin[:, g8, 0:16], scalar=rho, in1=ms0,
                                                         op0=ALU.mult, op1=ALU.add),
                        [hin.r(), m.r(), P["rho"].r()], [m.r()], acc=True)
                dve(lambda e: e.tensor_tensor_scan(out=G[:, 0:TB], data0=rhof[:, 0:TB], data1=m[:, 0:TB],
                                                   initial=hin[:, g8, 16:17], op0=ALU.mult, op1=ALU.add),
                    [rhof.r(), m.r(), hin.r()], [G.r()])
                for (Gb, tab) in ((G1b, Ct), (G2b, St)):
                    k.op("pool", lambda e: e.tensor_tensor(out=Gb[:, 0:1024], in0=G[:, 0:1024], in1=tab[:, 0:1024],
                                                            op=ALU.mult), R=[G.r(), tab.r()], W=[Gb.r()])
                    if ns:
                        r3 = lambda a: a.rearrange("p (b t) -> p b t", b=16)
                        k.op("pool", lambda e: e.tensor_tensor(
                            out=r3(Gb[:, 1024:1152]), in0=r3(G[:, 1024:1152]),
                            in1=tab[:, 0:8].unsqueeze(1).broadcast_to([128, 16, 8]), op=ALU.mult),
                            R=[G.r(), tab.r()], W=[Gb.r()], acc=True)
                for i, (t0, tw) in enumerate(self.tbs):
                    by = self.bank[nbk[i]]
                    k.op("pe", lambda e: e.matmul(by[:, 0:tw], c1p[:, :], G1b[:, t0:t0 + tw], start=(g8 == 0), stop=False),
                         R=[c1p.r(), G1b.r()], W=[by.r()])
                    k.op("pe", lambda e: e.matmul(by[:, 0:tw], c2p[:, :], G2b[:, t0:t0 + tw], start=False, stop=(g8 == 7)),
                         R=[c2p.r(), G2b.r()], W=[by.r()])
                ncol = 17 if ns else 1
                cols = []
                if ns:
                    Gs7 = G[:, 1024:1152].rearrange("p (b t) -> p b t", b=16)[:, :, 7]
                    dve(lambda e: e.tensor_scalar(out=ftmp[:, 0, 0:16], in0=Gs7, scalar1=Ct[:, 7:8], scalar2=None,
                                                  op0=ALU.mult), [G.r(), Ct.r()], [ftmp.r()], acc=True)
                    dve(lambda e: e.tensor_scalar(out=ftmp[:, 1, 0:16], in0=Gs7, scalar1=St[:, 7:8], scalar2=None,
                                                  op0=ALU.mult), [G.r(), St.r()], [ftmp.r()], acc=True)
                dve(lambda e: e.tensor_tensor(out=ftmp[:, 0, 16:17], in0=G[:, 1023:1024], in1=Ct[:, 1023:1024],
                                              op=ALU.mult), [G.r(), Ct.r()], [ftmp.r()], acc=True)
                dve(lambda e: e.tensor_tensor(out=ftmp[:, 1, 16:17], in0=G[:, 1023:1024], in1=St[:, 1023:1024],
                                              op=ALU.mult), [G.r(), St.r()], [ftmp.r()], acc=True)
                c_lo = 0 if ns else 16
                bk = self.bank[4]
                k.op("pe", lambda e: e.matmul(bk[:, c_lo:17], ident, ftmp[:, 0, c_lo:17], start=True, stop=False),
                     R=[cst.r(), ftmp.r()], W=[bk.r()])
                k.op("pe", lambda e: e.matmul(bk[:, c_lo:17], self.cs("J"), ftmp[:, 1, c_lo:17], start=False, stop=True),
                     R=[cst.r(), ftmp.r()], W=[bk.r()])
                k.op("act", lambda e: e.copy(out=hfin[:, g8, c_lo:17], in_=bk[:, c_lo:17]), R=[bk.r()], W=[hfin.r()], acc=True)
            k.dma("sp", self.o_s5.t[j][:, g0:g0 + 8, c_lo:17], hfin[:, :, c_lo:17], R=[hfin.r()], W=[self.o_s5.r()],
                  allow_slow_non_contiguous=True)
            for i, (t0, tw) in enumerate(self.tbs):
                by = self.bank[nbk[i]]
                dve(lambda e: e.scalar_tensor_tensor(out=yf[:, t0:t0 + tw], in0=xn[:, c, t0:t0 + tw],
                                                     scalar=self.vcol("d_c%d" % j, c), in1=by[:, 0:tw],
                                                     op0=ALU.mult, op1=ALU.add),
                    [xn.r(), by.r(), vec.r()], [yf.r()])
            zt = Tile(xn[:, c, :], "xnc")
            zt.reg = xn.r()
            self.gelu_(zt, yf, gt, TB)
        k.barrier()
        Wg = self.W["w_glu_c"][j]
        for s in range(D // 256):
            sg, vg = self.load_slab(Wg, KC, s * 256, 256)
            for oc2 in range(2):
                oc = s * 2 + oc2
                for (t0, tw) in self.tbs:
                    bg = self.bank[self.rot("pj", 2)]
                    for kc in range(KC):
                        k.op("pe", lambda e: e.matmul(bg[:, 0:tw], vg[:, kc, oc2 * 128:(oc2 + 1) * 128],
                                                       xn[:, kc, t0:t0 + tw], start=(kc == 0), stop=(kc == KC - 1)),
                             R=[sg.r(), xn.r()], W=[bg.r()])
                    t1 = yf
                    k.op("act", lambda e: e.activation(out=t1[:, 0:tw], in_=bg[:, 0:tw], func=AF.Sigmoid,
                                                       bias=self.vcol("b_glu_c%d" % j, oc)),
                         R=[bg.r(), vec.r()], W=[t1.r()])
                    dve(lambda e: e.tensor_tensor(out=t1[:, 0:tw], in0=t1[:, 0:tw], in1=xn[:, oc, t0:t0 + tw], op=ALU.mult),
                        [t1.r(), xn.r()], [t1.r()])
                    dve(lambda e: e.tensor_tensor(out=x[:, oc, t0:t0 + tw], in0=x[:, oc, t0:t0 + tw], in1=t1[:, 0:tw],
                                                  op=ALU.add), [t1.r(), x.r(oc)], [x.r(oc)])


def build_inputs(inp, core):
    s = core % 4
    xs = np.asarray(inp["x_sample"], np.float32)[core * NS:(core + 1) * NS].reshape(NS * TS, D)
    xp = np.asarray(inp["x_prompt"], np.float32)[s]
    xT = np.ascontiguousarray(np.concatenate([xp, xs], 0).T)
    pp = np.asarray(inp["p_prompt"], np.float32)[:, s]
    psm = np.asarray(inp["p_sample"], np.float32)[:, core * NS:(core + 1) * NS].reshape(4, NS * TS, 256)
    pT = np.ascontiguousarray(np.concatenate([pp, psm], 1).transpose(0, 2, 1))
    return {"xT": xT, "pT": pT}


_PROG = {}


def kernel(**inputs):
    enable = inputs.pop("_enable", ("ffn", "ple", "mix"))
    if enable not in _PROG:
        _PROG[enable] = Prog(enable)
    prog = _PROG[enable]
    vec = pack_vec(inputs)
    shared = {"vec": vec, "cst": pack_cst()}
    for nm in prog.W:
        shared[nm] = np.ascontiguousarray(np.asarray(inputs[nm], np.float32))
    in_maps = []
    ncores = int(os.environ.get('KCORES', '8'))
    for c in range(ncores):
        m = dict(shared)
        m.update(build_inputs(inputs, c))
        m.update(pack_states(inputs, c))
        in_maps.append(m)
    res = run_bass_kernel_spmd(prog.nc, in_maps, core_ids=list(range(ncores)))
    R = list(res.results) + [res.results[0]] * (8 - ncores)
    global _LAST
    _LAST = R
    y_prompt = np.stack([R[s]["yT"][:, 0:2048].T for s in range(4)])
    y_sample = np.concatenate([R[c]["yT"][:, 2048:].T.reshape(NS, TS, D) for c in range(8)], 0)
    A = np.ascontiguousarray

    def per_layer(fn):
        return np.stack([fn(j) for j in range(2)])
    pw = per_layer(lambda j: np.stack([R[s]["o_wkv"][j][:, 16, :].reshape(16, 64, 64) for s in range(4)]))
    psh = per_layer(lambda j: np.stack([R[s]["o_shift"][j][:, :, 16].T.reshape(-1)[:3360] for s in range(4)]))
    ph = per_layer(lambda j: np.stack([R[s]["o_h"][j][:, :, 16].T.reshape(1024) for s in range(4)]))
    pc = per_layer(lambda j: np.stack([R[s]["o_conv"][j][:, :, 16, :].transpose(2, 1, 0).reshape(3, 1024)
                                       for s in range(4)]))
    pre = per_layer(lambda j: np.stack([R[s]["o_s5"][j][0:64, :, 16].T for s in range(4)]))
    pim = per_layer(lambda j: np.stack([R[s]["o_s5"][j][64:128, :, 16].T for s in range(4)]))
    sw = per_layer(lambda j: np.concatenate([
        R[c]["o_wkv"][j][:, 0:16, :].reshape(16, 8, 16, 8, 64).transpose(2, 0, 1, 3, 4).reshape(16, 16, 64, 64)
        for c in range(8)], 0))
    ssh = per_layer(lambda j: np.concatenate([
        R[c]["o_shift"][j][:, :, 0:16].transpose(2, 1, 0).reshape(16, -1)[:, :3360] for c in range(8)], 0))
    sh = per_layer(lambda j: np.concatenate([
        R[c]["o_h"][j][:, :, 0:16].transpose(2, 1, 0).reshape(16, 1024) for c in range(8)], 0))
    sc = per_layer(lambda j: np.concatenate([
        R[c]["o_conv"][j][:, :, 0:16, :].transpose(2, 3, 1, 0).reshape(16, 3, 1024) for c in range(8)], 0))
    sre = per_layer(lambda j: np.concatenate([R[c]["o_s5"][j][0:64, :, 0:16].transpose(2, 1, 0) for c in range(8)], 0))
    sim = per_layer(lambda j: np.concatenate([R[c]["o_s5"][j][64:128, :, 0:16].transpose(2, 1, 0) for c in range(8)], 0))
    outs = (y_prompt, y_sample, pw, psh, ph, pc, pre, pim, sw, ssh, sh, sc, sre, sim)
    return tuple(A(o.astype(np.float32)) for o in outs)
```

```python
import contextlib
import numpy as np
import concourse.bass as bass
import concourse.mybir as mybir
from concourse.bass_utils import run_bass_kernel_spmd

F32 = mybir.dt.float32
BF16 = mybir.dt.bfloat16
I32 = mybir.dt.int32
ALU = mybir.AluOpType
AF = mybir.ActivationFunctionType
AX = mybir.AxisListType
import os as _os
EPOCH_N = int(_os.environ.get("KEPOCH", "30000"))
SAME_ENGINE_WAITS = int(_os.environ.get("KSEW", "1"))


class Reg:
    __slots__ = ("name", "writes", "reads", "dsem", "dcnt")

    def __init__(self, name):
        self.name = name
        self.writes = []
        self.reads = []
        self.dsem = None
        self.dcnt = 0


class Tile:
    def __init__(self, t, name):
        self.t = t
        self.name = name
        self.reg = Reg(name)
        self.subs = {}

    def __getitem__(self, idx):
        return self.t[idx]

    def r(self, key=None):
        if key is None:
            return self.reg
        if key not in self.subs:
            self.subs[key] = Reg("%s/%s" % (self.name, key))
        return self.subs[key]

    def all(self):
        return [self.reg] + list(self.subs.values())


class K:
    def __init__(self):
        self.nc = bass.Bass("TRN2", target_bir_lowering=False)
        nc = self.nc
        self.es = contextlib.ExitStack()
        self.eng = {"pe": nc.tensor, "act": nc.scalar, "dve": nc.vector,
                    "pool": nc.gpsimd, "sp": nc.sync}
        self.sem = {}
        self.cnt = {}
        self.epoch = {}
        for e in ("pe", "act", "dve", "pool"):
            self.sem[(e, 0)] = self.es.enter_context(nc.semaphore("s_" + e))
            self.cnt[e] = 0
            self.epoch[e] = 0
        self.waited = {e: {} for e in self.eng}
        self.sew = True
        self.nsem = 4
        self.ninst = 0
        self.dregs = []

    def sb(self, name, shape, dt=F32):
        t = self.es.enter_context(self.nc.sbuf_tensor(name, list(shape), dt))
        return Tile(t, name)

    def ps(self, name, shape, dt=F32):
        t = self.es.enter_context(self.nc.psum_tensor(name, list(shape), dt))
        return Tile(t, name)

    def dram(self, name, shape, dt=F32, kind="Internal"):
        t = self.nc.dram_tensor(name, list(shape), dt, kind=kind)
        return Tile(t.ap(), name)

    def _need(self, e, ev):
        key, val, src = ev
        if src == e and (e == "pe" or not SAME_ENGINE_WAITS or not self.sew):
            return
        w = self.waited[e]
        if w.get(key, 0) >= val:
            return
        sem = self.sem[key] if isinstance(key, tuple) else key.dsem
        self.eng[e].wait_ge(sem, val)
        w[key] = val
        self.ninst += 1

    def _deps(self, e, R, W, acc=False, dma_fill=False):
        for r in R:
            for ev in r.writes:
                self._need(e, ev)
        for r in W:
            for ev in r.writes:
                if dma_fill and ev[2] == "dma":
                    continue
                if ev[2] != e:
                    self._need(e, ev)
            for ev in r.reads:
                if ev[2] != e:
                    self._need(e, ev)

    def _mark(self, ev, R, W, acc=False):
        for r in R:
            r.reads = [x for x in r.reads if x[0] != ev[0]] + [ev]
        for r in W:
            if acc:
                r.writes = [x for x in r.writes if x[0] != ev[0]] + [ev]
            else:
                r.writes = [ev]
            r.reads = []

    def op(self, e, fn, R=(), W=(), acc=False, sew=True):
        self.sew = sew
        self._deps(e, R, W)
        self.sew = True
        ins = fn(self.eng[e])
        if self.cnt[e] >= EPOCH_N:
            self.epoch[e] += 1
            self.cnt[e] = 0
            self.sem[(e, self.epoch[e])] = self.es.enter_context(
                self.nc.semaphore("s_%s_%d" % (e, self.epoch[e])))
        self.cnt[e] += 1
        key = (e, self.epoch[e])
        ins.then_inc(self.sem[key], 1)
        ev = (key, self.cnt[e], e)
        self._mark(ev, R, W, acc)
        self.ninst += 1
        return ins

    def dma(self, q, out, in_, R=(), W=(), acc=True, **kw):
        self._deps(q, R, W, dma_fill=acc)
        d = W[0]
        if d.dsem is None:
            d.dsem = self.es.enter_context(self.nc.semaphore("d%d" % self.nsem))
            self.nsem += 1
            self.dregs.append(d)
        ins = self.eng[q].dma_start(out=out, in_=in_, **kw)
        d.dcnt += 16
        ins.then_inc(d.dsem, 16)
        ev = (d, d.dcnt, "dma")
        self._mark(ev, R, W, acc)
        self.ninst += 1
        return ins

    def barrier(self):
        evs = [((e, self.epoch[e]), self.cnt[e], e) for e in ("pe", "act", "dve", "pool")]
        evs += [(d, d.dcnt, "dma") for d in self.dregs]
        for e in ("pe", "act", "dve", "pool", "sp"):
            for ev in evs:
                if ev[1] > 0 and not (ev[2] == e and e == "pe"):
                    w = self.waited[e]
                    if w.get(ev[0], 0) < ev[1]:
                        sem = self.sem[ev[0]] if isinstance(ev[0], tuple) else ev[0].dsem
                        self.eng[e].wait_ge(sem, ev[1])
                        w[ev[0]] = ev[1]
                        self.ninst += 1

    def finish(self, regs):
        for r in regs:
            for ev in r.writes:
                self._need("sp", ev)

    def close(self):
        self.es.close()

D = 2048
DFF = 5632
KC = 16
NPT = 1024
NS = 16
TS = 8
NTOK = 2048 + NS * TS
EPS = 1e-6
import os
DEPTH = int(os.environ.get('KDEPTH', '4'))
NPASS = int(os.environ.get('KNPASS', '2'))


def vec_layout():
    lay = {}
    off = [0]

    def add(name, n):
        lay[name] = (off[0], n)
        off[0] += n
    for l in range(4):
        for nm in ("norm_ffn1", "norm_mix", "norm_ffn2", "norm_ple"):
            add("%s%d" % (nm, l), 16)
    add("final_norm", 16)
    for j in range(2):
        add("d_c%d" % j, 16)
        add("b_glu_c%d" % j, 16)
    for j in range(2):
        for nm in ("w0_a", "a0_a", "kk_a", "ka_a", "rk_a", "lnx_g", "lnx_b", "conv_b_b",
                   "ba_b", "bx_b", "lam_b", "mu_r", "mu_k", "mu_v"):
            add("%s%d" % (nm, j), 8)
        for q in range(4):
            add("conv_w%d_%d" % (q, j), 8)
        add("mu_x%d" % j, 4)
    return lay, off[0]


def pack_vec(inp):
    lay, n = vec_layout()
    v = np.zeros((128, n), np.float32)

    def put(name, arr):
        o, c = lay[name]
        a = np.asarray(arr, np.float32).reshape(-1)
        v[:, o:o + c] = a.reshape(c, 128).T
    for l in range(4):
        for nm in ("norm_ffn1", "norm_mix", "norm_ffn2", "norm_ple"):
            put("%s%d" % (nm, l), inp[nm][l])
    put("final_norm", inp["final_norm"])
    for j in range(2):
        put("d_c%d" % j, inp["d_c"][j])
        put("b_glu_c%d" % j, inp["b_glu_c"][j])
        for nm in ("w0_a", "a0_a", "kk_a", "ka_a", "rk_a", "lnx_g", "lnx_b", "conv_b_b",
                   "ba_b", "bx_b", "lam_b"):
            put("%s%d" % (nm, j), inp[nm][j])
        mu = np.asarray(inp["mu_a"][j], np.float32)
        put("mu_r%d" % j, mu[0:1024])
        put("mu_k%d" % j, mu[1024:2048])
        put("mu_v%d" % j, mu[2048:3072])
        for q in range(4):
            put("conv_w%d_%d" % (q, j), inp["conv_w_b"][j][q])
        o, _ = lay["mu_x%d" % j]
        v[:, o] = mu[3072:3200]
        v[:, o + 2] = mu[3200:3328]
        v[0:32, o + 3] = mu[3328:3360]
    return v


CST = {"ident": (0, 128), "iota": (128, 1024), "mask128": (1152, 128), "J": (1280, 128),
       "blk": (1408, 128), "sgn": (1536, 4), "rep": (1540, 128), "rowm": (1668, 8), "one": (1676, 1), "negpi": (1677, 1), "halfpi": (1678, 1),
       "iotaAB": (1680, 64)}
NCST = 1744
GN_EPS = 64e-5
W_A = 1024
COLS_A = 3360


def pack_cst():
    c = np.zeros((128, NCST), np.float32)

    def put(nm, arr):
        o, n = CST[nm]
        c[:arr.shape[0], o:o + n] = arr
    put("ident", np.eye(128, dtype=np.float32))
    put("iota", np.broadcast_to(np.arange(1, 1025, dtype=np.float32)[None, :], (128, 1024)))
    m = np.ones(128, np.float32)
    m[0::8] = 0.0
    put("mask128", np.broadcast_to(m[None, :], (128, 128)))
    J = np.zeros((128, 128), np.float32)
    for p in range(64):
        J[64 + p, p] = -1.0
        J[p, 64 + p] = 1.0
    put("J", J)
    blk = np.zeros((128, 128), np.float32)
    blk[0:64, 0:64] = 1.0
    blk[64:128, 64:128] = 1.0
    put("blk", blk)
    sg = np.zeros((128, 4), np.float32)
    sg[0:64, 0] = -1.0
    sg[64:, 0] = 1.0
    sg[0:64, 1] = 1.0
    sg[64:, 1] = -1.0
    sg[:, 2] = -1.0
    sg[:, 3] = 1.0
    put("sgn", sg)
    rep = np.zeros((16, 128), np.float32)
    for hh in range(16):
        rep[hh, hh * 8:hh * 8 + 8] = 1.0
    put("rep", rep)
    rm = np.zeros((128, 8), np.float32)
    for p in range(128):
        rm[p, p // 16] = 1.0
    put("rowm", rm)
    put("one", np.ones((128, 1), np.float32))
    put("negpi", np.full((128, 1), -np.pi, np.float32))
    put("halfpi", np.full((128, 1), 0.5 * np.pi, np.float32))
    iab = np.concatenate([32.0 * np.arange(32), 1.0 + np.arange(32)]).astype(np.float32)
    put("iotaAB", np.broadcast_to(iab[None, :], (128, 64)))
    return c


def pack_states(inp, core):
    f = lambda a: np.ascontiguousarray(np.asarray(a, np.float32))
    sl = slice(core * NS, (core + 1) * NS)
    o = {}
    o["st_h"] = f(np.asarray(inp["state_b_h"])[:, sl].reshape(2, NS, 8, 128).transpose(0, 3, 2, 1))
    o["st_conv"] = f(np.asarray(inp["state_b_conv"])[:, sl].reshape(2, NS, 3, 8, 128).transpose(0, 4, 3, 1, 2))
    re = np.asarray(inp["state_c_re"])[:, sl].transpose(0, 3, 2, 1)
    im = np.asarray(inp["state_c_im"])[:, sl].transpose(0, 3, 2, 1)
    o["st_s5"] = f(np.concatenate([re, im], 1))
    sh = np.asarray(inp["state_a_shift"], np.float32)[:, sl]
    shp = np.zeros((2, NS, 27 * 128), np.float32)
    shp[:, :, 0:3360] = sh
    o["st_shift"] = f(shp.reshape(2, NS, 27, 128).transpose(0, 3, 2, 1))
    wk = np.asarray(inp["state_a_wkv"], np.float32)[:, sl].reshape(2, NS, 16, 8, 8, 64)
    o["st_wkv"] = f(wk.transpose(0, 2, 3, 1, 4, 5).reshape(2, 128, NS, 512))
    for nm in ("w2_a", "a2_a", "g2_a"):
        o[nm] = f(inp[nm])
    for nm in ("wa_b", "wx_b"):
        w = np.asarray(inp[nm], np.float32)
        bd = np.zeros((2, 8, 128, 128), np.float32)
        for c in range(8):
            bd[:, c, 0:64, 0:64] = w[:, 2 * c]
            bd[:, c, 64:128, 64:128] = w[:, 2 * c + 1]
        o[nm + "d"] = f(bd.transpose(0, 2, 1, 3))
    are = np.asarray(inp["a_re_c"], np.float32).transpose(0, 2, 1)
    aim = np.asarray(inp["a_im_c"], np.float32).transpose(0, 2, 1)
    o["s5_are"] = f(np.concatenate([are, are], 1))
    o["s5_aim"] = f(np.concatenate([aim, aim], 1))
    o["s5_ldt"] = f(np.broadcast_to(np.asarray(inp["log_dt_c"], np.float32)[:, None, :], (2, 128, 128)))
    bre = np.asarray(inp["b_re_c"], np.float32).transpose(0, 2, 1, 3)
    bim = np.asarray(inp["b_im_c"], np.float32).transpose(0, 2, 1, 3)
    o["s5_b1"] = f(np.concatenate([bre, bim], 1))
    o["s5_b2"] = f(np.concatenate([bim, bre], 1))
    cre = np.asarray(inp["c_re_c"], np.float32).transpose(0, 3, 1, 2)
    cim = np.asarray(inp["c_im_c"], np.float32).transpose(0, 3, 1, 2)
    o["s5_c1"] = f(np.concatenate([cre, cim], 1))
    o["s5_c2"] = f(np.concatenate([cim, cre], 1))
    return o


class Prog:
    def __init__(self, enable=("ffn", "ple", "mix")):
        self.enable = enable
        k = self.k = K()
        nc = self.nc = k.nc
        self.lay, self.nv = vec_layout()

        def din(name, shape):
            return nc.dram_tensor(name, list(shape), F32, kind="ExternalInput").ap()
        self.xT = din("xT", [D, NTOK])
        self.pT = din("pT", [4, 256, NTOK])
        self.vecd = din("vec", [128, self.nv])
        self.W = {}
        for nm, shp in (("ffn1_wg", [4, D, DFF]), ("ffn1_wu", [4, D, DFF]), ("ffn1_wd", [4, DFF, D]),
                        ("ffn2_wg", [4, D, DFF]), ("ffn2_wu", [4, D, DFF]), ("ffn2_wd", [4, DFF, D]),
                        ("ple_gate", [4, D, D]), ("ple_proj", [4, 256, D])):
            self.W[nm] = din(nm, shp)
        for nm, shp in (("w_in_ab", [2, D, 5408]), ("w_out_ab", [2, D, D]), ("w_glu_c", [2, D, D])):
            self.W[nm] = din(nm, shp)
        self.cstd = din("cst", [128, NCST])
        self.I = {}
        for nm, shp in (("st_h", [2, 128, 8, 16]), ("st_conv", [2, 128, 8, 16, 3]), ("st_s5", [2, 128, 128, 16]),
                        ("wa_bd", [2, 128, 8, 128]), ("wx_bd", [2, 128, 8, 128]),
                        ("s5_are", [2, 128, 128]), ("s5_aim", [2, 128, 128]), ("s5_ldt", [2, 128, 128]),
                        ("s5_b1", [2, 128, 128, 16]), ("s5_b2", [2, 128, 128, 16]),
                        ("s5_c1", [2, 128, 128, 16]), ("s5_c2", [2, 128, 128, 16])):
            self.I[nm] = din(nm, shp)
        for nm, shp in (("w2_a", [2, 64, 1024]), ("a2_a", [2, 64, 1024]), ("g2_a", [2, 160, 1024]),
                        ("st_shift", [2, 128, 27, 16]), ("st_wkv", [2, 128, 16, 512])):
            self.I[nm] = din(nm, shp)
        self.o_shift = k.dram("o_shift", [2, 128, 27, 17], F32, kind="ExternalOutput")
        self.o_wkv = k.dram("o_wkv", [2, 128, 17, 512], F32, kind="ExternalOutput")
        self.Zs = k.dram("Zs", [1152, 16, 5, 64], F32)
        self.Vs = k.dram("Vs", [1152, 1024], F32)
        self.Ys = k.dram("Ys", [1152, 1024], F32)
        self.Bon = k.dram("Bon", [8, 128, 1152], F32)
        self.yT = k.dram("yT", [D, NTOK], F32, kind="ExternalOutput")
        self.o_h = k.dram("o_h", [2, 128, 8, 17], F32, kind="ExternalOutput")
        self.o_conv = k.dram("o_conv", [2, 128, 8, 17, 3], F32, kind="ExternalOutput")
        self.o_s5 = k.dram("o_s5", [2, 128, 128, 17], F32, kind="ExternalOutput")
        self.outs = [self.yT, self.o_h, self.o_conv, self.o_s5, self.o_shift, self.o_wkv]

        self.x = k.sb("x", [128, KC, 1152], F32)
        self.xn = k.sb("xn", [128, KC, 1152], BF16)
        self.rstd = k.sb("rstd", [128, 1152], F32)
        self.sq = [k.sb("sq%d" % i, [128, 1152], BF16) for i in range(2)]
        self.vec = k.sb("vecs", [128, self.nv], F32)
        self.ones_bf = k.sb("ones_bf", [128, 128], BF16)
        self.AW = 18688
        self._avc = {}
        self.A = k.sb("arena", [128, self.AW], F32)
        self.slab = [self.av("slab%d" % i, i * 2048, 2048, BF16) for i in range(6)]
        self.h = self.av("hbuf", 12288, 1152, BF16, a=2)
        self.t1 = [self.av("t1_%d" % i, 13440 + i * 512, 512) for i in range(2)]
        self.t2 = [self.av("t2_%d" % i, 14464 + i * 512, 512) for i in range(2)]
        self.pt = self.av("ptile", 15488, 1152, BF16, a=2)
        self.pproj = self.av("pproj", 16640, 2048, BF16)
        self.ytmp = [self.av("ytmp%d" % i, i * 1152, 1152) for i in range(2)]
        self.nslab = 6
        self.bank = [k.ps("bank%d" % i, [128, 512], F32) for i in range(8)]
        self.lru_small = (k.sb("l_hst", [128, 8, 17], F32), k.sb("l_cvo", [128, 8, 17, 3], F32),
                          k.sb("l_h0", [128, 8, 17], F32), k.sb("l_cv0", [128, 8, 17, 3], F32))
        self.lru_nsp = k.sb("l_nsp", [128, 16], F32)
        self.sg1b = k.sb("r_sg1b", [128, 1152], BF16)
        self.ybuf = self.av("ybuf", 4096, 4608, BF16, a=8)
        self.rr = {}

        k.dma("sp", self.vec[:], self.vecd, W=[self.vec.r()])
        self.cst = k.sb("cst_sb", [128, NCST], F32)
        k.dma("sp", self.cst[:], self.cstd, W=[self.cst.r()])
        k.op("dve", lambda e: e.memset(self.ones_bf[:], 1.0), W=[self.ones_bf.r()])
        self.epst = k.sb("epst", [128, 2], F32)
        k.op("dve", lambda e: e.memset(self.epst[:], EPS), W=[self.epst.r()])

        for ps in range(NPASS):
            self.run_pass(ps)
        k.finish([o.r() for o in self.outs])
        k.close()

    def av(self, name, off, n, dt=F32, a=None):
        assert off + n <= self.AW, (name, off, n)
        key = (name, off, n, str(dt), a)
        if key in self._avc:
            return self._avc[key]
        ap = self.A.t[:, off:off + n]
        if dt != F32:
            ap = ap.bitcast(dt)
        if a is not None:
            ap = ap.rearrange("p (a b) -> p a b", a=a)
        self._avc[key] = Tile(ap, name)
        return self._avc[key]

    def rot(self, name, n):
        i = self.rr.get(name, 0)
        self.rr[name] = i + 1
        return i % n

    def vcol(self, name, c):
        o, n = self.lay[name]
        return self.vec[:, o + c:o + c + 1]

    def run_pass(self, ps):
        k = self.k
        self.ps_ = ps
        self.TB = TB = 1024 if ps == 0 else 1152
        self.g0 = 0 if ps == 0 else 1024
        self.tbs = [(0, 512), (512, 512)] + ([(1024, 128)] if ps == 1 else [])
        x = self.x
        xTv = self.xT.rearrange("(c p) t -> p c t", p=128)
        k.dma("sp", x[:, :, 0:TB], xTv[:, :, self.g0:self.g0 + TB], W=[x.r(c) for c in range(KC)])
        for l in range(DEPTH):
            if "ffn" in self.enable:
                self.ffn(l, 1)
            if "mix" in self.enable:
                self.mixer(l)
            if "ffn" in self.enable:
                self.ffn(l, 2)
            if "ple" in self.enable:
                self.ple(l)
        k.barrier()
        self.norm("final_norm", final=True)
        k.barrier()

    def norm(self, gname, final=False):
        k, x, xn, TB = self.k, self.x, self.xn, self.TB
        nb = [5, 6, 7]
        for c in range(KC):
            s = self.sq[c % 2]
            k.op("act", lambda e: e.activation(out=s[:, 0:TB], in_=x[:, c, 0:TB], func=AF.Square),
                 R=[x.r(c)], W=[s.r()])
            for i, (t0, tw) in enumerate(self.tbs):
                b = self.bank[nb[i]]
                k.op("pe", lambda e: e.matmul(b[:, 0:tw], self.ones_bf[:], s[:, t0:t0 + tw],
                                               start=(c == 0), stop=(c == KC - 1)),
                     R=[self.ones_bf.r(), s.r()], W=[b.r()])
        for i, (t0, tw) in enumerate(self.tbs):
            b = self.bank[nb[i]]
            k.op("act", lambda e: e.activation(out=self.rstd[:, t0:t0 + tw], in_=b[:, 0:tw], func=AF.Sqrt,
                                               bias=self.epst[:, 0:1], scale=1.0 / D),
                 R=[b.r(), self.epst.r()], W=[self.rstd.r()])
        k.op("dve", lambda e: e.reciprocal(out=self.rstd[:, 0:TB], in_=self.rstd[:, 0:TB]),
             R=[self.rstd.r()], W=[self.rstd.r()])
        yTv = self.yT.t.rearrange("(c p) t -> p c t", p=128)
        for c in range(KC):
            if final:
                o = self.ytmp[c % 2]
                k.op("dve", lambda e: e.scalar_tensor_tensor(out=o[:, 0:TB], in0=x[:, c, 0:TB],
                                                              scalar=self.vcol(gname, c), in1=self.rstd[:, 0:TB],
                                                              op0=ALU.mult, op1=ALU.mult),
                     R=[x.r(c), self.rstd.r(), self.vec.r()], W=[o.r()])
                k.dma("sp", yTv[:, c, self.g0:self.g0 + TB], o[:, 0:TB], R=[o.r()], W=[self.yT.r()])
            else:
                k.op("dve", lambda e: e.scalar_tensor_tensor(out=xn[:, c, 0:TB], in0=x[:, c, 0:TB],
                                                              scalar=self.vcol(gname, c), in1=self.rstd[:, 0:TB],
                                                              op0=ALU.mult, op1=ALU.mult),
                     R=[x.r(c), self.rstd.r(), self.vec.r()], W=[xn.r()])

    def load_slab(self, W2d, kchunks, c0, cw, rows0=0):
        k = self.k
        s = self.slab[self.rot("slab", self.nslab)]
        src = W2d[rows0:rows0 + kchunks * 128, :].rearrange("(kc p) f -> p kc f", p=128)[:, :, c0:c0 + cw]
        dst = s[:, 0:kchunks * cw].rearrange("p (kc f) -> p kc f", kc=kchunks)
        step = 4 if kchunks > 4 else kchunks
        for q in range(0, kchunks, step):
            k.dma("pool", dst[:, q:q + step, :], src[:, q:q + step, :], W=[s.r()])
        return s, dst

    def ffn(self, l, which):
        k, x, xn, h = self.k, self.x, self.xn, self.h
        self.norm("norm_ffn%d%d" % (which, l))
        Wg = self.W["ffn%d_wg" % which][l]
        Wu = self.W["ffn%d_wu" % which][l]
        Wd = self.W["ffn%d_wd" % which][l]
        for s in range(DFF // 256):
            sg, vg = self.load_slab(Wg, KC, s * 256, 256)
            su, vu = self.load_slab(Wu, KC, s * 256, 256)
            sd, vd = self.load_slab(Wd, 2, 0, D, rows0=s * 256)
            for fc in range(2):
                for (t0, tw) in self.tbs:
                    i = self.rot("gu", 2)
                    bg, bu = self.bank[i], self.bank[2 + i]
                    for kc in range(KC):
                        k.op("pe", lambda e: e.matmul(bg[:, 0:tw], vg[:, kc, fc * 128:(fc + 1) * 128],
                                                       xn[:, kc, t0:t0 + tw], start=(kc == 0), stop=(kc == KC - 1)),
                             R=[sg.r(), xn.r()], W=[bg.r()])
                    for kc in range(KC):
                        k.op("pe", lambda e: e.matmul(bu[:, 0:tw], vu[:, kc, fc * 128:(fc + 1) * 128],
                                                       xn[:, kc, t0:t0 + tw], start=(kc == 0), stop=(kc == KC - 1)),
                             R=[su.r(), xn.r()], W=[bu.r()])
                    t1 = self.t1[self.rot("t1", 2)]
                    k.op("act", lambda e: e.activation(out=t1[:, 0:tw], in_=bg[:, 0:tw], func=AF.Silu),
                         R=[bg.r()], W=[t1.r()])
                    k.op("dve", lambda e: e.tensor_tensor(out=h[:, fc, t0:t0 + tw], in0=t1[:, 0:tw],
                                                           in1=bu[:, 0:tw], op=ALU.mult),
                         R=[t1.r(), bu.r()], W=[h.r(fc)])
            for dc in range(KC):
                for (t0, tw) in self.tbs:
                    bd = self.bank[4 + self.rot("dn", 2)]
                    for fc in range(2):
                        k.op("pe", lambda e: e.matmul(bd[:, 0:tw], vd[:, fc, dc * 128:(dc + 1) * 128],
                                                       h[:, fc, t0:t0 + tw], start=(fc == 0), stop=(fc == 1)),
                             R=[sd.r(), h.r(fc)], W=[bd.r()])
                    k.op("dve", lambda e: e.scalar_tensor_tensor(out=x[:, dc, t0:t0 + tw], in0=bd[:, 0:tw],
                                                                  scalar=0.5, in1=x[:, dc, t0:t0 + tw],
                                                                  op0=ALU.mult, op1=ALU.add),
                         R=[bd.r(), x.r(dc)], W=[x.r(dc)])

    def ple(self, l):
        k, x, xn, TB = self.k, self.x, self.xn, self.TB
        self.norm("norm_ple%d" % l)
        pt = self.pt
        k.dma("pool", pt[:, :, 0:TB],
              self.pT[l].rearrange("(c p) t -> p c t", p=128)[:, :, self.g0:self.g0 + TB], W=[pt.r()])
        sp_ = self.pproj
        vp = sp_[:, :].rearrange("p (kc f) -> p kc f", kc=2)
        k.dma("pool", vp, self.W["ple_proj"][l].rearrange("(kc p) f -> p kc f", p=128), W=[sp_.r()])
        for s in range(D // 256):
            sg, vg = self.load_slab(self.W["ple_gate"][l], KC, s * 256, 256)
            for oc2 in range(2):
                oc = s * 2 + oc2
                for (t0, tw) in self.tbs:
                    i = self.rot("gu", 2)
                    bg, bu = self.bank[i], self.bank[2 + i]
                    for kc in range(KC):
                        k.op("pe", lambda e: e.matmul(bg[:, 0:tw], vg[:, kc, oc2 * 128:(oc2 + 1) * 128],
                                                       xn[:, kc, t0:t0 + tw], start=(kc == 0), stop=(kc == KC - 1)),
                             R=[sg.r(), xn.r()], W=[bg.r()])
                    for kc in range(2):
                        k.op("pe", lambda e: e.matmul(bu[:, 0:tw], vp[:, kc, oc * 128:(oc + 1) * 128],
                                                       pt[:, kc, t0:t0 + tw], start=(kc == 0), stop=(kc == 1)),
                             R=[sp_.r(), pt.r()], W=[bu.r()])
                    t1 = self.t1[self.rot("t1", 2)]
                    k.op("act", lambda e: e.activation(out=t1[:, 0:tw], in_=bg[:, 0:tw], func=AF.Sigmoid),
                         R=[bg.r()], W=[t1.r()])
                    t2 = self.t2[self.rot("t2", 2)]
                    k.op("dve", lambda e: e.tensor_tensor(out=t2[:, 0:tw], in0=t1[:, 0:tw], in1=bu[:, 0:tw],
                                                           op=ALU.mult),
                         R=[t1.r(), bu.r()], W=[t2.r()])
                    k.op("pool", lambda e: e.tensor_tensor(out=x[:, oc, t0:t0 + tw], in0=x[:, oc, t0:t0 + tw],
                                                            in1=t2[:, 0:tw], op=ALU.add),
                         R=[t2.r(), x.r(oc)], W=[x.r(oc)])

    def cs(self, name, rows=128):
        o, n = CST[name]
        return self.cst[0:rows, o:o + n]

    def mixer(self, l):
        self.k.barrier()
        self.nslab = 2
        self.norm("norm_mix%d" % l)
        if l % 2 == 0:
            self.lru(l // 2)
            if "norwkv" not in self.enable:
                self.k.barrier()
                self.rwkv(l // 2)
        else:
            self.s5(l // 2)
        self.nslab = 6
        self.k.barrier()

    def gelu_(self, eng_out, src, tmp, TBc):
        k = self.k
        k.op("dve", lambda e: e.tensor_tensor(out=tmp[:, 0:TBc], in0=src[:, 0:TBc], in1=src[:, 0:TBc], op=ALU.mult),
             R=[src.r()], W=[tmp.r()])
        k.op("dve", lambda e: e.tensor_scalar(out=tmp[:, 0:TBc], in0=tmp[:, 0:TBc], scalar1=0.044715, scalar2=1.0,
                                               op0=ALU.mult, op1=ALU.add), R=[tmp.r()], W=[tmp.r()])
        k.op("dve", lambda e: e.tensor_tensor(out=tmp[:, 0:TBc], in0=tmp[:, 0:TBc], in1=src[:, 0:TBc], op=ALU.mult),
             R=[tmp.r(), src.r()], W=[tmp.r()])
        k.op("act", lambda e: e.activation(out=tmp[:, 0:TBc], in_=tmp[:, 0:TBc], func=AF.Sigmoid, scale=1.5957691216),
             R=[tmp.r()], W=[tmp.r()])
        k.op("dve", lambda e: e.tensor_tensor(out=eng_out[:, 0:TBc], in0=tmp[:, 0:TBc], in1=src[:, 0:TBc], op=ALU.mult),
             R=[tmp.r(), src.r()], W=[eng_out.r()])

    def proj_chunk(self, W2d, c0, cw, evac):
        k, xn = self.k, self.xn
        sg, vg = self.load_slab(W2d, KC, c0, cw)
        for (t0, tw) in self.tbs:
            bg = self.bank[self.rot("pj", 2)]
            for kc in range(KC):
                k.op("pe", lambda e: e.matmul(bg[0:cw, 0:tw], vg[:, kc, 0:cw], xn[:, kc, t0:t0 + tw],
                                               start=(kc == 0), stop=(kc == KC - 1)),
                     R=[sg.r(), xn.r()], W=[bg.r()])
            evac(bg, t0, tw)

    def lru(self, j):
        k, x, TB, ps = self.k, self.x, self.TB, self.ps_
        ns = NS if ps == 1 else 0
        Wi = self.W["w_in_ab"][j]
        wa = self.av("lwa", 8704, 512, BF16, a=8)
        wx = self.av("lwx", 9216, 512, BF16, a=8)
        k.dma("pool", wa[:, :, :], self.I["wa_bd"][j], W=[wa.r()])
        k.dma("pool", wx[:, :, :], self.I["wx_bd"][j], W=[wx.r()])
        XP = self.av("lXP", 9728, 1216)
        names = ["xc", "gr", "gi", "aa", "tt", "hh"]
        T = {nm: self.av("l" + nm, 10944 + i * 1152, 1152) for i, nm in enumerate(names)}
        xcb = self.sq[0]
        hst, cvo, h0s, cv0 = self.lru_small
        nsp = self.lru_nsp
        if ns:
            k.dma("sp", h0s[:, :, 0:16], self.I["st_h"][j], W=[h0s.r()])
            k.dma("sp", cv0[:, :, 0:16, :], self.I["st_conv"][j], W=[cv0.r()])
        if ps == 0:
            k.op("dve", lambda e: e.memset(h0s[:, :, 16:17], 0.0), W=[h0s.r()], acc=True)
            k.op("dve", lambda e: e.memset(cv0[:, :, 16:17, :], 0.0), W=[cv0.r()], acc=True)
        else:
            k.dma("sp", h0s[:, :, 16:17], self.o_h.t[j][:, :, 16:17], R=[self.o_h.r()], W=[h0s.r()], allow_slow_non_contiguous=True)
            k.dma("sp", cv0[:, :, 16:17, :], self.o_conv.t[j][:, :, 16:17, :], R=[self.o_conv.r()], W=[cv0.r()], allow_slow_non_contiguous=True)
        lo, _ = self.lay["lam_b%d" % j]
        k.op("act", lambda e: e.activation(out=nsp[:, 0:8], in_=self.vec[:, lo:lo + 8], func=AF.Exp, scale=-1.0),
             R=[self.vec.r()], W=[nsp.r()])
        k.op("act", lambda e: e.activation(out=nsp[:, 0:8], in_=nsp[:, 0:8], func=AF.Ln, bias=self.cs("one")),
             R=[nsp.r(), self.cst.r()], W=[nsp.r()])
        k.op("dve", lambda e: e.tensor_scalar(out=nsp[:, 8:16], in0=nsp[:, 0:8], scalar1=-16.0, scalar2=None,
                                               op0=ALU.mult), R=[nsp.r()], W=[nsp.r()])
        k.op("dve", lambda e: e.tensor_scalar(out=nsp[:, 0:8], in0=nsp[:, 0:8], scalar1=-8.0, scalar2=None,
                                               op0=ALU.mult), R=[nsp.r()], W=[nsp.r()])
        xc, gr, gi, aa, tt, hh = (T[n] for n in names)
        gb = gr
        XPp = XP[:, 0:1027]
        XPs = XP[:, 1027:1027 + 176].rearrange("p (b t) -> p b t", b=16)
        for c in range(8):
            def ev_x(bg, t0, tw):
                if t0 < 1024:
                    k.op("act", lambda e: e.copy(out=XP[:, 3 + t0:3 + t0 + tw], in_=bg[:, 0:tw]),
                         R=[bg.r()], W=[XP.r()])
                else:
                    k.op("act", lambda e: e.copy(out=XPs[:, :, 3:11],
                                                 in_=bg[:, 0:128].rearrange("p (b t) -> p b t", b=16)),
                         R=[bg.r()], W=[XP.r()])
            self.proj_chunk(Wi, COLS_A + c * 128, 128, ev_x)

            k.op("dve", lambda e: e.tensor_copy(out=XP[:, 0:3], in_=cv0[:, c, 16, :]), R=[cv0.r()], W=[XP.r()])
            if ns:
                k.op("dve", lambda e: e.tensor_copy(out=XPs[:, :, 0:3], in_=cv0[:, c, 0:16, :]),
                     R=[cv0.r()], W=[XP.r()])
            k.op("dve", lambda e: e.tensor_copy(out=cvo[:, c, 16, :], in_=XP[:, 1024:1027]), R=[XP.r()], W=[cvo.r()])
            if ns:
                k.op("dve", lambda e: e.tensor_copy(out=cvo[:, c, 0:16, :], in_=XPs[:, :, 8:11]),
                     R=[XP.r()], W=[cvo.r()])
            segs = [(xc[:, 0:1024], lambda q: XP[:, q:q + 1024])]
            if ns:
                segs.append((xc[:, 1024:1152].rearrange("p (b t) -> p b t", b=16), lambda q: XPs[:, :, q:q + 8]))
            for (dst, srcf) in segs:
                k.op("dve", lambda e: e.tensor_scalar(out=dst, in0=srcf(0), scalar1=self.vcol("conv_w0_%d" % j, c),
                                                       scalar2=self.vcol("conv_b_b%d" % j, c),
                                                       op0=ALU.mult, op1=ALU.add),
                     R=[XP.r(), self.vec.r()], W=[xc.r()])
                for q in range(1, 4):
                    k.op("dve", lambda e: e.scalar_tensor_tensor(out=dst, in0=srcf(q),
                                                                  scalar=self.vcol("conv_w%d_%d" % (q, j), c),
                                                                  in1=dst, op0=ALU.mult, op1=ALU.add),
                         R=[XP.r(), xc.r(), self.vec.r()], W=[xc.r()])
            k.op("act", lambda e: e.copy(out=xcb[:, 0:TB], in_=xc[:, 0:TB]), R=[xc.r()], W=[xcb.r()])
            for (wt, bname, dst) in ((wa, "ba_b%d" % j, gr), (wx, "bx_b%d" % j, gi)):
                for (t0, tw) in self.tbs:
                    bg = self.bank[self.rot("pj", 2)]
                    k.op("pe", lambda e: e.matmul(bg[:, 0:tw], wt[:, c, :], xcb[:, t0:t0 + tw], start=True, stop=True),
                         R=[wt.r(), xcb.r()], W=[bg.r()])
                    k.op("act", lambda e: e.activation(out=dst[:, t0:t0 + tw], in_=bg[:, 0:tw], func=AF.Sigmoid,
                                                       bias=self.vcol(bname, c)),
                         R=[bg.r(), self.vec.r()], W=[dst.r()])
            k.op("act", lambda e: e.activation(out=aa[:, 0:TB], in_=gr[:, 0:TB], func=AF.Exp, scale=nsp[:, c:c + 1]),
                 R=[gr.r(), nsp.r()], W=[aa.r()])
            k.op("act", lambda e: e.activation(out=tt[:, 0:TB], in_=gr[:, 0:TB], func=AF.Exp,
                                               scale=nsp[:, 8 + c:9 + c]), R=[gr.r(), nsp.r()], W=[tt.r()])
            k.op("dve", lambda e: e.tensor_scalar(out=tt[:, 0:TB], in0=tt[:, 0:TB], scalar1=-1.0, scalar2=1.0,
                                                   op0=ALU.mult, op1=ALU.add), R=[tt.r()], W=[tt.r()])
            k.op("act", lambda e: e.activation(out=tt[:, 0:TB], in_=tt[:, 0:TB], func=AF.Sqrt),
                 R=[tt.r()], W=[tt.r()])
            k.op("dve", lambda e: e.tensor_tensor(out=gi[:, 0:TB], in0=gi[:, 0:TB], in1=xc[:, 0:TB], op=ALU.mult),
                 R=[gi.r(), xc.r()], W=[gi.r()])
            k.op("dve", lambda e: e.tensor_tensor(out=tt[:, 0:TB], in0=tt[:, 0:TB], in1=gi[:, 0:TB], op=ALU.mult),
                 R=[gi.r(), tt.r()], W=[tt.r()])
            k.op("dve", lambda e: e.tensor_tensor_scan(out=hh[:, 0:1024], data0=aa[:, 0:1024], data1=tt[:, 0:1024],
                                                        initial=h0s[:, c, 16:17], op0=ALU.mult, op1=ALU.add),
                 R=[aa.r(), tt.r(), h0s.r()], W=[hh.r()])
            for bq in range(ns):
                o_ = 1024 + bq * 8
                k.op("dve", lambda e: e.tensor_tensor_scan(out=hh[:, o_:o_ + 8], data0=aa[:, o_:o_ + 8],
                                                            data1=tt[:, o_:o_ + 8], initial=h0s[:, c, bq:bq + 1],
                                                            op0=ALU.mult, op1=ALU.add),
                     R=[aa.r(), tt.r(), h0s.r()], W=[hh.r()], acc=True)
            k.op("dve", lambda e: e.tensor_copy(out=hst[:, c, 16:17], in_=hh[:, 1023:1024]), R=[hh.r()], W=[hst.r()])
            if ns:
                k.op("dve", lambda e: e.tensor_copy(
                    out=hst[:, c, 0:16], in_=hh[:, 1024:1152].rearrange("p (b t) -> p b t", b=16)[:, :, 7]),
                    R=[hh.r()], W=[hst.r()])
            def ev_g(bg, t0, tw):
                k.op("act", lambda e: e.copy(out=gb[:, t0:t0 + tw], in_=bg[:, 0:tw]), R=[bg.r()], W=[gb.r()])
            self.proj_chunk(Wi, COLS_A + 1024 + c * 128, 128, ev_g)
            self.gelu_(gi, gb, aa, TB)
            yb = self.ybuf
            k.op("dve", lambda e: e.tensor_tensor(out=yb[:, c, 0:TB], in0=gi[:, 0:TB], in1=hh[:, 0:TB], op=ALU.mult),
                 R=[gi.r(), hh.r()], W=[yb.r()])
        k.dma("sp", self.o_h.t[j], hst[:, :, :], R=[hst.r()], W=[self.o_h.r()])
        k.dma("sp", self.o_conv.t[j], cvo[:, :, :, :], R=[cvo.r()], W=[self.o_conv.r()])
        Wo = self.W["w_out_ab"][j]
        yb = self.ybuf
        for s in range(D // 256):
            sg, vg = self.load_slab(Wo, 8, s * 256, 256, rows0=1024)
            for oc2 in range(2):
                oc = s * 2 + oc2
                for (t0, tw) in self.tbs:
                    bd = self.bank[4 + self.rot("dn", 2)]
                    for kc in range(8):
                        k.op("pe", lambda e: e.matmul(bd[:, 0:tw], vg[:, kc, oc2 * 128:(oc2 + 1) * 128],
                                                       yb[:, kc, t0:t0 + tw], start=(kc == 0), stop=(kc == 7)),
                             R=[sg.r(), yb.r()], W=[bd.r()])
                    k.op("dve", lambda e: e.tensor_tensor(out=x[:, oc, t0:t0 + tw], in0=bd[:, 0:tw],
                                                           in1=x[:, oc, t0:t0 + tw], op=ALU.add),
                         R=[bd.r(), x.r(oc)], W=[x.r(oc)])


    def rwkv(self, j):
        k, x, xn, TB, ps = self.k, self.x, self.xn, self.TB, self.ps_
        ns = NS if ps == 1 else 0
        av, cst, vec = self.av, self.cst, self.vec
        Wi = self.W["w_in_ab"][j]
        ident, blk = self.cs("ident"), self.cs("blk")
        NT = TB // 128
        self.nslab = 1

        def dve(fn, R, W, **kw):
            return k.op("dve", fn, R=R, W=W, **kw)
        w2a2 = av("r_w2a2", 2048, 512, BF16)
        g2a = av("r_g2a", 2560, 512, BF16)
        g2b = av("r_g2b", 3072, 512, BF16)
        k.dma("pool", w2a2[0:64, :], self.I["w2_a"][j], W=[w2a2.r()])
        k.dma("pool", w2a2[64:128, :], self.I["a2_a"][j], W=[w2a2.r()])
        lx, xg0, xg1 = av("r_lx", 4096, 1152), av("r_xg0", 5248, 1152), av("r_xg1", 6400, 1152)
        Tt = [av("r_T%d" % i, 7552 + i * 1152, 1152) for i in range(7)]
        stg = av("r_stg", 15616, 1152)
        shin = av("r_shin", 16768, 459, a=27)
        shout = av("r_shout", 17227, 459, a=27)
        txa, sg0b, sg1b = self.sq[0], self.sq[1], self.sg1b
        if ns:
            k.dma("sp", shin[:, :, 0:16], self.I["st_shift"][j], W=[shin.r()])
        if ps == 0:
            dve(lambda e: e.memset(shin[:, :, 16:17], 0.0), [], [shin.r()], acc=True)
        else:
            k.dma("sp", shin[:, :, 16:17], self.o_shift.t[j][:, :, 16:17], R=[self.o_shift.r()], W=[shin.r()],
                  allow_slow_non_contiguous=True)
        r3 = lambda a: a.rearrange("p (b t) -> p b t", b=16)

        def project_shift(col0, cw, zc, Z, mu_ap):
            D_ = Tt[6]

            def ev(bg, t0, tw):
                k.op("act", lambda e: e.copy(out=Z[0:cw, t0:t0 + tw], in_=bg[0:cw, 0:tw]), R=[bg.r()], W=[Z.r()], acc=True)
            self.proj_chunk(Wi, col0, cw, ev)
            dve(lambda e: e.tensor_copy(out=shout[0:cw, zc, 16:17], in_=Z[0:cw, 1023:1024]), [Z.r()], [shout.r()], acc=True)
            dve(lambda e: e.tensor_tensor(out=D_[0:cw, 1:1024], in0=Z[0:cw, 0:1023], in1=Z[0:cw, 1:1024], op=ALU.subtract),
                [Z.r()], [D_.r()])
            dve(lambda e: e.tensor_tensor(out=D_[0:cw, 0:1], in0=shin[0:cw, zc, 16:17], in1=Z[0:cw, 0:1], op=ALU.subtract),
                [Z.r(), shin.r()], [D_.r()], acc=True)
            if ns:
                Zs_, Ds_ = r3(Z[0:cw, 1024:1152]), r3(D_[0:cw, 1024:1152])
                dve(lambda e: e.tensor_copy(out=shout[0:cw, zc, 0:16], in_=Zs_[:, :, 7]), [Z.r()], [shout.r()], acc=True)
                dve(lambda e: e.tensor_tensor(out=Ds_[:, :, 1:8], in0=Zs_[:, :, 0:7], in1=Zs_[:, :, 1:8], op=ALU.subtract),
                    [Z.r()], [D_.r()], acc=True)
                dve(lambda e: e.tensor_tensor(out=Ds_[:, :, 0], in0=shin[0:cw, zc, 0:16], in1=Zs_[:, :, 0], op=ALU.subtract),
                    [Z.r(), shin.r()], [D_.r()], acc=True)
            dve(lambda e: e.scalar_tensor_tensor(out=Z[0:cw, 0:TB], in0=D_[0:cw, 0:TB], scalar=mu_ap, in1=Z[0:cw, 0:TB],
                                                 op0=ALU.mult, op1=ALU.add), [D_.r(), Z.r(), vec.r()], [Z.r()])
        mo = self.lay["mu_x%d" % j][0]
        dve(lambda e: e.memset(shout[:, 26, :], 0.0), [], [shout.r()], acc=True)
        project_shift(3072, 128, 24, lx, vec[:, mo:mo + 1])
        project_shift(3200, 128, 25, xg0, vec[:, mo + 2:mo + 3])
        project_shift(3328, 32, 26, xg1, vec[0:32, mo + 3:mo + 4])
        k.op("act", lambda e: e.activation(out=txa[0:64, 0:TB], in_=lx[0:64, 0:TB], func=AF.Tanh), R=[lx.r()], W=[txa.r()])
        k.op("act", lambda e: e.copy(out=txa[64:128, 0:TB], in_=lx[64:128, 0:TB]), R=[lx.r()], W=[txa.r()], acc=True)
        k.op("act", lambda e: e.activation(out=sg0b[:, 0:TB], in_=xg0[:, 0:TB], func=AF.Sigmoid), R=[xg0.r()], W=[sg0b.r()])
        k.op("act", lambda e: e.activation(out=sg1b[0:32, 0:TB], in_=xg1[0:32, 0:TB], func=AF.Sigmoid),
             R=[xg1.r()], W=[sg1b.r()])
        Zsd, Ysd, Bon = self.Zs, self.Ys, self.Bon

        def emit(q, c, Q):
            for tt in range(NT):
                bk = self.bank[6 + (tt % 2)]
                k.op("pe", lambda e: e.transpose(bk[:, 0:128], Q[:, tt * 128:(tt + 1) * 128], ident),
                     R=[Q.r(), cst.r()], W=[bk.r()])
                if tt % 2 == 0:
                    k.op("act", lambda e: e.copy(out=stg[:, tt * 128:(tt + 1) * 128], in_=bk[:, 0:128]),
                         R=[bk.r()], W=[stg.r()], acc=True)
                else:
                    dve(lambda e: e.tensor_copy(out=stg[:, tt * 128:(tt + 1) * 128], in_=bk[:, 0:128]),
                        [bk.r()], [stg.r()], acc=True)
            sv = stg[:, 0:TB].rearrange("p (tt f) -> p tt f", f=128)
            if q == 5:
                dst = self.Vs.t[0:TB, c * 128:(c + 1) * 128].rearrange("(tt p) f -> p tt f", p=128)
                k.dma("sp", dst, sv, R=[stg.r()], W=[self.Vs.r()])
            else:
                for h2 in range(2):
                    dst = Zsd.t[0:TB, 2 * c + h2, q, :].rearrange("(tt p) j -> p tt j", p=128)
                    k.dma("sp", dst, sv[:, :, h2 * 64:(h2 + 1) * 64], R=[stg.r()], W=[Zsd.r()])

        def headsum(user, src):
            for (t0, tw) in self.tbs:
                bg = self.bank[4 + self.rot("hs", 2)]
                k.op("pe", lambda e: e.matmul(bg[:, 0:tw], blk, src[:, t0:t0 + tw], start=True, stop=True),
                     R=[cst.r(), src.r()], W=[bg.r()])
                user(bg, t0, tw)

        for c in range(8):
            T0, T1, T2, T3, T4, T5 = Tt[0:6]
            vc = lambda nm: self.vcol("%s%d" % (nm, j), c)
            project_shift(c * 128, 128, c, T0, vc("mu_r"))
            project_shift(1024 + c * 128, 128, 8 + c, T1, vc("mu_k"))
            project_shift(2048 + c * 128, 128, 16 + c, T2, vc("mu_v"))
            for (t0, tw) in self.tbs:
                bg = self.bank[4 + self.rot("hs", 2)]
                k.op("pe", lambda e: e.matmul(bg[:, 0:tw], w2a2[64:128, c * 128:(c + 1) * 128], txa[64:128, t0:t0 + tw],
                                               start=True, stop=True), R=[w2a2.r(), txa.r()], W=[bg.r()])
                k.op("act", lambda e: e.activation(out=T3[:, t0:t0 + tw], in_=bg[:, 0:tw], func=AF.Sigmoid, bias=vc("a0_a")),
                     R=[bg.r(), vec.r()], W=[T3.r()], acc=True)
            dve(lambda e: e.tensor_scalar(out=T4[:, 0:TB], in0=T1[:, 0:TB], scalar1=vc("kk_a"), scalar2=None, op0=ALU.mult),
                [T1.r(), vec.r()], [T4.r()])
            dve(lambda e: e.tensor_tensor(out=T5[:, 0:TB], in0=T4[:, 0:TB], in1=T4[:, 0:TB], op=ALU.mult), [T4.r()], [T5.r()])

            def nrm_ev(bg, t0, tw):
                k.op("act", lambda e: e.activation(out=Tt[6][:, t0:t0 + tw], in_=bg[:, 0:tw], func=AF.Sqrt),
                     R=[bg.r()], W=[Tt[6].r()], acc=True)
            headsum(nrm_ev, T5)
            dve(lambda e: e.tensor_scalar(out=Tt[6][:, 0:TB], in0=Tt[6][:, 0:TB], scalar1=1e-12, scalar2=None, op0=ALU.max),
                [Tt[6].r()], [Tt[6].r()])
            dve(lambda e: e.reciprocal(out=Tt[6][:, 0:TB], in_=Tt[6][:, 0:TB]), [Tt[6].r()], [Tt[6].r()])
            dve(lambda e: e.tensor_tensor(out=T4[:, 0:TB], in0=T4[:, 0:TB], in1=Tt[6][:, 0:TB], op=ALU.mult),
                [T4.r(), Tt[6].r()], [T4.r()])
            dve(lambda e: e.tensor_tensor(out=T5[:, 0:TB], in0=T4[:, 0:TB], in1=T3[:, 0:TB], op=ALU.mult),
                [T4.r(), T3.r()], [T5.r()])
            emit(0, c, T4)
            emit(2, c, T5)
            dve(lambda e: e.tensor_scalar(out=T4[:, 0:TB], in0=T3[:, 0:TB], scalar1=-1.0, scalar2=vc("ka_a"),
                                          op0=ALU.add, op1=ALU.mult), [T3.r(), vec.r()], [T4.r()])
            dve(lambda e: e.scalar_tensor_tensor(out=T4[:, 0:TB], in0=T4[:, 0:TB], scalar=1.0, in1=T1[:, 0:TB],
                                                 op0=ALU.add, op1=ALU.mult), [T4.r(), T1.r()], [T4.r()])
            emit(3, c, T4)
            dve(lambda e: e.scalar_tensor_tensor(out=T5[:, 0:TB], in0=T0[:, 0:TB], scalar=vc("rk_a"), in1=T4[:, 0:TB],
                                                 op0=ALU.mult, op1=ALU.mult), [T0.r(), T4.r(), vec.r()], [T5.r()])

            def bon_ev(bg, t0, tw):
                dve(lambda e: e.tensor_tensor(out=Tt[6][:, t0:t0 + tw], in0=bg[:, 0:tw], in1=T2[:, t0:t0 + tw], op=ALU.mult),
                    [bg.r(), T2.r()], [Tt[6].r()], acc=True)
            headsum(bon_ev, T5)
            k.dma("sp", Bon.t[c][:, 0:TB], Tt[6][:, 0:TB], R=[Tt[6].r()], W=[Bon.r()])
            for (t0, tw) in self.tbs:
                bg = self.bank[4 + self.rot("hs", 2)]
                k.op("pe", lambda e: e.matmul(bg[:, 0:tw], w2a2[0:64, c * 128:(c + 1) * 128], txa[0:64, t0:t0 + tw],
                                               start=True, stop=True), R=[w2a2.r(), txa.r()], W=[bg.r()])
                k.op("act", lambda e: e.activation(out=T1[:, t0:t0 + tw], in_=bg[:, 0:tw], func=AF.Sigmoid, bias=vc("w0_a")),
                     R=[bg.r(), vec.r()], W=[T1.r()], acc=True)
            k.op("act", lambda e: e.activation(out=T1[:, 0:TB], in_=T1[:, 0:TB], func=AF.Exp, scale=-float(np.exp(-0.5))),
                 R=[T1.r()], W=[T1.r()])
            emit(1, c, T1)
            emit(4, c, T0)
            emit(5, c, T2)
        k.dma("sp", self.o_shift.t[j], shout[:, :, :], R=[shout.r()], W=[self.o_shift.r()])
        k.barrier()
        Spf, T1p = av("q_Sp", 0, 512), av("q_T1p", 512, 512, a=8)
        Sp = Tile(Spf[:, :].rearrange("p (e j) -> p e j", e=8), "q_Sp3")
        Sp.reg = Spf.reg
        Ss = av("q_Ss", 1024, 8192)
        T1s = av("q_T1s", 9216, 2048)
        Xb = [av("q_X%d" % i, 11264 + i * 1280, 1280) for i in range(2)]
        vt = [av("q_vt%d" % i, 13824 + i * 128, 128, a=16) for i in range(2)]
        yt = [av("q_yt%d" % i, 14080 + i * 128, 128, a=16) for i in range(2)]
        vs_, ys_ = av("q_vs", 14336, 1024), av("q_ys", 15360, 1024)
        sa_t = av("q_sa", 16384, 32)
        rep = self.cs("rep", rows=16)
        Zv = Zsd.t.rearrange("t h q j -> h t (q j)")
        Vv = self.Vs.t.rearrange("t (p e) -> p t e", e=8)
        Vsr = self.Vs.r()
        Yv = Ysd.t.rearrange("t (p e) -> p t e", e=8)
        if ps == 0:
            dve(lambda e: e.memset(Sp[:, :, :], 0.0), [], [Sp.r()])
        else:
            k.dma("sp", Spf[:, :], self.o_wkv.t[j][:, 16, :], R=[self.o_wkv.r()], W=[Sp.r()])

        def step(S, T1, reg, shp, sa, vv, yy, RW):
            nd = len(shp)
            bj = lambda a: a.unsqueeze(nd - 1).broadcast_to([128] + shp)
            be = lambda a: a.unsqueeze(nd).broadcast_to([128] + shp)
            Rr, Wr = RW
            dve(lambda e: e.tensor_tensor(out=T1, in0=S, in1=bj(reg(0)), op=ALU.mult), Rr, Wr, sew=False)
            dve(lambda e: e.tensor_reduce(out=sa, in_=T1, axis=AX.X, op=ALU.add), Wr, Wr, sew=False)
            dve(lambda e: e.tensor_tensor(out=S, in0=S, in1=bj(reg(1)), op=ALU.mult), Rr, Wr, sew=False)
            dve(lambda e: e.tensor_tensor(out=T1, in0=bj(reg(2)), in1=be(sa), op=ALU.mult), Rr, Wr, sew=False)
            dve(lambda e: e.tensor_tensor(out=S, in0=S, in1=T1, op=ALU.subtract), Wr, Wr, sew=False)
            dve(lambda e: e.tensor_tensor(out=T1, in0=bj(reg(3)), in1=be(vv), op=ALU.mult), Rr, Wr, sew=False)
            dve(lambda e: e.tensor_tensor(out=S, in0=S, in1=T1, op=ALU.add), Wr, Wr, sew=False)
            dve(lambda e: e.tensor_tensor(out=T1, in0=S, in1=bj(reg(4)), op=ALU.mult), Rr, Wr, sew=False)
            dve(lambda e: e.tensor_reduce(out=yy, in_=T1, axis=AX.X, op=ALU.add), Wr, Wr, sew=False)

        def rep_mm(Xt, bset):
            Xv = Xt[0:16, 0:1280].rearrange("h (t q j) -> h t q j", t=4, q=5)
            for q in range(5):
                bk = self.bank[bset * 3 + q // 2]
                k.op("pe", lambda e: e.matmul(bk[:, (q % 2) * 256:(q % 2) * 256 + 256], rep, Xv[:, :, q, :],
                                               start=True, stop=True), R=[cst.r(), Xt.r()], W=[bk.r()], acc=True)

        wreg = Reg("q_work")
        vtt = ytt = None
        for g4 in range(1024 // 4):
            t0 = g4 * 4
            Xt = Xb[g4 % 2]
            k.dma("sp", Xt[0:16, 0:1280].rearrange("h (t f) -> h t f", t=4), Zv[:, t0:t0 + 4, :],
                  R=[Zsd.r()], W=[Xt.r()])
            if t0 % 16 == 0:
                vtt, ytt = vt[(t0 // 16) % 2], yt[(t0 // 16) % 2]
                k.dma("sp", vtt[:, :, :], Vv[:, t0:t0 + 16, :], R=[Vsr], W=[vtt.r()], allow_slow_non_contiguous=True)
            bset = g4 % 2
            rep_mm(Xt, bset)
            banks = [self.bank[bset * 3 + i] for i in range(3)]
            for tl in range(4):
                reg = lambda q: banks[q // 2][:, (q % 2) * 256 + tl * 64:(q % 2) * 256 + tl * 64 + 64]
                ti = (t0 + tl) % 16
                step(Sp[:, :, :], T1p[:, :, :], reg, [8, 64], sa_t[:, 0:8], vtt[:, ti, :], ytt[:, ti, :],
                     ([b_.r() for b_ in banks] + [vtt.r(), wreg, Sp.r()], [wreg, ytt.r()]))
            if t0 % 16 == 12:
                k.dma("sp", Yv[:, t0 - 12:t0 + 4, :], ytt[:, :, :], R=[ytt.r()], W=[Ysd.r()], allow_slow_non_contiguous=True)
        k.dma("sp", self.o_wkv.t[j][:, 16, :], Spf[:, :], R=[Sp.r(), wreg], W=[self.o_wkv.r()])
        if ns:
            k.dma("sp", Ss[:, :], self.I["st_wkv"][j].rearrange("p b f -> p (b f)"), W=[Ss.r()])
            vs4 = vs_[:, :].rearrange("p (b t e) -> p b t e", b=16, t=8)
            ys4 = ys_[:, :].rearrange("p (b t e) -> p b t e", b=16, t=8)
            for bq in range(8):
                k.dma("sp", vs_[:, :].rearrange("p (n e) -> p n e", e=8)[:, 16 * bq:16 * bq + 16, :],
                      Vv[:, 1024 + 16 * bq:1024 + 16 * bq + 16, :],
                      R=[Vsr], W=[vs_.r()], allow_slow_non_contiguous=True)
            Ss4 = Ss[:, :].rearrange("p (b e j) -> p b e j", b=16, e=8)
            T1s4 = T1s[:, :].rearrange("p (b e j) -> p b e j", b=4, e=8)
            sreg = Reg("q_works")
            it = 0
            for t in range(8):
                for qd in range(4):
                    Xt = Xb[it % 2]
                    bset = it % 2
                    it += 1
                    r0 = 1024 + 32 * qd + t
                    k.dma("sp", Xt[0:16, 0:1280].rearrange("h (t f) -> h t f", t=4),
                          Zv[:, r0:r0 + 25:8, :], R=[Zsd.r()], W=[Xt.r()])
                    rep_mm(Xt, bset)
                    banks = [self.bank[bset * 3 + i] for i in range(3)]
                    reg = lambda q: banks[q // 2][:, (q % 2) * 256:(q % 2) * 256 + 256].rearrange("p (b j) -> p b j", b=4)
                    step(Ss4[:, 4 * qd:4 * qd + 4, :, :], T1s4[:, :, :, :], reg, [4, 8, 64],
                         sa_t[:, 0:32].rearrange("p (b e) -> p b e", b=4), vs4[:, 4 * qd:4 * qd + 4, t, :],
                         ys4[:, 4 * qd:4 * qd + 4, t, :],
                         ([b_.r() for b_ in banks] + [vs_.r(), Ss.r(), sreg], [sreg, ys_.r()]))
            k.dma("sp", Yv[:, 1024:1152, :], ys_[:, :].rearrange("p (n e) -> p n e", e=8), R=[ys_.r(), sreg], W=[Ysd.r()],
                  allow_slow_non_contiguous=True)
            k.dma("sp", self.o_wkv.t[j][:, 0:16, :].rearrange("p b f -> p (b f)"), Ss[:, :], R=[Ss.r(), sreg],
                  W=[self.o_wkv.r()])
        k.barrier()
        self.nslab = 2
        yab = av("p_yab", 4096, 4608, BF16, a=8)
        ytk = av("p_ytk", 8704, 1152)
        yc, ysq, mean, bon = (av("p_t%d" % i, 9856 + i * 1152, 1152) for i in range(4))
        g2a = av("p_g2a", 14464, 512, BF16)
        g2b = av("p_g2b", 14976, 512, BF16)
        gne = av("p_gne", 15488, 4)
        dve(lambda e: e.memset(gne[:, :], GN_EPS), [], [gne.r()])
        k.dma("pool", g2a[:, :], self.I["g2_a"][j][0:128, :], W=[g2a.r()])
        k.dma("pool", g2b[0:32, :], self.I["g2_a"][j][128:160, :], W=[g2b.r()])
        for c in range(8):
            vc = lambda nm: self.vcol("%s%d" % (nm, j), c)
            k.dma("sp", ytk[:, 0:TB].rearrange("p (tt f) -> p tt f", f=128),
                  Ysd.t[0:TB, c * 128:(c + 1) * 128].rearrange("(tt p) f -> p tt f", p=128), R=[Ysd.r()], W=[ytk.r()])
            k.dma("sp", bon[:, 0:TB], Bon.t[c][:, 0:TB], R=[Bon.r()], W=[bon.r()])
            for tt in range(NT):
                bk = self.bank[6 + (tt % 2)]
                k.op("pe", lambda e: e.transpose(bk[:, 0:128], ytk[:, tt * 128:(tt + 1) * 128], ident),
                     R=[ytk.r(), cst.r()], W=[bk.r()])
                k.op("act", lambda e: e.copy(out=yc[:, tt * 128:(tt + 1) * 128], in_=bk[:, 0:128]), R=[bk.r()], W=[yc.r()], acc=True)
            dve(lambda e: e.tensor_tensor(out=ysq[:, 0:TB], in0=yc[:, 0:TB], in1=yc[:, 0:TB], op=ALU.mult), [yc.r()], [ysq.r()])

            def mean_ev(bg, t0, tw):
                k.op("act", lambda e: e.activation(out=mean[:, t0:t0 + tw], in_=bg[:, 0:tw], func=AF.Copy, scale=1.0 / 64),
                     R=[bg.r()], W=[mean.r()], acc=True)
            headsum(mean_ev, yc)

            def var_ev(bg, t0, tw):
                k.op("act", lambda e: e.activation(out=ysq[:, t0:t0 + tw], in_=bg[:, 0:tw], func=AF.Copy, scale=1.0 / 64),
                     R=[bg.r()], W=[ysq.r()], acc=True)
            headsum(var_ev, ysq)
            dve(lambda e: e.tensor_tensor(out=yc[:, 0:TB], in0=yc[:, 0:TB], in1=mean[:, 0:TB], op=ALU.subtract),
                [yc.r(), mean.r()], [yc.r()])
            dve(lambda e: e.tensor_tensor(out=mean[:, 0:TB], in0=mean[:, 0:TB], in1=mean[:, 0:TB], op=ALU.mult),
                [mean.r()], [mean.r()])
            dve(lambda e: e.tensor_tensor(out=ysq[:, 0:TB], in0=ysq[:, 0:TB], in1=mean[:, 0:TB], op=ALU.subtract),
                [ysq.r(), mean.r()], [ysq.r()])
            k.op("act", lambda e: e.activation(out=ysq[:, 0:TB], in_=ysq[:, 0:TB], func=AF.Sqrt, bias=gne[:, 0:1]),
                 R=[ysq.r(), gne.r()], W=[ysq.r()])
            dve(lambda e: e.reciprocal(out=ysq[:, 0:TB], in_=ysq[:, 0:TB]), [ysq.r()], [ysq.r()])
            dve(lambda e: e.tensor_tensor(out=yc[:, 0:TB], in0=yc[:, 0:TB], in1=ysq[:, 0:TB], op=ALU.mult),
                [yc.r(), ysq.r()], [yc.r()])
            dve(lambda e: e.tensor_scalar(out=yc[:, 0:TB], in0=yc[:, 0:TB], scalar1=vc("lnx_g"), scalar2=vc("lnx_b"),
                                          op0=ALU.mult, op1=ALU.add), [yc.r(), vec.r()], [yc.r()])
            dve(lambda e: e.tensor_tensor(out=yc[:, 0:TB], in0=yc[:, 0:TB], in1=bon[:, 0:TB], op=ALU.add),
                [yc.r(), bon.r()], [yc.r()])
            for (t0, tw) in self.tbs:
                bg = self.bank[4 + self.rot("hs", 2)]
                k.op("pe", lambda e: e.matmul(bg[:, 0:tw], g2a[:, c * 128:(c + 1) * 128], sg0b[:, t0:t0 + tw],
                                               start=True, stop=False), R=[g2a.r(), sg0b.r()], W=[bg.r()])
                k.op("pe", lambda e: e.matmul(bg[:, 0:tw], g2b[0:32, c * 128:(c + 1) * 128], sg1b[0:32, t0:t0 + tw],
                                               start=False, stop=True), R=[g2b.r(), sg1b.r()], W=[bg.r()])
                dve(lambda e: e.tensor_tensor(out=yab[:, c, t0:t0 + tw], in0=yc[:, t0:t0 + tw], in1=bg[:, 0:tw], op=ALU.mult),
                    [yc.r(), bg.r()], [yab.r()], acc=True)
        Wo = self.W["w_out_ab"][j]
        for s in range(D // 256):
            sg, vg = self.load_slab(Wo, 8, s * 256, 256, rows0=0)
            for oc2 in range(2):
                oc = s * 2 + oc2
                for (t0, tw) in self.tbs:
                    bd = self.bank[self.rot("pj", 2)]
                    for kc in range(8):
                        k.op("pe", lambda e: e.matmul(bd[:, 0:tw], vg[:, kc, oc2 * 128:(oc2 + 1) * 128],
                                                       yab[:, kc, t0:t0 + tw], start=(kc == 0), stop=(kc == 7)),
                             R=[sg.r(), yab.r()], W=[bd.r()])
                    dve(lambda e: e.tensor_tensor(out=x[:, oc, t0:t0 + tw], in0=bd[:, 0:tw], in1=x[:, oc, t0:t0 + tw],
                                                  op=ALU.add), [bd.r(), x.r(oc)], [x.r(oc)])

    def s5(self, j):
        k, x, xn, TB, ps = self.k, self.x, self.xn, self.TB, self.ps_
        ns = NS if ps == 1 else 0
        PI = float(np.pi)
        av = self.av
        P = {nm: av("s5" + nm, 4096 + i * 128, 128) for i, nm in enumerate(
            ["are", "aim", "dtt", "rho", "tht", "ta", "tb", "fre", "fim", "tc"])}
        Bp1, Bp2, Cp1, Cp2, Bb1, Bb2, tA, tB = (av("s5s%d" % i, 5376 + i * 128, 128, a=8) for i in range(8))
        T1c = av("s5T1", 6400, 64, BF16)
        T2c = av("s5T2", 6464, 64, BF16)
        LB1 = [av("s5LB1%d" % i, 6528 + i * 64, 64, BF16) for i in range(2)]
        LB2 = [av("s5LB2%d" % i, 6656 + i * 64, 64, BF16) for i in range(2)]
        C1p = [av("s5C1%d" % i, 6784 + i * 64, 64, BF16) for i in range(2)]
        C2p = [av("s5C2%d" % i, 6912 + i * 64, 64, BF16) for i in range(2)]
        hin = av("s5hin", 7040, 136, a=8)
        hfin = av("s5hfin", 7176, 136, a=8)
        ftmp = av("s5ft", 7312, 68, a=4)
        Ct, St, targ = av("s5Ct", 7424, 1024), av("s5St", 8448, 1024), av("s5ta", 9472, 1024)
        Ct1, St1, targ1 = av("s5Ct1", 0, 1024), av("s5St1", 1024, 1024), av("s5ta1", 2048, 1024)
        rhof, m, G = av("s5rf", 10496, 1152), av("s5m", 11648, 1152), av("s5G", 12800, 1152)
        G1b, G2b = av("s5G1", 13952, 576, BF16), av("s5G2", 14528, 576, BF16)
        yf, gt = av("s5yf", 15104, 1152), av("s5gt", 16256, 1152)
        cst, vec = self.cst, self.vec
        ident = self.cs("ident")
        negpi = self.cs("negpi")
        Sg = lambda i: cst[:, CST["sgn"][0] + i:CST["sgn"][0] + i + 1]
        V = lambda e: e

        def dve(fn, R, W, **kw):
            return k.op("dve", fn, R=R, W=W, **kw)

        k.dma("sp", P["are"][:, :], self.I["s5_are"][j], W=[P["are"].r()])
        k.dma("sp", P["aim"][:, :], self.I["s5_aim"][j], W=[P["aim"].r()])
        k.dma("sp", P["dtt"][:, :], self.I["s5_ldt"][j], W=[P["dtt"].r()])
        k.op("act", lambda e: e.activation(out=P["dtt"][:, :], in_=P["dtt"][:, :], func=AF.Exp),
             R=[P["dtt"].r()], W=[P["dtt"].r()])
        dve(lambda e: e.tensor_tensor(out=P["tht"][:, :], in0=P["dtt"][:, :], in1=P["aim"][:, :], op=ALU.mult),
            [P["dtt"].r(), P["aim"].r()], [P["tht"].r()])
        dve(lambda e: e.tensor_tensor(out=P["rho"][:, :], in0=P["dtt"][:, :], in1=P["are"][:, :], op=ALU.mult),
            [P["dtt"].r(), P["are"].r()], [P["rho"].r()])
        k.op("act", lambda e: e.activation(out=P["rho"][:, :], in_=P["rho"][:, :], func=AF.Exp),
             R=[P["rho"].r()], W=[P["rho"].r()])

        qi = av("s5qi", 17408, 1024, I32)
        qi1 = av("s5qi1", 3072, 1024, I32)
        halfpi = self.cs("halfpi")
        TSET = [(Ct, St, targ, qi, gt), (Ct1, St1, targ1, qi1, self.rstd)]

        def sincos(dst_s, dst_c, src, w, Rs, qi=qi, gt=gt):
            for (dst, addc, lo, hi, bias) in ((dst_s, 0.0, -PI, PI, None), (dst_c, 0.5 * PI, -1.5 * PI, 0.5 * PI, halfpi)):
                k.op("pool", lambda e: e.tensor_scalar(out=qi[:, 0:w], in0=src, scalar1=addc, scalar2=1.0 / (2 * PI),
                                                       op0=ALU.add, op1=ALU.mult), R=Rs, W=[qi.r()])
                k.op("pool", lambda e: e.tensor_copy(out=gt[:, 0:w], in_=qi[:, 0:w]), R=[qi.r()], W=[gt.r()])
                dve(lambda e: e.scalar_tensor_tensor(out=dst[:, 0:w], in0=gt[:, 0:w], scalar=-2 * PI, in1=src,
                                                     op0=ALU.mult, op1=ALU.add), Rs + [gt.r()], [dst.r()])
                dve(lambda e: e.tensor_scalar(out=dst[:, 0:w], in0=dst[:, 0:w], scalar1=lo, scalar2=hi,
                                              op0=ALU.max, op1=ALU.min), [dst.r()], [dst.r()])
                if bias is None:
                    k.op("act", lambda e: e.activation(out=dst[:, 0:w], in_=dst[:, 0:w], func=AF.Sin),
                         R=[dst.r()], W=[dst.r()])
                else:
                    k.op("act", lambda e: e.activation(out=dst[:, 0:w], in_=dst[:, 0:w], func=AF.Sin, bias=bias),
                         R=[dst.r(), cst.r()], W=[dst.r()])
        sincos(P["ta"], P["tb"], P["tht"][:, :], 128, [P["tht"].r()])
        dve(lambda e: e.tensor_tensor(out=P["ta"][:, :], in0=P["ta"][:, :], in1=P["rho"][:, :], op=ALU.mult),
            [P["ta"].r(), P["rho"].r()], [P["ta"].r()])
        dve(lambda e: e.tensor_tensor(out=P["tb"][:, :], in0=P["tb"][:, :], in1=P["rho"][:, :], op=ALU.mult),
            [P["tb"].r(), P["rho"].r()], [P["tb"].r()])
        dve(lambda e: e.tensor_scalar(out=P["tb"][:, :], in0=P["tb"][:, :], scalar1=-1.0, scalar2=None, op0=ALU.add),
            [P["tb"].r()], [P["tb"].r()])
        dve(lambda e: e.tensor_tensor(out=P["tc"][:, :], in0=P["are"][:, :], in1=P["are"][:, :], op=ALU.mult),
            [P["are"].r()], [P["tc"].r()])
        dve(lambda e: e.tensor_tensor(out=P["fre"][:, :], in0=P["aim"][:, :], in1=P["aim"][:, :], op=ALU.mult),
            [P["aim"].r()], [P["fre"].r()])
        dve(lambda e: e.tensor_tensor(out=P["tc"][:, :], in0=P["tc"][:, :], in1=P["fre"][:, :], op=ALU.add),
            [P["tc"].r(), P["fre"].r()], [P["tc"].r()])
        dve(lambda e: e.reciprocal(out=P["tc"][:, :], in_=P["tc"][:, :]), [P["tc"].r()], [P["tc"].r()])
        dve(lambda e: e.tensor_tensor(out=P["fre"][:, :], in0=P["tb"][:, :], in1=P["are"][:, :], op=ALU.mult),
            [P["tb"].r(), P["are"].r()], [P["fre"].r()])
        dve(lambda e: e.tensor_tensor(out=P["fim"][:, :], in0=P["ta"][:, :], in1=P["aim"][:, :], op=ALU.mult),
            [P["ta"].r(), P["aim"].r()], [P["fim"].r()])
        dve(lambda e: e.tensor_tensor(out=P["fre"][:, :], in0=P["fre"][:, :], in1=P["fim"][:, :], op=ALU.add),
            [P["fre"].r(), P["fim"].r()], [P["fre"].r()])
        dve(lambda e: e.tensor_tensor(out=P["fim"][:, :], in0=P["ta"][:, :], in1=P["are"][:, :], op=ALU.mult),
            [P["ta"].r(), P["are"].r()], [P["fim"].r()])
        dve(lambda e: e.tensor_tensor(out=P["ta"][:, :], in0=P["tb"][:, :], in1=P["aim"][:, :], op=ALU.mult),
            [P["tb"].r(), P["aim"].r()], [P["ta"].r()])
        dve(lambda e: e.tensor_tensor(out=P["fim"][:, :], in0=P["fim"][:, :], in1=P["ta"][:, :], op=ALU.subtract),
            [P["fim"].r(), P["ta"].r()], [P["fim"].r()])
        dve(lambda e: e.tensor_tensor(out=P["fre"][:, :], in0=P["fre"][:, :], in1=P["tc"][:, :], op=ALU.mult),
            [P["fre"].r(), P["tc"].r()], [P["fre"].r()])
        dve(lambda e: e.tensor_tensor(out=P["fim"][:, :], in0=P["fim"][:, :], in1=P["tc"][:, :], op=ALU.mult),
            [P["fim"].r(), P["tc"].r()], [P["fim"].r()])
        iota = self.cs("iota")
        nbk = [5, 6, 7]

        ang, sAB, cAB = av("s5ang", 0, 512), av("s5sAB", 512, 512), av("s5cAB", 1024, 512)
        iAB = self.cs("iotaAB").unsqueeze(1).broadcast_to([128, 8, 64])

        def chunk_tables(g0):
            thb = P["tht"][:, g0:g0 + 8].unsqueeze(2).broadcast_to([128, 8, 64])
            dve(lambda e: e.tensor_tensor(out=ang[:, :].rearrange("p (g f) -> p g f", g=8), in0=iAB, in1=thb, op=ALU.mult),
                [cst.r(), P["tht"].r()], [ang.r()])
            sincos(sAB, cAB, ang[:, :], 512, [ang.r()])

        def tables(g8):
            o = g8 * 64
            A = lambda t_: t_[:, o:o + 32].unsqueeze(2).broadcast_to([128, 32, 32])
            B = lambda t_: t_[:, o + 32:o + 64].unsqueeze(1).broadcast_to([128, 32, 32])
            v3 = lambda t_: t_[:, :].rearrange("p (a b) -> p a b", a=32)
            RT = [sAB.r(), cAB.r()]
            dve(lambda e: e.tensor_tensor(out=v3(Ct), in0=A(cAB), in1=B(cAB), op=ALU.mult), RT, [Ct.r()])
            dve(lambda e: e.tensor_tensor(out=v3(targ), in0=A(sAB), in1=B(sAB), op=ALU.mult), RT, [targ.r()])
            dve(lambda e: e.tensor_tensor(out=v3(Ct), in0=v3(Ct), in1=v3(targ), op=ALU.subtract), [Ct.r(), targ.r()], [Ct.r()])
            dve(lambda e: e.tensor_tensor(out=v3(St), in0=A(sAB), in1=B(cAB), op=ALU.mult), RT, [St.r()])
            dve(lambda e: e.tensor_tensor(out=v3(targ), in0=A(cAB), in1=B(sAB), op=ALU.mult), RT, [targ.r()])
            dve(lambda e: e.tensor_tensor(out=v3(St), in0=v3(St), in1=v3(targ), op=ALU.add), [St.r(), targ.r()], [St.r()])
        for c in range(KC):
            g0 = c * 8
            chunk_tables(g0)
            for (t_, src) in ((Bp1, "s5_b1"), (Bp2, "s5_b2"), (Cp1, "s5_c1"), (Cp2, "s5_c2")):
                k.dma("sp", t_[:, :, :], self.I[src][j][:, g0:g0 + 8, :], W=[t_.r()])
            if ns:
                k.dma("sp", hin[:, :, 0:16], self.I["st_s5"][j][:, g0:g0 + 8, :], W=[hin.r()])
            if ps == 0:
                dve(lambda e: e.memset(hin[:, :, 16:17], 0.0), [], [hin.r()], acc=True)
            else:
                k.dma("sp", hin[:, :, 16:17], self.o_s5.t[j][:, g0:g0 + 8, 16:17], R=[self.o_s5.r()], W=[hin.r()],
                      allow_slow_non_contiguous=True)
            F1b = P["fre"][:, g0:g0 + 8].unsqueeze(2).broadcast_to([128, 8, 16])
            F2b = P["fim"][:, g0:g0 + 8].unsqueeze(2).broadcast_to([128, 8, 16])
            RP = [P["fre"].r(), P["fim"].r()]
            dve(lambda e: e.tensor_tensor(out=tA[:, :, :], in0=Bp1[:, :, :], in1=F1b, op=ALU.mult), RP + [Bp1.r()], [tA.r()])
            dve(lambda e: e.tensor_tensor(out=tB[:, :, :], in0=Bp2[:, :, :], in1=F2b, op=ALU.mult), RP + [Bp2.r()], [tB.r()])
            dve(lambda e: e.scalar_tensor_tensor(out=Bb1[:, :, :], in0=tB[:, :, :], scalar=Sg(0), in1=tA[:, :, :],
                                                 op0=ALU.mult, op1=ALU.add), [tA.r(), tB.r(), cst.r()], [Bb1.r()])
            dve(lambda e: e.tensor_tensor(out=tA[:, :, :], in0=Bp2[:, :, :], in1=F1b, op=ALU.mult), RP + [Bp2.r()], [tA.r()])
            dve(lambda e: e.tensor_tensor(out=tB[:, :, :], in0=Bp1[:, :, :], in1=F2b, op=ALU.mult), RP + [Bp1.r()], [tB.r()])
            dve(lambda e: e.scalar_tensor_tensor(out=Bb2[:, :, :], in0=tA[:, :, :], scalar=Sg(1), in1=tB[:, :, :],
                                                 op0=ALU.mult, op1=ALU.add), [tA.r(), tB.r(), cst.r()], [Bb2.r()])
            for (Bb, Tc) in ((Bb1, T1c), (Bb2, T2c)):
                bk = self.bank[4]
                k.op("pe", lambda e: e.transpose(bk[:, 0:128], Bb[:, :, :].rearrange("p a b -> p (a b)"), ident),
                     R=[Bb.r(), cst.r()], W=[bk.r()])
                k.op("act", lambda e: e.copy(out=Tc[:, :], in_=bk[:, 0:128]), R=[bk.r()], W=[Tc.r()])
            for g8 in range(8):
                g = g0 + g8
                tables(g8)
                r2 = self.rot("s5g", 2)
                lb1, lb2, c1p, c2p = LB1[r2], LB2[r2], C1p[r2], C2p[r2]
                rm = cst[:, CST["rowm"][0] + g8:CST["rowm"][0] + g8 + 1]
                dve(lambda e: e.tensor_scalar(out=lb1[:, :], in0=T1c[:, :], scalar1=rm, scalar2=None, op0=ALU.mult),
                    [T1c.r(), cst.r()], [lb1.r()])
                dve(lambda e: e.tensor_scalar(out=lb2[:, :], in0=T2c[:, :], scalar1=rm, scalar2=None, op0=ALU.mult),
                    [T2c.r(), cst.r()], [lb2.r()])
                k.op("pool", lambda e: e.memset(c1p[:, :], 0.0), W=[c1p.r()])
                k.op("pool", lambda e: e.memset(c2p[:, :], 0.0), W=[c2p.r()])
                dve(lambda e: e.tensor_scalar(out=c1p[:, g8 * 16:(g8 + 1) * 16], in0=Cp1[:, g8, :], scalar1=Sg(1),
                                              scalar2=None, op0=ALU.mult), [Cp1.r(), cst.r()], [c1p.r()])
                dve(lambda e: e.tensor_scalar(out=c2p[:, g8 * 16:(g8 + 1) * 16], in0=Cp2[:, g8, :], scalar1=-1.0,
                                              scalar2=None, op0=ALU.mult), [Cp2.r()], [c2p.r()])
                rho = P["rho"][:, g:g + 1]
                k.op("act", lambda e: e.activation(out=rhof[:, 0:1024], in_=iota, func=AF.Identity, bias=rho, scale=0.0),
                     R=[cst.r(), P["rho"].r()], W=[rhof.r()])
                if ns:
                    dve(lambda e: e.tensor_scalar(out=rhof[:, 1024:1152], in0=self.cs("mask128"), scalar1=rho,
                                                  scalar2=None, op0=ALU.mult), [cst.r(), P["rho"].r()], [rhof.r()], acc=True)
                for (t0, tw) in self.tbs:
                    i = self.rot("gu", 2)
                    b1, b2 = self.bank[i], self.bank[2 + i]
                    k.op("pe", lambda e: e.matmul(b1[:, 0:tw], lb1[:, :], xn[:, c, t0:t0 + tw], start=True, stop=True),
                         R=[lb1.r(), xn.r()], W=[b1.r()])
                    k.op("pe", lambda e: e.matmul(b2[:, 0:tw], lb2[:, :], xn[:, c, t0:t0 + tw], start=True, stop=True),
                         R=[lb2.r(), xn.r()], W=[b2.r()])
                    if t0 < 1024:
                        cv, sv = Ct[:, t0:t0 + tw], St[:, t0:t0 + tw]
                        z1, z2, mo, yo = b1[:, 0:tw], b2[:, 0:tw], m[:, t0:t0 + tw], yf[:, t0:t0 + tw]
                    else:
                        cv = Ct[:, 0:8].unsqueeze(1).broadcast_to([128, 16, 8])
                        sv = St[:, 0:8].unsqueeze(1).broadcast_to([128, 16, 8])
                        r3 = lambda a: a.rearrange("p (b t) -> p b t", b=16)
                        z1, z2, mo, yo = r3(b1[:, 0:128]), r3(b2[:, 0:128]), r3(m[:, 1024:1152]), r3(yf[:, 1024:1152])
                    dve(lambda e: e.tensor_tensor(out=mo, in0=z1, in1=cv, op=ALU.mult), [b1.r(), Ct.r()], [m.r()], acc=True)
                    dve(lambda e: e.tensor_tensor(out=yo, in0=z2, in1=sv, op=ALU.mult), [b2.r(), St.r()], [yf.r()], acc=True)
                    dve(lambda e: e.tensor_tensor(out=mo, in0=mo, in1=yo, op=ALU.add), [m.r(), yf.r()], [m.r()], acc=True)
                if ns:
                    ms0 = m[:, 1024:1152].rearrange("p (b t) -> p b t", b=16)[:, :, 0]
                    dve(lambda e: e.scalar_tensor_tensor(out=ms0, in0=hin[:, g8, 0:16], scalar=rho, in1=ms0,
                                                         op0=ALU.mult, op1=ALU.add),
                        [hin.r(), m.r(), P["rho"].r()], [m.r()], acc=True)
                dve(lambda e: e.tensor_tensor_scan(out=G[:, 0:TB], data0=rhof[:, 0:TB], data1=m[:, 0:TB],
                                                   initial=hin[:, g8, 16:17], op0=ALU.mult, op1=ALU.add),
                    [rhof.r(), m.r(), hin.r()], [G.r()])
                for (Gb, tab) in ((G1b, Ct), (G2b, St)):
                    k.op("pool", lambda e: e.tensor_tensor(out=Gb[:, 0:1024], in0=G[:, 0:1024], in1=tab[:, 0:1024],
                                                            op=ALU.mult), R=[G.r(), tab.r()], W=[Gb.r()])
                    if ns:
                        r3 = lambda a: a.rearrange("p (b t) -> p b t", b=16)
                        k.op("pool", lambda e: e.tensor_tensor(
                            out=r3(Gb[:, 1024:1152]), in0=r3(G[:, 1024:1152]),
                            in1=tab[:, 0:8].unsqueeze(1).broadcast_to([128, 16, 8]), op=ALU.mult),
                            R=[G.r(), tab.r()], W=[Gb.r()], acc=True)
                for i, (t0, tw) in enumerate(self.tbs):
                    by = self.bank[nbk[i]]
                    k.op("pe", lambda e: e.matmul(by[:, 0:tw], c1p[:, :], G1b[:, t0:t0 + tw], start=(g8 == 0), stop=False),
                         R=[c1p.r(), G1b.r()], W=[by.r()])
                    k.op("pe", lambda e: e.matmul(by[:, 0:tw], c2p[:, :], G2b[:, t0:t0 + tw], start=False, stop=(g8 == 7)),
                         R=[c2p.r(), G2b.r()], W=[by.r()])
                ncol = 17 if ns else 1
                cols = []
                if ns:
                    Gs7 = G[:, 1024:1152].rearrange("p (b t) -> p b t", b=16)[:, :, 7]
                    dve(lambda e: e.tensor_scalar(out=ftmp[:, 0, 0:16], in0=Gs7, scalar1=Ct[:, 7:8], scalar2=None,
                                                  op0=ALU.mult), [G.r(), Ct.r()], [ftmp.r()], acc=True)
                    dve(lambda e: e.tensor_scalar(out=ftmp[:, 1, 0:16], in0=Gs7, scalar1=St[:, 7:8], scalar2=None,
                                                  op0=ALU.mult), [G.r(), St.r()], [ftmp.r()], acc=True)
                dve(lambda e: e.tensor_tensor(out=ftmp[:, 0, 16:17], in0=G[:, 1023:1024], in1=Ct[:, 1023:1024],
                                              op=ALU.mult), [G.r(), Ct.r()], [ftmp.r()], acc=True)
                dve(lambda e: e.tensor_tensor(out=ftmp[:, 1, 16:17], in0=G[:, 1023:1024], in1=St[:, 1023:1024],
                                              op=ALU.mult), [G.r(), St.r()], [ftmp.r()], acc=True)
                c_lo = 0 if ns else 16
                bk = self.bank[4]
                k.op("pe", lambda e: e.matmul(bk[:, c_lo:17], ident, ftmp[:, 0, c_lo:17], start=True, stop=False),
                     R=[cst.r(), ftmp.r()], W=[bk.r()])
                k.op("pe", lambda e: e.matmul(bk[:, c_lo:17], self.cs("J"), ftmp[:, 1, c_lo:17], start=False, stop=True),
                     R=[cst.r(), ftmp.r()], W=[bk.r()])
                k.op("act", lambda e: e.copy(out=hfin[:, g8, c_lo:17], in_=bk[:, c_lo:17]), R=[bk.r()], W=[hfin.r()], acc=True)
            k.dma("sp", self.o_s5.t[j][:, g0:g0 + 8, c_lo:17], hfin[:, :, c_lo:17], R=[hfin.r()], W=[self.o_s5.r()],
                  allow_slow_non_contiguous=True)
            for i, (t0, tw) in enumerate(self.tbs):
                by = self.bank[nbk[i]]
                dve(lambda e: e.scalar_tensor_tensor(out=yf[:, t0:t0 + tw], in0=xn[:, c, t0:t0 + tw],
                                                     scalar=self.vcol("d_c%d" % j, c), in1=by[:, 0:tw],
                                                     op0=ALU.mult, op1=ALU.add),
                    [xn.r(), by.r(), vec.r()], [yf.r()])
            zt = Tile(xn[:, c, :], "xnc")
            zt.reg = xn.r()
            self.gelu_(zt, yf, gt, TB)
        k.barrier()
        Wg = self.W["w_glu_c"][j]
        for s in range(D // 256):
            sg, vg = self.load_slab(Wg, KC, s * 256, 256)
            for oc2 in range(2):
                oc = s * 2 + oc2
                for (t0, tw) in self.tbs:
                    bg = self.bank[self.rot("pj", 2)]
                    for kc in range(KC):
                        k.op("pe", lambda e: e.matmul(bg[:, 0:tw], vg[:, kc, oc2 * 128:(oc2 + 1) * 128],
                                                       xn[:, kc, t0:t0 + tw], start=(kc == 0), stop=(kc == KC - 1)),
                             R=[sg.r(), xn.r()], W=[bg.r()])
                    t1 = yf
                    k.op("act", lambda e: e.activation(out=t1[:, 0:tw], in_=bg[:, 0:tw], func=AF.Sigmoid,
                                                       bias=self.vcol("b_glu_c%d" % j, oc)),
                         R=[bg.r(), vec.r()], W=[t1.r()])
                    dve(lambda e: e.tensor_tensor(out=t1[:, 0:tw], in0=t1[:, 0:tw], in1=xn[:, oc, t0:t0 + tw], op=ALU.mult),
                        [t1.r(), xn.r()], [t1.r()])
                    dve(lambda e: e.tensor_tensor(out=x[:, oc, t0:t0 + tw], in0=x[:, oc, t0:t0 + tw], in1=t1[:, 0:tw],
                                                  op=ALU.add), [t1.r(), x.r(oc)], [x.r(oc)])


def build_inputs(inp, core):
    s = core % 4
    xs = np.asarray(inp["x_sample"], np.float32)[core * NS:(core + 1) * NS].reshape(NS * TS, D)
    xp = np.asarray(inp["x_prompt"], np.float32)[s]
    xT = np.ascontiguousarray(np.concatenate([xp, xs], 0).T)
    pp = np.asarray(inp["p_prompt"], np.float32)[:, s]
    psm = np.asarray(inp["p_sample"], np.float32)[:, core * NS:(core + 1) * NS].reshape(4, NS * TS, 256)
    pT = np.ascontiguousarray(np.concatenate([pp, psm], 1).transpose(0, 2, 1))
    return {"xT": xT, "pT": pT}


_PROG = {}


def kernel(**inputs):
    enable = inputs.pop("_enable", ("ffn", "ple", "mix"))
    if enable not in _PROG:
        _PROG[enable] = Prog(enable)
    prog = _PROG[enable]
    vec = pack_vec(inputs)
    shared = {"vec": vec, "cst": pack_cst()}
    for nm in prog.W:
        shared[nm] = np.ascontiguousarray(np.asarray(inputs[nm], np.float32))
    in_maps = []
    ncores = int(os.environ.get('KCORES', '8'))
    for c in range(ncores):
        m = dict(shared)
        m.update(build_inputs(inputs, c))
        m.update(pack_states(inputs, c))
        in_maps.append(m)
    res = run_bass_kernel_spmd(prog.nc, in_maps, core_ids=list(range(ncores)))
    R = list(res.results) + [res.results[0]] * (8 - ncores)
    global _LAST
    _LAST = R
    y_prompt = np.stack([R[s]["yT"][:, 0:2048].T for s in range(4)])
    y_sample = np.concatenate([R[c]["yT"][:, 2048:].T.reshape(NS, TS, D) for c in range(8)], 0)
    A = np.ascontiguousarray

    def per_layer(fn):
        return np.stack([fn(j) for j in range(2)])
    pw = per_layer(lambda j: np.stack([R[s]["o_wkv"][j][:, 16, :].reshape(16, 64, 64) for s in range(4)]))
    psh = per_layer(lambda j: np.stack([R[s]["o_shift"][j][:, :, 16].T.reshape(-1)[:3360] for s in range(4)]))
    ph = per_layer(lambda j: np.stack([R[s]["o_h"][j][:, :, 16].T.reshape(1024) for s in range(4)]))
    pc = per_layer(lambda j: np.stack([R[s]["o_conv"][j][:, :, 16, :].transpose(2, 1, 0).reshape(3, 1024)
                                       for s in range(4)]))
    pre = per_layer(lambda j: np.stack([R[s]["o_s5"][j][0:64, :, 16].T for s in range(4)]))
    pim = per_layer(lambda j: np.stack([R[s]["o_s5"][j][64:128, :, 16].T for s in range(4)]))
    sw = per_layer(lambda j: np.concatenate([
        R[c]["o_wkv"][j][:, 0:16, :].reshape(16, 8, 16, 8, 64).transpose(2, 0, 1, 3, 4).reshape(16, 16, 64, 64)
        for c in range(8)], 0))
    ssh = per_layer(lambda j: np.concatenate([
        R[c]["o_shift"][j][:, :, 0:16].transpose(2, 1, 0).reshape(16, -1)[:, :3360] for c in range(8)], 0))
    sh = per_layer(lambda j: np.concatenate([
        R[c]["o_h"][j][:, :, 0:16].transpose(2, 1, 0).reshape(16, 1024) for c in range(8)], 0))
    sc = per_layer(lambda j: np.concatenate([
        R[c]["o_conv"][j][:, :, 0:16, :].transpose(2, 3, 1, 0).reshape(16, 3, 1024) for c in range(8)], 0))
    sre = per_layer(lambda j: np.concatenate([R[c]["o_s5"][j][0:64, :, 0:16].transpose(2, 1, 0) for c in range(8)], 0))
    sim = per_layer(lambda j: np.concatenate([R[c]["o_s5"][j][64:128, :, 0:16].transpose(2, 1, 0) for c in range(8)], 0))
    outs = (y_prompt, y_sample, pw, psh, ph, pc, pre, pim, sw, ssh, sh, sc, sre, sim)
    return tuple(A(o.astype(np.float32)) for o in outs)
```
